# Optimizing a Trainium2 kernel written in Bass

```python
import math
import jax
import jax.numpy as jnp
from jax import lax
import numpy as np

D_MODEL = 2048
BATCH = 16
SEQ = 256
DEPTH = 2
DEC_BATCH = 4
DEC_SEQ = 4096
PAST_LEN = 256

GRID_W = 64
F32 = jnp.float32
EPS = 1e-6
N_BRANCH = 3
N_MOD = 6

SSM_HEADS = 24
SSM_HEAD_DIM = 64
SSM_INNER = SSM_HEADS * SSM_HEAD_DIM
SSM_GROUPS = 4
SSM_STATE = 128
SSM_CHUNK = 128
SSM_CONV_DIM = SSM_INNER + 2 * SSM_GROUPS * SSM_STATE

WKV_HEADS = 24
WKV_HEAD = 64
WKV_DIM = WKV_HEADS * WKV_HEAD
DECAY_LORA = 96
ICL_LORA = 96
GATE_LORA = 256
WKV_GN_EPS = 64e-5
WKV_SHIFT_DIM = 3 * WKV_DIM + 2 * DECAY_LORA + 2 * ICL_LORA + GATE_LORA

S5_GROUP_CH = 16
S5_GROUPS = 64
S5_DIM = S5_GROUPS * S5_GROUP_CH
S5_P = 64

D_FF = 5632

IN_SIZES = (SSM_INNER, SSM_CONV_DIM, SSM_HEADS, SSM_HEADS, WKV_SHIFT_DIM, S5_DIM, N_BRANCH * D_MODEL)
N_IN = sum(IN_SIZES)
IN_SPLITS = tuple(int(i) for i in np.cumsum(IN_SIZES)[:-1])
WKV_SPLITS = tuple(int(i) for i in np.cumsum((WKV_DIM, WKV_DIM, WKV_DIM, DECAY_LORA, DECAY_LORA, ICL_LORA, ICL_LORA)))

kernel_name = 'bidir_hybrid_ssd_rwkv7_s5_diffusion_step'


def _rms(x, g):
    xf = x.astype(F32)
    return xf * lax.rsqrt(jnp.mean(xf * xf, axis=-1, keepdims=True) + EPS) * g


def _dwconv3(x, w, rows):
    b, t, ch = x.shape
    xs = x if rows is None else x.reshape(b * rows, GRID_W, ch)
    xp = jnp.pad(xs, ((0, 0), (1, 1), (0, 0)))
    y = xp[:, :-2] * w[0] + xp[:, 1:-1] * w[1] + xp[:, 2:] * w[2]
    return y.reshape(b, t, ch)


def _ssd(x, dt, a_head, bm, cm, h0):
    b, t, nh, hp = x.shape
    ng, ns = bm.shape[2], bm.shape[3]
    hg = nh // ng
    ln = SSM_CHUNK
    nc = t // ln
    x = x.reshape(b, nc, ln, ng, hg, hp)
    dt = dt.reshape(b, nc, ln, ng, hg)
    a_cum = jnp.cumsum(dt * a_head.reshape(ng, hg), axis=2)
    bm = bm.reshape(b, nc, ln, ng, ns)
    cm = cm.reshape(b, nc, ln, ng, ns)
    xdt = x * dt[..., None]
    ac = jnp.moveaxis(a_cum, 2, -1)
    diff = ac[..., :, None] - ac[..., None, :]
    lower = jnp.tril(jnp.ones((ln, ln), dtype=bool))
    decay = jnp.exp(jnp.where(lower, diff, -jnp.inf))
    cb = jnp.einsum('bclgn,bcsgn->bcgls', cm, bm)
    y_diag = jnp.einsum('bcghls,bcsghp->bclghp', cb[:, :, :, None] * decay, xdt)
    decay_to_end = jnp.exp(a_cum[:, :, -1:] - a_cum)
    chunk_states = jnp.einsum('bclgn,bclgh,bclghp->bcghpn', bm, decay_to_end, xdt)
    chunk_decay = jnp.exp(a_cum[:, :, -1])

    def step(h, inp):
        s, d = inp
        return h * d[..., None, None] + s, h

    h_fin, h_enter = lax.scan(step, h0.reshape(b, ng, hg, hp, ns),
                              (jnp.moveaxis(chunk_states, 1, 0), jnp.moveaxis(chunk_decay, 1, 0)))
    h_enter = jnp.moveaxis(h_enter, 0, 1)
    y_off = jnp.einsum('bclgn,bcghpn,bclgh->bclghp', cm, h_enter, jnp.exp(a_cum))
    return (y_diag + y_off).reshape(b, t, nh, hp), h_fin.reshape(b, nh, hp, ns)


def _mamba_branch(z, xbc, dtf, dtb, h0, p, rows):
    b, t, _ = z.shape
    xbc = jax.nn.silu(_dwconv3(xbc, p['ssm_conv_w'], rows) + p['ssm_conv_b'])
    gn = SSM_GROUPS * SSM_STATE
    xs = xbc[..., :SSM_INNER].reshape(b, t, SSM_HEADS, SSM_HEAD_DIM)
    bm = xbc[..., SSM_INNER:SSM_INNER + gn].reshape(b, t, SSM_GROUPS, SSM_STATE)
    cm = xbc[..., SSM_INNER + gn:].reshape(b, t, SSM_GROUPS, SSM_STATE)
    a = -jnp.exp(p['ssm_a_log'].astype(F32))
    dt_f = jax.nn.softplus(dtf + p['ssm_dt_bias'][0])
    dt_b = jax.nn.softplus(dtb + p['ssm_dt_bias'][1])
    y_f, h_f = _ssd(xs, dt_f, a[0], bm, cm, h0[:, 0])
    y_b, h_b = _ssd(jnp.flip(xs, 1), jnp.flip(dt_b, 1), a[1], jnp.flip(bm, 1), jnp.flip(cm, 1), h0[:, 1])
    y = y_f + jnp.flip(y_b, 1) + xs * p['ssm_d'][:, None]
    y = (y.reshape(b, t, SSM_INNER) * jax.nn.silu(z)).reshape(b, t, SSM_GROUPS, SSM_INNER // SSM_GROUPS)
    y = _rms(y, p['ssm_norm'].reshape(SSM_GROUPS, SSM_INNER // SSM_GROUPS)).reshape(b, t, SSM_INNER)
    return y @ p['w_ssm_out'], jnp.stack([h_f, h_b], axis=1)


def _wkv_scan(r, w, k, v, kk, a, s0):
    def step(s, inp):
        r_t, w_t, k_t, v_t, kk_t, a_t = inp
        sa = jnp.einsum('bhvk,bhk->bhv', s, -kk_t)
        s = s * w_t[:, :, None, :] + sa[..., None] * (kk_t * a_t)[:, :, None, :] + v_t[..., None] * k_t[:, :, None, :]
        return s, jnp.einsum('bhvk,bhk->bhv', s, r_t)

    seq = tuple(jnp.moveaxis(z, 1, 0) for z in (r, w, k, v, kk, a))
    s_fin, ys = lax.scan(step, s0, seq)
    return jnp.moveaxis(ys, 0, 1), s_fin


def _rwkv_branch(cols, s0, p, rows):
    b, t, _ = cols.shape
    mp, mn = p['wkv_mu_prev'], p['wkv_mu_next']
    cols = _dwconv3(cols, jnp.stack([mp, 1.0 - mp - mn, mn]), rows)
    r, k, v, wfd, wbd, afd, abd, gd = jnp.split(cols, WKV_SPLITS, axis=-1)

    def heads(z):
        return z.reshape(b, t, WKV_HEADS, WKV_HEAD)

    g = jax.nn.sigmoid(gd) @ p['wkv_g_up']
    kk = heads(k * p['wkv_k_k'])
    kk = kk * lax.rsqrt(jnp.maximum(jnp.sum(kk * kk, -1, keepdims=True), 1e-24))
    rh, kh, vh = heads(r), heads(k), heads(v)
    outs, finals = [], []
    for d, (wd, ad) in enumerate(((wfd, afd), (wbd, abd))):
        wl = -jax.nn.softplus(-(p['wkv_w0'][d] + jnp.tanh(wd) @ p['wkv_w_up'][d])) - 0.5
        decay = jnp.exp(-jnp.exp(wl))
        a = jax.nn.sigmoid(p['wkv_a0'][d] + ad @ p['wkv_a_up'][d])
        kd = k * (1.0 + (a - 1.0) * p['wkv_k_a'])
        seq = (rh, heads(decay), heads(kd), vh, kk, heads(a))
        if d == 1:
            seq = tuple(jnp.flip(z, 1) for z in seq)
        o, s = _wkv_scan(*seq, s0[:, d])
        outs.append(o if d == 0 else jnp.flip(o, 1))
        finals.append(s)
    o = outs[0] + outs[1]
    mu = jnp.mean(o, -1, keepdims=True)
    var = jnp.mean(jnp.square(o - mu), -1, keepdims=True)
    o = ((o - mu) * lax.rsqrt(var + WKV_GN_EPS)).reshape(b, t, WKV_DIM) * p['wkv_ln_w'] + p['wkv_ln_b']
    bonus = (jnp.sum(rh * kh * p['wkv_r_k'], -1, keepdims=True) * vh).reshape(b, t, WKV_DIM)
    return ((o + bonus) * g) @ p['w_wkv_out'], jnp.stack(finals, axis=1)


def _cplx(re, im):
    return lax.complex(re.astype(F32), im.astype(F32))


def _s5_scan(a_bar, bu, h0):
    bu = bu.at[0].add(a_bar * h0)
    a = jnp.broadcast_to(a_bar, (bu.shape[0], 1) + a_bar.shape)

    def combine(e1, e2):
        a1, b1 = e1
        a2, b2 = e2
        return a1 * a2, a2 * b1 + b2

    _, xs = lax.associative_scan(combine, (a, bu), axis=0)
    return xs, xs[-1]


def _s5_branch(u, h0, p):
    b, t, _ = u.shape
    ug = u.reshape(b, t, S5_GROUPS, S5_GROUP_CH)
    ugc = ug.astype(jnp.complex64)
    y = ug * p['s5_d'].reshape(S5_GROUPS, S5_GROUP_CH)
    finals = []
    for d in range(2):
        lam = _cplx(jnp.minimum(p['s5_lam_re'][d].astype(F32), -1e-4), p['s5_lam_im'][d])
        dt = jnp.exp(p['s5_log_dt'][d].astype(F32))[:, None]
        a_bar = jnp.exp(lam * dt)
        b_bar = ((a_bar - 1.0) / lam)[..., None] * _cplx(p['s5_b_re'][d], p['s5_b_im'][d])
        c_mat = _cplx(p['s5_c_re'][d], p['s5_c_im'][d])
        src = ugc if d == 0 else jnp.flip(ugc, 1)
        xs, fin = _s5_scan(a_bar, jnp.einsum('gpc,btgc->tbgp', b_bar, src), h0[:, d])
        if d == 1:
            xs = jnp.flip(xs, 0)
        y = y + jnp.einsum('gcp,tbgp->btgc', c_mat, xs).real
        finals.append(fin)
    y = jax.nn.gelu(y.reshape(b, t, S5_DIM))
    glu = y @ p['w_s5_glu']
    return glu[..., :D_MODEL] * jax.nn.sigmoid(glu[..., D_MODEL:]), jnp.stack(finals, axis=1)


def _mixer(h, states, p, rows):
    b, t, _ = h.shape
    ssm_h0, wkv_s0, s5_h0 = states
    z, xbc, dtf, dtb, wkv_cols, u, gates = jnp.split(h @ p['w_in'], IN_SPLITS, axis=-1)
    y_ssm, ssm_s = _mamba_branch(z, xbc, dtf, dtb, ssm_h0, p, rows)
    y_wkv, wkv_s = _rwkv_branch(wkv_cols, wkv_s0, p, rows)
    y_s5, s5_s = _s5_branch(u, s5_h0, p)
    gts = jax.nn.sigmoid(gates).reshape(b, t, N_BRANCH, D_MODEL)
    merged = gts[:, :, 0] * y_ssm + gts[:, :, 1] * y_wkv + gts[:, :, 2] * y_s5
    return merged @ p['w_out'], (ssm_s, wkv_s, s5_s)


def _conv_ffn(h, p, rows):
    gu = h @ p['w_ffn_in']
    gate = _dwconv3(gu[..., :D_FF], p['ffn_conv_w'], rows) + p['ffn_conv_b']
    return (jax.nn.silu(gate) * gu[..., D_FF:]) @ p['w_ffn_out']


def _modulation(cond, p):
    m = jax.nn.silu(cond.astype(F32)) @ p['w_mod'] + p['b_mod']
    return m.reshape(cond.shape[0], N_MOD, D_MODEL)


def _layer(x, mod, states, p, rows):
    sh1, sc1, g1, sh2, sc2, g2 = (mod[:, None, i] for i in range(N_MOD))
    h = _rms(x, p['norm_mix_pre']) * (1.0 + sc1) + sh1
    y, new_states = _mixer(h, states, p, rows)
    x = x + g1 * _rms(y, p['norm_mix_post'])
    h = _rms(x, p['norm_ffn_pre']) * (1.0 + sc2) + sh2
    x = x + g2 * _rms(_conv_ffn(h, p, rows), p['norm_ffn_post'])
    return x, new_states


def setup_inputs(seed: int = 0) -> dict:
    key = jax.random.key(seed)
    ks = iter(jax.random.split(key, 96))

    def nrm(shape, scale):
        return jax.random.normal(next(ks), shape, F32) * scale

    def uni(shape, lo, hi):
        return jax.random.uniform(next(ks), shape, F32, lo, hi)

    L, D = DEPTH, D_MODEL
    dt0 = jnp.exp(uni((L, 2, SSM_HEADS), math.log(1e-3), math.log(1e-1)))
    lam_im0 = math.pi * jnp.arange(S5_P, dtype=F32)
    return {
        'x_prompt': nrm((BATCH, SEQ, D), 1.0),
        'x_sample': nrm((DEC_BATCH, DEC_SEQ, D), 1.0),
        'state_ssm': nrm((DEC_BATCH, L, 2, SSM_HEADS, SSM_HEAD_DIM, SSM_STATE), 0.5),
        'state_wkv': nrm((DEC_BATCH, L, 2, WKV_HEADS, WKV_HEAD, WKV_HEAD), 0.5),
        'state_s5_re': nrm((DEC_BATCH, L, 2, S5_GROUPS, S5_P), 0.5),
        'state_s5_im': nrm((DEC_BATCH, L, 2, S5_GROUPS, S5_P), 0.5),
        'c': nrm((DEC_BATCH, D), 1.0),
        'c_ctx': nrm((D,), 1.0),
        'w_mod': nrm((L, D, N_MOD * D), D ** -0.5),
        'b_mod': nrm((L, N_MOD * D), 0.01),
        'norm_mix_pre': 1.0 + nrm((L, D), 0.01),
        'norm_mix_post': 1.0 + nrm((L, D), 0.01),
        'norm_ffn_pre': 1.0 + nrm((L, D), 0.01),
        'norm_ffn_post': 1.0 + nrm((L, D), 0.01),
        'w_in': nrm((L, D, N_IN), D ** -0.5),
        'ssm_conv_w': nrm((L, 3, SSM_CONV_DIM), 0.5),
        'ssm_conv_b': nrm((L, SSM_CONV_DIM), 0.01),
        'ssm_dt_bias': dt0 + jnp.log(-jnp.expm1(-dt0)),
        'ssm_a_log': jnp.log(uni((L, 2, SSM_HEADS), 1.0, 16.0)),
        'ssm_d': 1.0 + nrm((L, SSM_HEADS), 0.1),
        'ssm_norm': 1.0 + nrm((L, SSM_INNER), 0.01),
        'w_ssm_out': nrm((L, SSM_INNER, D), SSM_INNER ** -0.5),
        'wkv_mu_prev': uni((L, WKV_SHIFT_DIM), 0.0, 0.5),
        'wkv_mu_next': uni((L, WKV_SHIFT_DIM), 0.0, 0.5),
        'wkv_w0': uni((L, 2, WKV_DIM), -6.0, 1.0),
        'wkv_w_up': nrm((L, 2, DECAY_LORA, WKV_DIM), 0.1 * DECAY_LORA ** -0.5),
        'wkv_a0': nrm((L, 2, WKV_DIM), 0.1),
        'wkv_a_up': nrm((L, 2, ICL_LORA, WKV_DIM), 0.1 * ICL_LORA ** -0.5),
        'wkv_g_up': nrm((L, GATE_LORA, WKV_DIM), GATE_LORA ** -0.5),
        'wkv_k_k': 0.85 + nrm((L, WKV_DIM), 0.05),
        'wkv_k_a': 1.0 + nrm((L, WKV_DIM), 0.05),
        'wkv_r_k': nrm((L, WKV_HEADS, WKV_HEAD), 0.1),
        'wkv_ln_w': 1.0 + nrm((L, WKV_DIM), 0.01),
        'wkv_ln_b': nrm((L, WKV_DIM), 0.01),
        'w_wkv_out': nrm((L, WKV_DIM, D), WKV_DIM ** -0.5),
        's5_lam_re': -0.5 + nrm((L, 2, S5_GROUPS, S5_P), 0.01),
        's5_lam_im': lam_im0 + nrm((L, 2, S5_GROUPS, S5_P), 0.01),
        's5_log_dt': uni((L, 2, S5_GROUPS), math.log(1e-3), math.log(1e-1)),
        's5_b_re': nrm((L, 2, S5_GROUPS, S5_P, S5_GROUP_CH), (2 * S5_GROUP_CH) ** -0.5),
        's5_b_im': nrm((L, 2, S5_GROUPS, S5_P, S5_GROUP_CH), (2 * S5_GROUP_CH) ** -0.5),
        's5_c_re': nrm((L, 2, S5_GROUPS, S5_GROUP_CH, S5_P), (2 * S5_P) ** -0.5),
        's5_c_im': nrm((L, 2, S5_GROUPS, S5_GROUP_CH, S5_P), (2 * S5_P) ** -0.5),
        's5_d': nrm((L, S5_DIM), 1.0),
        'w_s5_glu': nrm((L, S5_DIM, 2 * D), S5_DIM ** -0.5),
        'w_out': nrm((L, D, D), D ** -0.5),
        'w_ffn_in': nrm((L, D, 2 * D_FF), D ** -0.5),
        'ffn_conv_w': nrm((L, 3, D_FF), 0.5),
        'ffn_conv_b': nrm((L, D_FF), 0.01),
        'w_ffn_out': nrm((L, D_FF, D), D_FF ** -0.5),
    }


def reference(x_prompt, x_sample, state_ssm, state_wkv, state_s5_re, state_s5_im, c, c_ctx,
              w_mod, b_mod, norm_mix_pre, norm_mix_post, norm_ffn_pre, norm_ffn_post, w_in,
              ssm_conv_w, ssm_conv_b, ssm_dt_bias, ssm_a_log, ssm_d, ssm_norm, w_ssm_out,
              wkv_mu_prev, wkv_mu_next, wkv_w0, wkv_w_up, wkv_a0, wkv_a_up, wkv_g_up, wkv_k_k, wkv_k_a,
              wkv_r_k, wkv_ln_w, wkv_ln_b, w_wkv_out,
              s5_lam_re, s5_lam_im, s5_log_dt, s5_b_re, s5_b_im, s5_c_re, s5_c_im, s5_d, w_s5_glu,
              w_out, w_ffn_in, ffn_conv_w, ffn_conv_b, w_ffn_out):
    stacked = dict(
        w_mod=w_mod, b_mod=b_mod, norm_mix_pre=norm_mix_pre, norm_mix_post=norm_mix_post,
        norm_ffn_pre=norm_ffn_pre, norm_ffn_post=norm_ffn_post, w_in=w_in,
        ssm_conv_w=ssm_conv_w, ssm_conv_b=ssm_conv_b, ssm_dt_bias=ssm_dt_bias, ssm_a_log=ssm_a_log,
        ssm_d=ssm_d, ssm_norm=ssm_norm, w_ssm_out=w_ssm_out,
        wkv_mu_prev=wkv_mu_prev, wkv_mu_next=wkv_mu_next, wkv_w0=wkv_w0, wkv_w_up=wkv_w_up,
        wkv_a0=wkv_a0, wkv_a_up=wkv_a_up, wkv_g_up=wkv_g_up, wkv_k_k=wkv_k_k, wkv_k_a=wkv_k_a,
        wkv_r_k=wkv_r_k, wkv_ln_w=wkv_ln_w, wkv_ln_b=wkv_ln_b, w_wkv_out=w_wkv_out,
        s5_lam_re=s5_lam_re, s5_lam_im=s5_lam_im, s5_log_dt=s5_log_dt, s5_b_re=s5_b_re,
        s5_b_im=s5_b_im, s5_c_re=s5_c_re, s5_c_im=s5_c_im, s5_d=s5_d, w_s5_glu=w_s5_glu,
        w_out=w_out, w_ffn_in=w_ffn_in, ffn_conv_w=ffn_conv_w, ffn_conv_b=ffn_conv_b,
        w_ffn_out=w_ffn_out)
    rows = x_sample.shape[1] // GRID_W
    bp = x_prompt.shape[0]
    zero_states = (jnp.zeros((bp, 2, SSM_HEADS, SSM_HEAD_DIM, SSM_STATE), F32),
                   jnp.zeros((bp, 2, WKV_HEADS, WKV_HEAD, WKV_HEAD), F32),
                   jnp.zeros((bp, 2, S5_GROUPS, S5_P), jnp.complex64))
    y_prompt = x_prompt.astype(F32)
    y_sample = x_sample.astype(F32)
    ctx_ssm, ctx_wkv, ctx_s5 = [], [], []
    for l in range(DEPTH):
        lp = {name: arr[l] for name, arr in stacked.items()}
        y_prompt, (s_ssm, s_wkv, s_s5) = _layer(y_prompt, _modulation(c_ctx[None], lp), zero_states, lp, None)
        ctx_ssm.append(s_ssm)
        ctx_wkv.append(s_wkv)
        ctx_s5.append(s_s5)
        cached = (state_ssm[:, l].astype(F32), state_wkv[:, l].astype(F32),
                  _cplx(state_s5_re[:, l], state_s5_im[:, l]))
        y_sample, _ = _layer(y_sample, _modulation(c, lp), cached, lp, rows)
    new_state_s5 = jnp.stack(ctx_s5, axis=1)
    return (y_prompt, y_sample, jnp.stack(ctx_ssm, axis=1), jnp.stack(ctx_wkv, axis=1), new_state_s5.real, new_state_s5.imag)
```

```python
import contextlib
import math
import numpy as np
import concourse.bass as bass
import concourse.mybir as mybir
from concourse.bass_utils import run_bass_kernel_spmd

F32 = mybir.dt.float32
BF16 = mybir.dt.bfloat16
I32 = mybir.dt.int32
AF = mybir.ActivationFunctionType
ALU = mybir.AluOpType
AX = mybir.AxisListType

D = 2048
KC = 16
DEPTH = 2
N_IN = 16560
D_FF = 5632
TS = 4096
TP = 1024
BLK = 512
EPS = 1e-6
TWO_PI = 2.0 * math.pi
SIN_SCALE = 6.28318

C_Z = 0
C_XBC = 1536
C_DTF = 4096
C_WKV = 4144
C_U = 9392
C_GATE = 10416
NWKV = 5248


class Sched:
    def __init__(self, nc, es, n_dma_sems=24):
        self.nc = nc
        self.eng = {"pe": nc.tensor, "act": nc.scalar, "dve": nc.vector, "pool": nc.gpsimd, "sp": nc.sync}
        self.sem = {e: es.enter_context(nc.semaphore("sem_" + e)) for e in ("pe", "act", "dve", "pool")}
        self.cnt = {e: 0 for e in self.sem}
        self.dsem = [es.enter_context(nc.semaphore("dsem%d" % i)) for i in range(n_dma_sems)]
        self.dval = [0] * n_dma_sems
        self.dnext = 0
        self.waited = {e: {} for e in self.eng}
        self.lastw = {}
        self.readers = {}
        self.ninst = 0

    def _wait(self, e, tok, force=False):
        sem, val, src = tok
        w = self.waited[e]
        if w.get(id(sem), 0) >= val:
            return
        if src == e == "pe" and not force:
            return
        self.eng[e].wait_ge(sem, val)
        w[id(sem)] = val

    def _deps(self, e, r, w):
        for k in list(r) + list(w):
            t = self.lastw.get(k)
            if t is not None:
                self._wait(e, t)
        for k in w:
            for t in self.readers.get(k, ()):
                self._wait(e, t)

    def _commit(self, tok, r, w):
        for k in w:
            self.lastw[k] = tok
            self.readers[k] = []
        for k in r:
            lst = self.readers.setdefault(k, [])
            lst.append(tok)
            if len(lst) > 64:
                best = {}
                for t in lst:
                    if id(t[0]) not in best or best[id(t[0])][1] < t[1]:
                        best[id(t[0])] = t
                self.readers[k] = list(best.values())

    def op(self, e, fn, r=(), w=()):
        self._deps(e, r, w)
        ins = fn(self.eng[e])
        self.cnt[e] += 1
        ins.then_inc(self.sem[e], 1)
        tok = (self.sem[e], self.cnt[e], e)
        self._commit(tok, r, w)
        self.ninst += 1
        return tok

    def dma(self, q, out, in_, r=(), w=(), **kw):
        i = self.dnext
        self.dnext = (self.dnext + 1) % len(self.dsem)
        sem = self.dsem[i]
        if self.dval[i] > 0:
            self._wait(q, (sem, self.dval[i], "dma"))
        self._deps(q, r, w)
        ins = self.eng[q].dma_start(out=out, in_=in_, **kw)
        self.dval[i] += 16
        ins.then_inc(sem, 16)
        tok = (sem, self.dval[i], "dma")
        self._commit(tok, r, w)
        self.ninst += 1
        return tok

    def barrier(self):
        toks = [(self.sem[e], self.cnt[e], e) for e in self.sem if self.cnt[e] > 0]
        toks += [(self.dsem[i], self.dval[i], "dma") for i in range(len(self.dsem)) if self.dval[i] > 0]
        for e in self.eng:
            for t in toks:
                self._wait(e, t, force=True)
        self.lastw.clear()
        self.readers.clear()


class KB:
    def __init__(self, debug=None):
        self.debug = debug or {}
        self.nc = bass.Bass("TRN2", target_bir_lowering=False)
        self.I = {}
        self.O = {}

    def din(self, name, shape, dt=F32):
        self.I[name] = self.nc.dram_tensor(name, list(shape), dt, kind="ExternalInput").ap()
        return self.I[name]

    def dout(self, name, shape, dt=F32):
        self.O[name] = self.nc.dram_tensor(name, list(shape), dt, kind="ExternalOutput").ap()
        return self.O[name]

    def dscr(self, name, shape, dt=F32):
        kind = "ExternalOutput" if name in self.debug.get("dump", ()) else "Internal"
        return self.nc.dram_tensor(name, list(shape), dt, kind=kind).ap()


PARAM_SHAPES = {
    "w_mod": (DEPTH, D, 6 * D), "b_mod": (DEPTH, 6 * D),
    "norm_mix_pre": (DEPTH, D), "norm_mix_post": (DEPTH, D), "norm_ffn_pre": (DEPTH, D), "norm_ffn_post": (DEPTH, D),
    "w_in": (DEPTH, D, N_IN),
    "ssm_conv_w": (DEPTH, 3, 2560), "ssm_conv_b": (DEPTH, 2560), "ssm_dt_bias": (DEPTH, 48),
    "ssm_a_log": (DEPTH, 48), "ssm_d": (DEPTH, 24), "ssm_norm": (DEPTH, 1536), "w_ssm_out": (DEPTH, 1536, D),
    "wkv_mu_prev": (DEPTH, NWKV), "wkv_mu_next": (DEPTH, NWKV), "wkv_w0": (DEPTH, 2, 1536),
    "wkv_w_up": (DEPTH, 2, 96, 1536), "wkv_a0": (DEPTH, 2, 1536), "wkv_a_up": (DEPTH, 2, 96, 1536),
    "wkv_g_up": (DEPTH, 256, 1536), "wkv_k_k": (DEPTH, 1536), "wkv_k_a": (DEPTH, 1536), "wkv_r_k": (DEPTH, 1536),
    "wkv_ln_w": (DEPTH, 1536), "wkv_ln_b": (DEPTH, 1536), "w_wkv_out": (DEPTH, 1536, D),
    "s5_lam_re": (DEPTH, 2, 64, 64), "s5_lam_im": (DEPTH, 2, 64, 64), "s5_log_dt": (DEPTH, 2, 64),
    "s5_b_re": (DEPTH, 2, 64, 64, 16), "s5_b_im": (DEPTH, 2, 64, 64, 16),
    "s5_c_re": (DEPTH, 2, 64, 16, 64), "s5_c_im": (DEPTH, 2, 64, 16, 64), "s5_d": (DEPTH, 1024),
    "w_s5_glu": (DEPTH, 1024, 2 * D), "w_out": (DEPTH, D, D), "w_ffn_in": (DEPTH, D, 2 * D_FF),
    "ffn_conv_w": (DEPTH, 3, D_FF), "ffn_conv_b": (DEPTH, D_FF), "w_ffn_out": (DEPTH, D_FF, D),
}


def build(debug=None):
    debug = debug or {}
    kb = KB(debug)
    nc = kb.nc
    I = kb.I
    O = kb.O
    es = contextlib.ExitStack()

    kb.din("xs", [TS, D])
    kb.din("xp", [TP, D])
    kb.din("cond2", [2, D])
    kb.din("st_ssm", [DEPTH, 2, 24, 64, 128])
    kb.din("st_wkv", [DEPTH, 2, 24, 64, 64])
    kb.din("st_s5re", [DEPTH, 2, 64, 64])
    kb.din("st_s5im", [DEPTH, 2, 64, 64])
    for nm, shp in PARAM_SHAPES.items():
        kb.din(nm, shp)
    kb.din("ident", [128, 128])
    kb.din("tri", [128, 128])
    kb.din("trit", [128, 128])
    kb.din("jrev", [128, 128])
    kb.din("mask48", [48, 768])
    kb.din("jj", [128, 512])
    kb.din("zrow", [1, 512])

    kb.dout("ys", [TS, D])
    kb.dout("yp", [TP, D])
    kb.dout("ns_ssm", [4, DEPTH, 2, 24, 64, 128])
    kb.dout("ns_wkv", [4, DEPTH, 2, 24, 64, 64])
    kb.dout("ns_s5re", [4, DEPTH, 2, 64, 64])
    kb.dout("ns_s5im", [4, DEPTH, 2, 64, 64])

    MOD = kb.dscr("MOD", [2, 6 * D])
    PA = kb.dscr("PA", [TS, C_U])
    PB = kb.dscr("PB", [TS, N_IN - C_U])

    class _P:
        def __getitem__(self, key):
            rs, cs = key
            c0, c1 = cs.start, cs.stop
            if c1 <= C_U:
                return PA[rs, c0:c1]
            assert c0 >= C_U, (c0, c1)
            return PB[rs, c0 - C_U:c1 - C_U]
    P = _P()
    GU = kb.dscr("GU", [TS, 2 * D_FF])
    XA = kb.dscr("XA", [TS, D])
    XB = kb.dscr("XB", [TS, D])
    XPA = kb.dscr("XPA", [TP, D])
    XPB = kb.dscr("XPB", [TP, D])
    MG = kb.dscr("MG", [TS, D])
    XC = kb.dscr("XC", [TS, 2560])
    SH = kb.dscr("SH", [TS, NWKV])
    BCT = kb.dscr("BCT", [TS // 128, 128, 8, 128], BF16)
    DT = kb.dscr("DT", [TS, 48])
    YS = kb.dscr("YS", [TS, 1536])
    ZW = kb.dscr("ZW", [TS, 1536])
    BD = [kb.dscr("BD%d" % d, [TS, 1536]) for d in range(2)]
    KD = [kb.dscr("KD%d" % d, [TS, 1536]) for d in range(2)]
    BRKR = [kb.dscr("BRKR%d" % d, [TS, 48]) for d in range(2)]
    WT = [kb.dscr("WT%d" % d, [128, 12, TS]) for d in range(2)]
    WRT = [kb.dscr("WRT%d" % d, [128, 12, TS]) for d in range(2)]
    NKKT = kb.dscr("NKKT", [128, 12, TS])
    GG = kb.dscr("GG", [TS, 1536])
    VEO = [kb.dscr("VEO%d" % d, [TS, 768]) for d in range(2)]
    RK = kb.dscr("RK", [TS, 24])
    SAY = [kb.dscr("SAY%d" % d, [48, TS, 64]) for d in range(2)]
    UT2 = [kb.dscr("UT2_%d" % d, [32, 32, TS]) for d in range(2)]
    YS5 = [kb.dscr("YS5_%d" % d, [TS, 1024]) for d in range(2)]
    ACTS = kb.dscr("ACTS", [TS, D_FF])

    with es:
        S = Sched(nc, es)
        kb.S = S
        gsb = lambda name, shape, dt=F32: es.enter_context(nc.sbuf_tensor(name, list(shape), dt))
        psb = [es.enter_context(nc.psum_tensor("psb%d" % i, [128, 512], F32)) for i in range(8)]
        PK = ["psb%d" % i for i in range(8)]
        ident = gsb("ident_sb", [128, 128])
        tri = gsb("tri_sb", [128, 128])
        trit = gsb("trit_sb", [128, 128])
        jrev = gsb("jrev_sb", [128, 128])
        ones = gsb("ones_sb", [128, 128])
        S.dma("sp", ident[:], I["ident"][:, :], w=["ident"])
        S.dma("sp", tri[:], I["tri"][:, :], w=["tri"])
        S.dma("sp", trit[:], I["trit"][:, :], w=["trit"])
        S.dma("sp", jrev[:], I["jrev"][:, :], w=["jrev"])
        S.op("dve", lambda e: e.memset(ones[:], 1.0), w=["ones"])

        _uid = [0]

        def uname(n):
            _uid[0] += 1
            return "%s_%d" % (n, _uid[0])

        def bcload(dst, key, src1d, q="sp"):
            S.dma(q, dst, src1d.partition_broadcast(dst.shape[0]), r=["MOD"], w=[key])

        def transposes_to(src, skey, nblk, dst_fn, dkey, bw=128, pbanks=(6, 7), eng="act", scale=None, rhs=None,
                          rkey=None, inw=128):
            per = 512 // inw
            for q in range(0, nblk, per):
                nj = min(per, nblk - q)
                bi = pbanks[(q // per) % len(pbanks)]
                pt = psb[bi]
                for j in range(nj):
                    blk = src[:, (q + j) * bw:(q + j + 1) * bw]
                    if rhs is None:
                        S.op("pe", lambda e, j=j, blk=blk: e.transpose(pt[0:bw, j * inw:(j + 1) * inw], blk,
                                                                        ident[0:inw, 0:inw]),
                             r=[skey, "ident"], w=[PK[bi]])
                    else:
                        S.op("pe", lambda e, j=j, blk=blk: e.matmul(pt[0:bw, j * inw:(j + 1) * inw], lhsT=blk,
                                                                     rhs=rhs, start=True, stop=True),
                             r=[skey, rkey], w=[PK[bi]])
                src_ps = pt[0:bw, 0:nj * inw].rearrange("p (j t) -> p j t", j=nj)
                dst = dst_fn(q, nj)
                if eng == "act":
                    if scale is None:
                        S.op("act", lambda e: e.activation(out=dst, in_=src_ps, func=AF.Copy), r=[PK[bi]], w=[dkey])
                    else:
                        S.op("act", lambda e: e.activation(out=dst, in_=src_ps, func=AF.Copy, scale=scale),
                             r=[PK[bi]], w=[dkey])
                else:
                    S.op("dve", lambda e: e.tensor_copy(out=dst, in_=src_ps), r=[PK[bi]], w=[dkey])

        def phase_mod(l):
            with contextlib.ExitStack() as ps:
                lsb = lambda name, shape, dt=F32: ps.enter_context(nc.sbuf_tensor(uname(name), list(shape), dt))
                cT = lsb("cT", [128, 2, KC])
                sg = lsb("sg", [128, 2, KC])
                wm = [lsb("wm%d" % i, [128, KC, 512]) for i in range(2)]
                bm = lsb("bm", [1, 6 * D])
                mo = lsb("mo", [2, 6 * D])
                with nc.allow_non_contiguous_dma(reason="tiny cond transpose"):
                    S.dma("sp", cT[:], I["cond2"].rearrange("j (k p) -> p j k", p=128), w=["cT"])
                S.dma("sp", bm[:], I["b_mod"][l:l + 1, :], w=["bm"])
                S.op("act", lambda e: e.activation(out=sg[:], in_=cT[:], func=AF.Sigmoid), r=["cT"], w=["sg"])
                S.op("dve", lambda e: e.tensor_mul(out=sg[:], in0=sg[:], in1=cT[:]), r=["cT", "sg"], w=["sg"])
                for nb in range(24):
                    wb = wm[nb % 2]
                    wk = "wm%d" % (nb % 2)
                    S.dma("sp", wb[:], I["w_mod"][l, :, nb * 512:(nb + 1) * 512].rearrange("(k p) n -> p k n", p=128),
                          w=[wk])
                    pt = psb[nb % 2]
                    pk = PK[nb % 2]
                    for k in range(KC):
                        S.op("pe", lambda e, k=k: e.matmul(pt[0:2, :], lhsT=sg[:, :, k], rhs=wb[:, k, :],
                                                           start=(k == 0), stop=False), r=["sg", wk], w=[pk])
                    S.op("pe", lambda e: e.matmul(pt[0:2, :], lhsT=ones[0:1, 0:2], rhs=bm[:, nb * 512:(nb + 1) * 512],
                                                  start=False, stop=True), r=["ones", "bm"], w=[pk])
                    S.op("act", lambda e: e.activation(out=mo[:, nb * 512:(nb + 1) * 512], in_=pt[0:2, :],
                                                       func=AF.Copy), r=[pk], w=["mo"])
                S.dma("sp", MOD[:, :], mo[:], r=["mo"], w=["MOD"])
                S.barrier()

        def rms_rstd(ss, key, n, eps):
            S.op("dve", lambda e: e.tensor_scalar(out=ss, in0=ss, scalar1=1.0 / n, scalar2=eps, op0=ALU.mult,
                                                  op1=ALU.add), r=[key], w=[key])
            S.op("act", lambda e: e.activation(out=ss, in_=ss, func=AF.Sqrt), r=[key], w=[key])
            S.op("dve", lambda e: e.reciprocal(out=ss, in_=ss), r=[key], w=[key])

        def norm_to_fm(ls, xsrc, t0, ntile, hT, Gt, SHt):
            for i in range(ntile):
                xt = ls["xt"][i % 2]
                xk = "xt%d" % (i % 2)
                S.dma("sp", xt[:], xsrc[t0 + i * 128:t0 + (i + 1) * 128, :], r=["XSRC"], w=[xk])
                S.op("act", lambda e: e.activation(out=ls["junk"][:], in_=xt[:], func=AF.Square,
                                                   accum_out=ls["ss"][:]), r=[xk], w=["junk", "ss"])
                rms_rstd(ls["ss"][:], "ss", D, EPS)
                S.op("dve", lambda e: e.scalar_tensor_tensor(out=ls["junk"][:], in0=xt[:], scalar=ls["ss"][:, 0:1],
                                                             in1=Gt[:], op0=ALU.mult, op1=ALU.mult),
                     r=[xk, "ss", "Gt"], w=["junk"])
                S.op("dve", lambda e: e.tensor_add(out=ls["junk"][:], in0=ls["junk"][:], in1=SHt[:]),
                     r=["junk", "SHt"], w=["junk"])
                transposes_to(ls["junk"][:], "junk", KC,
                              lambda q, nj, i=i: hT[:, q:q + nj, i * 128:(i + 1) * 128], "hT")

        def proj(wbs, hT, hkey, kchunks, ncols, ntile, epilogue, wload, cwmax=512, wmul=1):
            nb = (ncols + cwmax - 1) // cwmax
            for b in range(nb):
                c0 = b * cwmax
                cw = min(cwmax, ncols - c0)
                nw = cw * wmul
                wflat = wbs[b % 2]
                wk = "wb%d" % (b % 2)
                wb = wflat[:, 0:kchunks * nw].rearrange("p (k n) -> p k n", k=kchunks)
                wload(wb, wk, c0, cw)
                for i in range(ntile):
                    pt = psb[i % 4]
                    pk = PK[i % 4]
                    for k in range(kchunks):
                        S.op("pe", lambda e, k=k: e.matmul(pt[:, 0:nw], lhsT=hT[:, k, i * 128:(i + 1) * 128],
                                                           rhs=wb[:, k, :], start=(k == 0),
                                                           stop=(k == kchunks - 1)), r=[hkey, wk], w=[pk])
                    epilogue(i, c0, cw, pt, pk)

        def wload_plain(Wsrc):
            def f(wb, wk, c0, cw):
                S.dma("pool", wb, Wsrc[:, c0:c0 + cw].rearrange("(k p) n -> p k n", p=128), w=[wk])
            return f

        def load3(src, sc, cw, t0, GW, bufs, keys):
            prv, cur, nxt = bufs
            kp, kc_, kn = keys
            S.dma("sp", cur[:, 0:cw], src[t0:t0 + 128, sc:sc + cw], r=["CSRC"], w=[kc_])
            a = 0
            while a < 128:
                g0 = t0 + a
                b = min(128, a + GW - (g0 % GW))
                sb_ = (g0 % GW) == 0
                eb_ = ((t0 + b) % GW) == 0
                if sb_:
                    S.dma("sp", prv[a:a + 1, 0:cw], I["zrow"][0:1, 0:cw], w=[kp])
                    if b - a > 1:
                        S.dma("sp", prv[a + 1:b, 0:cw], src[t0 + a:t0 + b - 1, sc:sc + cw], r=["CSRC"], w=[kp])
                else:
                    S.dma("sp", prv[a:b, 0:cw], src[t0 + a - 1:t0 + b - 1, sc:sc + cw], r=["CSRC"], w=[kp])
                if eb_:
                    S.dma("sp", nxt[b - 1:b, 0:cw], I["zrow"][0:1, 0:cw], w=[kn])
                    if b - a > 1:
                        S.dma("sp", nxt[a:b - 1, 0:cw], src[t0 + a + 1:t0 + b, sc:sc + cw], r=["CSRC"], w=[kn])
                else:
                    S.dma("sp", nxt[a:b, 0:cw], src[t0 + a + 1:t0 + b + 1, sc:sc + cw], r=["CSRC"], w=[kn])
                a = b

        def conv_pass(ps, T, GW, src, sc0, ncols, wrows, post, prep=None):
            lsb = lambda name, shape, dt=F32: ps.enter_context(nc.sbuf_tensor(uname(name), list(shape), dt))
            wt = [lsb("cvw%d" % j, [128, 512]) for j in range(4)]
            bufs = [[lsb("cv%s%d" % (n, j), [128, 512]) for n in ("p", "c", "n")] for j in range(2)]
            yb = [lsb("cvy%d" % j, [128, 512]) for j in range(2)]
            tb = lsb("cvt", [128, 512])
            for c0 in range(0, ncols, 512):
                cw = min(512, ncols - c0)
                rows = wrows(c0, cw)
                for j in range(4):
                    if rows[j] is not None:
                        bcload(wt[j][:, 0:cw], "cvw%d" % j, rows[j])
                if prep is not None:
                    prep(wt, cw)
                for i in range(T // 128):
                    bb = bufs[i % 2]
                    keys = ["cv%s%d" % (n, i % 2) for n in ("p", "c", "n")]
                    load3(src, sc0 + c0, cw, i * 128, GW, bb, keys)
                    y = yb[i % 2]
                    yk = "cvy%d" % (i % 2)
                    S.op("dve", lambda e: e.tensor_mul(out=y[:, 0:cw], in0=bb[1][:, 0:cw], in1=wt[1][:, 0:cw]),
                         r=[keys[1], "cvw1"], w=[yk])
                    S.op("pool", lambda e: e.tensor_mul(out=tb[:, 0:cw], in0=bb[0][:, 0:cw], in1=wt[0][:, 0:cw]),
                         r=[keys[0], "cvw0"], w=["cvt"])
                    S.op("dve", lambda e: e.tensor_add(out=y[:, 0:cw], in0=y[:, 0:cw], in1=tb[:, 0:cw]),
                         r=[yk, "cvt"], w=[yk])
                    S.op("pool", lambda e: e.tensor_mul(out=tb[:, 0:cw], in0=bb[2][:, 0:cw], in1=wt[2][:, 0:cw]),
                         r=[keys[2], "cvw2"], w=["cvt"])
                    S.op("dve", lambda e: e.tensor_add(out=y[:, 0:cw], in0=y[:, 0:cw], in1=tb[:, 0:cw]),
                         r=[yk, "cvt"], w=[yk])
                    if rows[3] is not None:
                        S.op("dve", lambda e: e.tensor_add(out=y[:, 0:cw], in0=y[:, 0:cw], in1=wt[3][:, 0:cw]),
                             r=[yk, "cvw3"], w=[yk])
                    post(i, c0, cw, y, yk)

        def postnorm_residual(ls, Ybuf, ntile, t0, xsrc, xdst, GPt):
            for i in range(ntile):
                y = Ybuf[:, i, :]
                S.op("act", lambda e: e.activation(out=ls["junk"][:], in_=y, func=AF.Square, accum_out=ls["ss"][:]),
                     r=["Ybuf"], w=["junk", "ss"])
                rms_rstd(ls["ss"][:], "ss", D, EPS)
                xt = ls["xt"][i % 2]
                xk = "xt%d" % (i % 2)
                S.dma("sp", xt[:], xsrc[t0 + i * 128:t0 + (i + 1) * 128, :], r=["XSRC"], w=[xk])
                S.op("dve", lambda e: e.scalar_tensor_tensor(out=ls["junk"][:], in0=y, scalar=ls["ss"][:, 0:1],
                                                             in1=GPt[:], op0=ALU.mult, op1=ALU.mult),
                     r=["Ybuf", "ss", "GPt"], w=["junk"])
                S.op("dve", lambda e: e.tensor_add(out=xt[:], in0=xt[:], in1=ls["junk"][:]), r=["junk", xk], w=[xk])
                S.dma("sp", xdst[t0 + i * 128:t0 + (i + 1) * 128, :], xt[:], r=[xk], w=["XDST"])

        def mod_tiles(lsb, ls, l, jrow, i_sc, i_sh, i_g, pre, post):
            Gt = lsb("Gt", [128, D])
            SHt = lsb("SHt", [128, D])
            GPt = lsb("GPt", [128, D])
            bcload(Gt[:], "Gt", MOD[jrow, i_sc * D:(i_sc + 1) * D])
            bcload(ls["junk"][:], "junk", I[pre][l, :])
            S.op("dve", lambda e: e.scalar_tensor_tensor(out=Gt[:], in0=Gt[:], scalar=1.0, in1=ls["junk"][:],
                                                         op0=ALU.add, op1=ALU.mult), r=["Gt", "junk"], w=["Gt"])
            bcload(SHt[:], "SHt", MOD[jrow, i_sh * D:(i_sh + 1) * D])
            bcload(GPt[:], "GPt", MOD[jrow, i_g * D:(i_g + 1) * D])
            bcload(ls["junk"][:], "junk", I[post][l, :])
            S.op("dve", lambda e: e.tensor_mul(out=GPt[:], in0=GPt[:], in1=ls["junk"][:]), r=["GPt", "junk"],
                 w=["GPt"])
            return Gt, SHt, GPt

        def phase_inproj(l, xsrc, T, jrow):
            with contextlib.ExitStack() as ps:
                lsb = lambda name, shape, dt=F32: ps.enter_context(nc.sbuf_tensor(uname(name), list(shape), dt))
                ls = {"xt": [lsb("xt0", [128, D]), lsb("xt1", [128, D])], "junk": lsb("junk", [128, D]),
                      "ss": lsb("ss", [128, 1])}
                wbs = [lsb("wb0", [128, 16384], BF16), lsb("wb1", [128, 16384], BF16)]
                st = [lsb("st0", [128, 512]), lsb("st1", [128, 512])]
                hT = lsb("hT", [128, KC, BLK], BF16)
                Gt, SHt, GPt = mod_tiles(lsb, ls, l, jrow, 1, 0, 2, "norm_mix_pre", "norm_mix_post")
                cnt = [0]
                for bi in range(T // BLK):
                    t0 = bi * BLK
                    norm_to_fm(ls, xsrc, t0, BLK // 128, hT, Gt, SHt)

                    def epi(i, c0, cw, pt, pk, t0=t0):
                        cnt[0] += 1
                        s_ = st[cnt[0] % 2]
                        sk = "st%d" % (cnt[0] % 2)
                        S.op("act", lambda e: e.activation(out=s_[:, 0:cw], in_=pt[:, 0:cw], func=AF.Copy),
                             r=[pk], w=[sk])
                        rs_ = slice(t0 + i * 128, t0 + (i + 1) * 128)
                        if c0 < C_U < c0 + cw:
                            m_ = C_U - c0
                            S.dma("sp", P[rs_, c0:C_U], s_[:, 0:m_], r=[sk], w=["P"])
                            S.dma("sp", P[rs_, C_U:c0 + cw], s_[:, m_:cw], r=[sk], w=["P"])
                        else:
                            S.dma("sp", P[rs_, c0:c0 + cw], s_[:, 0:cw], r=[sk], w=["P"])

                    proj(wbs, hT, "hT", KC, debug.get("ncols", N_IN), BLK // 128, epi, wload_plain(I["w_in"][l]))
                S.barrier()

        def phase_convs(l, T, GW):
            with contextlib.ExitStack() as ps:
                def post_ssm(i, c0, cw, y, yk):
                    lsg = post_ssm.sg
                    S.op("act", lambda e: e.activation(out=lsg[:, 0:cw], in_=y[:, 0:cw], func=AF.Sigmoid),
                         r=[yk], w=["cvsg"])
                    S.op("dve", lambda e: e.tensor_mul(out=y[:, 0:cw], in0=y[:, 0:cw], in1=lsg[:, 0:cw]),
                         r=[yk, "cvsg"], w=[yk])
                    S.dma("sp", XC[i * 128:(i + 1) * 128, c0:c0 + cw], y[:, 0:cw], r=[yk], w=["XC"])
                post_ssm.sg = ps.enter_context(nc.sbuf_tensor(uname("cvsg"), [128, 512], F32))
                conv_pass(ps, T, GW, P, C_XBC, 2560,
                          lambda c0, cw: (I["ssm_conv_w"][l, 0, c0:c0 + cw], I["ssm_conv_w"][l, 1, c0:c0 + cw],
                                          I["ssm_conv_w"][l, 2, c0:c0 + cw], I["ssm_conv_b"][l, c0:c0 + cw]),
                          post_ssm)
                S.barrier()
            with contextlib.ExitStack() as ps:
                def post_wkv(i, c0, cw, y, yk):
                    S.dma("sp", SH[i * 128:(i + 1) * 128, c0:c0 + cw], y[:, 0:cw], r=[yk], w=["SH"])

                def prep(wt, cw):
                    S.op("dve", lambda e: e.tensor_add(out=wt[1][:, 0:cw], in0=wt[0][:, 0:cw], in1=wt[2][:, 0:cw]),
                         r=["cvw0", "cvw2"], w=["cvw1"])
                    S.op("dve", lambda e: e.tensor_scalar(out=wt[1][:, 0:cw], in0=wt[1][:, 0:cw], scalar1=-1.0,
                                                          scalar2=1.0, op0=ALU.mult, op1=ALU.add),
                         r=["cvw1"], w=["cvw1"])
                conv_pass(ps, T, GW, P, C_WKV, NWKV,
                          lambda c0, cw: (I["wkv_mu_prev"][l, c0:c0 + cw], None, I["wkv_mu_next"][l, c0:c0 + cw],
                                          None), post_wkv, prep=prep)
                S.barrier()

        def mm_cols(ps3, c0, c1, lhsT, rhs_fn, rkeys):
            c = c0
            while c < c1:
                bnk = c // 512
                ce = min(c1, (bnk + 1) * 512)
                S.op("pe", lambda e, c=c, ce=ce, bnk=bnk: e.matmul(psb[ps3[bnk]][:, c - bnk * 512:ce - bnk * 512],
                                                                   lhsT=lhsT, rhs=rhs_fn(c, ce), start=True,
                                                                   stop=True), r=rkeys, w=[PK[ps3[bnk]]])
                c = ce

        def phase_ssd(l, T, L, sample):
            NT = T // 128
            with contextlib.ExitStack() as ps:
                lsb = lambda name, shape, dt=F32: ps.enter_context(nc.sbuf_tensor(uname(name), list(shape), dt))
                bcv = [lsb("sbc%d" % j, [128, 1024]) for j in range(2)]
                bct = [lsb("sbct%d" % j, [128, 8, 128], BF16) for j in range(2)]
                dtt = [lsb("sdt%d" % j, [128, 48]) for j in range(2)]
                dbias = lsb("dbias", [128, 48])
                bcload(dbias[:], "dbias", I["ssm_dt_bias"][l, :])
                for i in range(NT):
                    b_ = bcv[i % 2]
                    bk = "sbc%d" % (i % 2)
                    S.dma("sp", b_[:], XC[i * 128:(i + 1) * 128, 1536:2560], r=["XC"], w=[bk])
                    o_ = bct[i % 2]
                    ok = "sbct%d" % (i % 2)
                    transposes_to(b_[:], bk, 8, lambda q, nj: o_[:, q:q + nj, :], ok)
                    S.dma("sp", BCT[i], o_[:], r=[ok], w=["BCT"])
                    d_ = dtt[i % 2]
                    dk = "sdt%d" % (i % 2)
                    S.dma("sp", d_[:], P[i * 128:(i + 1) * 128, C_DTF:C_DTF + 48], r=["P"], w=[dk])
                    S.op("dve", lambda e: e.tensor_add(out=d_[:], in0=d_[:], in1=dbias[:]), r=[dk, "dbias"], w=[dk])
                    S.op("act", lambda e: e.activation(out=d_[:], in_=d_[:], func=AF.Exp), r=[dk], w=[dk])
                    S.op("act", lambda e: e.activation(out=d_[:], in_=d_[:], func=AF.Ln, bias=1.0), r=[dk], w=[dk])
                    S.dma("sp", DT[i * 128:(i + 1) * 128, :], d_[:], r=[dk], w=["DT"])
                S.barrier()
            with contextlib.ExitStack() as ps:
                lsb = lambda name, shape, dt=F32: ps.enter_context(nc.sbuf_tensor(uname(name), list(shape), dt))
                xc = [lsb("xc%d" % j, [128, 2560]) for j in range(2)]
                bct = [lsb("bct%d" % j, [128, 8, 128], BF16) for j in range(2)]
                dtt = [lsb("dt%d" % j, [128, 24]) for j in range(2)]
                Abc = lsb("Abc", [128, 24])
                Dbc = lsb("Dbc", [128, 24])
                nrm = lsb("nrm", [128, 1536])
                dtA = lsb("dtA", [128, 24])
                acol = lsb("acol", [128, 24])
                tot = lsb("tot", [128, 24])
                cd = lsb("cd", [128, 24])
                ea = lsb("ea", [128, 24])
                dte = lsb("dte", [128, 24])
                xdt = lsb("xdt", [128, 1536], BF16)
                xw = lsb("xw", [128, 1536], BF16)
                bB = lsb("bB", [128, 512], BF16)
                Gm = lsb("Gm", [128, 512])
                dd = [lsb("dd%d" % j, [128, 512]) for j in range(2)]
                wt = [lsb("wt%d" % j, [128, 4, 128], BF16) for j in range(2)]
                HT = lsb("HT", [128, 1536])
                HTb = lsb("HTb", [128, 1536], BF16)
                yo = lsb("yo", [128, 1536])
                yy = lsb("yy", [128, 1536])
                zz = lsb("zz", [128, 1536])
                zs = lsb("zs", [128, 1536])
                ss4 = lsb("ss4", [128, 4])
                hio = lsb("hio", [128, 12, 128])
                bcload(Dbc[:], "Dbc", I["ssm_d"][l, :])
                bcload(nrm[:], "nrm", I["ssm_norm"][l, :])
                PY = (4, 5, 6)
                v3 = lambda t: t[:].rearrange("p (h q) -> p h q", h=24)
                for d in range(2):
                    bcload(Abc[:], "Abc", I["ssm_a_log"][l, d * 24:(d + 1) * 24])
                    S.op("act", lambda e: e.activation(out=Abc[:], in_=Abc[:], func=AF.Exp), r=["Abc"], w=["Abc"])
                    S.op("dve", lambda e: e.tensor_scalar(out=Abc[:], in0=Abc[:], scalar1=-1.0, scalar2=None,
                                                          op0=ALU.mult), r=["Abc"], w=["Abc"])
                    TR = tri if d == 0 else trit
                    TRk = "tri" if d == 0 else "trit"
                    for s_ in range(T // L):
                        if sample:
                            S.dma("sp", hio[:], I["st_ssm"][l, d].rearrange("(g h2) p n -> (h2 p) g n", h2=2),
                                  w=["hio"])
                            for g in range(12):
                                bnk = PY[(g * 128) // 512]
                                S.op("pe", lambda e, g=g, bnk=bnk: e.transpose(
                                    psb[bnk][:, (g * 128) % 512:(g * 128) % 512 + 128], hio[:, g, :], ident[:]),
                                    r=["hio", "ident"], w=[PK[bnk]])
                            for j in range(3):
                                S.op("act", lambda e, j=j: e.activation(out=HT[:, j * 512:(j + 1) * 512],
                                                                        in_=psb[PY[j]][:, :], func=AF.Copy),
                                     r=[PK[PY[j]]], w=["HT"])
                        else:
                            S.op("dve", lambda e: e.memset(HT[:], 0.0), w=["HT"])
                        S.op("act", lambda e: e.activation(out=HTb[:], in_=HT[:], func=AF.Copy), r=["HT"], w=["HTb"])
                        tiles = list(range(L // 128))
                        if d == 1:
                            tiles = tiles[::-1]
                        for ti in tiles:
                            i = s_ * (L // 128) + ti
                            t0 = i * 128
                            x_ = xc[i % 2]
                            xk = "xc%d" % (i % 2)
                            b_ = bct[i % 2]
                            bk = "bct%d" % (i % 2)
                            d_ = dtt[i % 2]
                            dk = "dt%d" % (i % 2)
                            S.dma("sp", x_[:], XC[t0:t0 + 128, :], r=["XC"], w=[xk])
                            S.dma("sp", b_[:], BCT[i], r=["BCT"], w=[bk])
                            S.dma("sp", d_[:], DT[t0:t0 + 128, d * 24:(d + 1) * 24], r=["DT"], w=[dk])
                            S.op("dve", lambda e: e.tensor_mul(out=dtA[:], in0=d_[:], in1=Abc[:]),
                                 r=[dk, "Abc"], w=["dtA"])
                            S.op("pe", lambda e: e.matmul(psb[0][:, 0:24], lhsT=TR[:], rhs=dtA[:], start=True,
                                                          stop=True), r=[TRk, "dtA"], w=[PK[0]])
                            S.op("pe", lambda e: e.matmul(psb[0][:, 32:56], lhsT=ones[:], rhs=dtA[:], start=True,
                                                          stop=True), r=["ones", "dtA"], w=[PK[0]])
                            S.op("act", lambda e: e.activation(out=acol[:], in_=psb[0][:, 0:24], func=AF.Copy),
                                 r=[PK[0]], w=["acol"])
                            S.op("act", lambda e: e.activation(out=ea[:], in_=psb[0][:, 0:24], func=AF.Exp),
                                 r=[PK[0]], w=["ea"])
                            S.op("act", lambda e: e.activation(out=cd[:], in_=psb[0][:, 32:56], func=AF.Exp),
                                 r=[PK[0]], w=["cd"])
                            S.op("dve", lambda e: e.tensor_sub(out=dte[:], in0=psb[0][:, 32:56], in1=acol[:]),
                                 r=[PK[0], "acol"], w=["dte"])
                            S.op("act", lambda e: e.activation(out=dte[:], in_=dte[:], func=AF.Exp), r=["dte"],
                                 w=["dte"])
                            S.op("dve", lambda e: e.tensor_mul(out=dte[:], in0=dte[:], in1=d_[:]), r=["dte", dk],
                                 w=["dte"])
                            S.op("dve", lambda e: e.tensor_tensor(
                                out=v3(xdt), in0=x_[:, 0:1536].rearrange("p (h q) -> p h q", h=24),
                                in1=d_[:, :].unsqueeze(2).to_broadcast([128, 24, 64]), op=ALU.mult),
                                r=[xk, dk], w=["xdt"])
                            S.op("dve", lambda e: e.tensor_tensor(
                                out=v3(xw), in0=x_[:, 0:1536].rearrange("p (h q) -> p h q", h=24),
                                in1=dte[:, :].unsqueeze(2).to_broadcast([128, 24, 64]), op=ALU.mult),
                                r=[xk, "dte"], w=["xw"])
                            S.op("act", lambda e: e.activation(out=bB[:], in_=x_[:, 1536:2048], func=AF.Copy),
                                 r=[xk], w=["bB"])
                            for g in range(4):
                                S.op("pe", lambda e, g=g: e.matmul(psb[1][:, g * 128:(g + 1) * 128], lhsT=b_[:, g, :],
                                                                   rhs=b_[:, 4 + g, :], start=True, stop=True),
                                     r=[bk], w=[PK[1]])
                            S.op("dve", lambda e: e.tensor_tensor(
                                out=Gm[:].rearrange("p (g t) -> p g t", g=4),
                                in0=psb[1][:, :].rearrange("p (g t) -> p g t", g=4),
                                in1=TR[:, :].unsqueeze(1).to_broadcast([128, 4, 128]), op=ALU.mult),
                                r=[PK[1], TRk], w=["Gm"])
                            for g in range(4):
                                mm_cols(PY, g * 384, (g + 1) * 384, b_[:, 4 + g, :], lambda c, ce: HTb[:, c:ce],
                                        [bk, "HTb"])
                            for j in range(3):
                                S.op("dve", lambda e, j=j: e.tensor_tensor(
                                    out=yo[:, j * 512:(j + 1) * 512].rearrange("p (h q) -> p h q", h=8),
                                    in0=psb[PY[j]][:, :].rearrange("p (h q) -> p h q", h=8),
                                    in1=ea[:, j * 8:(j + 1) * 8].unsqueeze(2).to_broadcast([128, 8, 64]),
                                    op=ALU.mult), r=[PK[PY[j]], "ea"], w=["yo"])
                            for hq in range(6):
                                pb = 2 + hq % 2
                                ddq = dd[hq % 2]
                                dkq = "dd%d" % (hq % 2)
                                wq = wt[hq % 2]
                                wkq = "wt%d" % (hq % 2)
                                for j in range(4):
                                    h = hq * 4 + j
                                    S.op("pe", lambda e, j=j, h=h: e.matmul(
                                        psb[pb][:, j * 128:(j + 1) * 128],
                                        lhsT=dtA[:, h:h + 1].to_broadcast([128, 128]), rhs=TR[:], start=True,
                                        stop=True), r=["dtA", TRk], w=[PK[pb]])
                                for j in range(4):
                                    h = hq * 4 + j
                                    S.op("dve", lambda e, j=j, h=h: e.tensor_scalar(
                                        out=ddq[:, j * 128:(j + 1) * 128], in0=psb[pb][:, j * 128:(j + 1) * 128],
                                        scalar1=acol[:, h:h + 1], scalar2=0.0, op0=ALU.subtract, op1=ALU.min),
                                        r=[PK[pb], "acol"], w=[dkq])
                                S.op("act", lambda e: e.activation(out=ddq[:], in_=ddq[:], func=AF.Exp), r=[dkq],
                                     w=[dkq])
                                for j in range(4):
                                    h = hq * 4 + j
                                    g = h // 6
                                    S.op("dve", lambda e, j=j, g=g: e.tensor_mul(
                                        out=wq[:, j, :], in0=Gm[:, g * 128:(g + 1) * 128],
                                        in1=ddq[:, j * 128:(j + 1) * 128]), r=["Gm", dkq], w=[wkq])
                                for j in range(4):
                                    h = hq * 4 + j
                                    bnk = PY[(h * 64) // 512]
                                    S.op("pe", lambda e, j=j, h=h, bnk=bnk: e.matmul(
                                        psb[bnk][:, (h * 64) % 512:(h * 64) % 512 + 64], lhsT=wq[:, j, :],
                                        rhs=xdt[:, h * 64:(h + 1) * 64], start=True, stop=True),
                                        r=[wkq, "xdt"], w=[PK[bnk]])
                            for j in range(3):
                                S.op("dve", lambda e, j=j: e.tensor_add(out=yy[:, j * 512:(j + 1) * 512],
                                                                        in0=yo[:, j * 512:(j + 1) * 512],
                                                                        in1=psb[PY[j]][:, :]),
                                     r=["yo", PK[PY[j]]], w=["yy"])
                            for g in range(4):
                                mm_cols(PY, g * 384, (g + 1) * 384, bB[:, g * 128:(g + 1) * 128],
                                        lambda c, ce: xw[:, c:ce], ["bB", "xw"])
                            S.op("dve", lambda e: e.tensor_tensor(out=v3(HT), in0=v3(HT),
                                                                  in1=cd[:, :].unsqueeze(2).to_broadcast([128, 24, 64]),
                                                                  op=ALU.mult), r=["HT", "cd"], w=["HT"])
                            for j in range(3):
                                S.op("dve", lambda e, j=j: e.tensor_add(out=HT[:, j * 512:(j + 1) * 512],
                                                                        in0=HT[:, j * 512:(j + 1) * 512],
                                                                        in1=psb[PY[j]][:, :]),
                                     r=["HT", PK[PY[j]]], w=["HT"])
                            S.op("act", lambda e: e.activation(out=HTb[:], in_=HT[:], func=AF.Copy), r=["HT"],
                                 w=["HTb"])
                            if d == 0:
                                S.dma("sp", YS[t0:t0 + 128, :], yy[:], r=["yy"], w=["YS"])
                            else:
                                S.dma("sp", yo[:], YS[t0:t0 + 128, :], r=["YS"], w=["yo"])
                                S.op("dve", lambda e: e.tensor_add(out=yy[:], in0=yy[:], in1=yo[:]), r=["yy", "yo"],
                                     w=["yy"])
                                S.op("dve", lambda e: e.tensor_tensor(
                                    out=v3(yo), in0=x_[:, 0:1536].rearrange("p (h q) -> p h q", h=24),
                                    in1=Dbc[:, :].unsqueeze(2).to_broadcast([128, 24, 64]), op=ALU.mult),
                                    r=[xk, "Dbc"], w=["yo"])
                                S.op("dve", lambda e: e.tensor_add(out=yy[:], in0=yy[:], in1=yo[:]), r=["yy", "yo"],
                                     w=["yy"])
                                S.dma("sp", zz[:], P[t0:t0 + 128, C_Z:C_Z + 1536], r=["P"], w=["zz"])
                                S.op("act", lambda e: e.activation(out=zs[:], in_=zz[:], func=AF.Sigmoid), r=["zz"],
                                     w=["zs"])
                                S.op("dve", lambda e: e.tensor_mul(out=zs[:], in0=zs[:], in1=zz[:]), r=["zz", "zs"],
                                     w=["zs"])
                                S.op("dve", lambda e: e.tensor_mul(out=yy[:], in0=yy[:], in1=zs[:]), r=["yy", "zs"],
                                     w=["yy"])
                                S.op("pool", lambda e: e.tensor_mul(out=zs[:], in0=yy[:], in1=yy[:]), r=["yy"],
                                     w=["zs"])
                                S.op("dve", lambda e: e.tensor_reduce(out=ss4[:],
                                                                      in_=zs[:].rearrange("p (g q) -> p g q", g=4),
                                                                      axis=AX.X, op=ALU.add), r=["zs"], w=["ss4"])
                                rms_rstd(ss4[:], "ss4", 384, EPS)
                                S.op("dve", lambda e: e.tensor_tensor(
                                    out=yy[:].rearrange("p (g q) -> p g q", g=4),
                                    in0=yy[:].rearrange("p (g q) -> p g q", g=4),
                                    in1=ss4[:, :].unsqueeze(2).to_broadcast([128, 4, 384]), op=ALU.mult),
                                    r=["yy", "ss4"], w=["yy"])
                                S.op("dve", lambda e: e.tensor_mul(out=yy[:], in0=yy[:], in1=nrm[:]),
                                     r=["yy", "nrm"], w=["yy"])
                                S.dma("sp", YS[t0:t0 + 128, :], yy[:], r=["yy"], w=["YS"])
                        if not sample:
                            for g in range(12):
                                bnk = PY[(g * 128) // 512]
                                S.op("pe", lambda e, g=g, bnk=bnk: e.transpose(
                                    psb[bnk][:, (g * 128) % 512:(g * 128) % 512 + 128],
                                    HT[:, g * 128:(g + 1) * 128], ident[:]), r=["HT", "ident"], w=[PK[bnk]])
                            for j in range(3):
                                S.op("act", lambda e, j=j: e.activation(
                                    out=hio[:, j * 4:(j + 1) * 4, :],
                                    in_=psb[PY[j]][:, :].rearrange("p (g n) -> p g n", g=4), func=AF.Copy),
                                    r=[PK[PY[j]]], w=["hio"])
                            S.dma("sp", O["ns_ssm"][s_, l, d].rearrange("(g h2) p n -> (h2 p) g n", h2=2), hio[:],
                                  r=["hio"], w=["ns_ssm"])
                S.barrier()

        def phase_wkv_prep(l, T):
            NT = T // 128
            with contextlib.ExitStack() as ps:
                lsb = lambda name, shape, dt=F32: ps.enter_context(nc.sbuf_tensor(uname(name), list(shape), dt))
                sh = lsb("sh", [128, NWKV])
                kkbc = lsb("kkbc", [128, 1536])
                kabc = lsb("kabc", [128, 1536])
                omka = lsb("omka", [128, 1536])
                rkbc = lsb("rkbc", [128, 1536])
                w0bc = lsb("w0bc", [128, 1536])
                a0bc = lsb("a0bc", [128, 1536])
                wup = [lsb("wup%d" % d, [96, 1536]) for d in range(2)]
                aup = [lsb("aup%d" % d, [96, 1536]) for d in range(2)]
                gup = lsb("gup", [128, 2, 1536])
                A = lsb("wA", [128, 1536])
                Bt = lsb("wB", [128, 1536])
                Ct = lsb("wC", [128, 1536])
                Dt_ = lsb("wD", [128, 1536])
                E = lsb("wE", [128, 1536])
                nkk = lsb("nkk", [128, 1536])
                sm = lsb("wsm", [128, 48])
                rs = lsb("wrs", [128, 24])
                twT = lsb("twT", [96, 2, 128])
                aT = lsb("aT", [96, 2, 128])
                sgT = lsb("sgT", [128, 2, 128])
                fm = [lsb("wfm%d" % j, [128, 12, 128]) for j in range(2)]
                fmc = [0]
                bcload(kkbc[:], "kkbc", I["wkv_k_k"][l, :])
                bcload(kabc[:], "kabc", I["wkv_k_a"][l, :])
                bcload(rkbc[:], "rkbc", I["wkv_r_k"][l, :])
                S.op("dve", lambda e: e.tensor_scalar(out=omka[:], in0=kabc[:], scalar1=-1.0, scalar2=1.0,
                                                      op0=ALU.mult, op1=ALU.add), r=["kabc"], w=["omka"])
                for d in range(2):
                    S.dma("sp", wup[d][:], I["wkv_w_up"][l, d], w=["wup%d" % d])
                    S.dma("sp", aup[d][:], I["wkv_a_up"][l, d], w=["aup%d" % d])
                S.dma("sp", gup[:], I["wkv_g_up"][l].rearrange("(c p) n -> p c n", p=128), w=["gup"])
                PY = (3, 4, 5)
                h3 = lambda ap: ap.rearrange("p (h q) -> p h q", h=24)

                def fm_store(src, skey, dst3):
                    f = fm[fmc[0] % 2]
                    fk = "wfm%d" % (fmc[0] % 2)
                    fmc[0] += 1
                    transposes_to(src, skey, 12, lambda q, nj: f[:, q:q + nj, :], fk, eng="dve")
                    S.dma("sp", dst3, f[:], r=[fk], w=["FMOUT"])

                for i in range(NT):
                    t0 = i * 128
                    S.dma("sp", sh[:], SH[t0:t0 + 128, :], r=["SH"], w=["sh"])
                    r_ = sh[:, 0:1536]
                    k_ = sh[:, 1536:3072]
                    for hh in range(2):
                        S.dma("sp", VEO[hh][t0:t0 + 128, :].rearrange("t (g v) -> t g v", g=12),
                              sh[:, 3072:4608].rearrange("p (g h v) -> p g h v", g=12, h=2)[:, :, hh, :],
                              r=["sh"], w=["VEO"])
                    S.op("dve", lambda e: e.tensor_mul(out=A[:], in0=k_, in1=kkbc[:]), r=["sh", "kkbc"], w=["wA"])
                    S.op("pool", lambda e: e.tensor_mul(out=Bt[:], in0=A[:], in1=A[:]), r=["wA"], w=["wB"])
                    S.op("dve", lambda e: e.tensor_reduce(out=rs[:], in_=h3(Bt[:]), axis=AX.X, op=ALU.add),
                         r=["wB"], w=["wrs"])
                    S.op("dve", lambda e: e.tensor_scalar(out=rs[:], in0=rs[:], scalar1=1e-24, scalar2=None,
                                                          op0=ALU.max), r=["wrs"], w=["wrs"])
                    S.op("act", lambda e: e.activation(out=rs[:], in_=rs[:], func=AF.Sqrt), r=["wrs"], w=["wrs"])
                    S.op("dve", lambda e: e.reciprocal(out=rs[:], in_=rs[:]), r=["wrs"], w=["wrs"])
                    S.op("dve", lambda e: e.tensor_scalar(out=rs[:], in0=rs[:], scalar1=-1.0, scalar2=None,
                                                          op0=ALU.mult), r=["wrs"], w=["wrs"])
                    S.op("dve", lambda e: e.tensor_tensor(out=h3(nkk[:]), in0=h3(A[:]),
                                                          in1=rs[:, :].unsqueeze(2).to_broadcast([128, 24, 64]),
                                                          op=ALU.mult), r=["wA", "wrs"], w=["nkk"])
                    fm_store(nkk[:], "nkk", NKKT[:, :, t0:t0 + 128])
                    S.op("pool", lambda e: e.tensor_mul(out=Bt[:], in0=r_, in1=k_), r=["sh"], w=["wB"])
                    S.op("dve", lambda e: e.tensor_mul(out=Bt[:], in0=Bt[:], in1=rkbc[:]), r=["wB", "rkbc"], w=["wB"])
                    S.op("dve", lambda e: e.tensor_reduce(out=sm[:, 0:24], in_=h3(Bt[:]), axis=AX.X, op=ALU.add),
                         r=["wB"], w=["wsm"])
                    S.dma("sp", RK[t0:t0 + 128, :], sm[:, 0:24], r=["wsm"], w=["RK"])
                    S.op("act", lambda e: e.activation(out=Ct[:, 0:192], in_=sh[:, 4608:4800], func=AF.Tanh),
                         r=["sh"], w=["wC"])
                    S.op("act", lambda e: e.activation(out=Ct[:, 192:448], in_=sh[:, 4992:5248], func=AF.Sigmoid),
                         r=["sh"], w=["wC"])
                    transposes_to(Ct[:, 0:192], "wC", 2, lambda q, nj: twT[:, q:q + nj, :], "twT", bw=96, eng="dve")
                    transposes_to(sh[:, 4800:4992], "sh", 2, lambda q, nj: aT[:, q:q + nj, :], "aT", bw=96, eng="dve")
                    transposes_to(Ct[:, 192:448], "wC", 2, lambda q, nj: sgT[:, q:q + nj, :], "sgT", eng="dve")
                    for cb in range(3):
                        for c in range(2):
                            S.op("pe", lambda e, cb=cb, c=c: e.matmul(psb[PY[cb]][:, :], lhsT=sgT[:, c, :],
                                                                      rhs=gup[:, c, cb * 512:(cb + 1) * 512],
                                                                      start=(c == 0), stop=(c == 1)),
                                 r=["sgT", "gup"], w=[PK[PY[cb]]])
                        S.op("act", lambda e, cb=cb: e.activation(out=E[:, cb * 512:(cb + 1) * 512],
                                                                  in_=psb[PY[cb]][:, :], func=AF.Copy),
                             r=[PK[PY[cb]]], w=["wE"])
                    S.dma("sp", GG[t0:t0 + 128, :], E[:], r=["wE"], w=["GG"])
                    for d in range(2):
                        bcload(w0bc[:], "w0bc", I["wkv_w0"][l, d, :])
                        bcload(a0bc[:], "a0bc", I["wkv_a0"][l, d, :])
                        for cb in range(3):
                            S.op("pe", lambda e, cb=cb: e.matmul(psb[PY[cb]][:, :], lhsT=twT[:, d, :],
                                                                 rhs=wup[d][:, cb * 512:(cb + 1) * 512], start=True,
                                                                 stop=True), r=["twT", "wup%d" % d], w=[PK[PY[cb]]])
                            S.op("dve", lambda e, cb=cb: e.tensor_add(out=Bt[:, cb * 512:(cb + 1) * 512],
                                                                      in0=psb[PY[cb]][:, :],
                                                                      in1=w0bc[:, cb * 512:(cb + 1) * 512]),
                                 r=[PK[PY[cb]], "w0bc"], w=["wB"])
                        S.op("act", lambda e: e.activation(out=Bt[:], in_=Bt[:], func=AF.Sigmoid), r=["wB"], w=["wB"])
                        S.op("act", lambda e: e.activation(out=Bt[:], in_=Bt[:], func=AF.Exp,
                                                           scale=-math.exp(-0.5)), r=["wB"], w=["wB"])
                        for cb in range(3):
                            S.op("pe", lambda e, cb=cb: e.matmul(psb[PY[cb]][:, :], lhsT=aT[:, d, :],
                                                                 rhs=aup[d][:, cb * 512:(cb + 1) * 512], start=True,
                                                                 stop=True), r=["aT", "aup%d" % d], w=[PK[PY[cb]]])
                            S.op("dve", lambda e, cb=cb: e.tensor_add(out=Dt_[:, cb * 512:(cb + 1) * 512],
                                                                      in0=psb[PY[cb]][:, :],
                                                                      in1=a0bc[:, cb * 512:(cb + 1) * 512]),
                                 r=[PK[PY[cb]], "a0bc"], w=["wD"])
                        S.op("act", lambda e: e.activation(out=Dt_[:], in_=Dt_[:], func=AF.Sigmoid), r=["wD"],
                             w=["wD"])
                        S.op("pool", lambda e: e.tensor_mul(out=A[:], in0=Dt_[:], in1=kabc[:]), r=["wD", "kabc"],
                             w=["wA"])
                        S.op("dve", lambda e: e.tensor_add(out=A[:], in0=A[:], in1=omka[:]), r=["wA", "omka"],
                             w=["wA"])
                        S.op("dve", lambda e: e.tensor_mul(out=A[:], in0=A[:], in1=k_), r=["wA", "sh"], w=["wA"])
                        S.dma("sp", KD[d][t0:t0 + 128, :], A[:], r=["wA"], w=["KD"])
                        S.op("dve", lambda e: e.scalar_tensor_tensor(out=Ct[:], in0=nkk[:], scalar=-1.0, in1=Dt_[:],
                                                                     op0=ALU.mult, op1=ALU.mult),
                             r=["nkk", "wD"], w=["wC"])
                        S.dma("sp", BD[d][t0:t0 + 128, :], Ct[:], r=["wC"], w=["BD"])
                        S.op("pool", lambda e: e.tensor_mul(out=E[:], in0=Ct[:], in1=r_), r=["wC", "sh"], w=["wE"])
                        S.op("dve", lambda e: e.tensor_reduce(out=sm[:, 0:24], in_=h3(E[:]), axis=AX.X, op=ALU.add),
                             r=["wE"], w=["wsm"])
                        S.op("pool", lambda e: e.tensor_mul(out=E[:], in0=A[:], in1=r_), r=["wA", "sh"], w=["wE"])
                        S.op("dve", lambda e: e.tensor_reduce(out=sm[:, 24:48], in_=h3(E[:]), axis=AX.X, op=ALU.add),
                             r=["wE"], w=["wsm"])
                        S.dma("sp", BRKR[d][t0:t0 + 128, :], sm[:], r=["wsm"], w=["BRKR"])
                        S.op("dve", lambda e: e.tensor_mul(out=E[:], in0=Bt[:], in1=r_), r=["wB", "sh"], w=["wE"])
                        fm_store(E[:], "wE", WRT[d][:, :, t0:t0 + 128])
                        fm_store(Bt[:], "wB", WT[d][:, :, t0:t0 + 128])
                S.barrier()

        def phase_wkv_scan(l, T, L, sample):
            TBK = 8
            with contextlib.ExitStack() as ps:
                lsb = lambda name, shape, dt=F32: ps.enter_context(nc.sbuf_tensor(uname(name), list(shape), dt))
                m48 = lsb("m48", [48, 768])
                Ap = lsb("Ap", [128, 128, 48])
                nkT = lsb("nkT", [128, 12, 128])
                wrT = lsb("wrT", [128, 12, 128])
                wT = lsb("wT", [128, 12, 128])
                Lb = lsb("Lb", [48, TBK, 128])
                Lk = lsb("Lk", [24, TBK, 128])
                Rv = lsb("Rv", [24, TBK, 768])
                Ra = lsb("Ra", [48, TBK, 768])
                Cst = lsb("Cst", [48, TBK, 64])
                ST = lsb("ST", [128, 768])
                sio = lsb("sio", [64, 12, 128])
                S.dma("sp", m48[:], I["mask48"][:, :], w=["m48"])
                S.op("dve", lambda e: e.memset(Ap[:], 0.0), w=["Ap"])
                S.op("dve", lambda e: e.memset(Lb[:], 0.0), w=["Lb"])
                S.op("dve", lambda e: e.memset(Lk[:], 0.0), w=["Lk"])
                ST3 = ST[:].rearrange("p (g v) -> p g v", g=12)
                for d in range(2):
                    for s_ in range(T // L):
                        if sample:
                            S.dma("sp", sio[:].rearrange("v g (h k) -> v g h k", h=2),
                                  I["st_wkv"][l, d].rearrange("(g h) v k -> v g h k", h=2), w=["sio"])
                            transposes_to(sio[:].rearrange("v g q -> v (g q)"), "sio", 12,
                                          lambda q, nj: ST[:, q * 64:(q + nj) * 64].rearrange("p (j v) -> p j v", j=nj),
                                          "ST", bw=128, inw=64, pbanks=(4, 5))
                        else:
                            S.op("dve", lambda e: e.memset(ST[:], 0.0), w=["ST"])
                        tiles = list(range(L // 128))
                        if d == 1:
                            tiles = tiles[::-1]
                        for ti in tiles:
                            t0 = s_ * L + ti * 128
                            S.dma("sp", nkT[:], NKKT[:, :, t0:t0 + 128], r=["NKKT"], w=["nkT"])
                            S.dma("sp", wrT[:], WRT[d][:, :, t0:t0 + 128], r=["WRT"], w=["wrT"])
                            S.dma("sp", wT[:], WT[d][:, :, t0:t0 + 128], r=["WT"], w=["wT"])
                            S.op("pool", lambda e: e.tensor_copy(out=Ap[0:64, :, 0:12],
                                                                 in_=nkT[0:64].rearrange("p g t -> p t g")),
                                 r=["nkT"], w=["Ap"])
                            S.op("pool", lambda e: e.tensor_copy(out=Ap[64:128, :, 12:24],
                                                                 in_=nkT[64:128].rearrange("p g t -> p t g")),
                                 r=["nkT"], w=["Ap"])
                            S.op("pool", lambda e: e.tensor_copy(out=Ap[0:64, :, 24:36],
                                                                 in_=wrT[0:64].rearrange("p g t -> p t g")),
                                 r=["wrT"], w=["Ap"])
                            S.op("pool", lambda e: e.tensor_copy(out=Ap[64:128, :, 36:48],
                                                                 in_=wrT[64:128].rearrange("p g t -> p t g")),
                                 r=["wrT"], w=["Ap"])
                            chunks = list(range(128 // TBK))
                            if d == 1:
                                chunks = chunks[::-1]
                            for ch in chunks:
                                c0 = t0 + ch * TBK
                                bsrc = BD[d][c0:c0 + TBK, :].rearrange("t (g h k) -> g t h k", g=12, h=2)
                                ksrc = KD[d][c0:c0 + TBK, :].rearrange("t (g h k) -> g t h k", g=12, h=2)
                                S.dma("sp", Lb[0:12, :, 0:64], bsrc[:, :, 0, :], r=["BD"], w=["Lb"])
                                S.dma("sp", Lb[12:24, :, 64:128], bsrc[:, :, 1, :], r=["BD"], w=["Lb"])
                                S.dma("sp", Lk[0:12, :, 0:64], ksrc[:, :, 0, :], r=["KD"], w=["Lk"])
                                S.dma("sp", Lk[12:24, :, 64:128], ksrc[:, :, 1, :], r=["KD"], w=["Lk"])
                                for hh in range(2):
                                    S.dma("sp", Rv[hh * 12:(hh + 1) * 12, :, :].rearrange("p t q -> p (t q)"),
                                          VEO[hh][c0:c0 + TBK, :].rearrange("t q -> (t q)").partition_broadcast(12),
                                          r=["VEO"], w=["Rv"])
                                S.op("pool", lambda e: e.tensor_tensor(
                                    out=Rv[:], in0=Rv[:], in1=m48[0:24, :].unsqueeze(1).to_broadcast([24, TBK, 768]),
                                    op=ALU.mult), r=["Rv", "m48"], w=["Rv"])
                                order = list(range(TBK))
                                if d == 1:
                                    order = order[::-1]
                                for tl in order:
                                    tok = ch * TBK + tl
                                    for hf in range(2):
                                        S.op("pe", lambda e, hf=hf: e.matmul(psb[hf][0:48, 0:384], lhsT=Ap[:, tok, :],
                                                                             rhs=ST[:, hf * 384:(hf + 1) * 384],
                                                                             start=True, stop=True),
                                             r=["Ap", "ST"], w=[PK[hf]])
                                    for hf in range(2):
                                        S.op("dve", lambda e, hf=hf: e.tensor_mul(
                                            out=Ra[:, tl, hf * 384:(hf + 1) * 384], in0=psb[hf][0:48, 0:384],
                                            in1=m48[:, hf * 384:(hf + 1) * 384]), r=[PK[hf], "m48"], w=["Ra"])
                                    for hf in range(2):
                                        S.op("pe", lambda e, hf=hf: e.matmul(psb[2 + hf][:, 0:384], lhsT=Lk[:, tl, :],
                                                                             rhs=Rv[:, tl, hf * 384:(hf + 1) * 384],
                                                                             start=True, stop=False),
                                             r=["Lk", "Rv"], w=[PK[2 + hf]])
                                        S.op("pe", lambda e, hf=hf: e.matmul(psb[2 + hf][:, 0:384], lhsT=Lb[:, tl, :],
                                                                             rhs=Ra[:, tl, hf * 384:(hf + 1) * 384],
                                                                             start=False, stop=True),
                                             r=["Lb", "Ra"], w=[PK[2 + hf]])
                                    S.op("dve", lambda e: e.tensor_tensor(
                                        out=ST3, in0=ST3, in1=wT[:, :, tok:tok + 1].to_broadcast([128, 12, 64]),
                                        op=ALU.mult), r=["ST", "wT"], w=["ST"])
                                    for hf in range(2):
                                        S.op("dve", lambda e, hf=hf: e.tensor_add(
                                            out=ST[:, hf * 384:(hf + 1) * 384], in0=ST[:, hf * 384:(hf + 1) * 384],
                                            in1=psb[2 + hf][:, 0:384]), r=["ST", PK[2 + hf]], w=["ST"])
                                S.op("dve", lambda e: e.tensor_reduce(
                                    out=Cst[:], in_=Ra[:].rearrange("p t (g v) -> p t v g", g=12), axis=AX.X,
                                    op=ALU.add), r=["Ra"], w=["Cst"])
                                S.dma("sp", SAY[d][:, c0:c0 + TBK, :], Cst[:], r=["Cst"], w=["SAY"])
                        if not sample:
                            transposes_to(ST[:], "ST", 12, lambda q, nj: sio[:, q:q + nj, :], "sio", bw=64, inw=128,
                                          pbanks=(4, 5))
                            S.dma("sp", O["ns_wkv"][s_, l, d].rearrange("(g h) v k -> v g h k", h=2),
                                  sio[:].rearrange("v g (h k) -> v g h k", h=2), r=["sio"], w=["ns_wkv"])
                S.barrier()

        def phase_wkv_post(l, T):
            with contextlib.ExitStack() as ps:
                lsb = lambda name, shape, dt=F32: ps.enter_context(nc.sbuf_tensor(uname(name), list(shape), dt))
                y0 = [lsb("py0%d" % d, [128, 1536]) for d in range(2)]
                sa = [lsb("psa%d" % d, [128, 1536]) for d in range(2)]
                vv = lsb("pvv", [128, 1536])
                gg = lsb("pgg", [128, 1536])
                o = lsb("po", [128, 1536])
                t_ = lsb("pt", [128, 1536])
                lnw = lsb("lnw", [128, 1536])
                lnb = lsb("lnb", [128, 1536])
                bk = [lsb("pbk%d" % d, [128, 48]) for d in range(2)]
                rk = lsb("prk", [128, 24])
                mu = lsb("pmu", [128, 24])
                bcload(lnw[:], "lnw", I["wkv_ln_w"][l, :])
                bcload(lnb[:], "lnb", I["wkv_ln_b"][l, :])
                h3 = lambda ap: ap.rearrange("p (h q) -> p h q", h=24)
                bc3 = lambda ap: ap.unsqueeze(2).to_broadcast([128, 24, 64])
                for i in range(T // 128):
                    t0 = i * 128
                    for d in range(2):
                        for hh in range(2):
                            S.dma("sp", sa[d][:].rearrange("p (g h v) -> p g h v", g=12, h=2)[:, :, hh, :],
                                  SAY[d][hh * 12:(hh + 1) * 12, t0:t0 + 128, :].rearrange("g t v -> t g v"),
                                  r=["SAY"], w=["psa%d" % d])
                            S.dma("sp", y0[d][:].rearrange("p (g h v) -> p g h v", g=12, h=2)[:, :, hh, :],
                                  SAY[d][(2 + hh) * 12:(3 + hh) * 12, t0:t0 + 128, :].rearrange("g t v -> t g v"),
                                  r=["SAY"], w=["py0%d" % d])
                        S.dma("sp", bk[d][:], BRKR[d][t0:t0 + 128, :], r=["BRKR"], w=["pbk%d" % d])
                    S.dma("sp", vv[:], SH[t0:t0 + 128, 3072:4608], r=["SH"], w=["pvv"])
                    S.dma("sp", gg[:], GG[t0:t0 + 128, :], r=["GG"], w=["pgg"])
                    S.dma("sp", rk[:], RK[t0:t0 + 128, :], r=["RK"], w=["prk"])
                    S.op("dve", lambda e: e.tensor_add(out=o[:], in0=y0[0][:], in1=y0[1][:]), r=["py00", "py01"],
                         w=["po"])
                    for d in range(2):
                        S.op("pool", lambda e, d=d: e.tensor_tensor(out=h3(t_[:]), in0=h3(sa[d][:]),
                                                                    in1=bc3(bk[d][:, 0:24]), op=ALU.mult),
                             r=["psa%d" % d, "pbk%d" % d], w=["pt"])
                        S.op("dve", lambda e: e.tensor_add(out=o[:], in0=o[:], in1=t_[:]), r=["po", "pt"], w=["po"])
                    S.op("dve", lambda e: e.tensor_add(out=mu[:], in0=bk[0][:, 24:48], in1=bk[1][:, 24:48]),
                         r=["pbk0", "pbk1"], w=["pmu"])
                    S.op("pool", lambda e: e.tensor_tensor(out=h3(t_[:]), in0=h3(vv[:]), in1=bc3(mu[:, :]),
                                                           op=ALU.mult), r=["pvv", "pmu"], w=["pt"])
                    S.op("dve", lambda e: e.tensor_add(out=o[:], in0=o[:], in1=t_[:]), r=["po", "pt"], w=["po"])
                    S.op("dve", lambda e: e.tensor_reduce(out=mu[:], in_=h3(o[:]), axis=AX.X, op=ALU.add), r=["po"],
                         w=["pmu"])
                    S.op("dve", lambda e: e.tensor_scalar(out=mu[:], in0=mu[:], scalar1=1.0 / 64, scalar2=None,
                                                          op0=ALU.mult), r=["pmu"], w=["pmu"])
                    S.op("dve", lambda e: e.tensor_tensor(out=h3(o[:]), in0=h3(o[:]), in1=bc3(mu[:, :]),
                                                          op=ALU.subtract), r=["po", "pmu"], w=["po"])
                    S.op("pool", lambda e: e.tensor_mul(out=t_[:], in0=o[:], in1=o[:]), r=["po"], w=["pt"])
                    S.op("dve", lambda e: e.tensor_reduce(out=mu[:], in_=h3(t_[:]), axis=AX.X, op=ALU.add), r=["pt"],
                         w=["pmu"])
                    rms_rstd(mu[:], "pmu", 64, 64e-5)
                    S.op("dve", lambda e: e.tensor_tensor(out=h3(o[:]), in0=h3(o[:]), in1=bc3(mu[:, :]),
                                                          op=ALU.mult), r=["po", "pmu"], w=["po"])
                    S.op("dve", lambda e: e.tensor_mul(out=o[:], in0=o[:], in1=lnw[:]), r=["po", "lnw"], w=["po"])
                    S.op("dve", lambda e: e.tensor_add(out=o[:], in0=o[:], in1=lnb[:]), r=["po", "lnb"], w=["po"])
                    S.op("pool", lambda e: e.tensor_tensor(out=h3(t_[:]), in0=h3(vv[:]), in1=bc3(rk[:, :]),
                                                           op=ALU.mult), r=["pvv", "prk"], w=["pt"])
                    S.op("dve", lambda e: e.tensor_add(out=o[:], in0=o[:], in1=t_[:]), r=["po", "pt"], w=["po"])
                    S.op("dve", lambda e: e.tensor_mul(out=o[:], in0=o[:], in1=gg[:]), r=["po", "pgg"], w=["po"])
                    S.dma("sp", ZW[t0:t0 + 128, :], o[:], r=["po"], w=["ZW"])
                S.barrier()

        def phase_s5(l, T, L, sample):
            NT = T // 128
            nseq = T // L
            LT = L // 128
            Ls = min(512, L)
            nseg = L // Ls
            nsub = Ls // 128
            with contextlib.ExitStack() as ps:
                lsb = lambda name, shape, dt=F32: ps.enter_context(nc.sbuf_tensor(uname(name), list(shape), dt))
                ut = [lsb("ut%d" % j, [128, 1024]) for j in range(2)]
                stg = [lsb("ustg%d" % j, [32, 32, 128]) for j in range(2)]
                cnt = 0
                for i in range(NT):
                    t0 = i * 128
                    s_ = t0 // L
                    ti = (t0 % L) // 128
                    t0r = s_ * L + (LT - 1 - ti) * 128
                    u = ut[i % 2]
                    uk = "ut%d" % (i % 2)
                    S.dma("sp", u[:], P[t0:t0 + 128, C_U:C_U + 1024], r=["P"], w=[uk])
                    for d in range(2):
                        st_ = stg[cnt % 2]
                        sk = "ustg%d" % (cnt % 2)
                        cnt += 1
                        transposes_to(u[:], uk, 32, lambda q, nj: st_[:, q:q + nj, :], sk, bw=32, inw=128,
                                      eng=("act" if d == 0 else "dve"), rhs=(None if d == 0 else jrev[:]),
                                      rkey="jrev", pbanks=((6, 7) if d == 0 else (4, 5)))
                        dt_ = t0 if d == 0 else t0r
                        S.dma("sp", UT2[d][:, :, dt_:dt_ + 128].rearrange("k r t -> r k t"), st_[:], r=[sk],
                              w=["UT2"])
                S.barrier()
            with contextlib.ExitStack() as ps:
                lsb = lambda name, shape, dt=F32: ps.enter_context(nc.sbuf_tensor(uname(name), list(shape), dt))
                pt_ = {n: lsb("s5_" + n, [128, 32]) for n in
                       ("lre", "lim", "ldt", "rho", "tht", "cs", "sn", "ar", "ai", "t1", "t2", "t3", "cr", "ci")}
                it_ = lsb("s5_it", [128, 32], I32)
                bre = lsb("bre", [128, 32, 16])
                bim = lsb("bim", [128, 32, 16])
                Bbr = lsb("Bbr", [128, 32, 16])
                Bbi = lsb("Bbi", [128, 32, 16])
                btmp = lsb("btmp", [128, 32, 16])
                BDr = lsb("BDr", [128, 32, 32])
                BDi = lsb("BDi", [128, 32, 32])
                BpTr = lsb("BpTr", [32, 32, 128])
                BpTi = lsb("BpTi", [32, 32, 128])
                Zr = lsb("Zr", [32, 32, 128])
                Zi = lsb("Zi", [32, 32, 128])
                Cre = lsb("Cre", [128, 32, 32], BF16)
                nCre = lsb("nCre", [128, 32, 32], BF16)
                nCim = lsb("nCim", [128, 32, 32], BF16)
                jjt = lsb("jjt", [128, 512])
                tj = lsb("tj", [128, 512])
                tf = lsb("tf", [128, 512])
                iti = lsb("iti", [128, 512], I32)
                cst = lsb("cst", [128, 512])
                snt = lsb("snt", [128, 512])
                u2 = [lsb("u2_%d" % j, [32, 512]) for j in range(2)]
                p1 = lsb("p1", [128, 512])
                p2 = lsb("p2", [128, 512])
                inre = lsb("inre", [128, 512])
                inim = lsb("inim", [128, 512])
                zr = lsb("zr", [128, 512])
                zi = lsb("zi", [128, 512])
                qq = [lsb("qq%d" % j, [128, 512], BF16) for j in range(4)]
                xr = lsb("xr", [128, 1])
                xi = lsb("xi", [128, 1])
                c4 = lsb("c4", [128, 4])
                hre = lsb("hre", [128, 32])
                him = lsb("him", [128, 32])
                finr = lsb("finr", [128, nseq, 32])
                fini = lsb("fini", [128, nseq, 32])
                ystg = [lsb("ystg%d" % j, [128, 4, 32]) for j in range(2)]
                S.dma("sp", jjt[:], I["jj"][:, :], w=["jjt"])
                for zt, zk in ((BDr, "BDr"), (BDi, "BDi"), (Zr, "Zr"), (Zi, "Zi")):
                    S.op("dve", lambda e, zt=zt: e.memset(zt[:], 0.0), w=[zk])
                K_ = "s5p"

                def tt(o, a, b, op):
                    S.op("dve", lambda e: e.tensor_tensor(out=pt_[o][:], in0=pt_[a][:], in1=pt_[b][:], op=op),
                         r=[K_], w=[K_])

                def ts(o, a, s1, s2, op0, op1=None):
                    if op1 is None:
                        S.op("dve", lambda e: e.tensor_scalar(out=pt_[o][:], in0=pt_[a][:], scalar1=s1, scalar2=None,
                                                              op0=op0), r=[K_], w=[K_])
                    else:
                        S.op("dve", lambda e: e.tensor_scalar(out=pt_[o][:], in0=pt_[a][:], scalar1=s1, scalar2=s2,
                                                              op0=op0, op1=op1), r=[K_], w=[K_])

                def frac_sin(o, a):
                    S.op("dve", lambda e: e.tensor_copy(out=it_[:], in_=pt_[a][:]), r=[K_], w=[K_])
                    S.op("dve", lambda e: e.tensor_copy(out=pt_["t2"][:], in_=it_[:]), r=[K_], w=[K_])
                    tt("t2", a, "t2", ALU.subtract)
                    S.op("act", lambda e: e.activation(out=pt_[o][:], in_=pt_["t2"][:], func=AF.Sin, scale=SIN_SCALE),
                         r=[K_], w=[K_])

                cnt = 0
                ucnt = 0
                for d in range(2):
                    with nc.allow_non_contiguous_dma(reason="small s5 parameter transposes"):
                        for g2 in range(2):
                            sl = slice(g2 * 64, (g2 + 1) * 64)
                            S.dma("sp", pt_["lre"][sl, :],
                                  I["s5_lam_re"][l, d].rearrange("(k g) p -> g p k", g=2)[g2], w=[K_])
                            S.dma("sp", pt_["lim"][sl, :],
                                  I["s5_lam_im"][l, d].rearrange("(k g) p -> g p k", g=2)[g2], w=[K_])
                            S.dma("sp", pt_["ldt"][sl, :],
                                  I["s5_log_dt"][l, d].rearrange("(k g) -> g k", g=2)[g2].partition_broadcast(64),
                                  w=[K_])
                            if sample:
                                S.dma("sp", hre[sl, :], I["st_s5re"][l, d].rearrange("(k g) p -> g p k", g=2)[g2],
                                      w=["hre"])
                                S.dma("sp", him[sl, :], I["st_s5im"][l, d].rearrange("(k g) p -> g p k", g=2)[g2],
                                      w=["him"])
                    ts("lre", "lre", -1e-4, None, ALU.min)
                    S.op("act", lambda e: e.activation(out=pt_["ldt"][:], in_=pt_["ldt"][:], func=AF.Exp), r=[K_],
                         w=[K_])
                    tt("t1", "lre", "ldt", ALU.mult)
                    S.op("act", lambda e: e.activation(out=pt_["rho"][:], in_=pt_["t1"][:], func=AF.Exp), r=[K_],
                         w=[K_])
                    tt("tht", "lim", "ldt", ALU.mult)
                    ts("tht", "tht", 1.0 / TWO_PI, None, ALU.mult)
                    frac_sin("sn", "tht")
                    ts("t3", "tht", 0.25, None, ALU.add)
                    frac_sin("cs", "t3")
                    tt("ar", "rho", "cs", ALU.mult)
                    tt("ai", "rho", "sn", ALU.mult)
                    ts("ar", "ar", -1.0, None, ALU.add)
                    tt("t1", "ar", "lre", ALU.mult)
                    tt("t2", "ai", "lim", ALU.mult)
                    tt("cr", "t1", "t2", ALU.add)
                    tt("t1", "ai", "lre", ALU.mult)
                    tt("t2", "ar", "lim", ALU.mult)
                    tt("ci", "t1", "t2", ALU.subtract)
                    tt("t1", "lre", "lre", ALU.mult)
                    tt("t2", "lim", "lim", ALU.mult)
                    tt("t1", "t1", "t2", ALU.add)
                    S.op("dve", lambda e: e.reciprocal(out=pt_["t1"][:], in_=pt_["t1"][:]), r=[K_], w=[K_])
                    tt("cr", "cr", "t1", ALU.mult)
                    tt("ci", "ci", "t1", ALU.mult)
                    S.dma("sp", bre[:], I["s5_b_re"][l, d].rearrange("(k g) p c -> (g p) k c", g=2), w=["bre"])
                    S.dma("sp", bim[:], I["s5_b_im"][l, d].rearrange("(k g) p c -> (g p) k c", g=2), w=["bim"])
                    crb = pt_["cr"][:, :].unsqueeze(2).to_broadcast([128, 32, 16])
                    cib = pt_["ci"][:, :].unsqueeze(2).to_broadcast([128, 32, 16])
                    S.op("dve", lambda e: e.tensor_tensor(out=Bbr[:], in0=bre[:], in1=crb, op=ALU.mult),
                         r=["bre", K_], w=["Bbr"])
                    S.op("dve", lambda e: e.tensor_tensor(out=btmp[:], in0=bim[:], in1=cib, op=ALU.mult),
                         r=["bim", K_], w=["btmp"])
                    S.op("dve", lambda e: e.tensor_sub(out=Bbr[:], in0=Bbr[:], in1=btmp[:]), r=["Bbr", "btmp"],
                         w=["Bbr"])
                    S.op("dve", lambda e: e.tensor_tensor(out=Bbi[:], in0=bre[:], in1=cib, op=ALU.mult),
                         r=["bre", K_], w=["Bbi"])
                    S.op("dve", lambda e: e.tensor_tensor(out=btmp[:], in0=bim[:], in1=crb, op=ALU.mult),
                         r=["bim", K_], w=["btmp"])
                    S.op("dve", lambda e: e.tensor_add(out=Bbi[:], in0=Bbi[:], in1=btmp[:]), r=["Bbi", "btmp"],
                         w=["Bbi"])
                    for (bsrc, bkey, bd, bdk, bp, bpk) in ((Bbr, "Bbr", BDr, "BDr", BpTr, "BpTr"),
                                                           (Bbi, "Bbi", BDi, "BDi", BpTi, "BpTi")):
                        S.op("dve", lambda e: e.tensor_copy(out=bd[0:64, :, 0:16], in_=bsrc[0:64, :, :]), r=[bkey],
                             w=[bdk])
                        S.op("dve", lambda e: e.tensor_copy(out=bd[64:128, :, 16:32], in_=bsrc[64:128, :, :]),
                             r=[bkey], w=[bdk])
                        transposes_to(bd[:].rearrange("p k c -> p (k c)"), bdk, 32,
                                      lambda q, nj, bp=bp: bp[:, q:q + nj, :], bpk, bw=32, inw=128)
                    for (zt, zk, nm) in ((Zr, "Zr", "s5_c_re"), (Zi, "Zi", "s5_c_im")):
                        csrc = I[nm][l, d].rearrange("(k g) c p -> g c k p", g=2)
                        S.dma("sp", zt[0:16, :, 0:64], csrc[0], w=[zk])
                        S.dma("sp", zt[16:32, :, 64:128], csrc[1], w=[zk])
                    zf = lambda zt: zt[:].rearrange("r k q -> r (k q)")
                    transposes_to(zf(Zr), "Zr", 32, lambda q, nj: Cre[:, q:q + nj, :], "Cre", bw=128, inw=32)
                    transposes_to(zf(Zr), "Zr", 32, lambda q, nj: nCre[:, q:q + nj, :], "nCre", bw=128, inw=32,
                                  scale=-1.0)
                    transposes_to(zf(Zi), "Zi", 32, lambda q, nj: nCim[:, q:q + nj, :], "nCim", bw=128, inw=32,
                                  scale=-1.0)
                    Cm = (Cre, nCre, nCim, nCim)
                    Ck = ("Cre", "nCre", "nCim", "nCim")
                    for kt in range(32):
                        S.op("dve", lambda e: e.tensor_scalar(out=tj[:, 0:Ls], in0=jjt[:, 0:Ls],
                                                              scalar1=pt_["tht"][:, kt:kt + 1], scalar2=None,
                                                              op0=ALU.mult), r=["jjt", K_], w=["tj"])
                        for (dst, dk_, off) in ((snt, "snt", 0.0), (cst, "cst", 0.25)):
                            if off != 0.0:
                                S.op("dve", lambda e: e.tensor_scalar(out=tj[:, 0:Ls], in0=tj[:, 0:Ls], scalar1=off,
                                                                      scalar2=None, op0=ALU.add), r=["tj"], w=["tj"])
                            S.op("dve", lambda e: e.tensor_copy(out=iti[:, 0:Ls], in_=tj[:, 0:Ls]), r=["tj"],
                                 w=["iti"])
                            S.op("dve", lambda e: e.tensor_copy(out=tf[:, 0:Ls], in_=iti[:, 0:Ls]), r=["iti"],
                                 w=["tf"])
                            S.op("dve", lambda e: e.tensor_sub(out=tf[:, 0:Ls], in0=tj[:, 0:Ls], in1=tf[:, 0:Ls]),
                                 r=["tj", "tf"], w=["tf"])
                            S.op("act", lambda e, dst=dst: e.activation(out=dst[:, 0:Ls], in_=tf[:, 0:Ls],
                                                                        func=AF.Sin, scale=SIN_SCALE),
                                 r=["tf"], w=[dk_])
                        rhob = pt_["rho"][:, kt:kt + 1].to_broadcast([128, Ls])
                        for s_ in range(nseq):
                            if sample:
                                S.op("dve", lambda e: e.tensor_copy(out=xr[:], in_=hre[:, kt:kt + 1]), r=["hre"],
                                     w=["xr"])
                                S.op("dve", lambda e: e.tensor_copy(out=xi[:], in_=him[:, kt:kt + 1]), r=["him"],
                                     w=["xi"])
                            else:
                                S.op("dve", lambda e: e.memset(xr[:], 0.0), w=["xr"])
                                S.op("dve", lambda e: e.memset(xi[:], 0.0), w=["xi"])
                            for seg in range(nseg):
                                tau0 = s_ * L + seg * Ls
                                u_ = u2[ucnt % 2]
                                uk = "u2_%d" % (ucnt % 2)
                                ucnt += 1
                                S.dma("sp", u_[:, 0:Ls], UT2[d][kt, :, tau0:tau0 + Ls], r=["UT2"], w=[uk])
                                S.op("pe", lambda e: e.matmul(psb[0][:, 0:Ls], lhsT=BpTr[:, kt, :], rhs=u_[:, 0:Ls],
                                                              start=True, stop=True), r=["BpTr", uk], w=[PK[0]])
                                S.op("pe", lambda e: e.matmul(psb[1][:, 0:Ls], lhsT=BpTi[:, kt, :], rhs=u_[:, 0:Ls],
                                                              start=True, stop=True), r=["BpTi", uk], w=[PK[1]])
                                c_ = cst[:, 0:Ls]
                                s__ = snt[:, 0:Ls]
                                S.op("dve", lambda e: e.tensor_mul(out=p1[:, 0:Ls], in0=psb[0][:, 0:Ls], in1=c_),
                                     r=[PK[0], "cst"], w=["p1"])
                                S.op("dve", lambda e: e.tensor_mul(out=p2[:, 0:Ls], in0=psb[1][:, 0:Ls], in1=s__),
                                     r=[PK[1], "snt"], w=["p2"])
                                S.op("pool", lambda e: e.tensor_add(out=inre[:, 0:Ls], in0=p1[:, 0:Ls],
                                                                    in1=p2[:, 0:Ls]), r=["p1", "p2"], w=["inre"])
                                S.op("dve", lambda e: e.tensor_mul(out=p1[:, 0:Ls], in0=psb[1][:, 0:Ls], in1=c_),
                                     r=[PK[1], "cst"], w=["p1"])
                                S.op("dve", lambda e: e.tensor_mul(out=p2[:, 0:Ls], in0=psb[0][:, 0:Ls], in1=s__),
                                     r=[PK[0], "snt"], w=["p2"])
                                S.op("pool", lambda e: e.tensor_sub(out=inim[:, 0:Ls], in0=p1[:, 0:Ls],
                                                                    in1=p2[:, 0:Ls]), r=["p1", "p2"], w=["inim"])
                                S.op("dve", lambda e: e.tensor_tensor_scan(out=zr[:, 0:Ls], data0=rhob,
                                                                           data1=inre[:, 0:Ls], initial=xr[:, 0:1],
                                                                           op0=ALU.mult, op1=ALU.add),
                                     r=[K_, "inre", "xr"], w=["zr"])
                                S.op("dve", lambda e: e.tensor_tensor_scan(out=zi[:, 0:Ls], data0=rhob,
                                                                           data1=inim[:, 0:Ls], initial=xi[:, 0:1],
                                                                           op0=ALU.mult, op1=ALU.add),
                                     r=[K_, "inim", "xi"], w=["zi"])
                                for (qi, a_, ak, b_, bk_, en) in ((0, cst, "cst", zr, "zr", "dve"),
                                                                  (1, snt, "snt", zi, "zi", "pool"),
                                                                  (2, snt, "snt", zr, "zr", "pool"),
                                                                  (3, cst, "cst", zi, "zi", "dve")):
                                    S.op(en, lambda e, qi=qi, a_=a_, b_=b_: e.tensor_mul(
                                        out=qq[qi][:, 0:Ls], in0=a_[:, 0:Ls], in1=b_[:, 0:Ls]),
                                        r=[ak, bk_], w=["qq%d" % qi])
                                e0 = Ls - 1
                                for (ci_, a_, ak, b_, bk_) in ((0, cst, "cst", zr, "zr"), (1, snt, "snt", zi, "zi"),
                                                               (2, snt, "snt", zr, "zr"), (3, cst, "cst", zi, "zi")):
                                    S.op("dve", lambda e, ci_=ci_, a_=a_, b_=b_: e.tensor_mul(
                                        out=c4[:, ci_:ci_ + 1], in0=a_[:, e0:e0 + 1], in1=b_[:, e0:e0 + 1]),
                                        r=[ak, bk_], w=["c4"])
                                S.op("dve", lambda e: e.tensor_sub(out=xr[:], in0=c4[:, 0:1], in1=c4[:, 1:2]),
                                     r=["c4"], w=["xr"])
                                S.op("dve", lambda e: e.tensor_add(out=xi[:], in0=c4[:, 2:3], in1=c4[:, 3:4]),
                                     r=["c4"], w=["xi"])
                                for jb in range(nsub):
                                    for qi in range(4):
                                        S.op("pe", lambda e, jb=jb, qi=qi: e.matmul(
                                            psb[2][:, jb * 32:(jb + 1) * 32], lhsT=qq[qi][:, jb * 128:(jb + 1) * 128],
                                            rhs=Cm[qi][:, kt, :], start=(qi == 0), stop=(qi == 3)),
                                            r=["qq%d" % qi, Ck[qi]], w=[PK[2]])
                                ys_ = ystg[cnt % 2]
                                yk = "ystg%d" % (cnt % 2)
                                cnt += 1
                                S.op("act", lambda e: e.activation(
                                    out=ys_[:, 0:nsub, :], in_=psb[2][:, 0:nsub * 32].rearrange("p (j c) -> p j c",
                                                                                                j=nsub),
                                    func=AF.Copy), r=[PK[2]], w=[yk])
                                S.dma("sp", YS5[d][tau0:tau0 + Ls, kt * 32:(kt + 1) * 32].rearrange(
                                    "(j t) c -> t j c", t=128), ys_[:, 0:nsub, :], r=[yk], w=["YS5"])
                            if not sample:
                                S.op("dve", lambda e: e.tensor_copy(out=finr[:, s_, kt:kt + 1], in_=xr[:]), r=["xr"],
                                     w=["finr"])
                                S.op("dve", lambda e: e.tensor_copy(out=fini[:, s_, kt:kt + 1], in_=xi[:]), r=["xi"],
                                     w=["fini"])
                    if not sample:
                        with nc.allow_non_contiguous_dma(reason="small s5 state outputs"):
                            for s_ in range(nseq):
                                for g2 in range(2):
                                    sl = slice(g2 * 64, (g2 + 1) * 64)
                                    S.dma("sp", O["ns_s5re"][s_, l, d].rearrange("(k g) p -> g p k", g=2)[g2],
                                          finr[sl, s_, :], r=["finr"], w=["ns_s5"])
                                    S.dma("sp", O["ns_s5im"][s_, l, d].rearrange("(k g) p -> g p k", g=2)[g2],
                                          fini[sl, s_, :], r=["fini"], w=["ns_s5"])
                S.barrier()

        def phase_tail(l, T, L, jrow, xsrc, xdst):
            LT = L // 128
            NB = BLK // 128
            with contextlib.ExitStack() as ps:
                lsb = lambda name, shape, dt=F32: ps.enter_context(nc.sbuf_tensor(uname(name), list(shape), dt))
                ls = {"xt": [lsb("xt0", [128, D]), lsb("xt1", [128, D])], "junk": lsb("junk", [128, D]),
                      "ss": lsb("ss", [128, 1])}
                wbs = [lsb("wb0", [128, 8192], BF16), lsb("wb1", [128, 8192], BF16)]
                hTa = lsb("hTa", [128, KC, BLK], BF16)
                Ybuf = lsb("Ybuf", [128, NB, D])
                GPt = lsb("GPt", [128, D])
                yt = [lsb("yt%d" % j, [128, 1536]) for j in range(2)]
                s5d = lsb("s5d", [128, 1024])
                gt = [lsb("gt%d" % j, [128, 512]) for j in range(2)]
                vt = [lsb("vt%d" % j, [128, 512]) for j in range(2)]
                mgt = [lsb("mgt%d" % j, [128, 512]) for j in range(2)]
                xg = lsb("xg", [128, 512])
                tg = lsb("tg", [128, 512])
                bcload(GPt[:], "GPt", MOD[jrow, 2 * D:3 * D])
                bcload(ls["junk"][:], "junk", I["norm_mix_post"][l, :])
                S.op("dve", lambda e: e.tensor_mul(out=GPt[:], in0=GPt[:], in1=ls["junk"][:]), r=["GPt", "junk"],
                     w=["GPt"])
                bcload(s5d[:], "s5d", I["s5_d"][l, :])
                ec = [0]

                def epi_merge(bi, t0):
                    def f(i, c0, cw, pt, pk):
                        n = ec[0] % 2
                        ec[0] += 1
                        rs_ = slice(t0 + i * 128, t0 + (i + 1) * 128)
                        g_, gk = gt[n], "gt%d" % n
                        v_, vk = vt[n], "vt%d" % n
                        m_, mk = mgt[n], "mgt%d" % n
                        gc = C_GATE + bi * D + c0
                        S.dma("sp", g_[:, 0:cw], P[rs_, gc:gc + cw], r=["P"], w=[gk])
                        S.op("act", lambda e: e.activation(out=g_[:, 0:cw], in_=g_[:, 0:cw], func=AF.Sigmoid), r=[gk],
                             w=[gk])
                        if bi == 2:
                            S.op("act", lambda e: e.activation(out=v_[:, 0:cw], in_=pt[:, cw:2 * cw],
                                                               func=AF.Sigmoid), r=[pk], w=[vk])
                            S.op("dve", lambda e: e.tensor_mul(out=v_[:, 0:cw], in0=pt[:, 0:cw], in1=v_[:, 0:cw]),
                                 r=[pk, vk], w=[vk])
                            S.op("dve", lambda e: e.tensor_mul(out=v_[:, 0:cw], in0=v_[:, 0:cw], in1=g_[:, 0:cw]),
                                 r=[vk, gk], w=[vk])
                        else:
                            S.op("dve", lambda e: e.tensor_mul(out=v_[:, 0:cw], in0=pt[:, 0:cw], in1=g_[:, 0:cw]),
                                 r=[pk, gk], w=[vk])
                        mgk = ("MG", i, c0 // 512)
                        if bi > 0:
                            S.dma("sp", m_[:, 0:cw], MG[rs_, c0:c0 + cw], r=[mgk], w=[mk])
                            S.op("dve", lambda e: e.tensor_add(out=v_[:, 0:cw], in0=v_[:, 0:cw], in1=m_[:, 0:cw]),
                                 r=[vk, mk], w=[vk])
                        S.dma("sp", MG[rs_, c0:c0 + cw], v_[:, 0:cw], r=[vk], w=[mgk])
                    return f

                def wload_glu(Wsrc):
                    def f(wb, wk, c0, cw):
                        S.dma("pool", wb[:, :, 0:cw], Wsrc[:, c0:c0 + cw].rearrange("(k p) n -> p k n", p=128),
                              w=[wk])
                        S.dma("pool", wb[:, :, cw:2 * cw],
                              Wsrc[:, D + c0:D + c0 + cw].rearrange("(k p) n -> p k n", p=128), w=[wk])
                    return f

                def epi_y(i, c0, cw, pt, pk):
                    S.op("act", lambda e: e.activation(out=Ybuf[:, i, c0:c0 + cw], in_=pt[:, 0:cw], func=AF.Copy),
                         r=[pk], w=["Ybuf"])

                for bi_ in range(T // BLK):
                    t0 = bi_ * BLK
                    for (br, src, Wn) in ((0, YS, "w_ssm_out"), (1, ZW, "w_wkv_out")):
                        for i in range(NB):
                            y_ = yt[i % 2]
                            yk = "yt%d" % (i % 2)
                            S.dma("sp", y_[:], src[t0 + i * 128:t0 + (i + 1) * 128, :], r=["BSRC"], w=[yk])
                            transposes_to(y_[:], yk, 12, lambda q, nj, i=i: hTa[:, q:q + nj, i * 128:(i + 1) * 128],
                                          "hTa")
                        proj(wbs, hTa, "hTa", 12, D, NB, epi_merge(br, t0), wload_plain(I[Wn][l]))
                    for i in range(NB):
                        ta = t0 + i * 128
                        s_ = ta // L
                        ti = (ta % L) // 128
                        tr = s_ * L + (LT - 1 - ti) * 128
                        yf = ls["xt"][0]
                        yb = ls["xt"][1]
                        uu = yt[i % 2]
                        uk = "yt%d" % (i % 2)
                        S.dma("sp", yf[:, 0:1024], YS5[0][ta:ta + 128, :], r=["YS5"], w=["xt0"])
                        S.dma("sp", yb[:, 0:1024], YS5[1][tr:tr + 128, :], r=["YS5"], w=["xt1"])
                        S.dma("sp", uu[:, 0:1024], P[ta:ta + 128, C_U:C_U + 1024], r=["P"], w=[uk])
                        S.op("dve", lambda e: e.tensor_mul(out=uu[:, 0:1024], in0=uu[:, 0:1024], in1=s5d[:]),
                             r=[uk, "s5d"], w=[uk])
                        for hb in range(2):
                            pb = 4 + hb
                            for j in range(4):
                                cb = hb * 4 + j
                                cs_ = slice(cb * 128, (cb + 1) * 128)
                                o_ = psb[pb][:, j * 128:(j + 1) * 128]
                                S.op("pe", lambda e: e.matmul(o_, lhsT=yf[:, cs_], rhs=ident[:], start=True,
                                                              stop=False), r=["xt0", "ident"], w=[PK[pb]])
                                S.op("pe", lambda e: e.matmul(o_, lhsT=yb[:, cs_], rhs=jrev[:], start=False,
                                                              stop=False), r=["xt1", "jrev"], w=[PK[pb]])
                                S.op("pe", lambda e: e.matmul(o_, lhsT=uu[:, cs_], rhs=ident[:], start=False,
                                                              stop=True), r=[uk, "ident"], w=[PK[pb]])
                            S.op("act", lambda e: e.activation(out=xg[:], in_=psb[pb][:, :], func=AF.Copy),
                                 r=[PK[pb]], w=["xg"])
                            S.op("dve", lambda e: e.tensor_mul(out=tg[:], in0=xg[:], in1=xg[:]), r=["xg"], w=["tg"])
                            S.op("dve", lambda e: e.tensor_scalar(out=tg[:], in0=tg[:], scalar1=0.044715, scalar2=1.0,
                                                                  op0=ALU.mult, op1=ALU.add), r=["tg"], w=["tg"])
                            S.op("dve", lambda e: e.tensor_mul(out=tg[:], in0=tg[:], in1=xg[:]), r=["tg", "xg"],
                                 w=["tg"])
                            S.op("act", lambda e: e.activation(out=tg[:], in_=tg[:], func=AF.Sigmoid,
                                                               scale=2.0 * math.sqrt(2.0 / math.pi)), r=["tg"],
                                 w=["tg"])
                            S.op("dve", lambda e: e.tensor_tensor(
                                out=hTa[:, hb * 4:(hb + 1) * 4, i * 128:(i + 1) * 128],
                                in0=xg[:].rearrange("p (j t) -> p j t", j=4),
                                in1=tg[:].rearrange("p (j t) -> p j t", j=4), op=ALU.mult), r=["xg", "tg"], w=["hTa"])
                    proj(wbs, hTa, "hTa", 8, D, NB, epi_merge(2, t0), wload_glu(I["w_s5_glu"][l]), cwmax=256, wmul=2)
                    for i in range(NB):
                        xt = ls["xt"][i % 2]
                        xk = "xt%d" % (i % 2)
                        S.dma("sp", xt[:], MG[t0 + i * 128:t0 + (i + 1) * 128, :],
                              r=[("MG", i, c) for c in range(4)],
                              w=[xk])
                        transposes_to(xt[:], xk, KC, lambda q, nj, i=i: hTa[:, q:q + nj, i * 128:(i + 1) * 128],
                                      "hTa")
                    proj(wbs, hTa, "hTa", KC, D, NB, epi_y, wload_plain(I["w_out"][l]))
                    postnorm_residual(ls, Ybuf, NB, t0, xsrc, xdst, GPt)
                S.barrier()

        def phase_ffn(l, T, GW, jrow, xsrc, xdst):
            NB = BLK // 128
            with contextlib.ExitStack() as ps:
                lsb = lambda name, shape, dt=F32: ps.enter_context(nc.sbuf_tensor(uname(name), list(shape), dt))
                ls = {"xt": [lsb("xt0", [128, D]), lsb("xt1", [128, D])], "junk": lsb("junk", [128, D]),
                      "ss": lsb("ss", [128, 1])}
                wbs = [lsb("wb0", [128, 8192], BF16), lsb("wb1", [128, 8192], BF16)]
                st = [lsb("st0", [128, 512]), lsb("st1", [128, 512])]
                hT = lsb("hT", [128, KC, BLK], BF16)
                Gt, SHt, GPt = mod_tiles(lsb, ls, l, jrow, 4, 3, 5, "norm_ffn_pre", "norm_ffn_post")
                cnt = [0]
                for bi in range(T // BLK):
                    t0 = bi * BLK
                    norm_to_fm(ls, xsrc, t0, NB, hT, Gt, SHt)

                    def epi(i, c0, cw, pt, pk, t0=t0):
                        cnt[0] += 1
                        s_ = st[cnt[0] % 2]
                        sk = "st%d" % (cnt[0] % 2)
                        S.op("act", lambda e: e.activation(out=s_[:, 0:cw], in_=pt[:, 0:cw], func=AF.Copy),
                             r=[pk], w=[sk])
                        S.dma("sp", GU[t0 + i * 128:t0 + (i + 1) * 128, c0:c0 + cw], s_[:, 0:cw], r=[sk], w=["GU"])

                    proj(wbs, hT, "hT", KC, 2 * D_FF, NB, epi, wload_plain(I["w_ffn_in"][l]))
                S.barrier()
            with contextlib.ExitStack() as ps:
                up = [ps.enter_context(nc.sbuf_tensor(uname("fup%d" % j), [128, 512], F32)) for j in range(2)]
                sg = ps.enter_context(nc.sbuf_tensor(uname("fsg"), [128, 512], F32))
                uc = [0]

                def post(i, c0, cw, y, yk):
                    u_ = up[uc[0] % 2]
                    uk = "fup%d" % (uc[0] % 2)
                    uc[0] += 1
                    S.dma("sp", u_[:, 0:cw], GU[i * 128:(i + 1) * 128, D_FF + c0:D_FF + c0 + cw], r=["GU"], w=[uk])
                    S.op("act", lambda e: e.activation(out=sg[:, 0:cw], in_=y[:, 0:cw], func=AF.Sigmoid), r=[yk],
                         w=["fsg"])
                    S.op("dve", lambda e: e.tensor_mul(out=y[:, 0:cw], in0=y[:, 0:cw], in1=sg[:, 0:cw]),
                         r=[yk, "fsg"], w=[yk])
                    S.op("dve", lambda e: e.tensor_mul(out=y[:, 0:cw], in0=y[:, 0:cw], in1=u_[:, 0:cw]),
                         r=[yk, uk], w=[yk])
                    S.dma("sp", ACTS[i * 128:(i + 1) * 128, c0:c0 + cw], y[:, 0:cw], r=[yk], w=["ACTS"])

                conv_pass(ps, T, GW, GU, 0, D_FF,
                          lambda c0, cw: (I["ffn_conv_w"][l, 0, c0:c0 + cw], I["ffn_conv_w"][l, 1, c0:c0 + cw],
                                          I["ffn_conv_w"][l, 2, c0:c0 + cw], I["ffn_conv_b"][l, c0:c0 + cw]), post)
                S.barrier()
            with contextlib.ExitStack() as ps:
                lsb = lambda name, shape, dt=F32: ps.enter_context(nc.sbuf_tensor(uname(name), list(shape), dt))
                ls = {"xt": [lsb("xt0", [128, D]), lsb("xt1", [128, D])], "junk": lsb("junk", [128, D]),
                      "ss": lsb("ss", [128, 1])}
                wbs = [lsb("wb0", [128, 8192], BF16), lsb("wb1", [128, 8192], BF16)]
                hTf = lsb("hTf", [128, 44, BLK], BF16)
                Ybuf = lsb("Ybuf", [128, NB, D])
                GPt = lsb("GPt", [128, D])
                at = [lsb("at%d" % j, [128, 2816]) for j in range(2)]
                bcload(GPt[:], "GPt", MOD[jrow, 5 * D:6 * D])
                bcload(ls["junk"][:], "junk", I["norm_ffn_post"][l, :])
                S.op("dve", lambda e: e.tensor_mul(out=GPt[:], in0=GPt[:], in1=ls["junk"][:]), r=["GPt", "junk"],
                     w=["GPt"])

                def epi_y(i, c0, cw, pt, pk):
                    S.op("act", lambda e: e.activation(out=Ybuf[:, i, c0:c0 + cw], in_=pt[:, 0:cw], func=AF.Copy),
                         r=[pk], w=["Ybuf"])

                ac = 0
                for bi in range(T // BLK):
                    t0 = bi * BLK
                    for i in range(NB):
                        for hf in range(2):
                            a_ = at[ac % 2]
                            ak = "at%d" % (ac % 2)
                            ac += 1
                            S.dma("sp", a_[:], ACTS[t0 + i * 128:t0 + (i + 1) * 128, hf * 2816:(hf + 1) * 2816],
                                  r=["ACTS"], w=[ak])
                            transposes_to(a_[:], ak, 22,
                                          lambda q, nj, i=i, hf=hf: hTf[:, hf * 22 + q:hf * 22 + q + nj,
                                                                        i * 128:(i + 1) * 128], "hTf")
                    proj(wbs, hTf, "hTf", 44, D, NB, epi_y, wload_plain(I["w_ffn_out"][l]), cwmax=128)
                    postnorm_residual(ls, Ybuf, NB, t0, xsrc, xdst, GPt)
                S.barrier()

        kb.fns = dict(phase_mod=phase_mod, phase_inproj=phase_inproj, phase_convs=phase_convs, phase_ssd=phase_ssd)

        groups = debug.get("groups", ["s", "p"])
        skip = debug.get("skip", ())
        nl = debug.get("layers", DEPTH)
        for l in range(nl):
            phase_mod(l)
            for gname in groups:
                last = (l == DEPTH - 1)
                if gname == "s":
                    T, L, GW, jrow, sample = TS, TS, 64, 0, True
                    xin = I["xs"] if l == 0 else XB
                    xmid = XA
                    xout = O["ys"] if last else XB
                else:
                    T, L, GW, jrow, sample = TP, 256, 256, 1, False
                    xin = I["xp"] if l == 0 else XPB
                    xmid = XPA
                    xout = O["yp"] if last else XPB
                phase_inproj(l, xin, T, jrow)
                phase_convs(l, T, GW)
                if "ssd" not in skip:
                    phase_ssd(l, T, L, sample)
                if "wkv" not in skip:
                    phase_wkv_prep(l, T)
                    phase_wkv_scan(l, T, L, sample)
                    phase_wkv_post(l, T)
                if "s5" not in skip:
                    phase_s5(l, T, L, sample)
                if "tail" not in skip:
                    phase_tail(l, T, L, jrow, xin, xmid)
                if "ffn" not in skip:
                    phase_ffn(l, T, GW, jrow, xmid, xout)
        S.barrier()
    return nc, S


def _consts():
    k = np.arange(128)
    tri = (k[:, None] <= k[None, :]).astype(np.float32)
    m48 = np.zeros((48, 768), np.float32)
    for j in range(4):
        for g in range(12):
            m48[j * 12 + g, g * 64:(g + 1) * 64] = 1.0
    return {
        "ident": np.eye(128, dtype=np.float32), "tri": tri, "trit": np.ascontiguousarray(tri.T),
        "jrev": np.ascontiguousarray(np.eye(128, dtype=np.float32)[::-1]), "mask48": m48,
        "jj": np.ascontiguousarray(np.broadcast_to(np.arange(1, 513, dtype=np.float32), (128, 512))),
        "zrow": np.zeros((1, 512), np.float32),
    }


def _prep_inputs(inputs):
    f = lambda a: np.ascontiguousarray(np.asarray(a, dtype=np.float32))
    shared = {nm: f(inputs[nm]).reshape(shp) for nm, shp in PARAM_SHAPES.items()}
    shared.update(_consts())
    maps = []
    for c in range(8):
        b = c % 4
        m = dict(shared)
        m["xs"] = f(inputs["x_sample"][b])
        m["xp"] = f(np.asarray(inputs["x_prompt"])[4 * b:4 * b + 4].reshape(TP, D))
        m["cond2"] = f(np.stack([np.asarray(inputs["c"])[b], np.asarray(inputs["c_ctx"])], 0))
        m["st_ssm"] = f(inputs["state_ssm"][b])
        m["st_wkv"] = f(inputs["state_wkv"][b])
        m["st_s5re"] = f(inputs["state_s5_re"][b])
        m["st_s5im"] = f(inputs["state_s5_im"][b])
        maps.append(m)
    return maps


def kernel(**inputs):
    nc, S = build()
    maps = _prep_inputs(inputs)
    res = run_bass_kernel_spmd(nc, maps, core_ids=list(range(8)))
    r = res.results
    y_prompt = np.zeros((16, 256, D), np.float32)
    y_sample = np.zeros((4, TS, D), np.float32)
    ns_ssm = np.zeros((16, DEPTH, 2, 24, 64, 128), np.float32)
    ns_wkv = np.zeros((16, DEPTH, 2, 24, 64, 64), np.float32)
    ns_re = np.zeros((16, DEPTH, 2, 64, 64), np.float32)
    ns_im = np.zeros((16, DEPTH, 2, 64, 64), np.float32)
    for b in range(4):
        o = r[b]
        y_sample[b] = np.asarray(o["ys"])
        y_prompt[4 * b:4 * b + 4] = np.asarray(o["yp"]).reshape(4, 256, D)
        ns_ssm[4 * b:4 * b + 4] = np.asarray(o["ns_ssm"])
        ns_wkv[4 * b:4 * b + 4] = np.asarray(o["ns_wkv"])
        ns_re[4 * b:4 * b + 4] = np.asarray(o["ns_s5re"])
        ns_im[4 * b:4 * b + 4] = np.asarray(o["ns_s5im"])
    return (y_prompt, y_sample, ns_ssm, ns_wkv, ns_re, ns_im)
```

```python
import contextlib
import math
import numpy as np
import concourse.bass as bass
import concourse.mybir as mybir
import concourse.ap as apm
from concourse.bass_utils import run_bass_kernel_spmd

F32 = mybir.dt.float32
BF16 = mybir.dt.bfloat16
I32 = mybir.dt.int32
AF = mybir.ActivationFunctionType
ALU = mybir.AluOpType
AX = mybir.AxisListType

D = 2048
KC = 16
DEPTH = 2
N_IN = 16560
D_FF = 5632
TS = 4096
TP = 1024
BLK = 512
BLK_IN = 2048
EPS = 1e-6
TWO_PI = 2.0 * math.pi
SIN_SCALE = 6.28318

C_Z = 0
C_XBC = 1536
C_DTF = 4096
C_WKV = 4144
C_U = 9392
C_GATE = 10416
NWKV = 5248


class Sched:
    def __init__(self, nc, es, n_dma_sems=24):
        self.nc = nc
        self.eng = {"pe": nc.tensor, "act": nc.scalar, "dve": nc.vector, "pool": nc.gpsimd, "sp": nc.sync}
        self.sem = {e: es.enter_context(nc.semaphore("sem_" + e)) for e in ("pe", "act", "dve", "pool")}
        self.cnt = {e: 0 for e in self.sem}
        self.dsem = [es.enter_context(nc.semaphore("dsem%d" % i)) for i in range(n_dma_sems)]
        self.dval = [0] * n_dma_sems
        self.dnext = 0
        self.waited = {e: {} for e in self.eng}
        self.lastw = {}
        self.readers = {}
        self.ninst = 0

    def _wait(self, e, tok, force=False):
        sem, val, src = tok
        w = self.waited[e]
        if w.get(id(sem), 0) >= val:
            return
        if src == e == "pe" and not force:
            return
        self.eng[e].wait_ge(sem, val)
        w[id(sem)] = val

    def _deps(self, e, r, w):
        for k in list(r) + list(w):
            t = self.lastw.get(k)
            if t is not None:
                self._wait(e, t)
        for k in w:
            for t in self.readers.get(k, ()):
                self._wait(e, t)

    def _commit(self, tok, r, w):
        for k in w:
            self.lastw[k] = tok
            self.readers[k] = []
        for k in r:
            lst = self.readers.setdefault(k, [])
            lst.append(tok)
            if len(lst) > 64:
                best = {}
                for t in lst:
                    if id(t[0]) not in best or best[id(t[0])][1] < t[1]:
                        best[id(t[0])] = t
                self.readers[k] = list(best.values())

    def op(self, e, fn, r=(), w=()):
        self._deps(e, r, w)
        ins = fn(self.eng[e])
        self.cnt[e] += 1
        ins.then_inc(self.sem[e], 1)
        tok = (self.sem[e], self.cnt[e], e)
        self._commit(tok, r, w)
        self.ninst += 1
        return tok

    def dma(self, q, out, in_, r=(), w=(), **kw):
        i = self.dnext
        self.dnext = (self.dnext + 1) % len(self.dsem)
        sem = self.dsem[i]
        if self.dval[i] > 0:
            self._wait(q, (sem, self.dval[i], "dma"))
        self._deps(q, r, w)
        ins = self.eng[q].dma_start(out=out, in_=in_, **kw)
        self.dval[i] += 16
        ins.then_inc(sem, 16)
        tok = (sem, self.dval[i], "dma")
        self._commit(tok, r, w)
        self.ninst += 1
        return tok

    def barrier(self):
        toks = [(self.sem[e], self.cnt[e], e) for e in self.sem if self.cnt[e] > 0]
        toks += [(self.dsem[i], self.dval[i], "dma") for i in range(len(self.dsem)) if self.dval[i] > 0]
        for e in self.eng:
            for t in toks:
                self._wait(e, t, force=True)
        self.lastw.clear()
        self.readers.clear()


class KB:
    def __init__(self, debug=None):
        self.debug = debug or {}
        self.nc = bass.Bass("TRN2", target_bir_lowering=False)
        self.I = {}
        self.O = {}

    def din(self, name, shape, dt=F32):
        self.I[name] = self.nc.dram_tensor(name, list(shape), dt, kind="ExternalInput").ap()
        return self.I[name]

    def dout(self, name, shape, dt=F32):
        self.O[name] = self.nc.dram_tensor(name, list(shape), dt, kind="ExternalOutput").ap()
        return self.O[name]

    def dscr(self, name, shape, dt=F32):
        kind = "ExternalOutput" if name in self.debug.get("dump", ()) else "Internal"
        return self.nc.dram_tensor(name, list(shape), dt, kind=kind).ap()


PARAM_SHAPES = {
    "w_mod": (DEPTH, D, 6 * D), "b_mod": (DEPTH, 6 * D),
    "norm_mix_pre": (DEPTH, D), "norm_mix_post": (DEPTH, D), "norm_ffn_pre": (DEPTH, D), "norm_ffn_post": (DEPTH, D),
    "w_in": (DEPTH, D, N_IN),
    "ssm_conv_w": (DEPTH, 3, 2560), "ssm_conv_b": (DEPTH, 2560), "ssm_dt_bias": (DEPTH, 48),
    "ssm_a_log": (DEPTH, 48), "ssm_d": (DEPTH, 24), "ssm_norm": (DEPTH, 1536), "w_ssm_out": (DEPTH, 1536, D),
    "wkv_mu_prev": (DEPTH, NWKV), "wkv_mu_next": (DEPTH, NWKV), "wkv_w0": (DEPTH, 2, 1536),
    "wkv_w_up": (DEPTH, 2, 96, 1536), "wkv_a0": (DEPTH, 2, 1536), "wkv_a_up": (DEPTH, 2, 96, 1536),
    "wkv_g_up": (DEPTH, 256, 1536), "wkv_k_k": (DEPTH, 1536), "wkv_k_a": (DEPTH, 1536), "wkv_r_k": (DEPTH, 1536),
    "wkv_ln_w": (DEPTH, 1536), "wkv_ln_b": (DEPTH, 1536), "w_wkv_out": (DEPTH, 1536, D),
    "s5_lam_re": (DEPTH, 2, 64, 64), "s5_lam_im": (DEPTH, 2, 64, 64), "s5_log_dt": (DEPTH, 2, 64),
    "s5_b_re": (DEPTH, 2, 64, 64, 16), "s5_b_im": (DEPTH, 2, 64, 64, 16),
    "s5_c_re": (DEPTH, 2, 64, 16, 64), "s5_c_im": (DEPTH, 2, 64, 16, 64), "s5_d": (DEPTH, 1024),
    "w_s5_glu": (DEPTH, 1024, 2 * D), "w_out": (DEPTH, D, D), "w_ffn_in": (DEPTH, D, 2 * D_FF),
    "ffn_conv_w": (DEPTH, 3, D_FF), "ffn_conv_b": (DEPTH, D_FF), "w_ffn_out": (DEPTH, D_FF, D),
}


def build(debug=None):
    debug = debug or {}
    kb = KB(debug)
    nc = kb.nc
    I = kb.I
    O = kb.O
    es = contextlib.ExitStack()

    kb.din("xs", [TS, D])
    kb.din("xp", [TP, D])
    kb.din("cond2", [2, D])
    kb.din("st_ssm", [DEPTH, 2, 24, 64, 128])
    kb.din("st_wkv", [DEPTH, 2, 24, 64, 64])
    kb.din("st_s5re", [DEPTH, 2, 64, 64])
    kb.din("st_s5im", [DEPTH, 2, 64, 64])
    for nm, shp in PARAM_SHAPES.items():
        kb.din(nm, shp)
    kb.din("ident", [128, 128])
    kb.din("tri", [128, 128])
    kb.din("trit", [128, 128])
    kb.din("jrev", [128, 128])
    kb.din("mask48", [48, 768])
    kb.din("jj", [128, 512])
    kb.din("zrow", [1, 512])

    kb.dout("ys", [TS, D])
    kb.dout("yp", [TP, D])
    kb.dout("ns_ssm", [4, DEPTH, 2, 24, 64, 128])
    kb.dout("ns_wkv", [4, DEPTH, 2, 24, 64, 64])
    kb.dout("ns_s5re", [4, DEPTH, 2, 64, 64])
    kb.dout("ns_s5im", [4, DEPTH, 2, 64, 64])

    MOD = kb.dscr("MOD", [2, 6 * D])
    PA = kb.dscr("PA", [TS, C_U])
    PB = kb.dscr("PB", [TS, N_IN - C_U])

    class _P:
        def __getitem__(self, key):
            rs, cs = key
            c0, c1 = cs.start, cs.stop
            if c1 <= C_U:
                return PA[rs, c0:c1]
            assert c0 >= C_U, (c0, c1)
            return PB[rs, c0 - C_U:c1 - C_U]
    P = _P()
    GU = kb.dscr("GU", [TS, 2 * D_FF])
    XA = kb.dscr("XA", [TS, D])
    XB = kb.dscr("XB", [TS, D])
    XPA = kb.dscr("XPA", [TP, D])
    XPB = kb.dscr("XPB", [TP, D])
    MG = kb.dscr("MG", [TS, D])
    XC = kb.dscr("XC", [TS, 2560])
    SH = kb.dscr("SH", [TS, NWKV])
    BCT = kb.dscr("BCT", [TS // 128, 128, 8, 128], BF16)
    DT = kb.dscr("DT", [TS, 48])
    YS = kb.dscr("YS", [TS, 1536])
    ZW = kb.dscr("ZW", [TS, 1536])
    BD = [kb.dscr("BD%d" % d, [TS, 1536], BF16) for d in range(2)]
    KD = [kb.dscr("KD%d" % d, [TS, 1536], BF16) for d in range(2)]
    BRKR = [kb.dscr("BRKR%d" % d, [TS, 48]) for d in range(2)]
    WT = [kb.dscr("WT%d" % d, [128, 12, TS]) for d in range(2)]
    WRT = [kb.dscr("WRT%d" % d, [128, 12, TS]) for d in range(2)]
    NKKT = kb.dscr("NKKT", [128, 12, TS])
    GG = kb.dscr("GG", [TS, 1536])
    VBD = [kb.dscr("VBD%d" % d, [TS, 12, 768], BF16) for d in range(2)]
    RK = kb.dscr("RK", [TS, 24])
    SAY = [kb.dscr("SAY%d" % d, [48, TS, 64]) for d in range(2)]
    UT2 = [kb.dscr("UT2_%d" % d, [32, 32, TS]) for d in range(2)]
    YS5 = [kb.dscr("YS5_%d" % d, [TS, 1024]) for d in range(2)]
    ACTS = kb.dscr("ACTS", [TS, D_FF])

    with es:
        S = Sched(nc, es)
        kb.S = S
        gsb = lambda name, shape, dt=F32: es.enter_context(nc.sbuf_tensor(name, list(shape), dt))
        psb = [es.enter_context(nc.psum_tensor("psb%d" % i, [128, 512], F32)) for i in range(8)]
        PK = ["psb%d" % i for i in range(8)]
        ident = gsb("ident_sb", [128, 128])
        tri = gsb("tri_sb", [128, 128])
        trit = gsb("trit_sb", [128, 128])
        jrev = gsb("jrev_sb", [128, 128])
        ones = gsb("ones_sb", [128, 128])
        S.dma("sp", ident[:], I["ident"][:, :], w=["ident"])
        S.dma("sp", tri[:], I["tri"][:, :], w=["tri"])
        S.dma("sp", trit[:], I["trit"][:, :], w=["trit"])
        S.dma("sp", jrev[:], I["jrev"][:, :], w=["jrev"])
        S.op("dve", lambda e: e.memset(ones[:], 1.0), w=["ones"])

        _uid = [0]

        def uname(n):
            _uid[0] += 1
            return "%s_%d" % (n, _uid[0])

        def bcload(dst, key, src1d, q="sp"):
            S.dma(q, dst, src1d.partition_broadcast(dst.shape[0]), r=["MOD"], w=[key])

        def transposes_to(src, skey, nblk, dst_fn, dkey, bw=128, pbanks=(6, 7), eng="act", scale=None, rhs=None,
                          rkey=None, inw=128):
            per = 512 // inw
            for q in range(0, nblk, per):
                nj = min(per, nblk - q)
                bi = pbanks[(q // per) % len(pbanks)]
                pt = psb[bi]
                for j in range(nj):
                    blk = src[:, (q + j) * bw:(q + j + 1) * bw]
                    if rhs is None:
                        S.op("pe", lambda e, j=j, blk=blk: e.transpose(pt[0:bw, j * inw:(j + 1) * inw], blk,
                                                                        ident[0:inw, 0:inw]),
                             r=[skey, "ident"], w=[PK[bi]])
                    else:
                        S.op("pe", lambda e, j=j, blk=blk: e.matmul(pt[0:bw, j * inw:(j + 1) * inw], lhsT=blk,
                                                                     rhs=rhs, start=True, stop=True),
                             r=[skey, rkey], w=[PK[bi]])
                src_ps = pt[0:bw, 0:nj * inw].rearrange("p (j t) -> p j t", j=nj)
                dst = dst_fn(q, nj)
                if eng == "act":
                    if scale is None:
                        S.op("act", lambda e: e.activation(out=dst, in_=src_ps, func=AF.Copy), r=[PK[bi]], w=[dkey])
                    else:
                        S.op("act", lambda e: e.activation(out=dst, in_=src_ps, func=AF.Copy, scale=scale),
                             r=[PK[bi]], w=[dkey])
                else:
                    S.op("dve", lambda e: e.tensor_copy(out=dst, in_=src_ps), r=[PK[bi]], w=[dkey])

        def phase_mod(l):
            with contextlib.ExitStack() as ps:
                lsb = lambda name, shape, dt=F32: ps.enter_context(nc.sbuf_tensor(uname(name), list(shape), dt))
                cT = lsb("cT", [128, 2, KC])
                sg = lsb("sg", [128, 2, KC])
                wm = [lsb("wm%d" % i, [128, KC, 512]) for i in range(2)]
                bm = lsb("bm", [1, 6 * D])
                mo = lsb("mo", [2, 6 * D])
                with nc.allow_non_contiguous_dma(reason="tiny cond transpose"):
                    S.dma("sp", cT[:], I["cond2"].rearrange("j (k p) -> p j k", p=128), w=["cT"])
                S.dma("sp", bm[:], I["b_mod"][l:l + 1, :], w=["bm"])
                S.op("act", lambda e: e.activation(out=sg[:], in_=cT[:], func=AF.Sigmoid), r=["cT"], w=["sg"])
                S.op("dve", lambda e: e.tensor_mul(out=sg[:], in0=sg[:], in1=cT[:]), r=["cT", "sg"], w=["sg"])
                for nb in range(24):
                    wb = wm[nb % 2]
                    wk = "wm%d" % (nb % 2)
                    S.dma("sp", wb[:], I["w_mod"][l, :, nb * 512:(nb + 1) * 512].rearrange("(k p) n -> p k n", p=128),
                          w=[wk])
                    pt = psb[nb % 2]
                    pk = PK[nb % 2]
                    for k in range(KC):
                        S.op("pe", lambda e, k=k: e.matmul(pt[0:2, :], lhsT=sg[:, :, k], rhs=wb[:, k, :],
                                                           start=(k == 0), stop=False), r=["sg", wk], w=[pk])
                    S.op("pe", lambda e: e.matmul(pt[0:2, :], lhsT=ones[0:1, 0:2], rhs=bm[:, nb * 512:(nb + 1) * 512],
                                                  start=False, stop=True), r=["ones", "bm"], w=[pk])
                    S.op("act", lambda e: e.activation(out=mo[:, nb * 512:(nb + 1) * 512], in_=pt[0:2, :],
                                                       func=AF.Copy), r=[pk], w=["mo"])
                S.dma("sp", MOD[:, :], mo[:], r=["mo"], w=["MOD"])
                S.barrier()

        def rms_rstd(ss, key, n, eps):
            S.op("dve", lambda e: e.tensor_scalar(out=ss, in0=ss, scalar1=1.0 / n, scalar2=eps, op0=ALU.mult,
                                                  op1=ALU.add), r=[key], w=[key])
            S.op("act", lambda e: e.activation(out=ss, in_=ss, func=AF.Sqrt), r=[key], w=[key])
            S.op("dve", lambda e: e.reciprocal(out=ss, in_=ss), r=[key], w=[key])

        def norm_to_fm(ls, xsrc, t0, ntile, hT, Gt, SHt):
            for i in range(ntile):
                xt = ls["xt"][i % 2]
                xk = "xt%d" % (i % 2)
                S.dma("sp", xt[:], xsrc[t0 + i * 128:t0 + (i + 1) * 128, :], r=["XSRC"], w=[xk])
                S.op("act", lambda e: e.activation(out=ls["junk"][:], in_=xt[:], func=AF.Square,
                                                   accum_out=ls["ss"][:]), r=[xk], w=["junk", "ss"])
                rms_rstd(ls["ss"][:], "ss", D, EPS)
                S.op("dve", lambda e: e.scalar_tensor_tensor(out=ls["junk"][:], in0=xt[:], scalar=ls["ss"][:, 0:1],
                                                             in1=Gt[:], op0=ALU.mult, op1=ALU.mult),
                     r=[xk, "ss", "Gt"], w=["junk"])
                S.op("dve", lambda e: e.tensor_add(out=ls["junk"][:], in0=ls["junk"][:], in1=SHt[:]),
                     r=["junk", "SHt"], w=["junk"])
                transposes_to(ls["junk"][:], "junk", KC,
                              lambda q, nj, i=i: hT[:, q:q + nj, i * 128:(i + 1) * 128], "hT")

        def proj(wbs, hT, hkey, kchunks, ncols, ntile, epilogue, wload, cwmax=512, wmul=1):
            nb = (ncols + cwmax - 1) // cwmax

            def blk(b):
                c0 = b * cwmax
                cw = min(cwmax, ncols - c0)
                nw = cw * wmul
                wb = wbs[b % 2][:, 0:kchunks * nw].rearrange("p (k n) -> p k n", k=kchunks)
                return c0, cw, nw, wb, "wb%d" % (b % 2)

            c0, cw, nw, wb, wk = blk(0)
            wload(wb, wk, c0, cw)
            for b in range(nb):
                c0, cw, nw, wb, wk = blk(b)
                if b + 1 < nb:
                    c0n, cwn, nwn, wbn, wkn = blk(b + 1)
                    wload(wbn, wkn, c0n, cwn)
                for i in range(ntile):
                    pt = psb[i % 4]
                    pk = PK[i % 4]
                    for k in range(kchunks):
                        S.op("pe", lambda e, k=k: e.matmul(pt[:, 0:nw], lhsT=hT[:, k, i * 128:(i + 1) * 128],
                                                           rhs=wb[:, k, :], start=(k == 0),
                                                           stop=(k == kchunks - 1)), r=[hkey, wk], w=[pk])
                    epilogue(i, c0, cw, pt, pk)

        WSTG = {}

        def wload_plain(Wsrc):
            def f(wb, wk, c0, cw):
                wst = WSTG["t"]
                kch = wb.shape[1]
                stv = wst[:, 0:kch * cw].rearrange("p (k n) -> p k n", k=kch)
                S.dma("sp", stv, Wsrc[:, c0:c0 + cw].rearrange("(k p) n -> p k n", p=128), w=["wst"])
                S.op("dve", lambda e: e.tensor_copy(out=wb, in_=stv), r=["wst"], w=[wk])
            return f

        def load3(src, sc, cw, t0, GW, bufs, keys):
            prv, cur, nxt = bufs
            kp, kc_, kn = keys
            S.dma("sp", cur[:, 0:cw], src[t0:t0 + 128, sc:sc + cw], r=["CSRC"], w=[kc_])
            a = 0
            while a < 128:
                g0 = t0 + a
                b = min(128, a + GW - (g0 % GW))
                sb_ = (g0 % GW) == 0
                eb_ = ((t0 + b) % GW) == 0
                if sb_:
                    S.dma("sp", prv[a:a + 1, 0:cw], I["zrow"][0:1, 0:cw], w=[kp])
                    if b - a > 1:
                        S.dma("sp", prv[a + 1:b, 0:cw], src[t0 + a:t0 + b - 1, sc:sc + cw], r=["CSRC"], w=[kp])
                else:
                    S.dma("sp", prv[a:b, 0:cw], src[t0 + a - 1:t0 + b - 1, sc:sc + cw], r=["CSRC"], w=[kp])
                if eb_:
                    S.dma("sp", nxt[b - 1:b, 0:cw], I["zrow"][0:1, 0:cw], w=[kn])
                    if b - a > 1:
                        S.dma("sp", nxt[a:b - 1, 0:cw], src[t0 + a + 1:t0 + b, sc:sc + cw], r=["CSRC"], w=[kn])
                else:
                    S.dma("sp", nxt[a:b, 0:cw], src[t0 + a + 1:t0 + b + 1, sc:sc + cw], r=["CSRC"], w=[kn])
                a = b

        def conv_pass(ps, T, GW, src, sc0, ncols, wrows, post, prep=None):
            lsb = lambda name, shape, dt=F32: ps.enter_context(nc.sbuf_tensor(uname(name), list(shape), dt))
            wt = [lsb("cvw%d" % j, [128, 512]) for j in range(4)]
            bufs = [[lsb("cv%s%d" % (n, j), [128, 512]) for n in ("p", "c", "n")] for j in range(2)]
            yb = [lsb("cvy%d" % j, [128, 512]) for j in range(2)]
            tb = lsb("cvt", [128, 512])
            for c0 in range(0, ncols, 512):
                cw = min(512, ncols - c0)
                rows = wrows(c0, cw)
                for j in range(4):
                    if rows[j] is not None:
                        bcload(wt[j][:, 0:cw], "cvw%d" % j, rows[j])
                if prep is not None:
                    prep(wt, cw)
                for i in range(T // 128):
                    bb = bufs[i % 2]
                    keys = ["cv%s%d" % (n, i % 2) for n in ("p", "c", "n")]
                    load3(src, sc0 + c0, cw, i * 128, GW, bb, keys)
                    y = yb[i % 2]
                    yk = "cvy%d" % (i % 2)
                    S.op("dve", lambda e: e.tensor_mul(out=y[:, 0:cw], in0=bb[1][:, 0:cw], in1=wt[1][:, 0:cw]),
                         r=[keys[1], "cvw1"], w=[yk])
                    S.op("dve", lambda e: e.tensor_mul(out=tb[:, 0:cw], in0=bb[0][:, 0:cw], in1=wt[0][:, 0:cw]),
                         r=[keys[0], "cvw0"], w=["cvt"])
                    S.op("dve", lambda e: e.tensor_add(out=y[:, 0:cw], in0=y[:, 0:cw], in1=tb[:, 0:cw]),
                         r=[yk, "cvt"], w=[yk])
                    S.op("dve", lambda e: e.tensor_mul(out=tb[:, 0:cw], in0=bb[2][:, 0:cw], in1=wt[2][:, 0:cw]),
                         r=[keys[2], "cvw2"], w=["cvt"])
                    S.op("dve", lambda e: e.tensor_add(out=y[:, 0:cw], in0=y[:, 0:cw], in1=tb[:, 0:cw]),
                         r=[yk, "cvt"], w=[yk])
                    if rows[3] is not None:
                        S.op("dve", lambda e: e.tensor_add(out=y[:, 0:cw], in0=y[:, 0:cw], in1=wt[3][:, 0:cw]),
                             r=[yk, "cvw3"], w=[yk])
                    post(i, c0, cw, y, yk)

        def postnorm_residual(ls, Ybuf, ntile, t0, xsrc, xdst, GPt):
            for i in range(ntile):
                y = Ybuf[:, i, :]
                S.op("act", lambda e: e.activation(out=ls["junk"][:], in_=y, func=AF.Square, accum_out=ls["ss"][:]),
                     r=["Ybuf"], w=["junk", "ss"])
                rms_rstd(ls["ss"][:], "ss", D, EPS)
                xt = ls["xt"][i % 2]
                xk = "xt%d" % (i % 2)
                S.dma("sp", xt[:], xsrc[t0 + i * 128:t0 + (i + 1) * 128, :], r=["XSRC"], w=[xk])
                S.op("dve", lambda e: e.scalar_tensor_tensor(out=ls["junk"][:], in0=y, scalar=ls["ss"][:, 0:1],
                                                             in1=GPt[:], op0=ALU.mult, op1=ALU.mult),
                     r=["Ybuf", "ss", "GPt"], w=["junk"])
                S.op("dve", lambda e: e.tensor_add(out=xt[:], in0=xt[:], in1=ls["junk"][:]), r=["junk", xk], w=[xk])
                S.dma("sp", xdst[t0 + i * 128:t0 + (i + 1) * 128, :], xt[:], r=[xk], w=["XDST"])

        def mod_tiles(lsb, ls, l, jrow, i_sc, i_sh, i_g, pre, post):
            Gt = lsb("Gt", [128, D])
            SHt = lsb("SHt", [128, D])
            bcload(Gt[:], "Gt", MOD[jrow, i_sc * D:(i_sc + 1) * D])
            bcload(ls["junk"][:], "junk", I[pre][l, :])
            S.op("dve", lambda e: e.scalar_tensor_tensor(out=Gt[:], in0=Gt[:], scalar=1.0, in1=ls["junk"][:],
                                                         op0=ALU.add, op1=ALU.mult), r=["Gt", "junk"], w=["Gt"])
            bcload(SHt[:], "SHt", MOD[jrow, i_sh * D:(i_sh + 1) * D])
            return Gt, SHt, None

        def phase_inproj(l, xsrc, T, jrow):
            with contextlib.ExitStack() as ps:
                lsb = lambda name, shape, dt=F32: ps.enter_context(nc.sbuf_tensor(uname(name), list(shape), dt))
                ls = {"xt": [lsb("xt0", [128, D]), lsb("xt1", [128, D])], "junk": lsb("junk", [128, D]),
                      "ss": lsb("ss", [128, 1])}
                wbs = [lsb("wb0", [128, 8192], BF16), lsb("wb1", [128, 8192], BF16)]
                WSTG["t"] = lsb("wst", [128, 8192])
                BI = min(BLK_IN, T)
                st = [lsb("st0", [128, 512]), lsb("st1", [128, 512])]
                hT = lsb("hT", [128, KC, BI], BF16)
                Gt, SHt, GPt = mod_tiles(lsb, ls, l, jrow, 1, 0, 2, "norm_mix_pre", "norm_mix_post")
                cnt = [0]
                for bi in range(T // BI):
                    t0 = bi * BI
                    norm_to_fm(ls, xsrc, t0, BI // 128, hT, Gt, SHt)

                    def epi(i, c0, cw, pt, pk, t0=t0):
                        cnt[0] += 1
                        s_ = st[cnt[0] % 2]
                        sk = "st%d" % (cnt[0] % 2)
                        S.op("act", lambda e: e.activation(out=s_[:, 0:cw], in_=pt[:, 0:cw], func=AF.Copy),
                             r=[pk], w=[sk])
                        rs_ = slice(t0 + i * 128, t0 + (i + 1) * 128)
                        if c0 < C_U < c0 + cw:
                            m_ = C_U - c0
                            S.dma("sp", P[rs_, c0:C_U], s_[:, 0:m_], r=[sk], w=["P"])
                            S.dma("sp", P[rs_, C_U:c0 + cw], s_[:, m_:cw], r=[sk], w=["P"])
                        else:
                            S.dma("sp", P[rs_, c0:c0 + cw], s_[:, 0:cw], r=[sk], w=["P"])

                    proj(wbs, hT, "hT", KC, debug.get("ncols", N_IN), BI // 128, epi, wload_plain(I["w_in"][l]))
                S.barrier()

        def phase_convs(l, T, GW):
            with contextlib.ExitStack() as ps:
                def post_ssm(i, c0, cw, y, yk):
                    lsg = post_ssm.sg
                    S.op("act", lambda e: e.activation(out=lsg[:, 0:cw], in_=y[:, 0:cw], func=AF.Sigmoid),
                         r=[yk], w=["cvsg"])
                    S.op("dve", lambda e: e.tensor_mul(out=y[:, 0:cw], in0=y[:, 0:cw], in1=lsg[:, 0:cw]),
                         r=[yk, "cvsg"], w=[yk])
                    S.dma("sp", XC[i * 128:(i + 1) * 128, c0:c0 + cw], y[:, 0:cw], r=[yk], w=["XC"])
                post_ssm.sg = ps.enter_context(nc.sbuf_tensor(uname("cvsg"), [128, 512], F32))
                conv_pass(ps, T, GW, P, C_XBC, 2560,
                          lambda c0, cw: (I["ssm_conv_w"][l, 0, c0:c0 + cw], I["ssm_conv_w"][l, 1, c0:c0 + cw],
                                          I["ssm_conv_w"][l, 2, c0:c0 + cw], I["ssm_conv_b"][l, c0:c0 + cw]),
                          post_ssm)
                S.barrier()
            with contextlib.ExitStack() as ps:
                def post_wkv(i, c0, cw, y, yk):
                    S.dma("sp", SH[i * 128:(i + 1) * 128, c0:c0 + cw], y[:, 0:cw], r=[yk], w=["SH"])

                def prep(wt, cw):
                    S.op("dve", lambda e: e.tensor_add(out=wt[1][:, 0:cw], in0=wt[0][:, 0:cw], in1=wt[2][:, 0:cw]),
                         r=["cvw0", "cvw2"], w=["cvw1"])
                    S.op("dve", lambda e: e.tensor_scalar(out=wt[1][:, 0:cw], in0=wt[1][:, 0:cw], scalar1=-1.0,
                                                          scalar2=1.0, op0=ALU.mult, op1=ALU.add),
                         r=["cvw1"], w=["cvw1"])
                conv_pass(ps, T, GW, P, C_WKV, NWKV,
                          lambda c0, cw: (I["wkv_mu_prev"][l, c0:c0 + cw], None, I["wkv_mu_next"][l, c0:c0 + cw],
                                          None), post_wkv, prep=prep)
                S.barrier()

        def mm_cols(ps3, c0, c1, lhsT, rhs_fn, rkeys):
            c = c0
            while c < c1:
                bnk = c // 512
                ce = min(c1, (bnk + 1) * 512)
                S.op("pe", lambda e, c=c, ce=ce, bnk=bnk: e.matmul(psb[ps3[bnk]][:, c - bnk * 512:ce - bnk * 512],
                                                                   lhsT=lhsT, rhs=rhs_fn(c, ce), start=True,
                                                                   stop=True), r=rkeys, w=[PK[ps3[bnk]]])
                c = ce

        def phase_ssd(l, T, L, sample):
            NT = T // 128
            with contextlib.ExitStack() as ps:
                lsb = lambda name, shape, dt=F32: ps.enter_context(nc.sbuf_tensor(uname(name), list(shape), dt))
                bcv = [lsb("sbc%d" % j, [128, 1024]) for j in range(2)]
                bct = [lsb("sbct%d" % j, [128, 8, 128], BF16) for j in range(2)]
                dtt = [lsb("sdt%d" % j, [128, 48]) for j in range(2)]
                dbias = lsb("dbias", [128, 48])
                bcload(dbias[:], "dbias", I["ssm_dt_bias"][l, :])
                for i in range(NT):
                    b_ = bcv[i % 2]
                    bk = "sbc%d" % (i % 2)
                    S.dma("sp", b_[:], XC[i * 128:(i + 1) * 128, 1536:2560], r=["XC"], w=[bk])
                    o_ = bct[i % 2]
                    ok = "sbct%d" % (i % 2)
                    transposes_to(b_[:], bk, 8, lambda q, nj: o_[:, q:q + nj, :], ok)
                    S.dma("sp", BCT[i], o_[:], r=[ok], w=["BCT"])
                    d_ = dtt[i % 2]
                    dk = "sdt%d" % (i % 2)
                    S.dma("sp", d_[:], P[i * 128:(i + 1) * 128, C_DTF:C_DTF + 48], r=["P"], w=[dk])
                    S.op("dve", lambda e: e.tensor_add(out=d_[:], in0=d_[:], in1=dbias[:]), r=[dk, "dbias"], w=[dk])
                    S.op("act", lambda e: e.activation(out=d_[:], in_=d_[:], func=AF.Exp), r=[dk], w=[dk])
                    S.op("act", lambda e: e.activation(out=d_[:], in_=d_[:], func=AF.Ln, bias=1.0), r=[dk], w=[dk])
                    S.dma("sp", DT[i * 128:(i + 1) * 128, :], d_[:], r=[dk], w=["DT"])
                S.barrier()
            with contextlib.ExitStack() as ps:
                lsb = lambda name, shape, dt=F32: ps.enter_context(nc.sbuf_tensor(uname(name), list(shape), dt))
                xc = [lsb("xc%d" % j, [128, 2560]) for j in range(2)]
                bct = [lsb("bct%d" % j, [128, 8, 128], BF16) for j in range(2)]
                dtt = [lsb("dt%d" % j, [128, 24]) for j in range(2)]
                Abc = lsb("Abc", [128, 24])
                Dbc = lsb("Dbc", [128, 24])
                nrm = lsb("nrm", [128, 1536])
                dtA = lsb("dtA", [128, 24])
                acol = lsb("acol", [128, 24])
                tot = lsb("tot", [128, 24])
                cd = lsb("cd", [128, 24])
                ea = lsb("ea", [128, 24])
                dte = lsb("dte", [128, 24])
                xdt = lsb("xdt", [128, 1536], BF16)
                xw = lsb("xw", [128, 1536], BF16)
                bB = lsb("bB", [128, 512], BF16)
                Gm = lsb("Gm", [128, 512])
                dd = [lsb("dd%d" % j, [128, 512]) for j in range(2)]
                wt = [lsb("wt%d" % j, [128, 4, 128], BF16) for j in range(2)]
                HT = lsb("HT", [128, 1536])
                HTb = lsb("HTb", [128, 1536], BF16)
                yo = lsb("yo", [128, 1536])
                yy = lsb("yy", [128, 1536])
                zz = lsb("zz", [128, 1536])
                zs = lsb("zs", [128, 1536])
                ss4 = lsb("ss4", [128, 4])
                hio = lsb("hio", [128, 12, 128])
                bcload(Dbc[:], "Dbc", I["ssm_d"][l, :])
                bcload(nrm[:], "nrm", I["ssm_norm"][l, :])
                PY = (4, 5, 6)
                v3 = lambda t: t[:].rearrange("p (h q) -> p h q", h=24)
                for d in range(2):
                    bcload(Abc[:], "Abc", I["ssm_a_log"][l, d * 24:(d + 1) * 24])
                    S.op("act", lambda e: e.activation(out=Abc[:], in_=Abc[:], func=AF.Exp), r=["Abc"], w=["Abc"])
                    S.op("dve", lambda e: e.tensor_scalar(out=Abc[:], in0=Abc[:], scalar1=-1.0, scalar2=None,
                                                          op0=ALU.mult), r=["Abc"], w=["Abc"])
                    TR = tri if d == 0 else trit
                    TRk = "tri" if d == 0 else "trit"
                    for s_ in range(T // L):
                        if sample:
                            S.dma("sp", hio[:], I["st_ssm"][l, d].rearrange("(g h2) p n -> (h2 p) g n", h2=2),
                                  w=["hio"])
                            for g in range(12):
                                bnk = PY[(g * 128) // 512]
                                S.op("pe", lambda e, g=g, bnk=bnk: e.transpose(
                                    psb[bnk][:, (g * 128) % 512:(g * 128) % 512 + 128], hio[:, g, :], ident[:]),
                                    r=["hio", "ident"], w=[PK[bnk]])
                            for j in range(3):
                                S.op("act", lambda e, j=j: e.activation(out=HT[:, j * 512:(j + 1) * 512],
                                                                        in_=psb[PY[j]][:, :], func=AF.Copy),
                                     r=[PK[PY[j]]], w=["HT"])
                        else:
                            S.op("dve", lambda e: e.memset(HT[:], 0.0), w=["HT"])
                        S.op("act", lambda e: e.activation(out=HTb[:], in_=HT[:], func=AF.Copy), r=["HT"], w=["HTb"])
                        tiles = list(range(L // 128))
                        if d == 1:
                            tiles = tiles[::-1]
                        for ti in tiles:
                            i = s_ * (L // 128) + ti
                            t0 = i * 128
                            x_ = xc[i % 2]
                            xk = "xc%d" % (i % 2)
                            b_ = bct[i % 2]
                            bk = "bct%d" % (i % 2)
                            d_ = dtt[i % 2]
                            dk = "dt%d" % (i % 2)
                            S.dma("sp", x_[:], XC[t0:t0 + 128, :], r=["XC"], w=[xk])
                            S.dma("sp", b_[:], BCT[i], r=["BCT"], w=[bk])
                            S.dma("sp", d_[:], DT[t0:t0 + 128, d * 24:(d + 1) * 24], r=["DT"], w=[dk])
                            S.op("dve", lambda e: e.tensor_mul(out=dtA[:], in0=d_[:], in1=Abc[:]),
                                 r=[dk, "Abc"], w=["dtA"])
                            S.op("pe", lambda e: e.matmul(psb[0][:, 0:24], lhsT=TR[:], rhs=dtA[:], start=True,
                                                          stop=True), r=[TRk, "dtA"], w=[PK[0]])
                            S.op("pe", lambda e: e.matmul(psb[0][:, 32:56], lhsT=ones[:], rhs=dtA[:], start=True,
                                                          stop=True), r=["ones", "dtA"], w=[PK[0]])
                            S.op("act", lambda e: e.activation(out=acol[:], in_=psb[0][:, 0:24], func=AF.Copy),
                                 r=[PK[0]], w=["acol"])
                            S.op("act", lambda e: e.activation(out=ea[:], in_=psb[0][:, 0:24], func=AF.Exp),
                                 r=[PK[0]], w=["ea"])
                            S.op("act", lambda e: e.activation(out=cd[:], in_=psb[0][:, 32:56], func=AF.Exp),
                                 r=[PK[0]], w=["cd"])
                            S.op("dve", lambda e: e.tensor_sub(out=dte[:], in0=psb[0][:, 32:56], in1=acol[:]),
                                 r=[PK[0], "acol"], w=["dte"])
                            S.op("act", lambda e: e.activation(out=dte[:], in_=dte[:], func=AF.Exp), r=["dte"],
                                 w=["dte"])
                            S.op("dve", lambda e: e.tensor_mul(out=dte[:], in0=dte[:], in1=d_[:]), r=["dte", dk],
                                 w=["dte"])
                            S.op("dve", lambda e: e.tensor_tensor(
                                out=v3(xdt), in0=x_[:, 0:1536].rearrange("p (h q) -> p h q", h=24),
                                in1=d_[:, :].unsqueeze(2).to_broadcast([128, 24, 64]), op=ALU.mult),
                                r=[xk, dk], w=["xdt"])
                            S.op("dve", lambda e: e.tensor_tensor(
                                out=v3(xw), in0=x_[:, 0:1536].rearrange("p (h q) -> p h q", h=24),
                                in1=dte[:, :].unsqueeze(2).to_broadcast([128, 24, 64]), op=ALU.mult),
                                r=[xk, "dte"], w=["xw"])
                            S.op("act", lambda e: e.activation(out=bB[:], in_=x_[:, 1536:2048], func=AF.Copy),
                                 r=[xk], w=["bB"])
                            for g in range(4):
                                S.op("pe", lambda e, g=g: e.matmul(psb[1][:, g * 128:(g + 1) * 128], lhsT=b_[:, g, :],
                                                                   rhs=b_[:, 4 + g, :], start=True, stop=True),
                                     r=[bk], w=[PK[1]])
                            S.op("dve", lambda e: e.tensor_tensor(
                                out=Gm[:].rearrange("p (g t) -> p g t", g=4),
                                in0=psb[1][:, :].rearrange("p (g t) -> p g t", g=4),
                                in1=TR[:, :].unsqueeze(1).to_broadcast([128, 4, 128]), op=ALU.mult),
                                r=[PK[1], TRk], w=["Gm"])
                            for g in range(4):
                                mm_cols(PY, g * 384, (g + 1) * 384, b_[:, 4 + g, :], lambda c, ce: HTb[:, c:ce],
                                        [bk, "HTb"])
                            for j in range(3):
                                S.op("dve", lambda e, j=j: e.tensor_tensor(
                                    out=yo[:, j * 512:(j + 1) * 512].rearrange("p (h q) -> p h q", h=8),
                                    in0=psb[PY[j]][:, :].rearrange("p (h q) -> p h q", h=8),
                                    in1=ea[:, j * 8:(j + 1) * 8].unsqueeze(2).to_broadcast([128, 8, 64]),
                                    op=ALU.mult), r=[PK[PY[j]], "ea"], w=["yo"])
                            for hq in range(6):
                                pb = 2 + hq % 2
                                ddq = dd[hq % 2]
                                dkq = "dd%d" % (hq % 2)
                                wq = wt[hq % 2]
                                wkq = "wt%d" % (hq % 2)
                                for j in range(4):
                                    h = hq * 4 + j
                                    S.op("pe", lambda e, j=j, h=h: e.matmul(
                                        psb[pb][:, j * 128:(j + 1) * 128],
                                        lhsT=dtA[:, h:h + 1].to_broadcast([128, 128]), rhs=TR[:], start=True,
                                        stop=True), r=["dtA", TRk], w=[PK[pb]])
                                for j in range(4):
                                    h = hq * 4 + j
                                    S.op("dve", lambda e, j=j, h=h: e.tensor_scalar(
                                        out=ddq[:, j * 128:(j + 1) * 128], in0=psb[pb][:, j * 128:(j + 1) * 128],
                                        scalar1=acol[:, h:h + 1], scalar2=0.0, op0=ALU.subtract, op1=ALU.min),
                                        r=[PK[pb], "acol"], w=[dkq])
                                S.op("act", lambda e: e.activation(out=ddq[:], in_=ddq[:], func=AF.Exp), r=[dkq],
                                     w=[dkq])
                                for j in range(4):
                                    h = hq * 4 + j
                                    g = h // 6
                                    S.op("dve", lambda e, j=j, g=g: e.tensor_mul(
                                        out=wq[:, j, :], in0=Gm[:, g * 128:(g + 1) * 128],
                                        in1=ddq[:, j * 128:(j + 1) * 128]), r=["Gm", dkq], w=[wkq])
                                for j in range(4):
                                    h = hq * 4 + j
                                    bnk = PY[(h * 64) // 512]
                                    S.op("pe", lambda e, j=j, h=h, bnk=bnk: e.matmul(
                                        psb[bnk][:, (h * 64) % 512:(h * 64) % 512 + 64], lhsT=wq[:, j, :],
                                        rhs=xdt[:, h * 64:(h + 1) * 64], start=True, stop=True),
                                        r=[wkq, "xdt"], w=[PK[bnk]])
                            for j in range(3):
                                S.op("dve", lambda e, j=j: e.tensor_add(out=yy[:, j * 512:(j + 1) * 512],
                                                                        in0=yo[:, j * 512:(j + 1) * 512],
                                                                        in1=psb[PY[j]][:, :]),
                                     r=["yo", PK[PY[j]]], w=["yy"])
                            for g in range(4):
                                mm_cols(PY, g * 384, (g + 1) * 384, bB[:, g * 128:(g + 1) * 128],
                                        lambda c, ce: xw[:, c:ce], ["bB", "xw"])
                            S.op("dve", lambda e: e.tensor_tensor(out=v3(HT), in0=v3(HT),
                                                                  in1=cd[:, :].unsqueeze(2).to_broadcast([128, 24, 64]),
                                                                  op=ALU.mult), r=["HT", "cd"], w=["HT"])
                            for j in range(3):
                                S.op("dve", lambda e, j=j: e.tensor_add(out=HT[:, j * 512:(j + 1) * 512],
                                                                        in0=HT[:, j * 512:(j + 1) * 512],
                                                                        in1=psb[PY[j]][:, :]),
                                     r=["HT", PK[PY[j]]], w=["HT"])
                            S.op("act", lambda e: e.activation(out=HTb[:], in_=HT[:], func=AF.Copy), r=["HT"],
                                 w=["HTb"])
                            if d == 0:
                                S.dma("sp", YS[t0:t0 + 128, :], yy[:], r=["yy"], w=["YS"])
                            else:
                                S.dma("sp", yo[:], YS[t0:t0 + 128, :], r=["YS"], w=["yo"])
                                S.op("dve", lambda e: e.tensor_add(out=yy[:], in0=yy[:], in1=yo[:]), r=["yy", "yo"],
                                     w=["yy"])
                                S.op("dve", lambda e: e.tensor_tensor(
                                    out=v3(yo), in0=x_[:, 0:1536].rearrange("p (h q) -> p h q", h=24),
                                    in1=Dbc[:, :].unsqueeze(2).to_broadcast([128, 24, 64]), op=ALU.mult),
                                    r=[xk, "Dbc"], w=["yo"])
                                S.op("dve", lambda e: e.tensor_add(out=yy[:], in0=yy[:], in1=yo[:]), r=["yy", "yo"],
                                     w=["yy"])
                                S.dma("sp", zz[:], P[t0:t0 + 128, C_Z:C_Z + 1536], r=["P"], w=["zz"])
                                S.op("act", lambda e: e.activation(out=zs[:], in_=zz[:], func=AF.Sigmoid), r=["zz"],
                                     w=["zs"])
                                S.op("dve", lambda e: e.tensor_mul(out=zs[:], in0=zs[:], in1=zz[:]), r=["zz", "zs"],
                                     w=["zs"])
                                S.op("dve", lambda e: e.tensor_mul(out=yy[:], in0=yy[:], in1=zs[:]), r=["yy", "zs"],
                                     w=["yy"])
                                S.op("act", lambda e: e.activation(out=zs[:], in_=yy[:], func=AF.Square), r=["yy"],
                                     w=["zs"])
                                S.op("dve", lambda e: e.tensor_reduce(out=ss4[:],
                                                                      in_=zs[:].rearrange("p (g q) -> p g q", g=4),
                                                                      axis=AX.X, op=ALU.add), r=["zs"], w=["ss4"])
                                rms_rstd(ss4[:], "ss4", 384, EPS)
                                S.op("dve", lambda e: e.tensor_tensor(
                                    out=yy[:].rearrange("p (g q) -> p g q", g=4),
                                    in0=yy[:].rearrange("p (g q) -> p g q", g=4),
                                    in1=ss4[:, :].unsqueeze(2).to_broadcast([128, 4, 384]), op=ALU.mult),
                                    r=["yy", "ss4"], w=["yy"])
                                S.op("dve", lambda e: e.tensor_mul(out=yy[:], in0=yy[:], in1=nrm[:]),
                                     r=["yy", "nrm"], w=["yy"])
                                S.dma("sp", YS[t0:t0 + 128, :], yy[:], r=["yy"], w=["YS"])
                        if not sample:
                            for g in range(12):
                                bnk = PY[(g * 128) // 512]
                                S.op("pe", lambda e, g=g, bnk=bnk: e.transpose(
                                    psb[bnk][:, (g * 128) % 512:(g * 128) % 512 + 128],
                                    HT[:, g * 128:(g + 1) * 128], ident[:]), r=["HT", "ident"], w=[PK[bnk]])
                            for j in range(3):
                                S.op("act", lambda e, j=j: e.activation(
                                    out=hio[:, j * 4:(j + 1) * 4, :],
                                    in_=psb[PY[j]][:, :].rearrange("p (g n) -> p g n", g=4), func=AF.Copy),
                                    r=[PK[PY[j]]], w=["hio"])
                            S.dma("sp", O["ns_ssm"][s_, l, d].rearrange("(g h2) p n -> (h2 p) g n", h2=2), hio[:],
                                  r=["hio"], w=["ns_ssm"])
                S.barrier()

        def phase_wkv_prep(l, T):
            NT = T // 128
            with contextlib.ExitStack() as ps:
                lsb = lambda name, shape, dt=F32: ps.enter_context(nc.sbuf_tensor(uname(name), list(shape), dt))
                sh = lsb("sh", [128, NWKV])
                kkbc = lsb("kkbc", [128, 1536])
                kabc = lsb("kabc", [128, 1536])
                omka = lsb("omka", [128, 1536])
                rkbc = lsb("rkbc", [128, 1536])
                w0bc = lsb("w0bc", [128, 1536])
                a0bc = lsb("a0bc", [128, 1536])
                wup = [lsb("wup%d" % d, [96, 1536]) for d in range(2)]
                aup = [lsb("aup%d" % d, [96, 1536]) for d in range(2)]
                gup = lsb("gup", [128, 2, 1536])
                A = lsb("wA", [128, 1536])
                Bt = lsb("wB", [128, 1536])
                Ct = lsb("wC", [128, 1536])
                Dt_ = lsb("wD", [128, 1536])
                E = lsb("wE", [128, 1536])
                nkk = lsb("nkk", [128, 1536])
                vb16 = lsb("vb16", [128, 1536], BF16)
                kb16 = lsb("kb16", [128, 1536], BF16)
                bb16 = lsb("bb16", [128, 1536], BF16)
                Vx = lsb("Vx", [128, 12, 768], BF16)
                S.op("dve", lambda e: e.memset(Vx[:], 0.0), w=["Vx"])
                sm = lsb("wsm", [128, 48])
                rs = lsb("wrs", [128, 24])
                twT = lsb("twT", [96, 2, 128])
                aT = lsb("aT", [96, 2, 128])
                sgT = lsb("sgT", [128, 2, 128])
                fm = [lsb("wfm%d" % j, [128, 12, 128]) for j in range(2)]
                fmc = [0]
                bcload(kkbc[:], "kkbc", I["wkv_k_k"][l, :])
                bcload(kabc[:], "kabc", I["wkv_k_a"][l, :])
                bcload(rkbc[:], "rkbc", I["wkv_r_k"][l, :])
                S.op("dve", lambda e: e.tensor_scalar(out=omka[:], in0=kabc[:], scalar1=-1.0, scalar2=1.0,
                                                      op0=ALU.mult, op1=ALU.add), r=["kabc"], w=["omka"])
                for d in range(2):
                    S.dma("sp", wup[d][:], I["wkv_w_up"][l, d], w=["wup%d" % d])
                    S.dma("sp", aup[d][:], I["wkv_a_up"][l, d], w=["aup%d" % d])
                S.dma("sp", gup[:], I["wkv_g_up"][l].rearrange("(c p) n -> p c n", p=128), w=["gup"])
                PY = (3, 4, 5)
                h3 = lambda ap: ap.rearrange("p (h q) -> p h q", h=24)

                def fm_store(src, skey, dst3):
                    f = fm[fmc[0] % 2]
                    fk = "wfm%d" % (fmc[0] % 2)
                    fmc[0] += 1
                    transposes_to(src, skey, 12, lambda q, nj: f[:, q:q + nj, :], fk, eng="dve")
                    S.dma("sp", dst3, f[:], r=[fk], w=["FMOUT"])

                for i in range(NT):
                    t0 = i * 128
                    S.dma("sp", sh[:], SH[t0:t0 + 128, :], r=["SH"], w=["sh"])
                    r_ = sh[:, 0:1536]
                    k_ = sh[:, 1536:3072]
                    S.op("act", lambda e: e.activation(out=vb16[:], in_=sh[:, 3072:4608], func=AF.Copy), r=["sh"],
                         w=["vb16"])
                    vb4 = vb16[:].rearrange("p (g h v) -> p g h v", g=12, h=2)
                    for hh in range(2):
                        for g in range(12):
                            en = "act" if g % 2 == 0 else "dve"
                            if en == "act":
                                S.op("act", lambda e, g=g: e.activation(out=Vx[:, g, g * 64:(g + 1) * 64],
                                                                        in_=vb4[:, g, hh, :], func=AF.Copy),
                                     r=["vb16"], w=["Vx"])
                            else:
                                S.op("dve", lambda e, g=g: e.tensor_copy(out=Vx[:, g, g * 64:(g + 1) * 64],
                                                                         in_=vb4[:, g, hh, :]), r=["vb16"], w=["Vx"])
                        S.dma("sp", VBD[hh][t0:t0 + 128, :, :], Vx[:], r=["Vx"], w=["VBD"])
                    S.op("dve", lambda e: e.tensor_mul(out=A[:], in0=k_, in1=kkbc[:]), r=["sh", "kkbc"], w=["wA"])
                    S.op("act", lambda e: e.activation(out=Bt[:], in_=A[:], func=AF.Square), r=["wA"], w=["wB"])
                    S.op("dve", lambda e: e.tensor_reduce(out=rs[:], in_=h3(Bt[:]), axis=AX.X, op=ALU.add),
                         r=["wB"], w=["wrs"])
                    S.op("dve", lambda e: e.tensor_scalar(out=rs[:], in0=rs[:], scalar1=1e-24, scalar2=None,
                                                          op0=ALU.max), r=["wrs"], w=["wrs"])
                    S.op("act", lambda e: e.activation(out=rs[:], in_=rs[:], func=AF.Sqrt), r=["wrs"], w=["wrs"])
                    S.op("dve", lambda e: e.reciprocal(out=rs[:], in_=rs[:]), r=["wrs"], w=["wrs"])
                    S.op("dve", lambda e: e.tensor_scalar(out=rs[:], in0=rs[:], scalar1=-1.0, scalar2=None,
                                                          op0=ALU.mult), r=["wrs"], w=["wrs"])
                    S.op("dve", lambda e: e.tensor_tensor(out=h3(nkk[:]), in0=h3(A[:]),
                                                          in1=rs[:, :].unsqueeze(2).to_broadcast([128, 24, 64]),
                                                          op=ALU.mult), r=["wA", "wrs"], w=["nkk"])
                    fm_store(nkk[:], "nkk", NKKT[:, :, t0:t0 + 128])
                    S.op("dve", lambda e: e.tensor_mul(out=Bt[:], in0=r_, in1=k_), r=["sh"], w=["wB"])
                    S.op("dve", lambda e: e.tensor_mul(out=Bt[:], in0=Bt[:], in1=rkbc[:]), r=["wB", "rkbc"], w=["wB"])
                    S.op("dve", lambda e: e.tensor_reduce(out=sm[:, 0:24], in_=h3(Bt[:]), axis=AX.X, op=ALU.add),
                         r=["wB"], w=["wsm"])
                    S.dma("sp", RK[t0:t0 + 128, :], sm[:, 0:24], r=["wsm"], w=["RK"])
                    S.op("act", lambda e: e.activation(out=Ct[:, 0:192], in_=sh[:, 4608:4800], func=AF.Tanh),
                         r=["sh"], w=["wC"])
                    S.op("act", lambda e: e.activation(out=Ct[:, 192:448], in_=sh[:, 4992:5248], func=AF.Sigmoid),
                         r=["sh"], w=["wC"])
                    transposes_to(Ct[:, 0:192], "wC", 2, lambda q, nj: twT[:, q:q + nj, :], "twT", bw=96, eng="dve")
                    transposes_to(sh[:, 4800:4992], "sh", 2, lambda q, nj: aT[:, q:q + nj, :], "aT", bw=96, eng="dve")
                    transposes_to(Ct[:, 192:448], "wC", 2, lambda q, nj: sgT[:, q:q + nj, :], "sgT", eng="dve")
                    for cb in range(3):
                        for c in range(2):
                            S.op("pe", lambda e, cb=cb, c=c: e.matmul(psb[PY[cb]][:, :], lhsT=sgT[:, c, :],
                                                                      rhs=gup[:, c, cb * 512:(cb + 1) * 512],
                                                                      start=(c == 0), stop=(c == 1)),
                                 r=["sgT", "gup"], w=[PK[PY[cb]]])
                        S.op("act", lambda e, cb=cb: e.activation(out=E[:, cb * 512:(cb + 1) * 512],
                                                                  in_=psb[PY[cb]][:, :], func=AF.Copy),
                             r=[PK[PY[cb]]], w=["wE"])
                    S.dma("sp", GG[t0:t0 + 128, :], E[:], r=["wE"], w=["GG"])
                    for d in range(2):
                        bcload(w0bc[:], "w0bc", I["wkv_w0"][l, d, :])
                        bcload(a0bc[:], "a0bc", I["wkv_a0"][l, d, :])
                        for cb in range(3):
                            S.op("pe", lambda e, cb=cb: e.matmul(psb[PY[cb]][:, :], lhsT=twT[:, d, :],
                                                                 rhs=wup[d][:, cb * 512:(cb + 1) * 512], start=True,
                                                                 stop=True), r=["twT", "wup%d" % d], w=[PK[PY[cb]]])
                            S.op("dve", lambda e, cb=cb: e.tensor_add(out=Bt[:, cb * 512:(cb + 1) * 512],
                                                                      in0=psb[PY[cb]][:, :],
                                                                      in1=w0bc[:, cb * 512:(cb + 1) * 512]),
                                 r=[PK[PY[cb]], "w0bc"], w=["wB"])
                        S.op("act", lambda e: e.activation(out=Bt[:], in_=Bt[:], func=AF.Sigmoid), r=["wB"], w=["wB"])
                        S.op("act", lambda e: e.activation(out=Bt[:], in_=Bt[:], func=AF.Exp,
                                                           scale=-math.exp(-0.5)), r=["wB"], w=["wB"])
                        for cb in range(3):
                            S.op("pe", lambda e, cb=cb: e.matmul(psb[PY[cb]][:, :], lhsT=aT[:, d, :],
                                                                 rhs=aup[d][:, cb * 512:(cb + 1) * 512], start=True,
                                                                 stop=True), r=["aT", "aup%d" % d], w=[PK[PY[cb]]])
                            S.op("dve", lambda e, cb=cb: e.tensor_add(out=Dt_[:, cb * 512:(cb + 1) * 512],
                                                                      in0=psb[PY[cb]][:, :],
                                                                      in1=a0bc[:, cb * 512:(cb + 1) * 512]),
                                 r=[PK[PY[cb]], "a0bc"], w=["wD"])
                        S.op("act", lambda e: e.activation(out=Dt_[:], in_=Dt_[:], func=AF.Sigmoid), r=["wD"],
                             w=["wD"])
                        S.op("dve", lambda e: e.tensor_mul(out=A[:], in0=Dt_[:], in1=kabc[:]), r=["wD", "kabc"],
                             w=["wA"])
                        S.op("dve", lambda e: e.tensor_add(out=A[:], in0=A[:], in1=omka[:]), r=["wA", "omka"],
                             w=["wA"])
                        S.op("dve", lambda e: e.tensor_mul(out=A[:], in0=A[:], in1=k_), r=["wA", "sh"], w=["wA"])
                        S.op("act", lambda e: e.activation(out=kb16[:], in_=A[:], func=AF.Copy), r=["wA"], w=["kb16"])
                        S.dma("sp", KD[d][t0:t0 + 128, :], kb16[:], r=["kb16"], w=["KD"])
                        S.op("dve", lambda e: e.scalar_tensor_tensor(out=Ct[:], in0=nkk[:], scalar=-1.0, in1=Dt_[:],
                                                                     op0=ALU.mult, op1=ALU.mult),
                             r=["nkk", "wD"], w=["wC"])
                        S.op("act", lambda e: e.activation(out=bb16[:], in_=Ct[:], func=AF.Copy), r=["wC"], w=["bb16"])
                        S.dma("sp", BD[d][t0:t0 + 128, :], bb16[:], r=["bb16"], w=["BD"])
                        S.op("dve", lambda e: e.tensor_mul(out=E[:], in0=Ct[:], in1=r_), r=["wC", "sh"], w=["wE"])
                        S.op("dve", lambda e: e.tensor_reduce(out=sm[:, 0:24], in_=h3(E[:]), axis=AX.X, op=ALU.add),
                             r=["wE"], w=["wsm"])
                        S.op("dve", lambda e: e.tensor_mul(out=E[:], in0=A[:], in1=r_), r=["wA", "sh"], w=["wE"])
                        S.op("dve", lambda e: e.tensor_reduce(out=sm[:, 24:48], in_=h3(E[:]), axis=AX.X, op=ALU.add),
                             r=["wE"], w=["wsm"])
                        S.dma("sp", BRKR[d][t0:t0 + 128, :], sm[:], r=["wsm"], w=["BRKR"])
                        S.op("dve", lambda e: e.tensor_tensor(out=h3(A[:]), in0=h3(nkk[:]),
                                                              in1=sm[:, 0:24].unsqueeze(2).to_broadcast([128, 24, 64]),
                                                              op=ALU.mult), r=["nkk", "wsm"], w=["wA"])
                        S.op("dve", lambda e: e.tensor_mul(out=E[:], in0=Bt[:], in1=r_), r=["wB", "sh"], w=["wE"])
                        S.op("dve", lambda e: e.tensor_add(out=E[:], in0=E[:], in1=A[:]), r=["wE", "wA"], w=["wE"])
                        fm_store(E[:], "wE", WRT[d][:, :, t0:t0 + 128])
                        fm_store(Bt[:], "wB", WT[d][:, :, t0:t0 + 128])
                S.barrier()

        def phase_wkv_scan(l, T, L, sample):
            TBK = 4
            LT = L // 128
            with contextlib.ExitStack() as ps:
                lsb = lambda name, shape, dt=F32: ps.enter_context(nc.sbuf_tensor(uname(name), list(shape), dt))
                m48 = lsb("m48", [48, 768])
                m24b = lsb("m24b", [24, 768], BF16)
                sio = lsb("sio", [64, 12, 128])
                B = []
                for d in range(2):
                    b = {n: lsb("%s_%d" % (n, d), shp, dt) for (n, shp, dt) in (
                        ("Ap", [128, 128, 48], F32), ("nkT", [128, 12, 128], F32), ("wrT", [128, 12, 128], F32),
                        ("wT", [128, 12, 128], F32), ("Lb", [24, TBK, 128], BF16), ("Lk", [24, TBK, 128], BF16),
                        ("Rv", [24, TBK, 768], BF16), ("Ra", [48, TBK, 768], F32), ("Rb", [24, TBK, 768], BF16),
                        ("Cst", [48, TBK, 64], F32), ("ST", [128, 768], F32))}
                    b["k"] = {n: "%s_%d" % (n, d) for n in ("Ap", "nkT", "wrT", "wT", "Lb", "Lk", "Rv", "Ra", "Rb",
                                                            "Cst", "ST")}
                    b["pb"] = (0, 1, 2, 3) if d == 0 else (4, 5, 6, 7)
                    B.append(b)
                S.dma("sp", m48[:], I["mask48"][:, :], w=["m48"])
                S.op("dve", lambda e: e.tensor_copy(out=m24b[:], in_=m48[0:24, :]), r=["m48"], w=["m24b"])
                for d in range(2):
                    b = B[d]
                    S.op("dve", lambda e: e.memset(b["Ap"][:], 0.0), w=[b["k"]["Ap"]])
                    S.op("dve", lambda e: e.memset(b["Lb"][:], 0.0), w=[b["k"]["Lb"]])
                    S.op("dve", lambda e: e.memset(b["Lk"][:], 0.0), w=[b["k"]["Lk"]])
                for s_ in range(T // L):
                    for d in range(2):
                        b = B[d]
                        ST = b["ST"]
                        if sample:
                            S.dma("sp", sio[:].rearrange("v g (h k) -> v g h k", h=2),
                                  I["st_wkv"][l, d].rearrange("(g h) v k -> v g h k", h=2), w=["sio"])
                            transposes_to(sio[:].rearrange("v g q -> v (g q)"), "sio", 12,
                                          lambda q, nj: ST[:, q * 64:(q + nj) * 64].rearrange("p (j v) -> p j v", j=nj),
                                          b["k"]["ST"], bw=128, inw=64, pbanks=b["pb"][2:4])
                            S.op("dve", lambda e: e.tensor_copy(out=ST[:, 0:1], in_=ST[:, 0:1]), r=[b["k"]["ST"]],
                                 w=[(b["k"]["ST"], g_) for g_ in range(12)])
                        else:
                            S.op("dve", lambda e: e.memset(ST[:], 0.0),
                                 w=[(b["k"]["ST"], g_) for g_ in range(12)] + [b["k"]["ST"]])
                    for kk_ in range(LT):
                        tis = (kk_, LT - 1 - kk_)
                        t0s = [s_ * L + ti * 128 for ti in tis]
                        for d in range(2):
                            b = B[d]
                            k = b["k"]
                            t0 = t0s[d]
                            S.dma("sp", b["nkT"][:], NKKT[:, :, t0:t0 + 128], r=["NKKT"], w=[k["nkT"]])
                            S.dma("sp", b["wrT"][:], WRT[d][:, :, t0:t0 + 128], r=["WRT"], w=[k["wrT"]])
                            S.dma("sp", b["wT"][:], WT[d][:, :, t0:t0 + 128], r=["WT"], w=[k["wT"]])
                            for (lo, hi, c0_, src, sk) in ((0, 64, 0, "nkT", k["nkT"]), (64, 128, 12, "nkT", k["nkT"]),
                                                           (0, 64, 24, "wrT", k["wrT"]), (64, 128, 36, "wrT", k["wrT"])):
                                S.op("act", lambda e, lo=lo, hi=hi, c0_=c0_, src=src: e.activation(
                                    out=b["Ap"][lo:hi, :, c0_:c0_ + 12], in_=b[src][lo:hi].rearrange("p g t -> p t g"),
                                    func=AF.Copy), r=[sk], w=[k["Ap"]])
                        for cc in range(128 // TBK):
                            chs = (cc, 128 // TBK - 1 - cc)
                            for d in range(2):
                                b = B[d]
                                k = b["k"]
                                c0 = t0s[d] + chs[d] * TBK
                                bsrc = BD[d][c0:c0 + TBK, :].rearrange("t (g h k) -> g t h k", g=12, h=2)
                                ksrc = KD[d][c0:c0 + TBK, :].rearrange("t (g h k) -> g t h k", g=12, h=2)
                                S.dma("sp", b["Lb"][0:12, :, 0:64], bsrc[:, :, 0, :], r=["BD"], w=[k["Lb"]])
                                S.dma("sp", b["Lb"][12:24, :, 64:128], bsrc[:, :, 1, :], r=["BD"], w=[k["Lb"]])
                                S.dma("sp", b["Lk"][0:12, :, 0:64], ksrc[:, :, 0, :], r=["KD"], w=[k["Lk"]])
                                S.dma("sp", b["Lk"][12:24, :, 64:128], ksrc[:, :, 1, :], r=["KD"], w=[k["Lk"]])
                                for hh in range(2):
                                    S.dma("sp", b["Rv"][hh * 12:(hh + 1) * 12, :, :],
                                          VBD[hh][c0:c0 + TBK, :, :].rearrange("t g q -> g t q"),
                                          r=["VBD"], w=[k["Rv"]])
                            for st_ in range(TBK):
                                tls = (st_, TBK - 1 - st_)
                                toks = [chs[d] * TBK + tls[d] for d in range(2)]
                                for d in range(2):
                                    b = B[d]
                                    k = b["k"]
                                    for hf in range(2):
                                        S.op("pe", lambda e, hf=hf: e.matmul(
                                            psb[b["pb"][hf]][0:48, 0:384], lhsT=b["Ap"][:, toks[d], :],
                                            rhs=b["ST"][:, hf * 384:(hf + 1) * 384], start=True, stop=True),
                                            r=[k["Ap"]] + [(k["ST"], g_) for g_ in range(hf * 6, hf * 6 + 6)],
                                            w=[PK[b["pb"][hf]]])
                                for d in range(2):
                                    b = B[d]
                                    k = b["k"]
                                    for hf in range(2):
                                        S.op("dve", lambda e, hf=hf: e.tensor_mul(
                                            out=b["Ra"][:, tls[d], hf * 384:(hf + 1) * 384],
                                            in0=psb[b["pb"][hf]][0:48, 0:384],
                                            in1=m48[:, hf * 384:(hf + 1) * 384]),
                                            r=[PK[b["pb"][hf]], "m48"], w=[(k["Ra"], hf)])
                                for d in range(2):
                                    b = B[d]
                                    k = b["k"]
                                    S.op("act", lambda e: e.activation(out=b["Rb"][:, tls[d], :],
                                                                       in_=b["Ra"][0:24, tls[d], :], func=AF.Copy),
                                         r=[(k["Ra"], 0), (k["Ra"], 1)], w=[k["Rb"]])
                                for d in range(2):
                                    b = B[d]
                                    k = b["k"]
                                    for hf in range(2):
                                        pb = b["pb"][2 + hf]
                                        S.op("pe", lambda e, hf=hf, pb=pb: e.matmul(
                                            psb[pb][:, 0:384], lhsT=b["Lk"][:, tls[d], :],
                                            rhs=b["Rv"][:, tls[d], hf * 384:(hf + 1) * 384], start=True, stop=False),
                                            r=[k["Lk"], k["Rv"]], w=[PK[pb]])
                                        S.op("pe", lambda e, hf=hf, pb=pb: e.matmul(
                                            psb[pb][:, 0:384], lhsT=b["Lb"][:, tls[d], :],
                                            rhs=b["Rb"][:, tls[d], hf * 384:(hf + 1) * 384], start=False, stop=True),
                                            r=[k["Lb"], k["Rb"]], w=[PK[pb]])
                                for d in range(2):
                                    b = B[d]
                                    k = b["k"]
                                    for g in range(12):
                                        pb = b["pb"][2 + g // 6]
                                        S.op("dve", lambda e, g=g, pb=pb: e.scalar_tensor_tensor(
                                            out=b["ST"][:, g * 64:(g + 1) * 64], in0=b["ST"][:, g * 64:(g + 1) * 64],
                                            scalar=b["wT"][:, g, toks[d]:toks[d] + 1],
                                            in1=psb[pb][:, (g % 6) * 64:(g % 6 + 1) * 64], op0=ALU.mult, op1=ALU.add),
                                            r=[(k["ST"], g), k["wT"], PK[pb]], w=[(k["ST"], g)])
                            for d in range(2):
                                b = B[d]
                                k = b["k"]
                                c0 = t0s[d] + chs[d] * TBK
                                S.op("dve", lambda e: e.tensor_reduce(
                                    out=b["Cst"][:], in_=b["Ra"][:].rearrange("p t (g v) -> p t v g", g=12),
                                    axis=AX.X, op=ALU.add), r=[(k["Ra"], 0), (k["Ra"], 1)], w=[k["Cst"]])
                                S.dma("sp", SAY[d][:, c0:c0 + TBK, :], b["Cst"][:], r=[k["Cst"]], w=["SAY"])
                    if not sample:
                        for d in range(2):
                            b = B[d]
                            S.op("dve", lambda e: e.tensor_copy(out=b["ST"][:, 0:1], in_=b["ST"][:, 0:1]),
                                 r=[(b["k"]["ST"], g_) for g_ in range(12)], w=[b["k"]["ST"]])
                            transposes_to(b["ST"][:], b["k"]["ST"], 12, lambda q, nj: sio[:, q:q + nj, :], "sio",
                                          bw=64, inw=128, pbanks=b["pb"][2:4])
                            S.dma("sp", O["ns_wkv"][s_, l, d].rearrange("(g h) v k -> v g h k", h=2),
                                  sio[:].rearrange("v g (h k) -> v g h k", h=2), r=["sio"], w=["ns_wkv"])
                S.barrier()

        def phase_wkv_post(l, T):
            with contextlib.ExitStack() as ps:
                lsb = lambda name, shape, dt=F32: ps.enter_context(nc.sbuf_tensor(uname(name), list(shape), dt))
                y0 = [lsb("py0%d" % d, [128, 1536]) for d in range(2)]
                vv = lsb("pvv", [128, 1536])
                gg = lsb("pgg", [128, 1536])
                o = lsb("po", [128, 1536])
                t_ = lsb("pt", [128, 1536])
                lnw = lsb("lnw", [128, 1536])
                lnb = lsb("lnb", [128, 1536])
                bk = [lsb("pbk%d" % d, [128, 48]) for d in range(2)]
                rk = lsb("prk", [128, 24])
                mu = lsb("pmu", [128, 24])
                bcload(lnw[:], "lnw", I["wkv_ln_w"][l, :])
                bcload(lnb[:], "lnb", I["wkv_ln_b"][l, :])
                h3 = lambda ap: ap.rearrange("p (h q) -> p h q", h=24)
                bc3 = lambda ap: ap.unsqueeze(2).to_broadcast([128, 24, 64])
                for i in range(T // 128):
                    t0 = i * 128
                    for d in range(2):
                        for hh in range(2):
                            S.dma("sp", y0[d][:].rearrange("p (g h v) -> p g h v", g=12, h=2)[:, :, hh, :],
                                  SAY[d][(2 + hh) * 12:(3 + hh) * 12, t0:t0 + 128, :].rearrange("g t v -> t g v"),
                                  r=["SAY"], w=["py0%d" % d])
                        S.dma("sp", bk[d][:], BRKR[d][t0:t0 + 128, :], r=["BRKR"], w=["pbk%d" % d])
                    S.dma("sp", vv[:], SH[t0:t0 + 128, 3072:4608], r=["SH"], w=["pvv"])
                    S.dma("sp", gg[:], GG[t0:t0 + 128, :], r=["GG"], w=["pgg"])
                    S.dma("sp", rk[:], RK[t0:t0 + 128, :], r=["RK"], w=["prk"])
                    S.op("dve", lambda e: e.tensor_add(out=o[:], in0=y0[0][:], in1=y0[1][:]), r=["py00", "py01"],
                         w=["po"])
                    S.op("dve", lambda e: e.tensor_add(out=mu[:], in0=bk[0][:, 24:48], in1=bk[1][:, 24:48]),
                         r=["pbk0", "pbk1"], w=["pmu"])
                    S.op("dve", lambda e: e.tensor_tensor(out=h3(t_[:]), in0=h3(vv[:]), in1=bc3(mu[:, :]),
                                                          op=ALU.mult), r=["pvv", "pmu"], w=["pt"])
                    S.op("dve", lambda e: e.tensor_add(out=o[:], in0=o[:], in1=t_[:]), r=["po", "pt"], w=["po"])
                    S.op("dve", lambda e: e.tensor_reduce(out=mu[:], in_=h3(o[:]), axis=AX.X, op=ALU.add), r=["po"],
                         w=["pmu"])
                    S.op("dve", lambda e: e.tensor_scalar(out=mu[:], in0=mu[:], scalar1=1.0 / 64, scalar2=None,
                                                          op0=ALU.mult), r=["pmu"], w=["pmu"])
                    S.op("dve", lambda e: e.tensor_tensor(out=h3(o[:]), in0=h3(o[:]), in1=bc3(mu[:, :]),
                                                          op=ALU.subtract), r=["po", "pmu"], w=["po"])
                    S.op("act", lambda e: e.activation(out=t_[:], in_=o[:], func=AF.Square), r=["po"], w=["pt"])
                    S.op("dve", lambda e: e.tensor_reduce(out=mu[:], in_=h3(t_[:]), axis=AX.X, op=ALU.add), r=["pt"],
                         w=["pmu"])
                    rms_rstd(mu[:], "pmu", 64, 64e-5)
                    S.op("dve", lambda e: e.tensor_tensor(out=h3(o[:]), in0=h3(o[:]), in1=bc3(mu[:, :]),
                                                          op=ALU.mult), r=["po", "pmu"], w=["po"])
                    S.op("dve", lambda e: e.tensor_mul(out=o[:], in0=o[:], in1=lnw[:]), r=["po", "lnw"], w=["po"])
                    S.op("dve", lambda e: e.tensor_add(out=o[:], in0=o[:], in1=lnb[:]), r=["po", "lnb"], w=["po"])
                    S.op("dve", lambda e: e.tensor_tensor(out=h3(t_[:]), in0=h3(vv[:]), in1=bc3(rk[:, :]),
                                                          op=ALU.mult), r=["pvv", "prk"], w=["pt"])
                    S.op("dve", lambda e: e.tensor_add(out=o[:], in0=o[:], in1=t_[:]), r=["po", "pt"], w=["po"])
                    S.op("dve", lambda e: e.tensor_mul(out=o[:], in0=o[:], in1=gg[:]), r=["po", "pgg"], w=["po"])
                    S.dma("sp", ZW[t0:t0 + 128, :], o[:], r=["po"], w=["ZW"])
                S.barrier()

        def phase_s5(l, T, L, sample):
            NT = T // 128
            nseq = T // L
            LT = L // 128
            Ls = min(512, L)
            nseg = L // Ls
            nsub = Ls // 128
            with contextlib.ExitStack() as ps:
                lsb = lambda name, shape, dt=F32: ps.enter_context(nc.sbuf_tensor(uname(name), list(shape), dt))
                ut = [lsb("ut%d" % j, [128, 1024]) for j in range(2)]
                stg = [lsb("ustg%d" % j, [32, 32, 128]) for j in range(2)]
                cnt = 0
                for i in range(NT):
                    t0 = i * 128
                    s_ = t0 // L
                    ti = (t0 % L) // 128
                    t0r = s_ * L + (LT - 1 - ti) * 128
                    u = ut[i % 2]
                    uk = "ut%d" % (i % 2)
                    S.dma("sp", u[:], P[t0:t0 + 128, C_U:C_U + 1024], r=["P"], w=[uk])
                    for d in range(2):
                        st_ = stg[cnt % 2]
                        sk = "ustg%d" % (cnt % 2)
                        cnt += 1
                        transposes_to(u[:], uk, 32, lambda q, nj: st_[:, q:q + nj, :], sk, bw=32, inw=128,
                                      eng=("act" if d == 0 else "dve"), rhs=(None if d == 0 else jrev[:]),
                                      rkey="jrev", pbanks=((6, 7) if d == 0 else (4, 5)))
                        dt_ = t0 if d == 0 else t0r
                        S.dma("sp", UT2[d][:, :, dt_:dt_ + 128].rearrange("k r t -> r k t"), st_[:], r=[sk],
                              w=["UT2"])
                S.barrier()
            with contextlib.ExitStack() as ps:
                lsb = lambda name, shape, dt=F32: ps.enter_context(nc.sbuf_tensor(uname(name), list(shape), dt))
                pt_ = {n: lsb("s5_" + n, [128, 32]) for n in
                       ("lre", "lim", "ldt", "rho", "tht", "cs", "sn", "ar", "ai", "t1", "t2", "t3", "cr", "ci")}
                it_ = lsb("s5_it", [128, 32], I32)
                bre = lsb("bre", [128, 32, 16])
                bim = lsb("bim", [128, 32, 16])
                Bbr = lsb("Bbr", [128, 32, 16])
                Bbi = lsb("Bbi", [128, 32, 16])
                btmp = lsb("btmp", [128, 32, 16])
                BDr = lsb("BDr", [128, 32, 32])
                BDi = lsb("BDi", [128, 32, 32])
                BpTr = lsb("BpTr", [32, 32, 128])
                BpTi = lsb("BpTi", [32, 32, 128])
                Zr = lsb("Zr", [32, 32, 128])
                Zi = lsb("Zi", [32, 32, 128])
                Cre = lsb("Cre", [128, 32, 32], BF16)
                nCre = lsb("nCre", [128, 32, 32], BF16)
                nCim = lsb("nCim", [128, 32, 32], BF16)
                jjt = lsb("jjt", [128, 512])
                tj = lsb("tj", [128, 512])
                tf = lsb("tf", [128, 512])
                iti = lsb("iti", [128, 512], I32)
                cst = lsb("cst", [128, 512])
                snt = lsb("snt", [128, 512])
                u2 = [lsb("u2_%d" % j, [32, 512]) for j in range(2)]
                p1 = lsb("p1", [128, 512])
                p2 = lsb("p2", [128, 512])
                inre = lsb("inre", [128, 512])
                inim = lsb("inim", [128, 512])
                zr = lsb("zr", [128, 512])
                zi = lsb("zi", [128, 512])
                qq = [lsb("qq%d" % j, [128, 512], BF16) for j in range(4)]
                xr = lsb("xr", [128, 1])
                xi = lsb("xi", [128, 1])
                c4 = lsb("c4", [128, 4])
                hre = lsb("hre", [128, 32])
                him = lsb("him", [128, 32])
                finr = lsb("finr", [128, nseq, 32])
                fini = lsb("fini", [128, nseq, 32])
                ystg = [lsb("ystg%d" % j, [128, 4, 32]) for j in range(2)]
                S.dma("sp", jjt[:], I["jj"][:, :], w=["jjt"])
                for zt, zk in ((BDr, "BDr"), (BDi, "BDi"), (Zr, "Zr"), (Zi, "Zi")):
                    S.op("dve", lambda e, zt=zt: e.memset(zt[:], 0.0), w=[zk])
                K_ = "s5p"

                def tt(o, a, b, op):
                    S.op("dve", lambda e: e.tensor_tensor(out=pt_[o][:], in0=pt_[a][:], in1=pt_[b][:], op=op),
                         r=[K_], w=[K_])

                def ts(o, a, s1, s2, op0, op1=None):
                    if op1 is None:
                        S.op("dve", lambda e: e.tensor_scalar(out=pt_[o][:], in0=pt_[a][:], scalar1=s1, scalar2=None,
                                                              op0=op0), r=[K_], w=[K_])
                    else:
                        S.op("dve", lambda e: e.tensor_scalar(out=pt_[o][:], in0=pt_[a][:], scalar1=s1, scalar2=s2,
                                                              op0=op0, op1=op1), r=[K_], w=[K_])

                def frac_sin(o, a):
                    S.op("dve", lambda e: e.tensor_copy(out=it_[:], in_=pt_[a][:]), r=[K_], w=[K_])
                    S.op("dve", lambda e: e.tensor_copy(out=pt_["t2"][:], in_=it_[:]), r=[K_], w=[K_])
                    tt("t2", a, "t2", ALU.subtract)
                    S.op("act", lambda e: e.activation(out=pt_[o][:], in_=pt_["t2"][:], func=AF.Sin, scale=SIN_SCALE),
                         r=[K_], w=[K_])

                cnt = 0
                ucnt = 0
                for d in range(2):
                    with nc.allow_non_contiguous_dma(reason="small s5 parameter transposes"):
                        for g2 in range(2):
                            sl = slice(g2 * 64, (g2 + 1) * 64)
                            S.dma("sp", pt_["lre"][sl, :],
                                  I["s5_lam_re"][l, d].rearrange("(k g) p -> g p k", g=2)[g2], w=[K_])
                            S.dma("sp", pt_["lim"][sl, :],
                                  I["s5_lam_im"][l, d].rearrange("(k g) p -> g p k", g=2)[g2], w=[K_])
                            S.dma("sp", pt_["ldt"][sl, :],
                                  I["s5_log_dt"][l, d].rearrange("(k g) -> g k", g=2)[g2].partition_broadcast(64),
                                  w=[K_])
                            if sample:
                                S.dma("sp", hre[sl, :], I["st_s5re"][l, d].rearrange("(k g) p -> g p k", g=2)[g2],
                                      w=["hre"])
                                S.dma("sp", him[sl, :], I["st_s5im"][l, d].rearrange("(k g) p -> g p k", g=2)[g2],
                                      w=["him"])
                    ts("lre", "lre", -1e-4, None, ALU.min)
                    S.op("act", lambda e: e.activation(out=pt_["ldt"][:], in_=pt_["ldt"][:], func=AF.Exp), r=[K_],
                         w=[K_])
                    tt("t1", "lre", "ldt", ALU.mult)
                    S.op("act", lambda e: e.activation(out=pt_["rho"][:], in_=pt_["t1"][:], func=AF.Exp), r=[K_],
                         w=[K_])
                    tt("tht", "lim", "ldt", ALU.mult)
                    ts("tht", "tht", 1.0 / TWO_PI, None, ALU.mult)
                    frac_sin("sn", "tht")
                    ts("t3", "tht", 0.25, None, ALU.add)
                    frac_sin("cs", "t3")
                    tt("ar", "rho", "cs", ALU.mult)
                    tt("ai", "rho", "sn", ALU.mult)
                    ts("ar", "ar", -1.0, None, ALU.add)
                    tt("t1", "ar", "lre", ALU.mult)
                    tt("t2", "ai", "lim", ALU.mult)
                    tt("cr", "t1", "t2", ALU.add)
                    tt("t1", "ai", "lre", ALU.mult)
                    tt("t2", "ar", "lim", ALU.mult)
                    tt("ci", "t1", "t2", ALU.subtract)
                    tt("t1", "lre", "lre", ALU.mult)
                    tt("t2", "lim", "lim", ALU.mult)
                    tt("t1", "t1", "t2", ALU.add)
                    S.op("dve", lambda e: e.reciprocal(out=pt_["t1"][:], in_=pt_["t1"][:]), r=[K_], w=[K_])
                    tt("cr", "cr", "t1", ALU.mult)
                    tt("ci", "ci", "t1", ALU.mult)
                    S.dma("sp", bre[:], I["s5_b_re"][l, d].rearrange("(k g) p c -> (g p) k c", g=2), w=["bre"])
                    S.dma("sp", bim[:], I["s5_b_im"][l, d].rearrange("(k g) p c -> (g p) k c", g=2), w=["bim"])
                    crb = pt_["cr"][:, :].unsqueeze(2).to_broadcast([128, 32, 16])
                    cib = pt_["ci"][:, :].unsqueeze(2).to_broadcast([128, 32, 16])
                    S.op("dve", lambda e: e.tensor_tensor(out=Bbr[:], in0=bre[:], in1=crb, op=ALU.mult),
                         r=["bre", K_], w=["Bbr"])
                    S.op("dve", lambda e: e.tensor_tensor(out=btmp[:], in0=bim[:], in1=cib, op=ALU.mult),
                         r=["bim", K_], w=["btmp"])
                    S.op("dve", lambda e: e.tensor_sub(out=Bbr[:], in0=Bbr[:], in1=btmp[:]), r=["Bbr", "btmp"],
                         w=["Bbr"])
                    S.op("dve", lambda e: e.tensor_tensor(out=Bbi[:], in0=bre[:], in1=cib, op=ALU.mult),
                         r=["bre", K_], w=["Bbi"])
                    S.op("dve", lambda e: e.tensor_tensor(out=btmp[:], in0=bim[:], in1=crb, op=ALU.mult),
                         r=["bim", K_], w=["btmp"])
                    S.op("dve", lambda e: e.tensor_add(out=Bbi[:], in0=Bbi[:], in1=btmp[:]), r=["Bbi", "btmp"],
                         w=["Bbi"])
                    for (bsrc, bkey, bd, bdk, bp, bpk) in ((Bbr, "Bbr", BDr, "BDr", BpTr, "BpTr"),
                                                           (Bbi, "Bbi", BDi, "BDi", BpTi, "BpTi")):
                        S.op("dve", lambda e: e.tensor_copy(out=bd[0:64, :, 0:16], in_=bsrc[0:64, :, :]), r=[bkey],
                             w=[bdk])
                        S.op("dve", lambda e: e.tensor_copy(out=bd[64:128, :, 16:32], in_=bsrc[64:128, :, :]),
                             r=[bkey], w=[bdk])
                        transposes_to(bd[:].rearrange("p k c -> p (k c)"), bdk, 32,
                                      lambda q, nj, bp=bp: bp[:, q:q + nj, :], bpk, bw=32, inw=128)
                    for (zt, zk, nm) in ((Zr, "Zr", "s5_c_re"), (Zi, "Zi", "s5_c_im")):
                        csrc = I[nm][l, d].rearrange("(k g) c p -> g c k p", g=2)
                        S.dma("sp", zt[0:16, :, 0:64], csrc[0], w=[zk])
                        S.dma("sp", zt[16:32, :, 64:128], csrc[1], w=[zk])
                    zf = lambda zt: zt[:].rearrange("r k q -> r (k q)")
                    transposes_to(zf(Zr), "Zr", 32, lambda q, nj: Cre[:, q:q + nj, :], "Cre", bw=128, inw=32)
                    transposes_to(zf(Zr), "Zr", 32, lambda q, nj: nCre[:, q:q + nj, :], "nCre", bw=128, inw=32,
                                  scale=-1.0)
                    transposes_to(zf(Zi), "Zi", 32, lambda q, nj: nCim[:, q:q + nj, :], "nCim", bw=128, inw=32,
                                  scale=-1.0)
                    Cm = (Cre, nCre, nCim, nCim)
                    Ck = ("Cre", "nCre", "nCim", "nCim")
                    for kt in range(32):
                        S.op("dve", lambda e: e.tensor_scalar(out=tj[:, 0:Ls], in0=jjt[:, 0:Ls],
                                                              scalar1=pt_["tht"][:, kt:kt + 1], scalar2=None,
                                                              op0=ALU.mult), r=["jjt", K_], w=["tj"])
                        for (dst, dk_, off) in ((snt, "snt", 0.0), (cst, "cst", 0.25)):
                            if off != 0.0:
                                S.op("dve", lambda e: e.tensor_scalar(out=tj[:, 0:Ls], in0=tj[:, 0:Ls], scalar1=off,
                                                                      scalar2=None, op0=ALU.add), r=["tj"], w=["tj"])
                            S.op("dve", lambda e: e.tensor_copy(out=iti[:, 0:Ls], in_=tj[:, 0:Ls]), r=["tj"],
                                 w=["iti"])
                            S.op("dve", lambda e: e.tensor_copy(out=tf[:, 0:Ls], in_=iti[:, 0:Ls]), r=["iti"],
                                 w=["tf"])
                            S.op("dve", lambda e: e.tensor_sub(out=tf[:, 0:Ls], in0=tj[:, 0:Ls], in1=tf[:, 0:Ls]),
                                 r=["tj", "tf"], w=["tf"])
                            S.op("act", lambda e, dst=dst: e.activation(out=dst[:, 0:Ls], in_=tf[:, 0:Ls],
                                                                        func=AF.Sin, scale=SIN_SCALE),
                                 r=["tf"], w=[dk_])
                        rhob = pt_["rho"][:, kt:kt + 1].to_broadcast([128, Ls])
                        for s_ in range(nseq):
                            if sample:
                                S.op("dve", lambda e: e.tensor_copy(out=xr[:], in_=hre[:, kt:kt + 1]), r=["hre"],
                                     w=["xr"])
                                S.op("dve", lambda e: e.tensor_copy(out=xi[:], in_=him[:, kt:kt + 1]), r=["him"],
                                     w=["xi"])
                            else:
                                S.op("dve", lambda e: e.memset(xr[:], 0.0), w=["xr"])
                                S.op("dve", lambda e: e.memset(xi[:], 0.0), w=["xi"])
                            for seg in range(nseg):
                                tau0 = s_ * L + seg * Ls
                                u_ = u2[ucnt % 2]
                                uk = "u2_%d" % (ucnt % 2)
                                ucnt += 1
                                S.dma("sp", u_[:, 0:Ls], UT2[d][kt, :, tau0:tau0 + Ls], r=["UT2"], w=[uk])
                                S.op("pe", lambda e: e.matmul(psb[0][:, 0:Ls], lhsT=BpTr[:, kt, :], rhs=u_[:, 0:Ls],
                                                              start=True, stop=True), r=["BpTr", uk], w=[PK[0]])
                                S.op("pe", lambda e: e.matmul(psb[1][:, 0:Ls], lhsT=BpTi[:, kt, :], rhs=u_[:, 0:Ls],
                                                              start=True, stop=True), r=["BpTi", uk], w=[PK[1]])
                                c_ = cst[:, 0:Ls]
                                s__ = snt[:, 0:Ls]
                                S.op("dve", lambda e: e.tensor_mul(out=p1[:, 0:Ls], in0=psb[0][:, 0:Ls], in1=c_),
                                     r=[PK[0], "cst"], w=["p1"])
                                S.op("dve", lambda e: e.tensor_mul(out=p2[:, 0:Ls], in0=psb[1][:, 0:Ls], in1=s__),
                                     r=[PK[1], "snt"], w=["p2"])
                                S.op("dve", lambda e: e.tensor_add(out=inre[:, 0:Ls], in0=p1[:, 0:Ls],
                                                                    in1=p2[:, 0:Ls]), r=["p1", "p2"], w=["inre"])
                                S.op("dve", lambda e: e.tensor_mul(out=p1[:, 0:Ls], in0=psb[1][:, 0:Ls], in1=c_),
                                     r=[PK[1], "cst"], w=["p1"])
                                S.op("dve", lambda e: e.tensor_mul(out=p2[:, 0:Ls], in0=psb[0][:, 0:Ls], in1=s__),
                                     r=[PK[0], "snt"], w=["p2"])
                                S.op("dve", lambda e: e.tensor_sub(out=inim[:, 0:Ls], in0=p1[:, 0:Ls],
                                                                    in1=p2[:, 0:Ls]), r=["p1", "p2"], w=["inim"])
                                S.op("dve", lambda e: e.tensor_tensor_scan(out=zr[:, 0:Ls], data0=rhob,
                                                                           data1=inre[:, 0:Ls], initial=xr[:, 0:1],
                                                                           op0=ALU.mult, op1=ALU.add),
                                     r=[K_, "inre", "xr"], w=["zr"])
                                S.op("dve", lambda e: e.tensor_tensor_scan(out=zi[:, 0:Ls], data0=rhob,
                                                                           data1=inim[:, 0:Ls], initial=xi[:, 0:1],
                                                                           op0=ALU.mult, op1=ALU.add),
                                     r=[K_, "inim", "xi"], w=["zi"])
                                for (qi, a_, ak, b_, bk_, en) in ((0, cst, "cst", zr, "zr", "dve"),
                                                                  (1, snt, "snt", zi, "zi", "dve"),
                                                                  (2, snt, "snt", zr, "zr", "dve"),
                                                                  (3, cst, "cst", zi, "zi", "dve")):
                                    S.op(en, lambda e, qi=qi, a_=a_, b_=b_: e.tensor_mul(
                                        out=qq[qi][:, 0:Ls], in0=a_[:, 0:Ls], in1=b_[:, 0:Ls]),
                                        r=[ak, bk_], w=["qq%d" % qi])
                                e0 = Ls - 1
                                for (ci_, a_, ak, b_, bk_) in ((0, cst, "cst", zr, "zr"), (1, snt, "snt", zi, "zi"),
                                                               (2, snt, "snt", zr, "zr"), (3, cst, "cst", zi, "zi")):
                                    S.op("dve", lambda e, ci_=ci_, a_=a_, b_=b_: e.tensor_mul(
                                        out=c4[:, ci_:ci_ + 1], in0=a_[:, e0:e0 + 1], in1=b_[:, e0:e0 + 1]),
                                        r=[ak, bk_], w=["c4"])
                                S.op("dve", lambda e: e.tensor_sub(out=xr[:], in0=c4[:, 0:1], in1=c4[:, 1:2]),
                                     r=["c4"], w=["xr"])
                                S.op("dve", lambda e: e.tensor_add(out=xi[:], in0=c4[:, 2:3], in1=c4[:, 3:4]),
                                     r=["c4"], w=["xi"])
                                for jb in range(nsub):
                                    for qi in range(4):
                                        S.op("pe", lambda e, jb=jb, qi=qi: e.matmul(
                                            psb[2][:, jb * 32:(jb + 1) * 32], lhsT=qq[qi][:, jb * 128:(jb + 1) * 128],
                                            rhs=Cm[qi][:, kt, :], start=(qi == 0), stop=(qi == 3)),
                                            r=["qq%d" % qi, Ck[qi]], w=[PK[2]])
                                ys_ = ystg[cnt % 2]
                                yk = "ystg%d" % (cnt % 2)
                                cnt += 1
                                S.op("act", lambda e: e.activation(
                                    out=ys_[:, 0:nsub, :], in_=psb[2][:, 0:nsub * 32].rearrange("p (j c) -> p j c",
                                                                                                j=nsub),
                                    func=AF.Copy), r=[PK[2]], w=[yk])
                                S.dma("sp", YS5[d][tau0:tau0 + Ls, kt * 32:(kt + 1) * 32].rearrange(
                                    "(j t) c -> t j c", t=128), ys_[:, 0:nsub, :], r=[yk], w=["YS5"])
                            if not sample:
                                S.op("dve", lambda e: e.tensor_copy(out=finr[:, s_, kt:kt + 1], in_=xr[:]), r=["xr"],
                                     w=["finr"])
                                S.op("dve", lambda e: e.tensor_copy(out=fini[:, s_, kt:kt + 1], in_=xi[:]), r=["xi"],
                                     w=["fini"])
                    if not sample:
                        with nc.allow_non_contiguous_dma(reason="small s5 state outputs"):
                            for s_ in range(nseq):
                                for g2 in range(2):
                                    sl = slice(g2 * 64, (g2 + 1) * 64)
                                    S.dma("sp", O["ns_s5re"][s_, l, d].rearrange("(k g) p -> g p k", g=2)[g2],
                                          finr[sl, s_, :], r=["finr"], w=["ns_s5"])
                                    S.dma("sp", O["ns_s5im"][s_, l, d].rearrange("(k g) p -> g p k", g=2)[g2],
                                          fini[sl, s_, :], r=["fini"], w=["ns_s5"])
                S.barrier()

        def phase_tail(l, T, L, jrow, xsrc, xdst):
            LT = L // 128
            NB = BLK // 128
            with contextlib.ExitStack() as ps:
                lsb = lambda name, shape, dt=F32: ps.enter_context(nc.sbuf_tensor(uname(name), list(shape), dt))
                ls = {"xt": [lsb("xt0", [128, D]), lsb("xt1", [128, D])], "junk": lsb("junk", [128, D]),
                      "ss": lsb("ss", [128, 1])}
                wbs = [lsb("wb0", [128, 8192], BF16), lsb("wb1", [128, 8192], BF16)]
                WSTG["t"] = lsb("wst", [128, 8192])
                hTa = lsb("hTa", [128, KC, BLK], BF16)
                Ybuf = lsb("Ybuf", [128, NB, D])
                GPt = lsb("GPt", [128, D])
                yt = [lsb("yt%d" % j, [128, 1536]) for j in range(2)]
                s5d = lsb("s5d", [128, 1024])
                gt = [lsb("gt%d" % j, [128, 512]) for j in range(2)]
                vt = [lsb("vt%d" % j, [128, 512]) for j in range(2)]
                mgt = [lsb("mgt%d" % j, [128, 512]) for j in range(2)]
                xg = lsb("xg", [128, 512])
                tg = lsb("tg", [128, 512])
                bcload(GPt[:], "GPt", MOD[jrow, 2 * D:3 * D])
                bcload(ls["junk"][:], "junk", I["norm_mix_post"][l, :])
                S.op("dve", lambda e: e.tensor_mul(out=GPt[:], in0=GPt[:], in1=ls["junk"][:]), r=["GPt", "junk"],
                     w=["GPt"])
                bcload(s5d[:], "s5d", I["s5_d"][l, :])
                ec = [0]

                def epi_merge(bi, t0):
                    def f(i, c0, cw, pt, pk):
                        n = ec[0] % 2
                        ec[0] += 1
                        rs_ = slice(t0 + i * 128, t0 + (i + 1) * 128)
                        g_, gk = gt[n], "gt%d" % n
                        v_, vk = vt[n], "vt%d" % n
                        m_, mk = mgt[n], "mgt%d" % n
                        gc = C_GATE + bi * D + c0
                        S.dma("sp", g_[:, 0:cw], P[rs_, gc:gc + cw], r=["P"], w=[gk])
                        S.op("act", lambda e: e.activation(out=g_[:, 0:cw], in_=g_[:, 0:cw], func=AF.Sigmoid), r=[gk],
                             w=[gk])
                        if bi == 2:
                            S.op("act", lambda e: e.activation(out=v_[:, 0:cw], in_=pt[:, cw:2 * cw],
                                                               func=AF.Sigmoid), r=[pk], w=[vk])
                            S.op("dve", lambda e: e.tensor_mul(out=v_[:, 0:cw], in0=pt[:, 0:cw], in1=v_[:, 0:cw]),
                                 r=[pk, vk], w=[vk])
                            S.op("dve", lambda e: e.tensor_mul(out=v_[:, 0:cw], in0=v_[:, 0:cw], in1=g_[:, 0:cw]),
                                 r=[vk, gk], w=[vk])
                        else:
                            S.op("dve", lambda e: e.tensor_mul(out=v_[:, 0:cw], in0=pt[:, 0:cw], in1=g_[:, 0:cw]),
                                 r=[pk, gk], w=[vk])
                        mgk = ("MG", i, c0 // 512)
                        if bi > 0:
                            S.dma("sp", m_[:, 0:cw], MG[rs_, c0:c0 + cw], r=[mgk], w=[mk])
                            S.op("dve", lambda e: e.tensor_add(out=v_[:, 0:cw], in0=v_[:, 0:cw], in1=m_[:, 0:cw]),
                                 r=[vk, mk], w=[vk])
                        S.dma("sp", MG[rs_, c0:c0 + cw], v_[:, 0:cw], r=[vk], w=[mgk])
                    return f

                def wload_glu(Wsrc):
                    def f(wb, wk, c0, cw):
                        wst = WSTG["t"]
                        kch = wb.shape[1]
                        stv = wst[:, 0:kch * 2 * cw].rearrange("p (k n) -> p k n", k=kch)
                        S.dma("sp", stv[:, :, 0:cw], Wsrc[:, c0:c0 + cw].rearrange("(k p) n -> p k n", p=128),
                              w=["wst"])
                        S.dma("sp", stv[:, :, cw:2 * cw],
                              Wsrc[:, D + c0:D + c0 + cw].rearrange("(k p) n -> p k n", p=128), w=["wst"])
                        S.op("dve", lambda e: e.tensor_copy(out=wb, in_=stv), r=["wst"], w=[wk])
                    return f

                def epi_y(i, c0, cw, pt, pk):
                    S.op("act", lambda e: e.activation(out=Ybuf[:, i, c0:c0 + cw], in_=pt[:, 0:cw], func=AF.Copy),
                         r=[pk], w=["Ybuf"])

                for bi_ in range(T // BLK):
                    t0 = bi_ * BLK
                    for (br, src, Wn) in ((0, YS, "w_ssm_out"), (1, ZW, "w_wkv_out")):
                        for i in range(NB):
                            y_ = yt[i % 2]
                            yk = "yt%d" % (i % 2)
                            S.dma("sp", y_[:], src[t0 + i * 128:t0 + (i + 1) * 128, :], r=["BSRC"], w=[yk])
                            transposes_to(y_[:], yk, 12, lambda q, nj, i=i: hTa[:, q:q + nj, i * 128:(i + 1) * 128],
                                          "hTa")
                        proj(wbs, hTa, "hTa", 12, D, NB, epi_merge(br, t0), wload_plain(I[Wn][l]))
                    for i in range(NB):
                        ta = t0 + i * 128
                        s_ = ta // L
                        ti = (ta % L) // 128
                        tr = s_ * L + (LT - 1 - ti) * 128
                        yf = ls["xt"][0]
                        yb = ls["xt"][1]
                        uu = yt[i % 2]
                        uk = "yt%d" % (i % 2)
                        S.dma("sp", yf[:, 0:1024], YS5[0][ta:ta + 128, :], r=["YS5"], w=["xt0"])
                        S.dma("sp", yb[:, 0:1024], YS5[1][tr:tr + 128, :], r=["YS5"], w=["xt1"])
                        S.dma("sp", uu[:, 0:1024], P[ta:ta + 128, C_U:C_U + 1024], r=["P"], w=[uk])
                        S.op("dve", lambda e: e.tensor_mul(out=uu[:, 0:1024], in0=uu[:, 0:1024], in1=s5d[:]),
                             r=[uk, "s5d"], w=[uk])
                        for hb in range(2):
                            pb = 4 + hb
                            for j in range(4):
                                cb = hb * 4 + j
                                cs_ = slice(cb * 128, (cb + 1) * 128)
                                o_ = psb[pb][:, j * 128:(j + 1) * 128]
                                S.op("pe", lambda e: e.matmul(o_, lhsT=yf[:, cs_], rhs=ident[:], start=True,
                                                              stop=False), r=["xt0", "ident"], w=[PK[pb]])
                                S.op("pe", lambda e: e.matmul(o_, lhsT=yb[:, cs_], rhs=jrev[:], start=False,
                                                              stop=False), r=["xt1", "jrev"], w=[PK[pb]])
                                S.op("pe", lambda e: e.matmul(o_, lhsT=uu[:, cs_], rhs=ident[:], start=False,
                                                              stop=True), r=[uk, "ident"], w=[PK[pb]])
                            S.op("act", lambda e: e.activation(out=xg[:], in_=psb[pb][:, :], func=AF.Copy),
                                 r=[PK[pb]], w=["xg"])
                            S.op("dve", lambda e: e.tensor_mul(out=tg[:], in0=xg[:], in1=xg[:]), r=["xg"], w=["tg"])
                            S.op("dve", lambda e: e.tensor_scalar(out=tg[:], in0=tg[:], scalar1=0.044715, scalar2=1.0,
                                                                  op0=ALU.mult, op1=ALU.add), r=["tg"], w=["tg"])
                            S.op("dve", lambda e: e.tensor_mul(out=tg[:], in0=tg[:], in1=xg[:]), r=["tg", "xg"],
                                 w=["tg"])
                            S.op("act", lambda e: e.activation(out=tg[:], in_=tg[:], func=AF.Sigmoid,
                                                               scale=2.0 * math.sqrt(2.0 / math.pi)), r=["tg"],
                                 w=["tg"])
                            S.op("dve", lambda e: e.tensor_tensor(
                                out=hTa[:, hb * 4:(hb + 1) * 4, i * 128:(i + 1) * 128],
                                in0=xg[:].rearrange("p (j t) -> p j t", j=4),
                                in1=tg[:].rearrange("p (j t) -> p j t", j=4), op=ALU.mult), r=["xg", "tg"], w=["hTa"])
                    proj(wbs, hTa, "hTa", 8, D, NB, epi_merge(2, t0), wload_glu(I["w_s5_glu"][l]), cwmax=256, wmul=2)
                    for i in range(NB):
                        xt = ls["xt"][i % 2]
                        xk = "xt%d" % (i % 2)
                        S.dma("sp", xt[:], MG[t0 + i * 128:t0 + (i + 1) * 128, :],
                              r=[("MG", i, c) for c in range(4)],
                              w=[xk])
                        transposes_to(xt[:], xk, KC, lambda q, nj, i=i: hTa[:, q:q + nj, i * 128:(i + 1) * 128],
                                      "hTa")
                    proj(wbs, hTa, "hTa", KC, D, NB, epi_y, wload_plain(I["w_out"][l]))
                    postnorm_residual(ls, Ybuf, NB, t0, xsrc, xdst, GPt)
                S.barrier()

        def phase_ffn(l, T, GW, jrow, xsrc, xdst):
            NB = BLK // 128
            with contextlib.ExitStack() as ps:
                lsb = lambda name, shape, dt=F32: ps.enter_context(nc.sbuf_tensor(uname(name), list(shape), dt))
                ls = {"xt": [lsb("xt0", [128, D]), lsb("xt1", [128, D])], "junk": lsb("junk", [128, D]),
                      "ss": lsb("ss", [128, 1])}
                wbs = [lsb("wb0", [128, 8192], BF16), lsb("wb1", [128, 8192], BF16)]
                WSTG["t"] = lsb("wst", [128, 8192])
                st = [lsb("st0", [128, 512]), lsb("st1", [128, 512])]
                BI = min(BLK_IN, T)
                hT = lsb("hT", [128, KC, BI], BF16)
                Gt, SHt, GPt = mod_tiles(lsb, ls, l, jrow, 4, 3, 5, "norm_ffn_pre", "norm_ffn_post")
                cnt = [0]
                for bi in range(T // BI):
                    t0 = bi * BI
                    norm_to_fm(ls, xsrc, t0, BI // 128, hT, Gt, SHt)

                    def epi(i, c0, cw, pt, pk, t0=t0):
                        cnt[0] += 1
                        s_ = st[cnt[0] % 2]
                        sk = "st%d" % (cnt[0] % 2)
                        S.op("act", lambda e: e.activation(out=s_[:, 0:cw], in_=pt[:, 0:cw], func=AF.Copy),
                             r=[pk], w=[sk])
                        S.dma("sp", GU[t0 + i * 128:t0 + (i + 1) * 128, c0:c0 + cw], s_[:, 0:cw], r=[sk], w=["GU"])

                    proj(wbs, hT, "hT", KC, 2 * D_FF, BI // 128, epi, wload_plain(I["w_ffn_in"][l]))
                S.barrier()
            with contextlib.ExitStack() as ps:
                up = [ps.enter_context(nc.sbuf_tensor(uname("fup%d" % j), [128, 512], F32)) for j in range(2)]
                sg = ps.enter_context(nc.sbuf_tensor(uname("fsg"), [128, 512], F32))
                uc = [0]

                def post(i, c0, cw, y, yk):
                    u_ = up[uc[0] % 2]
                    uk = "fup%d" % (uc[0] % 2)
                    uc[0] += 1
                    S.dma("sp", u_[:, 0:cw], GU[i * 128:(i + 1) * 128, D_FF + c0:D_FF + c0 + cw], r=["GU"], w=[uk])
                    S.op("act", lambda e: e.activation(out=sg[:, 0:cw], in_=y[:, 0:cw], func=AF.Sigmoid), r=[yk],
                         w=["fsg"])
                    S.op("dve", lambda e: e.tensor_mul(out=y[:, 0:cw], in0=y[:, 0:cw], in1=sg[:, 0:cw]),
                         r=[yk, "fsg"], w=[yk])
                    S.op("dve", lambda e: e.tensor_mul(out=y[:, 0:cw], in0=y[:, 0:cw], in1=u_[:, 0:cw]),
                         r=[yk, uk], w=[yk])
                    S.dma("sp", ACTS[i * 128:(i + 1) * 128, c0:c0 + cw], y[:, 0:cw], r=[yk], w=["ACTS"])

                conv_pass(ps, T, GW, GU, 0, D_FF,
                          lambda c0, cw: (I["ffn_conv_w"][l, 0, c0:c0 + cw], I["ffn_conv_w"][l, 1, c0:c0 + cw],
                                          I["ffn_conv_w"][l, 2, c0:c0 + cw], I["ffn_conv_b"][l, c0:c0 + cw]), post)
                S.barrier()
            with contextlib.ExitStack() as ps:
                lsb = lambda name, shape, dt=F32: ps.enter_context(nc.sbuf_tensor(uname(name), list(shape), dt))
                ls = {"xt": [lsb("xt0", [128, D]), lsb("xt1", [128, D])], "junk": lsb("junk", [128, D]),
                      "ss": lsb("ss", [128, 1])}
                wbs = [lsb("wb0", [128, 5632], BF16), lsb("wb1", [128, 5632], BF16)]
                WSTG["t"] = lsb("wst", [128, 5632])
                hTf = lsb("hTf", [128, 44, BLK], BF16)
                Ybuf = lsb("Ybuf", [128, NB, D])
                GPt = lsb("GPt", [128, D])
                at = [lsb("at%d" % j, [128, 2816]) for j in range(2)]
                bcload(GPt[:], "GPt", MOD[jrow, 5 * D:6 * D])
                bcload(ls["junk"][:], "junk", I["norm_ffn_post"][l, :])
                S.op("dve", lambda e: e.tensor_mul(out=GPt[:], in0=GPt[:], in1=ls["junk"][:]), r=["GPt", "junk"],
                     w=["GPt"])

                def epi_y(i, c0, cw, pt, pk):
                    S.op("act", lambda e: e.activation(out=Ybuf[:, i, c0:c0 + cw], in_=pt[:, 0:cw], func=AF.Copy),
                         r=[pk], w=["Ybuf"])

                ac = 0
                for bi in range(T // BLK):
                    t0 = bi * BLK
                    for i in range(NB):
                        for hf in range(2):
                            a_ = at[ac % 2]
                            ak = "at%d" % (ac % 2)
                            ac += 1
                            S.dma("sp", a_[:], ACTS[t0 + i * 128:t0 + (i + 1) * 128, hf * 2816:(hf + 1) * 2816],
                                  r=["ACTS"], w=[ak])
                            transposes_to(a_[:], ak, 22,
                                          lambda q, nj, i=i, hf=hf: hTf[:, hf * 22 + q:hf * 22 + q + nj,
                                                                        i * 128:(i + 1) * 128], "hTf")
                    proj(wbs, hTf, "hTf", 44, D, NB, epi_y, wload_plain(I["w_ffn_out"][l]), cwmax=128)
                    postnorm_residual(ls, Ybuf, NB, t0, xsrc, xdst, GPt)
                S.barrier()

        kb.fns = dict(phase_mod=phase_mod, phase_inproj=phase_inproj, phase_convs=phase_convs, phase_ssd=phase_ssd)

        groups = debug.get("groups", ["s", "p"])
        skip = debug.get("skip", ())
        nl = debug.get("layers", DEPTH)
        for l in range(nl):
            phase_mod(l)
            for gname in groups:
                last = (l == DEPTH - 1)
                if gname == "s":
                    T, L, GW, jrow, sample = TS, TS, 64, 0, True
                    xin = I["xs"] if l == 0 else XB
                    xmid = XA
                    xout = O["ys"] if last else XB
                else:
                    T, L, GW, jrow, sample = TP, 256, 256, 1, False
                    xin = I["xp"] if l == 0 else XPB
                    xmid = XPA
                    xout = O["yp"] if last else XPB
                phase_inproj(l, xin, T, jrow)
                phase_convs(l, T, GW)
                if "ssd" not in skip:
                    phase_ssd(l, T, L, sample)
                if "wkv" not in skip:
                    wp = debug.get("wkv_parts", ("prep", "scan", "post"))
                    if "prep" in wp:
                        phase_wkv_prep(l, T)
                    if "scan" in wp:
                        phase_wkv_scan(l, T, L, sample)
                    if "post" in wp:
                        phase_wkv_post(l, T)
                if "s5" not in skip:
                    phase_s5(l, T, L, sample)
                if "tail" not in skip:
                    phase_tail(l, T, L, jrow, xin, xmid)
                if "ffn" not in skip:
                    phase_ffn(l, T, GW, jrow, xmid, xout)
        S.barrier()
    return nc, S


def _consts():
    k = np.arange(128)
    tri = (k[:, None] <= k[None, :]).astype(np.float32)
    m48 = np.zeros((48, 768), np.float32)
    for j in range(4):
        for g in range(12):
            m48[j * 12 + g, g * 64:(g + 1) * 64] = 1.0
    return {
        "ident": np.eye(128, dtype=np.float32), "tri": tri, "trit": np.ascontiguousarray(tri.T),
        "jrev": np.ascontiguousarray(np.eye(128, dtype=np.float32)[::-1]), "mask48": m48,
        "jj": np.ascontiguousarray(np.broadcast_to(np.arange(1, 513, dtype=np.float32), (128, 512))),
        "zrow": np.zeros((1, 512), np.float32),
    }


def _prep_inputs(inputs):
    f = lambda a: np.ascontiguousarray(np.asarray(a, dtype=np.float32))
    shared = {nm: f(inputs[nm]).reshape(shp) for nm, shp in PARAM_SHAPES.items()}
    shared.update(_consts())
    maps = []
    for c in range(8):
        b = c % 4
        m = dict(shared)
        m["xs"] = f(inputs["x_sample"][b])
        m["xp"] = f(np.asarray(inputs["x_prompt"])[4 * b:4 * b + 4].reshape(TP, D))
        m["cond2"] = f(np.stack([np.asarray(inputs["c"])[b], np.asarray(inputs["c_ctx"])], 0))
        m["st_ssm"] = f(inputs["state_ssm"][b])
        m["st_wkv"] = f(inputs["state_wkv"][b])
        m["st_s5re"] = f(inputs["state_s5_re"][b])
        m["st_s5im"] = f(inputs["state_s5_im"][b])
        maps.append(m)
    return maps


def kernel(**inputs):
    nc, S = build()
    maps = _prep_inputs(inputs)
    res = run_bass_kernel_spmd(nc, maps, core_ids=list(range(8)))
    r = res.results
    y_prompt = np.zeros((16, 256, D), np.float32)
    y_sample = np.zeros((4, TS, D), np.float32)
    ns_ssm = np.zeros((16, DEPTH, 2, 24, 64, 128), np.float32)
    ns_wkv = np.zeros((16, DEPTH, 2, 24, 64, 64), np.float32)
    ns_re = np.zeros((16, DEPTH, 2, 64, 64), np.float32)
    ns_im = np.zeros((16, DEPTH, 2, 64, 64), np.float32)
    for b in range(4):
        o = r[b]
        y_sample[b] = np.asarray(o["ys"])
        y_prompt[4 * b:4 * b + 4] = np.asarray(o["yp"]).reshape(4, 256, D)
        ns_ssm[4 * b:4 * b + 4] = np.asarray(o["ns_ssm"])
        ns_wkv[4 * b:4 * b + 4] = np.asarray(o["ns_wkv"])
        ns_re[4 * b:4 * b + 4] = np.asarray(o["ns_s5re"])
        ns_im[4 * b:4 * b + 4] = np.asarray(o["ns_s5im"])
    return (y_prompt, y_sample, ns_ssm, ns_wkv, ns_re, ns_im)
```

```python
import contextlib
import math
import numpy as np
import concourse.bass as bass
import concourse.mybir as mybir
import concourse.ap as apm
from concourse.bass_utils import run_bass_kernel_spmd

F32 = mybir.dt.float32
BF16 = mybir.dt.bfloat16
I32 = mybir.dt.int32
AF = mybir.ActivationFunctionType
ALU = mybir.AluOpType
AX = mybir.AxisListType

D = 2048
KC = 16
DEPTH = 2
N_IN = 16560
D_FF = 5632
TS = 4096
NSP = 2
TP = NSP * 256
BLK = 512
BLK_IN = 2048
EPS = 1e-6
TWO_PI = 2.0 * math.pi
SIN_SCALE = 6.28318

C_Z = 0
C_XBC = 1536
C_DTF = 4096
C_WKV = 4144
C_U = 9392
C_GATE = 10416
NWKV = 5248


class Sched:
    def __init__(self, nc, es, n_dma_sems=24):
        self.nc = nc
        self.eng = {"pe": nc.tensor, "act": nc.scalar, "dve": nc.vector, "pool": nc.gpsimd, "sp": nc.sync}
        self.sem = {e: es.enter_context(nc.semaphore("sem_" + e)) for e in ("pe", "act", "dve", "pool")}
        self.cnt = {e: 0 for e in self.sem}
        self.dsem = [es.enter_context(nc.semaphore("dsem%d" % i)) for i in range(n_dma_sems)]
        self.dval = [0] * n_dma_sems
        self.dnext = 0
        self.waited = {e: {} for e in self.eng}
        self.lastw = {}
        self.readers = {}
        self.ninst = 0

    def _wait(self, e, tok, force=False):
        sem, val, src = tok
        w = self.waited[e]
        if w.get(id(sem), 0) >= val:
            return
        if src == e == "pe" and not force:
            return
        self.eng[e].wait_ge(sem, val)
        w[id(sem)] = val

    def _deps(self, e, r, w):
        for k in list(r) + list(w):
            t = self.lastw.get(k)
            if t is not None:
                self._wait(e, t)
        for k in w:
            for t in self.readers.get(k, ()):
                self._wait(e, t)

    def _commit(self, tok, r, w):
        for k in w:
            self.lastw[k] = tok
            self.readers[k] = []
        for k in r:
            lst = self.readers.setdefault(k, [])
            lst.append(tok)
            if len(lst) > 64:
                best = {}
                for t in lst:
                    if id(t[0]) not in best or best[id(t[0])][1] < t[1]:
                        best[id(t[0])] = t
                self.readers[k] = list(best.values())

    def op(self, e, fn, r=(), w=()):
        self._deps(e, r, w)
        ins = fn(self.eng[e])
        self.cnt[e] += 1
        ins.then_inc(self.sem[e], 1)
        tok = (self.sem[e], self.cnt[e], e)
        self._commit(tok, r, w)
        self.ninst += 1
        return tok

    def dma(self, q, out, in_, r=(), w=(), **kw):
        i = self.dnext
        self.dnext = (self.dnext + 1) % len(self.dsem)
        sem = self.dsem[i]
        if self.dval[i] > 0:
            self._wait(q, (sem, self.dval[i], "dma"))
        self._deps(q, r, w)
        ins = self.eng[q].dma_start(out=out, in_=in_, **kw)
        self.dval[i] += 16
        ins.then_inc(sem, 16)
        tok = (sem, self.dval[i], "dma")
        self._commit(tok, r, w)
        self.ninst += 1
        return tok

    def barrier(self):
        toks = [(self.sem[e], self.cnt[e], e) for e in self.sem if self.cnt[e] > 0]
        toks += [(self.dsem[i], self.dval[i], "dma") for i in range(len(self.dsem)) if self.dval[i] > 0]
        for e in self.eng:
            for t in toks:
                self._wait(e, t, force=True)
        self.lastw.clear()
        self.readers.clear()


class KB:
    def __init__(self, debug=None):
        self.debug = debug or {}
        self.nc = bass.Bass("TRN2", target_bir_lowering=False)
        self.I = {}
        self.O = {}

    def din(self, name, shape, dt=F32):
        self.I[name] = self.nc.dram_tensor(name, list(shape), dt, kind="ExternalInput").ap()
        return self.I[name]

    def dout(self, name, shape, dt=F32):
        self.O[name] = self.nc.dram_tensor(name, list(shape), dt, kind="ExternalOutput").ap()
        return self.O[name]

    def dscr(self, name, shape, dt=F32):
        kind = "ExternalOutput" if name in self.debug.get("dump", ()) else "Internal"
        return self.nc.dram_tensor(name, list(shape), dt, kind=kind).ap()


PARAM_SHAPES = {
    "w_mod": (DEPTH, D, 6 * D), "b_mod": (DEPTH, 6 * D),
    "norm_mix_pre": (DEPTH, D), "norm_mix_post": (DEPTH, D), "norm_ffn_pre": (DEPTH, D), "norm_ffn_post": (DEPTH, D),
    "w_in": (DEPTH, D, N_IN),
    "ssm_conv_w": (DEPTH, 3, 2560), "ssm_conv_b": (DEPTH, 2560), "ssm_dt_bias": (DEPTH, 48),
    "ssm_a_log": (DEPTH, 48), "ssm_d": (DEPTH, 24), "ssm_norm": (DEPTH, 1536), "w_ssm_out": (DEPTH, 1536, D),
    "wkv_mu_prev": (DEPTH, NWKV), "wkv_mu_next": (DEPTH, NWKV), "wkv_w0": (DEPTH, 2, 1536),
    "wkv_w_up": (DEPTH, 2, 96, 1536), "wkv_a0": (DEPTH, 2, 1536), "wkv_a_up": (DEPTH, 2, 96, 1536),
    "wkv_g_up": (DEPTH, 256, 1536), "wkv_k_k": (DEPTH, 1536), "wkv_k_a": (DEPTH, 1536), "wkv_r_k": (DEPTH, 1536),
    "wkv_ln_w": (DEPTH, 1536), "wkv_ln_b": (DEPTH, 1536), "w_wkv_out": (DEPTH, 1536, D),
    "s5_lam_re": (DEPTH, 2, 64, 64), "s5_lam_im": (DEPTH, 2, 64, 64), "s5_log_dt": (DEPTH, 2, 64),
    "s5_b_re": (DEPTH, 2, 64, 64, 16), "s5_b_im": (DEPTH, 2, 64, 64, 16),
    "s5_c_re": (DEPTH, 2, 64, 16, 64), "s5_c_im": (DEPTH, 2, 64, 16, 64), "s5_d": (DEPTH, 1024),
    "w_s5_glu": (DEPTH, 1024, 2 * D), "w_out": (DEPTH, D, D), "w_ffn_in": (DEPTH, D, 2 * D_FF),
    "ffn_conv_w": (DEPTH, 3, D_FF), "ffn_conv_b": (DEPTH, D_FF), "w_ffn_out": (DEPTH, D_FF, D),
}


RELAID = {"w_mod": (D, 6 * D, 512), "w_in": (D, N_IN, 512), "w_ssm_out": (1536, D, 512), "w_wkv_out": (1536, D, 512),
          "w_out": (D, D, 512), "w_ffn_in": (D, 2 * D_FF, 512), "w_ffn_out": (D_FF, D, 128)}


def _relayout(W, cb):
    K_, N_ = W.shape
    kch = K_ // 128
    out = np.empty(K_ * N_, np.float32)
    off = 0
    for c0 in range(0, N_, cb):
        cw = min(cb, N_ - c0)
        t = W[:, c0:c0 + cw].reshape(kch, 128, cw).transpose(1, 0, 2)
        out[off:off + K_ * cw] = t.reshape(-1)
        off += K_ * cw
    return out


def _relayout_glu(W):
    K_ = W.shape[0]
    out = np.empty(W.size, np.float32)
    off = 0
    for c0 in range(0, D, 256):
        t = np.concatenate([W[:, c0:c0 + 256], W[:, D + c0:D + c0 + 256]], axis=1)
        t = t.reshape(K_ // 128, 128, 512).transpose(1, 0, 2)
        out[off:off + K_ * 512] = t.reshape(-1)
        off += K_ * 512
    return out


def build(debug=None):
    debug = debug or {}
    kb = KB(debug)
    nc = kb.nc
    I = kb.I
    O = kb.O
    es = contextlib.ExitStack()

    kb.din("xs", [TS, D])
    kb.din("xp", [TP, D])
    kb.din("cond2", [2, D])
    kb.din("st_ssm", [DEPTH, 2, 24, 64, 128])
    kb.din("st_wkv", [DEPTH, 2, 24, 64, 64])
    kb.din("st_s5re", [DEPTH, 2, 64, 64])
    kb.din("st_s5im", [DEPTH, 2, 64, 64])
    for nm, shp in PARAM_SHAPES.items():
        if nm in RELAID or nm == "w_s5_glu":
            kb.din(nm, [DEPTH, int(np.prod(shp[1:]))])
        else:
            kb.din(nm, shp)
    kb.din("ident", [128, 128])
    kb.din("tri", [128, 128])
    kb.din("trit", [128, 128])
    kb.din("jrev", [128, 128])
    kb.din("mask48", [48, 768])
    kb.din("jj", [128, 512])
    kb.din("zrow", [1, 512])

    kb.dout("ys", [TS, D])
    kb.dout("yp", [TP, D])
    kb.dout("ns_ssm", [NSP, DEPTH, 2, 24, 64, 128])
    kb.dout("ns_wkv", [NSP, DEPTH, 2, 24, 64, 64])
    kb.dout("ns_s5re", [NSP, DEPTH, 2, 64, 64])
    kb.dout("ns_s5im", [NSP, DEPTH, 2, 64, 64])

    MOD = kb.dscr("MOD", [2, 6 * D])
    PA = kb.dscr("PA", [TS, C_U])
    PB = kb.dscr("PB", [TS, N_IN - C_U])

    class _P:
        def __getitem__(self, key):
            rs, cs = key
            c0, c1 = cs.start, cs.stop
            if c1 <= C_U:
                return PA[rs, c0:c1]
            assert c0 >= C_U, (c0, c1)
            return PB[rs, c0 - C_U:c1 - C_U]
    P = _P()
    GU = kb.dscr("GU", [TS, 2 * D_FF])
    XA = kb.dscr("XA", [TS, D])
    XB = kb.dscr("XB", [TS, D])
    XPA = kb.dscr("XPA", [TP, D])
    XPB = kb.dscr("XPB", [TP, D])
    MG = kb.dscr("MG", [TS, D])
    XC = kb.dscr("XC", [TS, 2560])
    SH = kb.dscr("SH", [TS, NWKV])
    BCT = kb.dscr("BCT", [TS // 128, 128, 8, 128], BF16)
    DT = kb.dscr("DT", [TS, 48])
    YS = kb.dscr("YS", [TS, 1536])
    ZW = kb.dscr("ZW", [TS, 1536])
    BD = [kb.dscr("BD%d" % d, [TS, 1536], BF16) for d in range(2)]
    KD = [kb.dscr("KD%d" % d, [TS, 1536], BF16) for d in range(2)]
    BRKR = [kb.dscr("BRKR%d" % d, [TS, 48]) for d in range(2)]
    WT = [kb.dscr("WT%d" % d, [128, 12, TS]) for d in range(2)]
    WRT = [kb.dscr("WRT%d" % d, [128, 12, TS]) for d in range(2)]
    NKKT = kb.dscr("NKKT", [128, 12, TS])
    GG = kb.dscr("GG", [TS, 1536])
    VBD = [kb.dscr("VBD%d" % d, [TS, 12, 768], BF16) for d in range(2)]
    RK = kb.dscr("RK", [TS, 24])
    SAY = [kb.dscr("SAY%d" % d, [48, TS, 64]) for d in range(2)]
    UT2 = [kb.dscr("UT2_%d" % d, [32, 32, TS]) for d in range(2)]
    YS5 = [kb.dscr("YS5_%d" % d, [TS, 1024]) for d in range(2)]
    ACTS = kb.dscr("ACTS", [TS, D_FF])

    with es:
        S = Sched(nc, es)
        kb.S = S
        gsb = lambda name, shape, dt=F32: es.enter_context(nc.sbuf_tensor(name, list(shape), dt))
        psb = [es.enter_context(nc.psum_tensor("psb%d" % i, [128, 512], F32)) for i in range(8)]
        PK = ["psb%d" % i for i in range(8)]
        ident = gsb("ident_sb", [128, 128])
        tri = gsb("tri_sb", [128, 128])
        trit = gsb("trit_sb", [128, 128])
        jrev = gsb("jrev_sb", [128, 128])
        ones = gsb("ones_sb", [128, 128])
        S.dma("sp", ident[:], I["ident"][:, :], w=["ident"])
        S.dma("sp", tri[:], I["tri"][:, :], w=["tri"])
        S.dma("sp", trit[:], I["trit"][:, :], w=["trit"])
        S.dma("sp", jrev[:], I["jrev"][:, :], w=["jrev"])
        S.op("dve", lambda e: e.memset(ones[:], 1.0), w=["ones"])

        _uid = [0]

        def uname(n):
            _uid[0] += 1
            return "%s_%d" % (n, _uid[0])

        def bcload(dst, key, src1d, q="sp"):
            S.dma(q, dst, src1d.partition_broadcast(dst.shape[0]), r=["MOD"], w=[key])

        def transposes_to(src, skey, nblk, dst_fn, dkey, bw=128, pbanks=(6, 7), eng="act", scale=None, rhs=None,
                          rkey=None, inw=128):
            per = 512 // inw
            for q in range(0, nblk, per):
                nj = min(per, nblk - q)
                bi = pbanks[(q // per) % len(pbanks)]
                pt = psb[bi]
                for j in range(nj):
                    blk = src[:, (q + j) * bw:(q + j + 1) * bw]
                    if rhs is None:
                        S.op("pe", lambda e, j=j, blk=blk: e.transpose(pt[0:bw, j * inw:(j + 1) * inw], blk,
                                                                        ident[0:inw, 0:inw]),
                             r=[skey, "ident"], w=[PK[bi]])
                    else:
                        S.op("pe", lambda e, j=j, blk=blk: e.matmul(pt[0:bw, j * inw:(j + 1) * inw], lhsT=blk,
                                                                     rhs=rhs, start=True, stop=True),
                             r=[skey, rkey], w=[PK[bi]])
                src_ps = pt[0:bw, 0:nj * inw].rearrange("p (j t) -> p j t", j=nj)
                dst = dst_fn(q, nj)
                if eng == "act":
                    if scale is None:
                        S.op("act", lambda e: e.activation(out=dst, in_=src_ps, func=AF.Copy), r=[PK[bi]], w=[dkey])
                    else:
                        S.op("act", lambda e: e.activation(out=dst, in_=src_ps, func=AF.Copy, scale=scale),
                             r=[PK[bi]], w=[dkey])
                else:
                    S.op("dve", lambda e: e.tensor_copy(out=dst, in_=src_ps), r=[PK[bi]], w=[dkey])

        def phase_mod(l):
            with contextlib.ExitStack() as ps:
                lsb = lambda name, shape, dt=F32: ps.enter_context(nc.sbuf_tensor(uname(name), list(shape), dt))
                cT = lsb("cT", [128, 2, KC])
                sg = lsb("sg", [128, 2, KC])
                wm = [lsb("wm%d" % i, [128, KC, 512]) for i in range(2)]
                bm = lsb("bm", [1, 6 * D])
                mo = lsb("mo", [2, 6 * D])
                with nc.allow_non_contiguous_dma(reason="tiny cond transpose"):
                    S.dma("sp", cT[:], I["cond2"].rearrange("j (k p) -> p j k", p=128), w=["cT"])
                S.dma("sp", bm[:], I["b_mod"][l:l + 1, :], w=["bm"])
                S.op("act", lambda e: e.activation(out=sg[:], in_=cT[:], func=AF.Sigmoid), r=["cT"], w=["sg"])
                S.op("dve", lambda e: e.tensor_mul(out=sg[:], in0=sg[:], in1=cT[:]), r=["cT", "sg"], w=["sg"])
                for nb in range(24):
                    wb = wm[nb % 2]
                    wk = "wm%d" % (nb % 2)
                    S.dma("sp", wb[:], I["w_mod"][l, nb * 128 * KC * 512:(nb + 1) * 128 * KC * 512].rearrange(
                        "(p k n) -> p k n", p=128, k=KC),
                          w=[wk])
                    pt = psb[nb % 2]
                    pk = PK[nb % 2]
                    for k in range(KC):
                        S.op("pe", lambda e, k=k: e.matmul(pt[0:2, :], lhsT=sg[:, :, k], rhs=wb[:, k, :],
                                                           start=(k == 0), stop=False), r=["sg", wk], w=[pk])
                    S.op("pe", lambda e: e.matmul(pt[0:2, :], lhsT=ones[0:1, 0:2], rhs=bm[:, nb * 512:(nb + 1) * 512],
                                                  start=False, stop=True), r=["ones", "bm"], w=[pk])
                    S.op("act", lambda e: e.activation(out=mo[:, nb * 512:(nb + 1) * 512], in_=pt[0:2, :],
                                                       func=AF.Copy), r=[pk], w=["mo"])
                S.dma("sp", MOD[:, :], mo[:], r=["mo"], w=["MOD"])
                S.barrier()

        def rms_rstd(ss, key, n, eps):
            S.op("dve", lambda e: e.tensor_scalar(out=ss, in0=ss, scalar1=1.0 / n, scalar2=eps, op0=ALU.mult,
                                                  op1=ALU.add), r=[key], w=[key])
            S.op("act", lambda e: e.activation(out=ss, in_=ss, func=AF.Sqrt), r=[key], w=[key])
            S.op("dve", lambda e: e.reciprocal(out=ss, in_=ss), r=[key], w=[key])

        def norm_to_fm(ls, xsrc, t0, ntile, hT, Gt, SHt):
            for i in range(ntile):
                xt = ls["xt"][i % 2]
                xk = "xt%d" % (i % 2)
                S.dma("sp", xt[:], xsrc[t0 + i * 128:t0 + (i + 1) * 128, :], r=["XSRC"], w=[xk])
                S.op("act", lambda e: e.activation(out=ls["junk"][:], in_=xt[:], func=AF.Square,
                                                   accum_out=ls["ss"][:]), r=[xk], w=["junk", "ss"])
                rms_rstd(ls["ss"][:], "ss", D, EPS)
                S.op("dve", lambda e: e.scalar_tensor_tensor(out=ls["junk"][:], in0=xt[:], scalar=ls["ss"][:, 0:1],
                                                             in1=Gt[:], op0=ALU.mult, op1=ALU.mult),
                     r=[xk, "ss", "Gt"], w=["junk"])
                S.op("dve", lambda e: e.tensor_add(out=ls["junk"][:], in0=ls["junk"][:], in1=SHt[:]),
                     r=["junk", "SHt"], w=["junk"])
                transposes_to(ls["junk"][:], "junk", KC,
                              lambda q, nj, i=i: hT[:, q:q + nj, i * 128:(i + 1) * 128], "hT")

        def proj(wbs, hT, hkey, kchunks, ncols, ntile, epilogue, wload, cwmax=512, wmul=1):
            nb = (ncols + cwmax - 1) // cwmax

            def blk(b):
                c0 = b * cwmax
                cw = min(cwmax, ncols - c0)
                nw = cw * wmul
                wb = wbs[b % 2][:, 0:kchunks * nw].rearrange("p (k n) -> p k n", k=kchunks)
                return c0, cw, nw, wb, "wb%d" % (b % 2)

            c0, cw, nw, wb, wk = blk(0)
            wload(wb, wk, c0, cw)
            for b in range(nb):
                c0, cw, nw, wb, wk = blk(b)
                if b + 1 < nb:
                    c0n, cwn, nwn, wbn, wkn = blk(b + 1)
                    wload(wbn, wkn, c0n, cwn)
                for i in range(ntile):
                    pt = psb[i % 4]
                    pk = PK[i % 4]
                    for k in range(kchunks):
                        S.op("pe", lambda e, k=k: e.matmul(pt[:, 0:nw], lhsT=hT[:, k, i * 128:(i + 1) * 128],
                                                           rhs=wb[:, k, :], start=(k == 0),
                                                           stop=(k == kchunks - 1)), r=[hkey, wk], w=[pk])
                    epilogue(i, c0, cw, pt, pk)

        WSTG = {}

        def wload_plain(Wflat, cb=512):
            def f(wb, wk, c0, cw):
                wst = WSTG["t"]
                kch = wb.shape[1]
                n = 128 * kch * cw
                off = 128 * kch * c0
                stv = wst[:, 0:kch * cw].rearrange("p (k n) -> p k n", k=kch)
                S.dma("sp", stv, Wflat[off:off + n].rearrange("(p k n) -> p k n", p=128, k=kch), w=["wst"])
                S.op("dve", lambda e: e.tensor_copy(out=wb, in_=stv), r=["wst"], w=[wk])
            return f

        def load3(src, sc, cw, t0, GW, bufs, keys):
            prv, cur, nxt = bufs
            kp, kc_, kn = keys
            S.dma("sp", cur[:, 0:cw], src[t0:t0 + 128, sc:sc + cw], r=["CSRC"], w=[kc_])
            a = 0
            while a < 128:
                g0 = t0 + a
                b = min(128, a + GW - (g0 % GW))
                sb_ = (g0 % GW) == 0
                eb_ = ((t0 + b) % GW) == 0
                if sb_:
                    S.dma("sp", prv[a:a + 1, 0:cw], I["zrow"][0:1, 0:cw], w=[kp])
                    if b - a > 1:
                        S.dma("sp", prv[a + 1:b, 0:cw], src[t0 + a:t0 + b - 1, sc:sc + cw], r=["CSRC"], w=[kp])
                else:
                    S.dma("sp", prv[a:b, 0:cw], src[t0 + a - 1:t0 + b - 1, sc:sc + cw], r=["CSRC"], w=[kp])
                if eb_:
                    S.dma("sp", nxt[b - 1:b, 0:cw], I["zrow"][0:1, 0:cw], w=[kn])
                    if b - a > 1:
                        S.dma("sp", nxt[a:b - 1, 0:cw], src[t0 + a + 1:t0 + b, sc:sc + cw], r=["CSRC"], w=[kn])
                else:
                    S.dma("sp", nxt[a:b, 0:cw], src[t0 + a + 1:t0 + b + 1, sc:sc + cw], r=["CSRC"], w=[kn])
                a = b

        def conv_pass(ps, T, GW, src, sc0, ncols, wrows, post, prep=None):
            lsb = lambda name, shape, dt=F32: ps.enter_context(nc.sbuf_tensor(uname(name), list(shape), dt))
            wt = [lsb("cvw%d" % j, [128, 512]) for j in range(4)]
            bufs = [[lsb("cv%s%d" % (n, j), [128, 512]) for n in ("p", "c", "n")] for j in range(2)]
            yb = [lsb("cvy%d" % j, [128, 512]) for j in range(2)]
            tb = lsb("cvt", [128, 512])
            for c0 in range(0, ncols, 512):
                cw = min(512, ncols - c0)
                rows = wrows(c0, cw)
                for j in range(4):
                    if rows[j] is not None:
                        bcload(wt[j][:, 0:cw], "cvw%d" % j, rows[j])
                if prep is not None:
                    prep(wt, cw)
                for i in range(T // 128):
                    bb = bufs[i % 2]
                    keys = ["cv%s%d" % (n, i % 2) for n in ("p", "c", "n")]
                    load3(src, sc0 + c0, cw, i * 128, GW, bb, keys)
                    y = yb[i % 2]
                    yk = "cvy%d" % (i % 2)
                    S.op("dve", lambda e: e.tensor_mul(out=y[:, 0:cw], in0=bb[1][:, 0:cw], in1=wt[1][:, 0:cw]),
                         r=[keys[1], "cvw1"], w=[yk])
                    S.op("dve", lambda e: e.tensor_mul(out=tb[:, 0:cw], in0=bb[0][:, 0:cw], in1=wt[0][:, 0:cw]),
                         r=[keys[0], "cvw0"], w=["cvt"])
                    S.op("dve", lambda e: e.tensor_add(out=y[:, 0:cw], in0=y[:, 0:cw], in1=tb[:, 0:cw]),
                         r=[yk, "cvt"], w=[yk])
                    S.op("dve", lambda e: e.tensor_mul(out=tb[:, 0:cw], in0=bb[2][:, 0:cw], in1=wt[2][:, 0:cw]),
                         r=[keys[2], "cvw2"], w=["cvt"])
                    S.op("dve", lambda e: e.tensor_add(out=y[:, 0:cw], in0=y[:, 0:cw], in1=tb[:, 0:cw]),
                         r=[yk, "cvt"], w=[yk])
                    if rows[3] is not None:
                        S.op("dve", lambda e: e.tensor_add(out=y[:, 0:cw], in0=y[:, 0:cw], in1=wt[3][:, 0:cw]),
                             r=[yk, "cvw3"], w=[yk])
                    post(i, c0, cw, y, yk)

        def postnorm_residual(ls, Ybuf, ntile, t0, xsrc, xdst, GPt):
            for i in range(ntile):
                y = Ybuf[:, i, :]
                S.op("act", lambda e: e.activation(out=ls["junk"][:], in_=y, func=AF.Square, accum_out=ls["ss"][:]),
                     r=["Ybuf"], w=["junk", "ss"])
                rms_rstd(ls["ss"][:], "ss", D, EPS)
                xt = ls["xt"][i % 2]
                xk = "xt%d" % (i % 2)
                S.dma("sp", xt[:], xsrc[t0 + i * 128:t0 + (i + 1) * 128, :], r=["XSRC"], w=[xk])
                S.op("dve", lambda e: e.scalar_tensor_tensor(out=ls["junk"][:], in0=y, scalar=ls["ss"][:, 0:1],
                                                             in1=GPt[:], op0=ALU.mult, op1=ALU.mult),
                     r=["Ybuf", "ss", "GPt"], w=["junk"])
                S.op("dve", lambda e: e.tensor_add(out=xt[:], in0=xt[:], in1=ls["junk"][:]), r=["junk", xk], w=[xk])
                S.dma("sp", xdst[t0 + i * 128:t0 + (i + 1) * 128, :], xt[:], r=[xk], w=["XDST"])

        def mod_tiles(lsb, ls, l, jrow, i_sc, i_sh, i_g, pre, post):
            Gt = lsb("Gt", [128, D])
            SHt = lsb("SHt", [128, D])
            bcload(Gt[:], "Gt", MOD[jrow, i_sc * D:(i_sc + 1) * D])
            bcload(ls["junk"][:], "junk", I[pre][l, :])
            S.op("dve", lambda e: e.scalar_tensor_tensor(out=Gt[:], in0=Gt[:], scalar=1.0, in1=ls["junk"][:],
                                                         op0=ALU.add, op1=ALU.mult), r=["Gt", "junk"], w=["Gt"])
            bcload(SHt[:], "SHt", MOD[jrow, i_sh * D:(i_sh + 1) * D])
            return Gt, SHt, None

        def phase_inproj(l, xsrc, T, jrow):
            with contextlib.ExitStack() as ps:
                lsb = lambda name, shape, dt=F32: ps.enter_context(nc.sbuf_tensor(uname(name), list(shape), dt))
                ls = {"xt": [lsb("xt0", [128, D]), lsb("xt1", [128, D])], "junk": lsb("junk", [128, D]),
                      "ss": lsb("ss", [128, 1])}
                wbs = [lsb("wb0", [128, 8192], BF16), lsb("wb1", [128, 8192], BF16)]
                WSTG["t"] = lsb("wst", [128, 8192])
                BI = min(BLK_IN, T)
                st = [lsb("st0", [128, 512]), lsb("st1", [128, 512])]
                hT = lsb("hT", [128, KC, BI], BF16)
                Gt, SHt, GPt = mod_tiles(lsb, ls, l, jrow, 1, 0, 2, "norm_mix_pre", "norm_mix_post")
                cnt = [0]
                for bi in range(T // BI):
                    t0 = bi * BI
                    norm_to_fm(ls, xsrc, t0, BI // 128, hT, Gt, SHt)

                    def epi(i, c0, cw, pt, pk, t0=t0):
                        cnt[0] += 1
                        s_ = st[cnt[0] % 2]
                        sk = "st%d" % (cnt[0] % 2)
                        S.op("act", lambda e: e.activation(out=s_[:, 0:cw], in_=pt[:, 0:cw], func=AF.Copy),
                             r=[pk], w=[sk])
                        rs_ = slice(t0 + i * 128, t0 + (i + 1) * 128)
                        if c0 < C_U < c0 + cw:
                            m_ = C_U - c0
                            S.dma("sp", P[rs_, c0:C_U], s_[:, 0:m_], r=[sk], w=["P"])
                            S.dma("sp", P[rs_, C_U:c0 + cw], s_[:, m_:cw], r=[sk], w=["P"])
                        else:
                            S.dma("sp", P[rs_, c0:c0 + cw], s_[:, 0:cw], r=[sk], w=["P"])

                    proj(wbs, hT, "hT", KC, debug.get("ncols", N_IN), BI // 128, epi, wload_plain(I["w_in"][l]))
                S.barrier()

        def phase_convs(l, T, GW):
            with contextlib.ExitStack() as ps:
                def post_ssm(i, c0, cw, y, yk):
                    lsg = post_ssm.sg
                    S.op("act", lambda e: e.activation(out=lsg[:, 0:cw], in_=y[:, 0:cw], func=AF.Sigmoid),
                         r=[yk], w=["cvsg"])
                    S.op("dve", lambda e: e.tensor_mul(out=y[:, 0:cw], in0=y[:, 0:cw], in1=lsg[:, 0:cw]),
                         r=[yk, "cvsg"], w=[yk])
                    S.dma("sp", XC[i * 128:(i + 1) * 128, c0:c0 + cw], y[:, 0:cw], r=[yk], w=["XC"])
                post_ssm.sg = ps.enter_context(nc.sbuf_tensor(uname("cvsg"), [128, 512], F32))
                conv_pass(ps, T, GW, P, C_XBC, 2560,
                          lambda c0, cw: (I["ssm_conv_w"][l, 0, c0:c0 + cw], I["ssm_conv_w"][l, 1, c0:c0 + cw],
                                          I["ssm_conv_w"][l, 2, c0:c0 + cw], I["ssm_conv_b"][l, c0:c0 + cw]),
                          post_ssm)
                S.barrier()
            with contextlib.ExitStack() as ps:
                def post_wkv(i, c0, cw, y, yk):
                    S.dma("sp", SH[i * 128:(i + 1) * 128, c0:c0 + cw], y[:, 0:cw], r=[yk], w=["SH"])

                def prep(wt, cw):
                    S.op("dve", lambda e: e.tensor_add(out=wt[1][:, 0:cw], in0=wt[0][:, 0:cw], in1=wt[2][:, 0:cw]),
                         r=["cvw0", "cvw2"], w=["cvw1"])
                    S.op("dve", lambda e: e.tensor_scalar(out=wt[1][:, 0:cw], in0=wt[1][:, 0:cw], scalar1=-1.0,
                                                          scalar2=1.0, op0=ALU.mult, op1=ALU.add),
                         r=["cvw1"], w=["cvw1"])
                conv_pass(ps, T, GW, P, C_WKV, NWKV,
                          lambda c0, cw: (I["wkv_mu_prev"][l, c0:c0 + cw], None, I["wkv_mu_next"][l, c0:c0 + cw],
                                          None), post_wkv, prep=prep)
                S.barrier()

        def mm_cols(ps3, c0, c1, lhsT, rhs_fn, rkeys):
            c = c0
            while c < c1:
                bnk = c // 512
                ce = min(c1, (bnk + 1) * 512)
                S.op("pe", lambda e, c=c, ce=ce, bnk=bnk: e.matmul(psb[ps3[bnk]][:, c - bnk * 512:ce - bnk * 512],
                                                                   lhsT=lhsT, rhs=rhs_fn(c, ce), start=True,
                                                                   stop=True), r=rkeys, w=[PK[ps3[bnk]]])
                c = ce

        def phase_ssd(l, T, L, sample):
            NT = T // 128
            with contextlib.ExitStack() as ps:
                lsb = lambda name, shape, dt=F32: ps.enter_context(nc.sbuf_tensor(uname(name), list(shape), dt))
                bcv = [lsb("sbc%d" % j, [128, 1024]) for j in range(2)]
                bct = [lsb("sbct%d" % j, [128, 8, 128], BF16) for j in range(2)]
                dtt = [lsb("sdt%d" % j, [128, 48]) for j in range(2)]
                dbias = lsb("dbias", [128, 48])
                bcload(dbias[:], "dbias", I["ssm_dt_bias"][l, :])
                for i in range(NT):
                    b_ = bcv[i % 2]
                    bk = "sbc%d" % (i % 2)
                    S.dma("sp", b_[:], XC[i * 128:(i + 1) * 128, 1536:2560], r=["XC"], w=[bk])
                    o_ = bct[i % 2]
                    ok = "sbct%d" % (i % 2)
                    transposes_to(b_[:], bk, 8, lambda q, nj: o_[:, q:q + nj, :], ok)
                    S.dma("sp", BCT[i], o_[:], r=[ok], w=["BCT"])
                    d_ = dtt[i % 2]
                    dk = "sdt%d" % (i % 2)
                    S.dma("sp", d_[:], P[i * 128:(i + 1) * 128, C_DTF:C_DTF + 48], r=["P"], w=[dk])
                    S.op("dve", lambda e: e.tensor_add(out=d_[:], in0=d_[:], in1=dbias[:]), r=[dk, "dbias"], w=[dk])
                    S.op("act", lambda e: e.activation(out=d_[:], in_=d_[:], func=AF.Exp), r=[dk], w=[dk])
                    S.op("act", lambda e: e.activation(out=d_[:], in_=d_[:], func=AF.Ln, bias=1.0), r=[dk], w=[dk])
                    S.dma("sp", DT[i * 128:(i + 1) * 128, :], d_[:], r=[dk], w=["DT"])
                S.barrier()
            with contextlib.ExitStack() as ps:
                lsb = lambda name, shape, dt=F32: ps.enter_context(nc.sbuf_tensor(uname(name), list(shape), dt))
                xc = [lsb("xc%d" % j, [128, 2560]) for j in range(2)]
                bct = [lsb("bct%d" % j, [128, 8, 128], BF16) for j in range(2)]
                dtt = [lsb("dt%d" % j, [128, 24]) for j in range(2)]
                Abc = lsb("Abc", [128, 24])
                Dbc = lsb("Dbc", [128, 24])
                nrm = lsb("nrm", [128, 1536])
                dtA = lsb("dtA", [128, 24])
                acol = lsb("acol", [128, 24])
                tot = lsb("tot", [128, 24])
                cd = lsb("cd", [128, 24])
                ea = lsb("ea", [128, 24])
                dte = lsb("dte", [128, 24])
                xdt = lsb("xdt", [128, 1536], BF16)
                xw = lsb("xw", [128, 1536], BF16)
                bB = lsb("bB", [128, 512], BF16)
                Gm = lsb("Gm", [128, 512])
                dd = [lsb("dd%d" % j, [128, 512]) for j in range(2)]
                wt = [lsb("wt%d" % j, [128, 4, 128], BF16) for j in range(2)]
                HT = lsb("HT", [128, 1536])
                HTb = lsb("HTb", [128, 1536], BF16)
                yo = lsb("yo", [128, 1536])
                yy = lsb("yy", [128, 1536])
                zz = lsb("zz", [128, 1536])
                zs = lsb("zs", [128, 1536])
                ss4 = lsb("ss4", [128, 4])
                hio = lsb("hio", [128, 12, 128])
                bcload(Dbc[:], "Dbc", I["ssm_d"][l, :])
                bcload(nrm[:], "nrm", I["ssm_norm"][l, :])
                PY = (4, 5, 6)
                v3 = lambda t: t[:].rearrange("p (h q) -> p h q", h=24)
                for d in range(2):
                    bcload(Abc[:], "Abc", I["ssm_a_log"][l, d * 24:(d + 1) * 24])
                    S.op("act", lambda e: e.activation(out=Abc[:], in_=Abc[:], func=AF.Exp), r=["Abc"], w=["Abc"])
                    S.op("dve", lambda e: e.tensor_scalar(out=Abc[:], in0=Abc[:], scalar1=-1.0, scalar2=None,
                                                          op0=ALU.mult), r=["Abc"], w=["Abc"])
                    TR = tri if d == 0 else trit
                    TRk = "tri" if d == 0 else "trit"
                    for s_ in range(T // L):
                        if sample:
                            S.dma("sp", hio[:], I["st_ssm"][l, d].rearrange("(g h2) p n -> (h2 p) g n", h2=2),
                                  w=["hio"])
                            for g in range(12):
                                bnk = PY[(g * 128) // 512]
                                S.op("pe", lambda e, g=g, bnk=bnk: e.transpose(
                                    psb[bnk][:, (g * 128) % 512:(g * 128) % 512 + 128], hio[:, g, :], ident[:]),
                                    r=["hio", "ident"], w=[PK[bnk]])
                            for j in range(3):
                                S.op("act", lambda e, j=j: e.activation(out=HT[:, j * 512:(j + 1) * 512],
                                                                        in_=psb[PY[j]][:, :], func=AF.Copy),
                                     r=[PK[PY[j]]], w=["HT"])
                        else:
                            S.op("dve", lambda e: e.memset(HT[:], 0.0), w=["HT"])
                        S.op("act", lambda e: e.activation(out=HTb[:], in_=HT[:], func=AF.Copy), r=["HT"], w=["HTb"])
                        tiles = list(range(L // 128))
                        if d == 1:
                            tiles = tiles[::-1]
                        for ti in tiles:
                            i = s_ * (L // 128) + ti
                            t0 = i * 128
                            x_ = xc[i % 2]
                            xk = "xc%d" % (i % 2)
                            b_ = bct[i % 2]
                            bk = "bct%d" % (i % 2)
                            d_ = dtt[i % 2]
                            dk = "dt%d" % (i % 2)
                            S.dma("sp", x_[:], XC[t0:t0 + 128, :], r=["XC"], w=[xk])
                            S.dma("sp", b_[:], BCT[i], r=["BCT"], w=[bk])
                            S.dma("sp", d_[:], DT[t0:t0 + 128, d * 24:(d + 1) * 24], r=["DT"], w=[dk])
                            S.op("dve", lambda e: e.tensor_mul(out=dtA[:], in0=d_[:], in1=Abc[:]),
                                 r=[dk, "Abc"], w=["dtA"])
                            S.op("pe", lambda e: e.matmul(psb[0][:, 0:24], lhsT=TR[:], rhs=dtA[:], start=True,
                                                          stop=True), r=[TRk, "dtA"], w=[PK[0]])
                            S.op("pe", lambda e: e.matmul(psb[0][:, 32:56], lhsT=ones[:], rhs=dtA[:], start=True,
                                                          stop=True), r=["ones", "dtA"], w=[PK[0]])
                            S.op("act", lambda e: e.activation(out=acol[:], in_=psb[0][:, 0:24], func=AF.Copy),
                                 r=[PK[0]], w=["acol"])
                            S.op("act", lambda e: e.activation(out=ea[:], in_=psb[0][:, 0:24], func=AF.Exp),
                                 r=[PK[0]], w=["ea"])
                            S.op("act", lambda e: e.activation(out=cd[:], in_=psb[0][:, 32:56], func=AF.Exp),
                                 r=[PK[0]], w=["cd"])
                            S.op("dve", lambda e: e.tensor_sub(out=dte[:], in0=psb[0][:, 32:56], in1=acol[:]),
                                 r=[PK[0], "acol"], w=["dte"])
                            S.op("act", lambda e: e.activation(out=dte[:], in_=dte[:], func=AF.Exp), r=["dte"],
                                 w=["dte"])
                            S.op("dve", lambda e: e.tensor_mul(out=dte[:], in0=dte[:], in1=d_[:]), r=["dte", dk],
                                 w=["dte"])
                            S.op("dve", lambda e: e.tensor_tensor(
                                out=v3(xdt), in0=x_[:, 0:1536].rearrange("p (h q) -> p h q", h=24),
                                in1=d_[:, :].unsqueeze(2).to_broadcast([128, 24, 64]), op=ALU.mult),
                                r=[xk, dk], w=["xdt"])
                            S.op("dve", lambda e: e.tensor_tensor(
                                out=v3(xw), in0=x_[:, 0:1536].rearrange("p (h q) -> p h q", h=24),
                                in1=dte[:, :].unsqueeze(2).to_broadcast([128, 24, 64]), op=ALU.mult),
                                r=[xk, "dte"], w=["xw"])
                            S.op("act", lambda e: e.activation(out=bB[:], in_=x_[:, 1536:2048], func=AF.Copy),
                                 r=[xk], w=["bB"])
                            for g in range(4):
                                S.op("pe", lambda e, g=g: e.matmul(psb[1][:, g * 128:(g + 1) * 128], lhsT=b_[:, g, :],
                                                                   rhs=b_[:, 4 + g, :], start=True, stop=True),
                                     r=[bk], w=[PK[1]])
                            S.op("dve", lambda e: e.tensor_tensor(
                                out=Gm[:].rearrange("p (g t) -> p g t", g=4),
                                in0=psb[1][:, :].rearrange("p (g t) -> p g t", g=4),
                                in1=TR[:, :].unsqueeze(1).to_broadcast([128, 4, 128]), op=ALU.mult),
                                r=[PK[1], TRk], w=["Gm"])
                            for g in range(4):
                                mm_cols(PY, g * 384, (g + 1) * 384, b_[:, 4 + g, :], lambda c, ce: HTb[:, c:ce],
                                        [bk, "HTb"])
                            for j in range(3):
                                S.op("dve", lambda e, j=j: e.tensor_tensor(
                                    out=yo[:, j * 512:(j + 1) * 512].rearrange("p (h q) -> p h q", h=8),
                                    in0=psb[PY[j]][:, :].rearrange("p (h q) -> p h q", h=8),
                                    in1=ea[:, j * 8:(j + 1) * 8].unsqueeze(2).to_broadcast([128, 8, 64]),
                                    op=ALU.mult), r=[PK[PY[j]], "ea"], w=["yo"])
                            for hq in range(6):
                                pb = 2 + hq % 2
                                ddq = dd[hq % 2]
                                dkq = "dd%d" % (hq % 2)
                                wq = wt[hq % 2]
                                wkq = "wt%d" % (hq % 2)
                                for j in range(4):
                                    h = hq * 4 + j
                                    S.op("pe", lambda e, j=j, h=h: e.matmul(
                                        psb[pb][:, j * 128:(j + 1) * 128],
                                        lhsT=dtA[:, h:h + 1].to_broadcast([128, 128]), rhs=TR[:], start=True,
                                        stop=True), r=["dtA", TRk], w=[PK[pb]])
                                for j in range(4):
                                    h = hq * 4 + j
                                    S.op("dve", lambda e, j=j, h=h: e.tensor_scalar(
                                        out=ddq[:, j * 128:(j + 1) * 128], in0=psb[pb][:, j * 128:(j + 1) * 128],
                                        scalar1=acol[:, h:h + 1], scalar2=0.0, op0=ALU.subtract, op1=ALU.min),
                                        r=[PK[pb], "acol"], w=[dkq])
                                S.op("act", lambda e: e.activation(out=ddq[:], in_=ddq[:], func=AF.Exp), r=[dkq],
                                     w=[dkq])
                                for j in range(4):
                                    h = hq * 4 + j
                                    g = h // 6
                                    S.op("dve", lambda e, j=j, g=g: e.tensor_mul(
                                        out=wq[:, j, :], in0=Gm[:, g * 128:(g + 1) * 128],
                                        in1=ddq[:, j * 128:(j + 1) * 128]), r=["Gm", dkq], w=[wkq])
                                for j in range(4):
                                    h = hq * 4 + j
                                    bnk = PY[(h * 64) // 512]
                                    S.op("pe", lambda e, j=j, h=h, bnk=bnk: e.matmul(
                                        psb[bnk][:, (h * 64) % 512:(h * 64) % 512 + 64], lhsT=wq[:, j, :],
                                        rhs=xdt[:, h * 64:(h + 1) * 64], start=True, stop=True),
                                        r=[wkq, "xdt"], w=[PK[bnk]])
                            for j in range(3):
                                S.op("dve", lambda e, j=j: e.tensor_add(out=yy[:, j * 512:(j + 1) * 512],
                                                                        in0=yo[:, j * 512:(j + 1) * 512],
                                                                        in1=psb[PY[j]][:, :]),
                                     r=["yo", PK[PY[j]]], w=["yy"])
                            for g in range(4):
                                mm_cols(PY, g * 384, (g + 1) * 384, bB[:, g * 128:(g + 1) * 128],
                                        lambda c, ce: xw[:, c:ce], ["bB", "xw"])
                            S.op("dve", lambda e: e.tensor_tensor(out=v3(HT), in0=v3(HT),
                                                                  in1=cd[:, :].unsqueeze(2).to_broadcast([128, 24, 64]),
                                                                  op=ALU.mult), r=["HT", "cd"], w=["HT"])
                            for j in range(3):
                                S.op("dve", lambda e, j=j: e.tensor_add(out=HT[:, j * 512:(j + 1) * 512],
                                                                        in0=HT[:, j * 512:(j + 1) * 512],
                                                                        in1=psb[PY[j]][:, :]),
                                     r=["HT", PK[PY[j]]], w=["HT"])
                            S.op("act", lambda e: e.activation(out=HTb[:], in_=HT[:], func=AF.Copy), r=["HT"],
                                 w=["HTb"])
                            if d == 0:
                                S.dma("sp", YS[t0:t0 + 128, :], yy[:], r=["yy"], w=["YS"])
                            else:
                                S.dma("sp", yo[:], YS[t0:t0 + 128, :], r=["YS"], w=["yo"])
                                S.op("dve", lambda e: e.tensor_add(out=yy[:], in0=yy[:], in1=yo[:]), r=["yy", "yo"],
                                     w=["yy"])
                                S.op("dve", lambda e: e.tensor_tensor(
                                    out=v3(yo), in0=x_[:, 0:1536].rearrange("p (h q) -> p h q", h=24),
                                    in1=Dbc[:, :].unsqueeze(2).to_broadcast([128, 24, 64]), op=ALU.mult),
                                    r=[xk, "Dbc"], w=["yo"])
                                S.op("dve", lambda e: e.tensor_add(out=yy[:], in0=yy[:], in1=yo[:]), r=["yy", "yo"],
                                     w=["yy"])
                                S.dma("sp", zz[:], P[t0:t0 + 128, C_Z:C_Z + 1536], r=["P"], w=["zz"])
                                S.op("act", lambda e: e.activation(out=zs[:], in_=zz[:], func=AF.Sigmoid), r=["zz"],
                                     w=["zs"])
                                S.op("dve", lambda e: e.tensor_mul(out=zs[:], in0=zs[:], in1=zz[:]), r=["zz", "zs"],
                                     w=["zs"])
                                S.op("dve", lambda e: e.tensor_mul(out=yy[:], in0=yy[:], in1=zs[:]), r=["yy", "zs"],
                                     w=["yy"])
                                S.op("act", lambda e: e.activation(out=zs[:], in_=yy[:], func=AF.Square), r=["yy"],
                                     w=["zs"])
                                S.op("dve", lambda e: e.tensor_reduce(out=ss4[:],
                                                                      in_=zs[:].rearrange("p (g q) -> p g q", g=4),
                                                                      axis=AX.X, op=ALU.add), r=["zs"], w=["ss4"])
                                rms_rstd(ss4[:], "ss4", 384, EPS)
                                S.op("dve", lambda e: e.tensor_tensor(
                                    out=yy[:].rearrange("p (g q) -> p g q", g=4),
                                    in0=yy[:].rearrange("p (g q) -> p g q", g=4),
                                    in1=ss4[:, :].unsqueeze(2).to_broadcast([128, 4, 384]), op=ALU.mult),
                                    r=["yy", "ss4"], w=["yy"])
                                S.op("dve", lambda e: e.tensor_mul(out=yy[:], in0=yy[:], in1=nrm[:]),
                                     r=["yy", "nrm"], w=["yy"])
                                S.dma("sp", YS[t0:t0 + 128, :], yy[:], r=["yy"], w=["YS"])
                        if not sample:
                            for g in range(12):
                                bnk = PY[(g * 128) // 512]
                                S.op("pe", lambda e, g=g, bnk=bnk: e.transpose(
                                    psb[bnk][:, (g * 128) % 512:(g * 128) % 512 + 128],
                                    HT[:, g * 128:(g + 1) * 128], ident[:]), r=["HT", "ident"], w=[PK[bnk]])
                            for j in range(3):
                                S.op("act", lambda e, j=j: e.activation(
                                    out=hio[:, j * 4:(j + 1) * 4, :],
                                    in_=psb[PY[j]][:, :].rearrange("p (g n) -> p g n", g=4), func=AF.Copy),
                                    r=[PK[PY[j]]], w=["hio"])
                            S.dma("sp", O["ns_ssm"][s_, l, d].rearrange("(g h2) p n -> (h2 p) g n", h2=2), hio[:],
                                  r=["hio"], w=["ns_ssm"])
                S.barrier()

        def phase_wkv_prep(l, T):
            NT = T // 128
            with contextlib.ExitStack() as ps:
                lsb = lambda name, shape, dt=F32: ps.enter_context(nc.sbuf_tensor(uname(name), list(shape), dt))
                sh = lsb("sh", [128, NWKV])
                kkbc = lsb("kkbc", [128, 1536])
                kabc = lsb("kabc", [128, 1536])
                omka = lsb("omka", [128, 1536])
                rkbc = lsb("rkbc", [128, 1536])
                w0bc = lsb("w0bc", [128, 1536])
                a0bc = lsb("a0bc", [128, 1536])
                wup = [lsb("wup%d" % d, [96, 1536]) for d in range(2)]
                aup = [lsb("aup%d" % d, [96, 1536]) for d in range(2)]
                gup = lsb("gup", [128, 2, 1536])
                A = lsb("wA", [128, 1536])
                Bt = lsb("wB", [128, 1536])
                Ct = lsb("wC", [128, 1536])
                Dt_ = lsb("wD", [128, 1536])
                E = lsb("wE", [128, 1536])
                nkk = lsb("nkk", [128, 1536])
                vb16 = lsb("vb16", [128, 1536], BF16)
                kb16 = lsb("kb16", [128, 1536], BF16)
                bb16 = lsb("bb16", [128, 1536], BF16)
                Vx = lsb("Vx", [128, 12, 768], BF16)
                S.op("dve", lambda e: e.memset(Vx[:], 0.0), w=["Vx"])
                sm = lsb("wsm", [128, 48])
                rs = lsb("wrs", [128, 24])
                twT = lsb("twT", [96, 2, 128])
                aT = lsb("aT", [96, 2, 128])
                sgT = lsb("sgT", [128, 2, 128])
                fm = [lsb("wfm%d" % j, [128, 12, 128]) for j in range(2)]
                fmc = [0]
                bcload(kkbc[:], "kkbc", I["wkv_k_k"][l, :])
                bcload(kabc[:], "kabc", I["wkv_k_a"][l, :])
                bcload(rkbc[:], "rkbc", I["wkv_r_k"][l, :])
                S.op("dve", lambda e: e.tensor_scalar(out=omka[:], in0=kabc[:], scalar1=-1.0, scalar2=1.0,
                                                      op0=ALU.mult, op1=ALU.add), r=["kabc"], w=["omka"])
                for d in range(2):
                    S.dma("sp", wup[d][:], I["wkv_w_up"][l, d], w=["wup%d" % d])
                    S.dma("sp", aup[d][:], I["wkv_a_up"][l, d], w=["aup%d" % d])
                S.dma("sp", gup[:], I["wkv_g_up"][l].rearrange("(c p) n -> p c n", p=128), w=["gup"])
                PY = (3, 4, 5)
                h3 = lambda ap: ap.rearrange("p (h q) -> p h q", h=24)

                def fm_store(src, skey, dst3):
                    f = fm[fmc[0] % 2]
                    fk = "wfm%d" % (fmc[0] % 2)
                    fmc[0] += 1
                    transposes_to(src, skey, 12, lambda q, nj: f[:, q:q + nj, :], fk, eng="dve")
                    S.dma("sp", dst3, f[:], r=[fk], w=["FMOUT"])

                for i in range(NT):
                    t0 = i * 128
                    S.dma("sp", sh[:], SH[t0:t0 + 128, :], r=["SH"], w=["sh"])
                    r_ = sh[:, 0:1536]
                    k_ = sh[:, 1536:3072]
                    S.op("act", lambda e: e.activation(out=vb16[:], in_=sh[:, 3072:4608], func=AF.Copy), r=["sh"],
                         w=["vb16"])
                    vb4 = vb16[:].rearrange("p (g h v) -> p g h v", g=12, h=2)
                    for hh in range(2):
                        for g in range(12):
                            en = "act" if g % 2 == 0 else "dve"
                            if en == "act":
                                S.op("act", lambda e, g=g: e.activation(out=Vx[:, g, g * 64:(g + 1) * 64],
                                                                        in_=vb4[:, g, hh, :], func=AF.Copy),
                                     r=["vb16"], w=["Vx"])
                            else:
                                S.op("dve", lambda e, g=g: e.tensor_copy(out=Vx[:, g, g * 64:(g + 1) * 64],
                                                                         in_=vb4[:, g, hh, :]), r=["vb16"], w=["Vx"])
                        S.dma("sp", VBD[hh][t0:t0 + 128, :, :], Vx[:], r=["Vx"], w=["VBD"])
                    S.op("dve", lambda e: e.tensor_mul(out=A[:], in0=k_, in1=kkbc[:]), r=["sh", "kkbc"], w=["wA"])
                    S.op("act", lambda e: e.activation(out=Bt[:], in_=A[:], func=AF.Square), r=["wA"], w=["wB"])
                    S.op("dve", lambda e: e.tensor_reduce(out=rs[:], in_=h3(Bt[:]), axis=AX.X, op=ALU.add),
                         r=["wB"], w=["wrs"])
                    S.op("dve", lambda e: e.tensor_scalar(out=rs[:], in0=rs[:], scalar1=1e-24, scalar2=None,
                                                          op0=ALU.max), r=["wrs"], w=["wrs"])
                    S.op("act", lambda e: e.activation(out=rs[:], in_=rs[:], func=AF.Sqrt), r=["wrs"], w=["wrs"])
                    S.op("dve", lambda e: e.reciprocal(out=rs[:], in_=rs[:]), r=["wrs"], w=["wrs"])
                    S.op("dve", lambda e: e.tensor_scalar(out=rs[:], in0=rs[:], scalar1=-1.0, scalar2=None,
                                                          op0=ALU.mult), r=["wrs"], w=["wrs"])
                    S.op("dve", lambda e: e.tensor_tensor(out=h3(nkk[:]), in0=h3(A[:]),
                                                          in1=rs[:, :].unsqueeze(2).to_broadcast([128, 24, 64]),
                                                          op=ALU.mult), r=["wA", "wrs"], w=["nkk"])
                    fm_store(nkk[:], "nkk", NKKT[:, :, t0:t0 + 128])
                    S.op("dve", lambda e: e.tensor_mul(out=Bt[:], in0=r_, in1=k_), r=["sh"], w=["wB"])
                    S.op("dve", lambda e: e.tensor_mul(out=Bt[:], in0=Bt[:], in1=rkbc[:]), r=["wB", "rkbc"], w=["wB"])
                    S.op("dve", lambda e: e.tensor_reduce(out=sm[:, 0:24], in_=h3(Bt[:]), axis=AX.X, op=ALU.add),
                         r=["wB"], w=["wsm"])
                    S.dma("sp", RK[t0:t0 + 128, :], sm[:, 0:24], r=["wsm"], w=["RK"])
                    S.op("act", lambda e: e.activation(out=Ct[:, 0:192], in_=sh[:, 4608:4800], func=AF.Tanh),
                         r=["sh"], w=["wC"])
                    S.op("act", lambda e: e.activation(out=Ct[:, 192:448], in_=sh[:, 4992:5248], func=AF.Sigmoid),
                         r=["sh"], w=["wC"])
                    transposes_to(Ct[:, 0:192], "wC", 2, lambda q, nj: twT[:, q:q + nj, :], "twT", bw=96, eng="dve")
                    transposes_to(sh[:, 4800:4992], "sh", 2, lambda q, nj: aT[:, q:q + nj, :], "aT", bw=96, eng="dve")
                    transposes_to(Ct[:, 192:448], "wC", 2, lambda q, nj: sgT[:, q:q + nj, :], "sgT", eng="dve")
                    for cb in range(3):
                        for c in range(2):
                            S.op("pe", lambda e, cb=cb, c=c: e.matmul(psb[PY[cb]][:, :], lhsT=sgT[:, c, :],
                                                                      rhs=gup[:, c, cb * 512:(cb + 1) * 512],
                                                                      start=(c == 0), stop=(c == 1)),
                                 r=["sgT", "gup"], w=[PK[PY[cb]]])
                        S.op("act", lambda e, cb=cb: e.activation(out=E[:, cb * 512:(cb + 1) * 512],
                                                                  in_=psb[PY[cb]][:, :], func=AF.Copy),
                             r=[PK[PY[cb]]], w=["wE"])
                    S.dma("sp", GG[t0:t0 + 128, :], E[:], r=["wE"], w=["GG"])
                    for d in range(2):
                        bcload(w0bc[:], "w0bc", I["wkv_w0"][l, d, :])
                        bcload(a0bc[:], "a0bc", I["wkv_a0"][l, d, :])
                        for cb in range(3):
                            S.op("pe", lambda e, cb=cb: e.matmul(psb[PY[cb]][:, :], lhsT=twT[:, d, :],
                                                                 rhs=wup[d][:, cb * 512:(cb + 1) * 512], start=True,
                                                                 stop=True), r=["twT", "wup%d" % d], w=[PK[PY[cb]]])
                            S.op("dve", lambda e, cb=cb: e.tensor_add(out=Bt[:, cb * 512:(cb + 1) * 512],
                                                                      in0=psb[PY[cb]][:, :],
                                                                      in1=w0bc[:, cb * 512:(cb + 1) * 512]),
                                 r=[PK[PY[cb]], "w0bc"], w=["wB"])
                        S.op("act", lambda e: e.activation(out=Bt[:], in_=Bt[:], func=AF.Sigmoid), r=["wB"], w=["wB"])
                        S.op("act", lambda e: e.activation(out=Bt[:], in_=Bt[:], func=AF.Exp,
                                                           scale=-math.exp(-0.5)), r=["wB"], w=["wB"])
                        for cb in range(3):
                            S.op("pe", lambda e, cb=cb: e.matmul(psb[PY[cb]][:, :], lhsT=aT[:, d, :],
                                                                 rhs=aup[d][:, cb * 512:(cb + 1) * 512], start=True,
                                                                 stop=True), r=["aT", "aup%d" % d], w=[PK[PY[cb]]])
                            S.op("dve", lambda e, cb=cb: e.tensor_add(out=Dt_[:, cb * 512:(cb + 1) * 512],
                                                                      in0=psb[PY[cb]][:, :],
                                                                      in1=a0bc[:, cb * 512:(cb + 1) * 512]),
                                 r=[PK[PY[cb]], "a0bc"], w=["wD"])
                        S.op("act", lambda e: e.activation(out=Dt_[:], in_=Dt_[:], func=AF.Sigmoid), r=["wD"],
                             w=["wD"])
                        S.op("dve", lambda e: e.tensor_mul(out=A[:], in0=Dt_[:], in1=kabc[:]), r=["wD", "kabc"],
                             w=["wA"])
                        S.op("dve", lambda e: e.tensor_add(out=A[:], in0=A[:], in1=omka[:]), r=["wA", "omka"],
                             w=["wA"])
                        S.op("dve", lambda e: e.tensor_mul(out=A[:], in0=A[:], in1=k_), r=["wA", "sh"], w=["wA"])
                        S.op("act", lambda e: e.activation(out=kb16[:], in_=A[:], func=AF.Copy), r=["wA"], w=["kb16"])
                        S.dma("sp", KD[d][t0:t0 + 128, :], kb16[:], r=["kb16"], w=["KD"])
                        S.op("dve", lambda e: e.scalar_tensor_tensor(out=Ct[:], in0=nkk[:], scalar=-1.0, in1=Dt_[:],
                                                                     op0=ALU.mult, op1=ALU.mult),
                             r=["nkk", "wD"], w=["wC"])
                        S.op("act", lambda e: e.activation(out=bb16[:], in_=Ct[:], func=AF.Copy), r=["wC"], w=["bb16"])
                        S.dma("sp", BD[d][t0:t0 + 128, :], bb16[:], r=["bb16"], w=["BD"])
                        S.op("dve", lambda e: e.tensor_mul(out=E[:], in0=Ct[:], in1=r_), r=["wC", "sh"], w=["wE"])
                        S.op("dve", lambda e: e.tensor_reduce(out=sm[:, 0:24], in_=h3(E[:]), axis=AX.X, op=ALU.add),
                             r=["wE"], w=["wsm"])
                        S.op("dve", lambda e: e.tensor_mul(out=E[:], in0=A[:], in1=r_), r=["wA", "sh"], w=["wE"])
                        S.op("dve", lambda e: e.tensor_reduce(out=sm[:, 24:48], in_=h3(E[:]), axis=AX.X, op=ALU.add),
                             r=["wE"], w=["wsm"])
                        S.dma("sp", BRKR[d][t0:t0 + 128, :], sm[:], r=["wsm"], w=["BRKR"])
                        S.op("dve", lambda e: e.tensor_tensor(out=h3(A[:]), in0=h3(nkk[:]),
                                                              in1=sm[:, 0:24].unsqueeze(2).to_broadcast([128, 24, 64]),
                                                              op=ALU.mult), r=["nkk", "wsm"], w=["wA"])
                        S.op("dve", lambda e: e.tensor_mul(out=E[:], in0=Bt[:], in1=r_), r=["wB", "sh"], w=["wE"])
                        S.op("dve", lambda e: e.tensor_add(out=E[:], in0=E[:], in1=A[:]), r=["wE", "wA"], w=["wE"])
                        fm_store(E[:], "wE", WRT[d][:, :, t0:t0 + 128])
                        fm_store(Bt[:], "wB", WT[d][:, :, t0:t0 + 128])
                S.barrier()

        def phase_wkv_scan(l, T, L, sample):
            TBK = 4
            LT = L // 128
            with contextlib.ExitStack() as ps:
                lsb = lambda name, shape, dt=F32: ps.enter_context(nc.sbuf_tensor(uname(name), list(shape), dt))
                m48 = lsb("m48", [48, 768])
                m24b = lsb("m24b", [24, 768], BF16)
                sio = lsb("sio", [64, 12, 128])
                B = []
                for d in range(2):
                    b = {n: lsb("%s_%d" % (n, d), shp, dt) for (n, shp, dt) in (
                        ("Ap", [128, 128, 48], F32), ("nkT", [128, 12, 128], F32), ("wrT", [128, 12, 128], F32),
                        ("wT", [128, 12, 128], F32), ("Lb", [24, TBK, 128], BF16), ("Lk", [24, TBK, 128], BF16),
                        ("Rv", [24, TBK, 768], BF16), ("Ra", [48, TBK, 768], F32), ("Rb", [24, TBK, 768], BF16),
                        ("Cst", [48, TBK, 64], F32), ("ST", [128, 768], F32))}
                    b["k"] = {n: "%s_%d" % (n, d) for n in ("Ap", "nkT", "wrT", "wT", "Lb", "Lk", "Rv", "Ra", "Rb",
                                                            "Cst", "ST")}
                    b["pb"] = (0, 1, 2, 3) if d == 0 else (4, 5, 6, 7)
                    B.append(b)
                S.dma("sp", m48[:], I["mask48"][:, :], w=["m48"])
                S.op("dve", lambda e: e.tensor_copy(out=m24b[:], in_=m48[0:24, :]), r=["m48"], w=["m24b"])
                for d in range(2):
                    b = B[d]
                    S.op("dve", lambda e: e.memset(b["Ap"][:], 0.0), w=[b["k"]["Ap"]])
                    S.op("dve", lambda e: e.memset(b["Lb"][:], 0.0), w=[b["k"]["Lb"]])
                    S.op("dve", lambda e: e.memset(b["Lk"][:], 0.0), w=[b["k"]["Lk"]])
                for s_ in range(T // L):
                    for d in range(2):
                        b = B[d]
                        ST = b["ST"]
                        if sample:
                            S.dma("sp", sio[:].rearrange("v g (h k) -> v g h k", h=2),
                                  I["st_wkv"][l, d].rearrange("(g h) v k -> v g h k", h=2), w=["sio"])
                            transposes_to(sio[:].rearrange("v g q -> v (g q)"), "sio", 12,
                                          lambda q, nj: ST[:, q * 64:(q + nj) * 64].rearrange("p (j v) -> p j v", j=nj),
                                          b["k"]["ST"], bw=128, inw=64, pbanks=b["pb"][2:4])
                            S.op("dve", lambda e: e.tensor_copy(out=ST[:, 0:1], in_=ST[:, 0:1]), r=[b["k"]["ST"]],
                                 w=[(b["k"]["ST"], g_) for g_ in range(12)])
                        else:
                            S.op("dve", lambda e: e.memset(ST[:], 0.0),
                                 w=[(b["k"]["ST"], g_) for g_ in range(12)] + [b["k"]["ST"]])
                    for kk_ in range(LT):
                        tis = (kk_, LT - 1 - kk_)
                        t0s = [s_ * L + ti * 128 for ti in tis]
                        for d in range(2):
                            b = B[d]
                            k = b["k"]
                            t0 = t0s[d]
                            S.dma("sp", b["nkT"][:], NKKT[:, :, t0:t0 + 128], r=["NKKT"], w=[k["nkT"]])
                            S.dma("sp", b["wrT"][:], WRT[d][:, :, t0:t0 + 128], r=["WRT"], w=[k["wrT"]])
                            S.dma("sp", b["wT"][:], WT[d][:, :, t0:t0 + 128], r=["WT"], w=[k["wT"]])
                            for (lo, hi, c0_, src, sk) in ((0, 64, 0, "nkT", k["nkT"]), (64, 128, 12, "nkT", k["nkT"]),
                                                           (0, 64, 24, "wrT", k["wrT"]), (64, 128, 36, "wrT", k["wrT"])):
                                S.op("act", lambda e, lo=lo, hi=hi, c0_=c0_, src=src: e.activation(
                                    out=b["Ap"][lo:hi, :, c0_:c0_ + 12], in_=b[src][lo:hi].rearrange("p g t -> p t g"),
                                    func=AF.Copy), r=[sk], w=[k["Ap"]])
                        for cc in range(128 // TBK):
                            chs = (cc, 128 // TBK - 1 - cc)
                            for d in range(2):
                                b = B[d]
                                k = b["k"]
                                c0 = t0s[d] + chs[d] * TBK
                                bsrc = BD[d][c0:c0 + TBK, :].rearrange("t (g h k) -> g t h k", g=12, h=2)
                                ksrc = KD[d][c0:c0 + TBK, :].rearrange("t (g h k) -> g t h k", g=12, h=2)
                                S.dma("sp", b["Lb"][0:12, :, 0:64], bsrc[:, :, 0, :], r=["BD"], w=[k["Lb"]])
                                S.dma("sp", b["Lb"][12:24, :, 64:128], bsrc[:, :, 1, :], r=["BD"], w=[k["Lb"]])
                                S.dma("sp", b["Lk"][0:12, :, 0:64], ksrc[:, :, 0, :], r=["KD"], w=[k["Lk"]])
                                S.dma("sp", b["Lk"][12:24, :, 64:128], ksrc[:, :, 1, :], r=["KD"], w=[k["Lk"]])
                                for hh in range(2):
                                    S.dma("sp", b["Rv"][hh * 12:(hh + 1) * 12, :, :],
                                          VBD[hh][c0:c0 + TBK, :, :].rearrange("t g q -> g t q"),
                                          r=["VBD"], w=[k["Rv"]])
                            for st_ in range(TBK):
                                tls = (st_, TBK - 1 - st_)
                                toks = [chs[d] * TBK + tls[d] for d in range(2)]
                                for d in range(2):
                                    b = B[d]
                                    k = b["k"]
                                    for hf in range(2):
                                        S.op("pe", lambda e, hf=hf: e.matmul(
                                            psb[b["pb"][hf]][0:48, 0:384], lhsT=b["Ap"][:, toks[d], :],
                                            rhs=b["ST"][:, hf * 384:(hf + 1) * 384], start=True, stop=True),
                                            r=[k["Ap"]] + [(k["ST"], g_) for g_ in range(hf * 6, hf * 6 + 6)],
                                            w=[PK[b["pb"][hf]]])
                                for d in range(2):
                                    b = B[d]
                                    k = b["k"]
                                    for hf in range(2):
                                        S.op("dve", lambda e, hf=hf: e.tensor_mul(
                                            out=b["Ra"][:, tls[d], hf * 384:(hf + 1) * 384],
                                            in0=psb[b["pb"][hf]][0:48, 0:384],
                                            in1=m48[:, hf * 384:(hf + 1) * 384]),
                                            r=[PK[b["pb"][hf]], "m48"], w=[(k["Ra"], hf)])
                                for d in range(2):
                                    b = B[d]
                                    k = b["k"]
                                    S.op("act", lambda e: e.activation(out=b["Rb"][:, tls[d], :],
                                                                       in_=b["Ra"][0:24, tls[d], :], func=AF.Copy),
                                         r=[(k["Ra"], 0), (k["Ra"], 1)], w=[k["Rb"]])
                                for d in range(2):
                                    b = B[d]
                                    k = b["k"]
                                    for hf in range(2):
                                        pb = b["pb"][2 + hf]
                                        S.op("pe", lambda e, hf=hf, pb=pb: e.matmul(
                                            psb[pb][:, 0:384], lhsT=b["Lk"][:, tls[d], :],
                                            rhs=b["Rv"][:, tls[d], hf * 384:(hf + 1) * 384], start=True, stop=False),
                                            r=[k["Lk"], k["Rv"]], w=[PK[pb]])
                                        S.op("pe", lambda e, hf=hf, pb=pb: e.matmul(
                                            psb[pb][:, 0:384], lhsT=b["Lb"][:, tls[d], :],
                                            rhs=b["Rb"][:, tls[d], hf * 384:(hf + 1) * 384], start=False, stop=True),
                                            r=[k["Lb"], k["Rb"]], w=[PK[pb]])
                                for d in range(2):
                                    b = B[d]
                                    k = b["k"]
                                    for g in range(12):
                                        pb = b["pb"][2 + g // 6]
                                        S.op("dve", lambda e, g=g, pb=pb: e.scalar_tensor_tensor(
                                            out=b["ST"][:, g * 64:(g + 1) * 64], in0=b["ST"][:, g * 64:(g + 1) * 64],
                                            scalar=b["wT"][:, g, toks[d]:toks[d] + 1],
                                            in1=psb[pb][:, (g % 6) * 64:(g % 6 + 1) * 64], op0=ALU.mult, op1=ALU.add),
                                            r=[(k["ST"], g), k["wT"], PK[pb]], w=[(k["ST"], g)])
                            for d in range(2):
                                b = B[d]
                                k = b["k"]
                                c0 = t0s[d] + chs[d] * TBK
                                S.op("dve", lambda e: e.tensor_reduce(
                                    out=b["Cst"][:], in_=b["Ra"][:].rearrange("p t (g v) -> p t v g", g=12),
                                    axis=AX.X, op=ALU.add), r=[(k["Ra"], 0), (k["Ra"], 1)], w=[k["Cst"]])
                                S.dma("sp", SAY[d][:, c0:c0 + TBK, :], b["Cst"][:], r=[k["Cst"]], w=["SAY"])
                    if not sample:
                        for d in range(2):
                            b = B[d]
                            S.op("dve", lambda e: e.tensor_copy(out=b["ST"][:, 0:1], in_=b["ST"][:, 0:1]),
                                 r=[(b["k"]["ST"], g_) for g_ in range(12)], w=[b["k"]["ST"]])
                            transposes_to(b["ST"][:], b["k"]["ST"], 12, lambda q, nj: sio[:, q:q + nj, :], "sio",
                                          bw=64, inw=128, pbanks=b["pb"][2:4])
                            S.dma("sp", O["ns_wkv"][s_, l, d].rearrange("(g h) v k -> v g h k", h=2),
                                  sio[:].rearrange("v g (h k) -> v g h k", h=2), r=["sio"], w=["ns_wkv"])
                S.barrier()

        def phase_wkv_post(l, T):
            with contextlib.ExitStack() as ps:
                lsb = lambda name, shape, dt=F32: ps.enter_context(nc.sbuf_tensor(uname(name), list(shape), dt))
                y0 = [lsb("py0%d" % d, [128, 1536]) for d in range(2)]
                vv = lsb("pvv", [128, 1536])
                gg = lsb("pgg", [128, 1536])
                o = lsb("po", [128, 1536])
                t_ = lsb("pt", [128, 1536])
                lnw = lsb("lnw", [128, 1536])
                lnb = lsb("lnb", [128, 1536])
                bk = [lsb("pbk%d" % d, [128, 48]) for d in range(2)]
                rk = lsb("prk", [128, 24])
                mu = lsb("pmu", [128, 24])
                bcload(lnw[:], "lnw", I["wkv_ln_w"][l, :])
                bcload(lnb[:], "lnb", I["wkv_ln_b"][l, :])
                h3 = lambda ap: ap.rearrange("p (h q) -> p h q", h=24)
                bc3 = lambda ap: ap.unsqueeze(2).to_broadcast([128, 24, 64])
                for i in range(T // 128):
                    t0 = i * 128
                    for d in range(2):
                        for hh in range(2):
                            S.dma("sp", y0[d][:].rearrange("p (g h v) -> p g h v", g=12, h=2)[:, :, hh, :],
                                  SAY[d][(2 + hh) * 12:(3 + hh) * 12, t0:t0 + 128, :].rearrange("g t v -> t g v"),
                                  r=["SAY"], w=["py0%d" % d])
                        S.dma("sp", bk[d][:], BRKR[d][t0:t0 + 128, :], r=["BRKR"], w=["pbk%d" % d])
                    S.dma("sp", vv[:], SH[t0:t0 + 128, 3072:4608], r=["SH"], w=["pvv"])
                    S.dma("sp", gg[:], GG[t0:t0 + 128, :], r=["GG"], w=["pgg"])
                    S.dma("sp", rk[:], RK[t0:t0 + 128, :], r=["RK"], w=["prk"])
                    S.op("dve", lambda e: e.tensor_add(out=o[:], in0=y0[0][:], in1=y0[1][:]), r=["py00", "py01"],
                         w=["po"])
                    S.op("dve", lambda e: e.tensor_add(out=mu[:], in0=bk[0][:, 24:48], in1=bk[1][:, 24:48]),
                         r=["pbk0", "pbk1"], w=["pmu"])
                    S.op("dve", lambda e: e.tensor_tensor(out=h3(t_[:]), in0=h3(vv[:]), in1=bc3(mu[:, :]),
                                                          op=ALU.mult), r=["pvv", "pmu"], w=["pt"])
                    S.op("dve", lambda e: e.tensor_add(out=o[:], in0=o[:], in1=t_[:]), r=["po", "pt"], w=["po"])
                    S.op("dve", lambda e: e.tensor_reduce(out=mu[:], in_=h3(o[:]), axis=AX.X, op=ALU.add), r=["po"],
                         w=["pmu"])
                    S.op("dve", lambda e: e.tensor_scalar(out=mu[:], in0=mu[:], scalar1=1.0 / 64, scalar2=None,
                                                          op0=ALU.mult), r=["pmu"], w=["pmu"])
                    S.op("dve", lambda e: e.tensor_tensor(out=h3(o[:]), in0=h3(o[:]), in1=bc3(mu[:, :]),
                                                          op=ALU.subtract), r=["po", "pmu"], w=["po"])
                    S.op("act", lambda e: e.activation(out=t_[:], in_=o[:], func=AF.Square), r=["po"], w=["pt"])
                    S.op("dve", lambda e: e.tensor_reduce(out=mu[:], in_=h3(t_[:]), axis=AX.X, op=ALU.add), r=["pt"],
                         w=["pmu"])
                    rms_rstd(mu[:], "pmu", 64, 64e-5)
                    S.op("dve", lambda e: e.tensor_tensor(out=h3(o[:]), in0=h3(o[:]), in1=bc3(mu[:, :]),
                                                          op=ALU.mult), r=["po", "pmu"], w=["po"])
                    S.op("dve", lambda e: e.tensor_mul(out=o[:], in0=o[:], in1=lnw[:]), r=["po", "lnw"], w=["po"])
                    S.op("dve", lambda e: e.tensor_add(out=o[:], in0=o[:], in1=lnb[:]), r=["po", "lnb"], w=["po"])
                    S.op("dve", lambda e: e.tensor_tensor(out=h3(t_[:]), in0=h3(vv[:]), in1=bc3(rk[:, :]),
                                                          op=ALU.mult), r=["pvv", "prk"], w=["pt"])
                    S.op("dve", lambda e: e.tensor_add(out=o[:], in0=o[:], in1=t_[:]), r=["po", "pt"], w=["po"])
                    S.op("dve", lambda e: e.tensor_mul(out=o[:], in0=o[:], in1=gg[:]), r=["po", "pgg"], w=["po"])
                    S.dma("sp", ZW[t0:t0 + 128, :], o[:], r=["po"], w=["ZW"])
                S.barrier()

        def phase_s5(l, T, L, sample):
            NT = T // 128
            nseq = T // L
            LT = L // 128
            Ls = min(512, L)
            nseg = L // Ls
            nsub = Ls // 128
            with contextlib.ExitStack() as ps:
                lsb = lambda name, shape, dt=F32: ps.enter_context(nc.sbuf_tensor(uname(name), list(shape), dt))
                ut = [lsb("ut%d" % j, [128, 1024]) for j in range(2)]
                stg = [lsb("ustg%d" % j, [32, 32, 128]) for j in range(2)]
                cnt = 0
                for i in range(NT):
                    t0 = i * 128
                    s_ = t0 // L
                    ti = (t0 % L) // 128
                    t0r = s_ * L + (LT - 1 - ti) * 128
                    u = ut[i % 2]
                    uk = "ut%d" % (i % 2)
                    S.dma("sp", u[:], P[t0:t0 + 128, C_U:C_U + 1024], r=["P"], w=[uk])
                    for d in range(2):
                        st_ = stg[cnt % 2]
                        sk = "ustg%d" % (cnt % 2)
                        cnt += 1
                        transposes_to(u[:], uk, 32, lambda q, nj: st_[:, q:q + nj, :], sk, bw=32, inw=128,
                                      eng=("act" if d == 0 else "dve"), rhs=(None if d == 0 else jrev[:]),
                                      rkey="jrev", pbanks=((6, 7) if d == 0 else (4, 5)))
                        dt_ = t0 if d == 0 else t0r
                        S.dma("sp", UT2[d][:, :, dt_:dt_ + 128].rearrange("k r t -> r k t"), st_[:], r=[sk],
                              w=["UT2"])
                S.barrier()
            with contextlib.ExitStack() as ps:
                lsb = lambda name, shape, dt=F32: ps.enter_context(nc.sbuf_tensor(uname(name), list(shape), dt))
                pt_ = {n: lsb("s5_" + n, [128, 32]) for n in
                       ("lre", "lim", "ldt", "rho", "tht", "cs", "sn", "ar", "ai", "t1", "t2", "t3", "cr", "ci")}
                it_ = lsb("s5_it", [128, 32], I32)
                bre = lsb("bre", [128, 32, 16])
                bim = lsb("bim", [128, 32, 16])
                Bbr = lsb("Bbr", [128, 32, 16])
                Bbi = lsb("Bbi", [128, 32, 16])
                btmp = lsb("btmp", [128, 32, 16])
                BDr = lsb("BDr", [128, 32, 32])
                BDi = lsb("BDi", [128, 32, 32])
                BpTr = lsb("BpTr", [32, 32, 128])
                BpTi = lsb("BpTi", [32, 32, 128])
                Zr = lsb("Zr", [32, 32, 128])
                Zi = lsb("Zi", [32, 32, 128])
                Cre = lsb("Cre", [128, 32, 32], BF16)
                nCre = lsb("nCre", [128, 32, 32], BF16)
                nCim = lsb("nCim", [128, 32, 32], BF16)
                jjt = lsb("jjt", [128, 512])
                tj = lsb("tj", [128, 512])
                tf = lsb("tf", [128, 512])
                iti = lsb("iti", [128, 512], I32)
                cst = lsb("cst", [128, 512])
                snt = lsb("snt", [128, 512])
                u2 = [lsb("u2_%d" % j, [32, 512]) for j in range(2)]
                p1 = lsb("p1", [128, 512])
                p2 = lsb("p2", [128, 512])
                inre = lsb("inre", [128, 512])
                inim = lsb("inim", [128, 512])
                zr = lsb("zr", [128, 512])
                zi = lsb("zi", [128, 512])
                qq = [lsb("qq%d" % j, [128, 512], BF16) for j in range(4)]
                xr = lsb("xr", [128, 1])
                xi = lsb("xi", [128, 1])
                c4 = lsb("c4", [128, 4])
                hre = lsb("hre", [128, 32])
                him = lsb("him", [128, 32])
                finr = lsb("finr", [128, nseq, 32])
                fini = lsb("fini", [128, nseq, 32])
                ystg = [lsb("ystg%d" % j, [128, 4, 32]) for j in range(2)]
                S.dma("sp", jjt[:], I["jj"][:, :], w=["jjt"])
                for zt, zk in ((BDr, "BDr"), (BDi, "BDi"), (Zr, "Zr"), (Zi, "Zi")):
                    S.op("dve", lambda e, zt=zt: e.memset(zt[:], 0.0), w=[zk])
                K_ = "s5p"

                def tt(o, a, b, op):
                    S.op("dve", lambda e: e.tensor_tensor(out=pt_[o][:], in0=pt_[a][:], in1=pt_[b][:], op=op),
                         r=[K_], w=[K_])

                def ts(o, a, s1, s2, op0, op1=None):
                    if op1 is None:
                        S.op("dve", lambda e: e.tensor_scalar(out=pt_[o][:], in0=pt_[a][:], scalar1=s1, scalar2=None,
                                                              op0=op0), r=[K_], w=[K_])
                    else:
                        S.op("dve", lambda e: e.tensor_scalar(out=pt_[o][:], in0=pt_[a][:], scalar1=s1, scalar2=s2,
                                                              op0=op0, op1=op1), r=[K_], w=[K_])

                def frac_sin(o, a):
                    S.op("dve", lambda e: e.tensor_copy(out=it_[:], in_=pt_[a][:]), r=[K_], w=[K_])
                    S.op("dve", lambda e: e.tensor_copy(out=pt_["t2"][:], in_=it_[:]), r=[K_], w=[K_])
                    tt("t2", a, "t2", ALU.subtract)
                    S.op("act", lambda e: e.activation(out=pt_[o][:], in_=pt_["t2"][:], func=AF.Sin, scale=SIN_SCALE),
                         r=[K_], w=[K_])

                cnt = 0
                ucnt = 0
                for d in range(2):
                    with nc.allow_non_contiguous_dma(reason="small s5 parameter transposes"):
                        for g2 in range(2):
                            sl = slice(g2 * 64, (g2 + 1) * 64)
                            S.dma("sp", pt_["lre"][sl, :],
                                  I["s5_lam_re"][l, d].rearrange("(k g) p -> g p k", g=2)[g2], w=[K_])
                            S.dma("sp", pt_["lim"][sl, :],
                                  I["s5_lam_im"][l, d].rearrange("(k g) p -> g p k", g=2)[g2], w=[K_])
                            S.dma("sp", pt_["ldt"][sl, :],
                                  I["s5_log_dt"][l, d].rearrange("(k g) -> g k", g=2)[g2].partition_broadcast(64),
                                  w=[K_])
                            if sample:
                                S.dma("sp", hre[sl, :], I["st_s5re"][l, d].rearrange("(k g) p -> g p k", g=2)[g2],
                                      w=["hre"])
                                S.dma("sp", him[sl, :], I["st_s5im"][l, d].rearrange("(k g) p -> g p k", g=2)[g2],
                                      w=["him"])
                    ts("lre", "lre", -1e-4, None, ALU.min)
                    S.op("act", lambda e: e.activation(out=pt_["ldt"][:], in_=pt_["ldt"][:], func=AF.Exp), r=[K_],
                         w=[K_])
                    tt("t1", "lre", "ldt", ALU.mult)
                    S.op("act", lambda e: e.activation(out=pt_["rho"][:], in_=pt_["t1"][:], func=AF.Exp), r=[K_],
                         w=[K_])
                    tt("tht", "lim", "ldt", ALU.mult)
                    ts("tht", "tht", 1.0 / TWO_PI, None, ALU.mult)
                    frac_sin("sn", "tht")
                    ts("t3", "tht", 0.25, None, ALU.add)
                    frac_sin("cs", "t3")
                    tt("ar", "rho", "cs", ALU.mult)
                    tt("ai", "rho", "sn", ALU.mult)
                    ts("ar", "ar", -1.0, None, ALU.add)
                    tt("t1", "ar", "lre", ALU.mult)
                    tt("t2", "ai", "lim", ALU.mult)
                    tt("cr", "t1", "t2", ALU.add)
                    tt("t1", "ai", "lre", ALU.mult)
                    tt("t2", "ar", "lim", ALU.mult)
                    tt("ci", "t1", "t2", ALU.subtract)
                    tt("t1", "lre", "lre", ALU.mult)
                    tt("t2", "lim", "lim", ALU.mult)
                    tt("t1", "t1", "t2", ALU.add)
                    S.op("dve", lambda e: e.reciprocal(out=pt_["t1"][:], in_=pt_["t1"][:]), r=[K_], w=[K_])
                    tt("cr", "cr", "t1", ALU.mult)
                    tt("ci", "ci", "t1", ALU.mult)
                    S.dma("sp", bre[:], I["s5_b_re"][l, d].rearrange("(k g) p c -> (g p) k c", g=2), w=["bre"])
                    S.dma("sp", bim[:], I["s5_b_im"][l, d].rearrange("(k g) p c -> (g p) k c", g=2), w=["bim"])
                    crb = pt_["cr"][:, :].unsqueeze(2).to_broadcast([128, 32, 16])
                    cib = pt_["ci"][:, :].unsqueeze(2).to_broadcast([128, 32, 16])
                    S.op("dve", lambda e: e.tensor_tensor(out=Bbr[:], in0=bre[:], in1=crb, op=ALU.mult),
                         r=["bre", K_], w=["Bbr"])
                    S.op("dve", lambda e: e.tensor_tensor(out=btmp[:], in0=bim[:], in1=cib, op=ALU.mult),
                         r=["bim", K_], w=["btmp"])
                    S.op("dve", lambda e: e.tensor_sub(out=Bbr[:], in0=Bbr[:], in1=btmp[:]), r=["Bbr", "btmp"],
                         w=["Bbr"])
                    S.op("dve", lambda e: e.tensor_tensor(out=Bbi[:], in0=bre[:], in1=cib, op=ALU.mult),
                         r=["bre", K_], w=["Bbi"])
                    S.op("dve", lambda e: e.tensor_tensor(out=btmp[:], in0=bim[:], in1=crb, op=ALU.mult),
                         r=["bim", K_], w=["btmp"])
                    S.op("dve", lambda e: e.tensor_add(out=Bbi[:], in0=Bbi[:], in1=btmp[:]), r=["Bbi", "btmp"],
                         w=["Bbi"])
                    for (bsrc, bkey, bd, bdk, bp, bpk) in ((Bbr, "Bbr", BDr, "BDr", BpTr, "BpTr"),
                                                           (Bbi, "Bbi", BDi, "BDi", BpTi, "BpTi")):
                        S.op("dve", lambda e: e.tensor_copy(out=bd[0:64, :, 0:16], in_=bsrc[0:64, :, :]), r=[bkey],
                             w=[bdk])
                        S.op("dve", lambda e: e.tensor_copy(out=bd[64:128, :, 16:32], in_=bsrc[64:128, :, :]),
                             r=[bkey], w=[bdk])
                        transposes_to(bd[:].rearrange("p k c -> p (k c)"), bdk, 32,
                                      lambda q, nj, bp=bp: bp[:, q:q + nj, :], bpk, bw=32, inw=128)
                    for (zt, zk, nm) in ((Zr, "Zr", "s5_c_re"), (Zi, "Zi", "s5_c_im")):
                        csrc = I[nm][l, d].rearrange("(k g) c p -> g c k p", g=2)
                        S.dma("sp", zt[0:16, :, 0:64], csrc[0], w=[zk])
                        S.dma("sp", zt[16:32, :, 64:128], csrc[1], w=[zk])
                    zf = lambda zt: zt[:].rearrange("r k q -> r (k q)")
                    transposes_to(zf(Zr), "Zr", 32, lambda q, nj: Cre[:, q:q + nj, :], "Cre", bw=128, inw=32)
                    transposes_to(zf(Zr), "Zr", 32, lambda q, nj: nCre[:, q:q + nj, :], "nCre", bw=128, inw=32,
                                  scale=-1.0)
                    transposes_to(zf(Zi), "Zi", 32, lambda q, nj: nCim[:, q:q + nj, :], "nCim", bw=128, inw=32,
                                  scale=-1.0)
                    Cm = (Cre, nCre, nCim, nCim)
                    Ck = ("Cre", "nCre", "nCim", "nCim")
                    for kt in range(32):
                        S.op("dve", lambda e: e.tensor_scalar(out=tj[:, 0:Ls], in0=jjt[:, 0:Ls],
                                                              scalar1=pt_["tht"][:, kt:kt + 1], scalar2=None,
                                                              op0=ALU.mult), r=["jjt", K_], w=["tj"])
                        for (dst, dk_, off) in ((snt, "snt", 0.0), (cst, "cst", 0.25)):
                            if off != 0.0:
                                S.op("dve", lambda e: e.tensor_scalar(out=tj[:, 0:Ls], in0=tj[:, 0:Ls], scalar1=off,
                                                                      scalar2=None, op0=ALU.add), r=["tj"], w=["tj"])
                            S.op("dve", lambda e: e.tensor_copy(out=iti[:, 0:Ls], in_=tj[:, 0:Ls]), r=["tj"],
                                 w=["iti"])
                            S.op("dve", lambda e: e.tensor_copy(out=tf[:, 0:Ls], in_=iti[:, 0:Ls]), r=["iti"],
                                 w=["tf"])
                            S.op("dve", lambda e: e.tensor_sub(out=tf[:, 0:Ls], in0=tj[:, 0:Ls], in1=tf[:, 0:Ls]),
                                 r=["tj", "tf"], w=["tf"])
                            S.op("act", lambda e, dst=dst: e.activation(out=dst[:, 0:Ls], in_=tf[:, 0:Ls],
                                                                        func=AF.Sin, scale=SIN_SCALE),
                                 r=["tf"], w=[dk_])
                        rhob = pt_["rho"][:, kt:kt + 1].to_broadcast([128, Ls])
                        for s_ in range(nseq):
                            if sample:
                                S.op("dve", lambda e: e.tensor_copy(out=xr[:], in_=hre[:, kt:kt + 1]), r=["hre"],
                                     w=["xr"])
                                S.op("dve", lambda e: e.tensor_copy(out=xi[:], in_=him[:, kt:kt + 1]), r=["him"],
                                     w=["xi"])
                            else:
                                S.op("dve", lambda e: e.memset(xr[:], 0.0), w=["xr"])
                                S.op("dve", lambda e: e.memset(xi[:], 0.0), w=["xi"])
                            for seg in range(nseg):
                                tau0 = s_ * L + seg * Ls
                                u_ = u2[ucnt % 2]
                                uk = "u2_%d" % (ucnt % 2)
                                ucnt += 1
                                S.dma("sp", u_[:, 0:Ls], UT2[d][kt, :, tau0:tau0 + Ls], r=["UT2"], w=[uk])
                                S.op("pe", lambda e: e.matmul(psb[0][:, 0:Ls], lhsT=BpTr[:, kt, :], rhs=u_[:, 0:Ls],
                                                              start=True, stop=True), r=["BpTr", uk], w=[PK[0]])
                                S.op("pe", lambda e: e.matmul(psb[1][:, 0:Ls], lhsT=BpTi[:, kt, :], rhs=u_[:, 0:Ls],
                                                              start=True, stop=True), r=["BpTi", uk], w=[PK[1]])
                                c_ = cst[:, 0:Ls]
                                s__ = snt[:, 0:Ls]
                                S.op("dve", lambda e: e.tensor_mul(out=p1[:, 0:Ls], in0=psb[0][:, 0:Ls], in1=c_),
                                     r=[PK[0], "cst"], w=["p1"])
                                S.op("dve", lambda e: e.tensor_mul(out=p2[:, 0:Ls], in0=psb[1][:, 0:Ls], in1=s__),
                                     r=[PK[1], "snt"], w=["p2"])
                                S.op("dve", lambda e: e.tensor_add(out=inre[:, 0:Ls], in0=p1[:, 0:Ls],
                                                                    in1=p2[:, 0:Ls]), r=["p1", "p2"], w=["inre"])
                                S.op("dve", lambda e: e.tensor_mul(out=p1[:, 0:Ls], in0=psb[1][:, 0:Ls], in1=c_),
                                     r=[PK[1], "cst"], w=["p1"])
                                S.op("dve", lambda e: e.tensor_mul(out=p2[:, 0:Ls], in0=psb[0][:, 0:Ls], in1=s__),
                                     r=[PK[0], "snt"], w=["p2"])
                                S.op("dve", lambda e: e.tensor_sub(out=inim[:, 0:Ls], in0=p1[:, 0:Ls],
                                                                    in1=p2[:, 0:Ls]), r=["p1", "p2"], w=["inim"])
                                S.op("dve", lambda e: e.tensor_tensor_scan(out=zr[:, 0:Ls], data0=rhob,
                                                                           data1=inre[:, 0:Ls], initial=xr[:, 0:1],
                                                                           op0=ALU.mult, op1=ALU.add),
                                     r=[K_, "inre", "xr"], w=["zr"])
                                S.op("dve", lambda e: e.tensor_tensor_scan(out=zi[:, 0:Ls], data0=rhob,
                                                                           data1=inim[:, 0:Ls], initial=xi[:, 0:1],
                                                                           op0=ALU.mult, op1=ALU.add),
                                     r=[K_, "inim", "xi"], w=["zi"])
                                for (qi, a_, ak, b_, bk_, en) in ((0, cst, "cst", zr, "zr", "dve"),
                                                                  (1, snt, "snt", zi, "zi", "dve"),
                                                                  (2, snt, "snt", zr, "zr", "dve"),
                                                                  (3, cst, "cst", zi, "zi", "dve")):
                                    S.op(en, lambda e, qi=qi, a_=a_, b_=b_: e.tensor_mul(
                                        out=qq[qi][:, 0:Ls], in0=a_[:, 0:Ls], in1=b_[:, 0:Ls]),
                                        r=[ak, bk_], w=["qq%d" % qi])
                                e0 = Ls - 1
                                for (ci_, a_, ak, b_, bk_) in ((0, cst, "cst", zr, "zr"), (1, snt, "snt", zi, "zi"),
                                                               (2, snt, "snt", zr, "zr"), (3, cst, "cst", zi, "zi")):
                                    S.op("dve", lambda e, ci_=ci_, a_=a_, b_=b_: e.tensor_mul(
                                        out=c4[:, ci_:ci_ + 1], in0=a_[:, e0:e0 + 1], in1=b_[:, e0:e0 + 1]),
                                        r=[ak, bk_], w=["c4"])
                                S.op("dve", lambda e: e.tensor_sub(out=xr[:], in0=c4[:, 0:1], in1=c4[:, 1:2]),
                                     r=["c4"], w=["xr"])
                                S.op("dve", lambda e: e.tensor_add(out=xi[:], in0=c4[:, 2:3], in1=c4[:, 3:4]),
                                     r=["c4"], w=["xi"])
                                for jb in range(nsub):
                                    for qi in range(4):
                                        S.op("pe", lambda e, jb=jb, qi=qi: e.matmul(
                                            psb[2][:, jb * 32:(jb + 1) * 32], lhsT=qq[qi][:, jb * 128:(jb + 1) * 128],
                                            rhs=Cm[qi][:, kt, :], start=(qi == 0), stop=(qi == 3)),
                                            r=["qq%d" % qi, Ck[qi]], w=[PK[2]])
                                ys_ = ystg[cnt % 2]
                                yk = "ystg%d" % (cnt % 2)
                                cnt += 1
                                S.op("act", lambda e: e.activation(
                                    out=ys_[:, 0:nsub, :], in_=psb[2][:, 0:nsub * 32].rearrange("p (j c) -> p j c",
                                                                                                j=nsub),
                                    func=AF.Copy), r=[PK[2]], w=[yk])
                                S.dma("sp", YS5[d][tau0:tau0 + Ls, kt * 32:(kt + 1) * 32].rearrange(
                                    "(j t) c -> t j c", t=128), ys_[:, 0:nsub, :], r=[yk], w=["YS5"])
                            if not sample:
                                S.op("dve", lambda e: e.tensor_copy(out=finr[:, s_, kt:kt + 1], in_=xr[:]), r=["xr"],
                                     w=["finr"])
                                S.op("dve", lambda e: e.tensor_copy(out=fini[:, s_, kt:kt + 1], in_=xi[:]), r=["xi"],
                                     w=["fini"])
                    if not sample:
                        with nc.allow_non_contiguous_dma(reason="small s5 state outputs"):
                            for s_ in range(nseq):
                                for g2 in range(2):
                                    sl = slice(g2 * 64, (g2 + 1) * 64)
                                    S.dma("sp", O["ns_s5re"][s_, l, d].rearrange("(k g) p -> g p k", g=2)[g2],
                                          finr[sl, s_, :], r=["finr"], w=["ns_s5"])
                                    S.dma("sp", O["ns_s5im"][s_, l, d].rearrange("(k g) p -> g p k", g=2)[g2],
                                          fini[sl, s_, :], r=["fini"], w=["ns_s5"])
                S.barrier()

        def phase_tail(l, T, L, jrow, xsrc, xdst):
            LT = L // 128
            NB = BLK // 128
            with contextlib.ExitStack() as ps:
                lsb = lambda name, shape, dt=F32: ps.enter_context(nc.sbuf_tensor(uname(name), list(shape), dt))
                ls = {"xt": [lsb("xt0", [128, D]), lsb("xt1", [128, D])], "junk": lsb("junk", [128, D]),
                      "ss": lsb("ss", [128, 1])}
                wbs = [lsb("wb0", [128, 8192], BF16), lsb("wb1", [128, 8192], BF16)]
                WSTG["t"] = lsb("wst", [128, 8192])
                hTa = lsb("hTa", [128, KC, BLK], BF16)
                Ybuf = lsb("Ybuf", [128, NB, D])
                GPt = lsb("GPt", [128, D])
                yt = [lsb("yt%d" % j, [128, 1536]) for j in range(2)]
                s5d = lsb("s5d", [128, 1024])
                gt = [lsb("gt%d" % j, [128, 512]) for j in range(2)]
                vt = [lsb("vt%d" % j, [128, 512]) for j in range(2)]
                mgt = [lsb("mgt%d" % j, [128, 512]) for j in range(2)]
                xg = lsb("xg", [128, 512])
                tg = lsb("tg", [128, 512])
                bcload(GPt[:], "GPt", MOD[jrow, 2 * D:3 * D])
                bcload(ls["junk"][:], "junk", I["norm_mix_post"][l, :])
                S.op("dve", lambda e: e.tensor_mul(out=GPt[:], in0=GPt[:], in1=ls["junk"][:]), r=["GPt", "junk"],
                     w=["GPt"])
                bcload(s5d[:], "s5d", I["s5_d"][l, :])
                ec = [0]

                def epi_merge(bi, t0):
                    def f(i, c0, cw, pt, pk):
                        n = ec[0] % 2
                        ec[0] += 1
                        rs_ = slice(t0 + i * 128, t0 + (i + 1) * 128)
                        g_, gk = gt[n], "gt%d" % n
                        v_, vk = vt[n], "vt%d" % n
                        m_, mk = mgt[n], "mgt%d" % n
                        gc = C_GATE + bi * D + c0
                        S.dma("sp", g_[:, 0:cw], P[rs_, gc:gc + cw], r=["P"], w=[gk])
                        S.op("act", lambda e: e.activation(out=g_[:, 0:cw], in_=g_[:, 0:cw], func=AF.Sigmoid), r=[gk],
                             w=[gk])
                        if bi == 2:
                            S.op("act", lambda e: e.activation(out=v_[:, 0:cw], in_=pt[:, cw:2 * cw],
                                                               func=AF.Sigmoid), r=[pk], w=[vk])
                            S.op("dve", lambda e: e.tensor_mul(out=v_[:, 0:cw], in0=pt[:, 0:cw], in1=v_[:, 0:cw]),
                                 r=[pk, vk], w=[vk])
                            S.op("dve", lambda e: e.tensor_mul(out=v_[:, 0:cw], in0=v_[:, 0:cw], in1=g_[:, 0:cw]),
                                 r=[vk, gk], w=[vk])
                        else:
                            S.op("dve", lambda e: e.tensor_mul(out=v_[:, 0:cw], in0=pt[:, 0:cw], in1=g_[:, 0:cw]),
                                 r=[pk, gk], w=[vk])
                        mgk = ("MG", i, c0 // 512)
                        if bi > 0:
                            S.dma("sp", m_[:, 0:cw], MG[rs_, c0:c0 + cw], r=[mgk], w=[mk])
                            S.op("dve", lambda e: e.tensor_add(out=v_[:, 0:cw], in0=v_[:, 0:cw], in1=m_[:, 0:cw]),
                                 r=[vk, mk], w=[vk])
                        S.dma("sp", MG[rs_, c0:c0 + cw], v_[:, 0:cw], r=[vk], w=[mgk])
                    return f

                def wload_glu(Wflat):
                    def f(wb, wk, c0, cw):
                        wst = WSTG["t"]
                        kch = wb.shape[1]
                        n = 128 * kch * 2 * cw
                        off = 128 * kch * 2 * c0
                        stv = wst[:, 0:kch * 2 * cw].rearrange("p (k n) -> p k n", k=kch)
                        S.dma("sp", stv, Wflat[off:off + n].rearrange("(p k n) -> p k n", p=128, k=kch), w=["wst"])
                        S.op("dve", lambda e: e.tensor_copy(out=wb, in_=stv), r=["wst"], w=[wk])
                    return f

                def epi_y(i, c0, cw, pt, pk):
                    S.op("act", lambda e: e.activation(out=Ybuf[:, i, c0:c0 + cw], in_=pt[:, 0:cw], func=AF.Copy),
                         r=[pk], w=["Ybuf"])

                for bi_ in range(T // BLK):
                    t0 = bi_ * BLK
                    for (br, src, Wn) in ((0, YS, "w_ssm_out"), (1, ZW, "w_wkv_out")):
                        for i in range(NB):
                            y_ = yt[i % 2]
                            yk = "yt%d" % (i % 2)
                            S.dma("sp", y_[:], src[t0 + i * 128:t0 + (i + 1) * 128, :], r=["BSRC"], w=[yk])
                            transposes_to(y_[:], yk, 12, lambda q, nj, i=i: hTa[:, q:q + nj, i * 128:(i + 1) * 128],
                                          "hTa")
                        proj(wbs, hTa, "hTa", 12, D, NB, epi_merge(br, t0), wload_plain(I[Wn][l]))
                    for i in range(NB):
                        ta = t0 + i * 128
                        s_ = ta // L
                        ti = (ta % L) // 128
                        tr = s_ * L + (LT - 1 - ti) * 128
                        yf = ls["xt"][0]
                        yb = ls["xt"][1]
                        uu = yt[i % 2]
                        uk = "yt%d" % (i % 2)
                        S.dma("sp", yf[:, 0:1024], YS5[0][ta:ta + 128, :], r=["YS5"], w=["xt0"])
                        S.dma("sp", yb[:, 0:1024], YS5[1][tr:tr + 128, :], r=["YS5"], w=["xt1"])
                        S.dma("sp", uu[:, 0:1024], P[ta:ta + 128, C_U:C_U + 1024], r=["P"], w=[uk])
                        S.op("dve", lambda e: e.tensor_mul(out=uu[:, 0:1024], in0=uu[:, 0:1024], in1=s5d[:]),
                             r=[uk, "s5d"], w=[uk])
                        for hb in range(2):
                            pb = 4 + hb
                            for j in range(4):
                                cb = hb * 4 + j
                                cs_ = slice(cb * 128, (cb + 1) * 128)
                                o_ = psb[pb][:, j * 128:(j + 1) * 128]
                                S.op("pe", lambda e: e.matmul(o_, lhsT=yf[:, cs_], rhs=ident[:], start=True,
                                                              stop=False), r=["xt0", "ident"], w=[PK[pb]])
                                S.op("pe", lambda e: e.matmul(o_, lhsT=yb[:, cs_], rhs=jrev[:], start=False,
                                                              stop=False), r=["xt1", "jrev"], w=[PK[pb]])
                                S.op("pe", lambda e: e.matmul(o_, lhsT=uu[:, cs_], rhs=ident[:], start=False,
                                                              stop=True), r=[uk, "ident"], w=[PK[pb]])
                            S.op("act", lambda e: e.activation(out=xg[:], in_=psb[pb][:, :], func=AF.Copy),
                                 r=[PK[pb]], w=["xg"])
                            S.op("dve", lambda e: e.tensor_mul(out=tg[:], in0=xg[:], in1=xg[:]), r=["xg"], w=["tg"])
                            S.op("dve", lambda e: e.tensor_scalar(out=tg[:], in0=tg[:], scalar1=0.044715, scalar2=1.0,
                                                                  op0=ALU.mult, op1=ALU.add), r=["tg"], w=["tg"])
                            S.op("dve", lambda e: e.tensor_mul(out=tg[:], in0=tg[:], in1=xg[:]), r=["tg", "xg"],
                                 w=["tg"])
                            S.op("act", lambda e: e.activation(out=tg[:], in_=tg[:], func=AF.Sigmoid,
                                                               scale=2.0 * math.sqrt(2.0 / math.pi)), r=["tg"],
                                 w=["tg"])
                            S.op("dve", lambda e: e.tensor_tensor(
                                out=hTa[:, hb * 4:(hb + 1) * 4, i * 128:(i + 1) * 128],
                                in0=xg[:].rearrange("p (j t) -> p j t", j=4),
                                in1=tg[:].rearrange("p (j t) -> p j t", j=4), op=ALU.mult), r=["xg", "tg"], w=["hTa"])
                    proj(wbs, hTa, "hTa", 8, D, NB, epi_merge(2, t0), wload_glu(I["w_s5_glu"][l]), cwmax=256, wmul=2)
                    for i in range(NB):
                        xt = ls["xt"][i % 2]
                        xk = "xt%d" % (i % 2)
                        S.dma("sp", xt[:], MG[t0 + i * 128:t0 + (i + 1) * 128, :],
                              r=[("MG", i, c) for c in range(4)],
                              w=[xk])
                        transposes_to(xt[:], xk, KC, lambda q, nj, i=i: hTa[:, q:q + nj, i * 128:(i + 1) * 128],
                                      "hTa")
                    proj(wbs, hTa, "hTa", KC, D, NB, epi_y, wload_plain(I["w_out"][l]))
                    postnorm_residual(ls, Ybuf, NB, t0, xsrc, xdst, GPt)
                S.barrier()

        def phase_ffn(l, T, GW, jrow, xsrc, xdst):
            NB = BLK // 128
            with contextlib.ExitStack() as ps:
                lsb = lambda name, shape, dt=F32: ps.enter_context(nc.sbuf_tensor(uname(name), list(shape), dt))
                ls = {"xt": [lsb("xt0", [128, D]), lsb("xt1", [128, D])], "junk": lsb("junk", [128, D]),
                      "ss": lsb("ss", [128, 1])}
                wbs = [lsb("wb0", [128, 8192], BF16), lsb("wb1", [128, 8192], BF16)]
                WSTG["t"] = lsb("wst", [128, 8192])
                st = [lsb("st0", [128, 512]), lsb("st1", [128, 512])]
                BI = min(BLK_IN, T)
                hT = lsb("hT", [128, KC, BI], BF16)
                Gt, SHt, GPt = mod_tiles(lsb, ls, l, jrow, 4, 3, 5, "norm_ffn_pre", "norm_ffn_post")
                cnt = [0]
                for bi in range(T // BI):
                    t0 = bi * BI
                    norm_to_fm(ls, xsrc, t0, BI // 128, hT, Gt, SHt)

                    def epi(i, c0, cw, pt, pk, t0=t0):
                        cnt[0] += 1
                        s_ = st[cnt[0] % 2]
                        sk = "st%d" % (cnt[0] % 2)
                        S.op("act", lambda e: e.activation(out=s_[:, 0:cw], in_=pt[:, 0:cw], func=AF.Copy),
                             r=[pk], w=[sk])
                        S.dma("sp", GU[t0 + i * 128:t0 + (i + 1) * 128, c0:c0 + cw], s_[:, 0:cw], r=[sk], w=["GU"])

                    proj(wbs, hT, "hT", KC, 2 * D_FF, BI // 128, epi, wload_plain(I["w_ffn_in"][l]))
                S.barrier()
            with contextlib.ExitStack() as ps:
                up = [ps.enter_context(nc.sbuf_tensor(uname("fup%d" % j), [128, 512], F32)) for j in range(2)]
                sg = ps.enter_context(nc.sbuf_tensor(uname("fsg"), [128, 512], F32))
                uc = [0]

                def post(i, c0, cw, y, yk):
                    u_ = up[uc[0] % 2]
                    uk = "fup%d" % (uc[0] % 2)
                    uc[0] += 1
                    S.dma("sp", u_[:, 0:cw], GU[i * 128:(i + 1) * 128, D_FF + c0:D_FF + c0 + cw], r=["GU"], w=[uk])
                    S.op("act", lambda e: e.activation(out=sg[:, 0:cw], in_=y[:, 0:cw], func=AF.Sigmoid), r=[yk],
                         w=["fsg"])
                    S.op("dve", lambda e: e.tensor_mul(out=y[:, 0:cw], in0=y[:, 0:cw], in1=sg[:, 0:cw]),
                         r=[yk, "fsg"], w=[yk])
                    S.op("dve", lambda e: e.tensor_mul(out=y[:, 0:cw], in0=y[:, 0:cw], in1=u_[:, 0:cw]),
                         r=[yk, uk], w=[yk])
                    S.dma("sp", ACTS[i * 128:(i + 1) * 128, c0:c0 + cw], y[:, 0:cw], r=[yk], w=["ACTS"])

                conv_pass(ps, T, GW, GU, 0, D_FF,
                          lambda c0, cw: (I["ffn_conv_w"][l, 0, c0:c0 + cw], I["ffn_conv_w"][l, 1, c0:c0 + cw],
                                          I["ffn_conv_w"][l, 2, c0:c0 + cw], I["ffn_conv_b"][l, c0:c0 + cw]), post)
                S.barrier()
            with contextlib.ExitStack() as ps:
                lsb = lambda name, shape, dt=F32: ps.enter_context(nc.sbuf_tensor(uname(name), list(shape), dt))
                ls = {"xt": [lsb("xt0", [128, D]), lsb("xt1", [128, D])], "junk": lsb("junk", [128, D]),
                      "ss": lsb("ss", [128, 1])}
                wbs = [lsb("wb0", [128, 5632], BF16), lsb("wb1", [128, 5632], BF16)]
                WSTG["t"] = lsb("wst", [128, 5632])
                hTf = lsb("hTf", [128, 44, BLK], BF16)
                Ybuf = lsb("Ybuf", [128, NB, D])
                GPt = lsb("GPt", [128, D])
                at = [lsb("at%d" % j, [128, 2816]) for j in range(2)]
                bcload(GPt[:], "GPt", MOD[jrow, 5 * D:6 * D])
                bcload(ls["junk"][:], "junk", I["norm_ffn_post"][l, :])
                S.op("dve", lambda e: e.tensor_mul(out=GPt[:], in0=GPt[:], in1=ls["junk"][:]), r=["GPt", "junk"],
                     w=["GPt"])

                def epi_y(i, c0, cw, pt, pk):
                    S.op("act", lambda e: e.activation(out=Ybuf[:, i, c0:c0 + cw], in_=pt[:, 0:cw], func=AF.Copy),
                         r=[pk], w=["Ybuf"])

                ac = 0
                for bi in range(T // BLK):
                    t0 = bi * BLK
                    for i in range(NB):
                        for hf in range(2):
                            a_ = at[ac % 2]
                            ak = "at%d" % (ac % 2)
                            ac += 1
                            S.dma("sp", a_[:], ACTS[t0 + i * 128:t0 + (i + 1) * 128, hf * 2816:(hf + 1) * 2816],
                                  r=["ACTS"], w=[ak])
                            transposes_to(a_[:], ak, 22,
                                          lambda q, nj, i=i, hf=hf: hTf[:, hf * 22 + q:hf * 22 + q + nj,
                                                                        i * 128:(i + 1) * 128], "hTf")
                    proj(wbs, hTf, "hTf", 44, D, NB, epi_y, wload_plain(I["w_ffn_out"][l]), cwmax=128)
                    postnorm_residual(ls, Ybuf, NB, t0, xsrc, xdst, GPt)
                S.barrier()

        kb.fns = dict(phase_mod=phase_mod, phase_inproj=phase_inproj, phase_convs=phase_convs, phase_ssd=phase_ssd)

        groups = debug.get("groups", ["s", "p"])
        skip = debug.get("skip", ())
        nl = debug.get("layers", DEPTH)
        for l in range(nl):
            phase_mod(l)
            for gname in groups:
                last = (l == DEPTH - 1)
                if gname == "s":
                    T, L, GW, jrow, sample = TS, TS, 64, 0, True
                    xin = I["xs"] if l == 0 else XB
                    xmid = XA
                    xout = O["ys"] if last else XB
                else:
                    T, L, GW, jrow, sample = TP, 256, 256, 1, False
                    xin = I["xp"] if l == 0 else XPB
                    xmid = XPA
                    xout = O["yp"] if last else XPB
                phase_inproj(l, xin, T, jrow)
                phase_convs(l, T, GW)
                if "ssd" not in skip:
                    phase_ssd(l, T, L, sample)
                if "wkv" not in skip:
                    wp = debug.get("wkv_parts", ("prep", "scan", "post"))
                    if "prep" in wp:
                        phase_wkv_prep(l, T)
                    if "scan" in wp:
                        phase_wkv_scan(l, T, L, sample)
                    if "post" in wp:
                        phase_wkv_post(l, T)
                if "s5" not in skip:
                    phase_s5(l, T, L, sample)
                if "tail" not in skip:
                    phase_tail(l, T, L, jrow, xin, xmid)
                if "ffn" not in skip:
                    phase_ffn(l, T, GW, jrow, xmid, xout)
        S.barrier()
    return nc, S


def _consts():
    k = np.arange(128)
    tri = (k[:, None] <= k[None, :]).astype(np.float32)
    m48 = np.zeros((48, 768), np.float32)
    for j in range(4):
        for g in range(12):
            m48[j * 12 + g, g * 64:(g + 1) * 64] = 1.0
    return {
        "ident": np.eye(128, dtype=np.float32), "tri": tri, "trit": np.ascontiguousarray(tri.T),
        "jrev": np.ascontiguousarray(np.eye(128, dtype=np.float32)[::-1]), "mask48": m48,
        "jj": np.ascontiguousarray(np.broadcast_to(np.arange(1, 513, dtype=np.float32), (128, 512))),
        "zrow": np.zeros((1, 512), np.float32),
    }


def _prep_inputs(inputs):
    f = lambda a: np.ascontiguousarray(np.asarray(a, dtype=np.float32))
    shared = {}
    for nm, shp in PARAM_SHAPES.items():
        a = f(inputs[nm]).reshape(shp)
        if nm in RELAID:
            a = np.stack([_relayout(a[l], RELAID[nm][2]) for l in range(DEPTH)], 0)
        elif nm == "w_s5_glu":
            a = np.stack([_relayout_glu(a[l]) for l in range(DEPTH)], 0)
        shared[nm] = a
    shared.update(_consts())
    maps = []
    for c in range(8):
        b = c % 4
        m = dict(shared)
        m["xs"] = f(inputs["x_sample"][b])
        m["xp"] = f(np.asarray(inputs["x_prompt"])[NSP * c:NSP * c + NSP].reshape(TP, D))
        m["cond2"] = f(np.stack([np.asarray(inputs["c"])[b], np.asarray(inputs["c_ctx"])], 0))
        m["st_ssm"] = f(inputs["state_ssm"][b])
        m["st_wkv"] = f(inputs["state_wkv"][b])
        m["st_s5re"] = f(inputs["state_s5_re"][b])
        m["st_s5im"] = f(inputs["state_s5_im"][b])
        maps.append(m)
    return maps


def kernel(**inputs):
    nc, S = build()
    maps = _prep_inputs(inputs)
    res = run_bass_kernel_spmd(nc, maps, core_ids=list(range(8)))
    r = res.results
    y_prompt = np.zeros((16, 256, D), np.float32)
    y_sample = np.zeros((4, TS, D), np.float32)
    ns_ssm = np.zeros((16, DEPTH, 2, 24, 64, 128), np.float32)
    ns_wkv = np.zeros((16, DEPTH, 2, 24, 64, 64), np.float32)
    ns_re = np.zeros((16, DEPTH, 2, 64, 64), np.float32)
    ns_im = np.zeros((16, DEPTH, 2, 64, 64), np.float32)
    for b in range(4):
        y_sample[b] = np.asarray(r[b]["ys"])
    for c in range(8):
        o = r[c]
        sl = slice(NSP * c, NSP * c + NSP)
        y_prompt[sl] = np.asarray(o["yp"]).reshape(NSP, 256, D)
        ns_ssm[sl] = np.asarray(o["ns_ssm"])
        ns_wkv[sl] = np.asarray(o["ns_wkv"])
        ns_re[sl] = np.asarray(o["ns_s5re"])
        ns_im[sl] = np.asarray(o["ns_s5im"])
    return (y_prompt, y_sample, ns_ssm, ns_wkv, ns_re, ns_im)
```

```python
import contextlib
import math
import numpy as np
import concourse.bass as bass
import concourse.mybir as mybir
import concourse.ap as apm
from concourse.bass_utils import run_bass_kernel_spmd

F32 = mybir.dt.float32
BF16 = mybir.dt.bfloat16
I32 = mybir.dt.int32
AF = mybir.ActivationFunctionType
ALU = mybir.AluOpType
AX = mybir.AxisListType

D = 2048
KC = 16
DEPTH = 2
N_IN = 16560
D_FF = 5632
TS = 4096
NSP = 2
TP = NSP * 256
BLK = 512
BLK_IN = 2048
EPS = 1e-6
TWO_PI = 2.0 * math.pi
SIN_SCALE = 6.28318

C_Z = 0
C_XBC = 1536
C_DTF = 4096
C_WKV = 4144
C_U = 9392
C_GATE = 10416
NWKV = 5248


class Sched:
    def __init__(self, nc, es, n_dma_sems=24):
        self.nc = nc
        self.eng = {"pe": nc.tensor, "act": nc.scalar, "dve": nc.vector, "pool": nc.gpsimd, "sp": nc.sync}
        self.sem = {e: es.enter_context(nc.semaphore("sem_" + e)) for e in ("pe", "act", "dve", "pool")}
        self.cnt = {e: 0 for e in self.sem}
        self.dsem = [es.enter_context(nc.semaphore("dsem%d" % i)) for i in range(n_dma_sems)]
        self.dval = [0] * n_dma_sems
        self.dnext = 0
        self.waited = {e: {} for e in self.eng}
        self.lastw = {}
        self.readers = {}
        self.ninst = 0

    def _wait(self, e, tok, force=False):
        sem, val, src = tok
        w = self.waited[e]
        if w.get(id(sem), 0) >= val:
            return
        if src == e == "pe" and not force:
            return
        self.eng[e].wait_ge(sem, val)
        w[id(sem)] = val

    def _deps(self, e, r, w):
        for k in list(r) + list(w):
            t = self.lastw.get(k)
            if t is not None:
                self._wait(e, t)
        for k in w:
            for t in self.readers.get(k, ()):
                self._wait(e, t)

    def _commit(self, tok, r, w):
        for k in w:
            self.lastw[k] = tok
            self.readers[k] = []
        for k in r:
            lst = self.readers.setdefault(k, [])
            lst.append(tok)
            if len(lst) > 64:
                best = {}
                for t in lst:
                    if id(t[0]) not in best or best[id(t[0])][1] < t[1]:
                        best[id(t[0])] = t
                self.readers[k] = list(best.values())

    def op(self, e, fn, r=(), w=()):
        self._deps(e, r, w)
        ins = fn(self.eng[e])
        self.cnt[e] += 1
        ins.then_inc(self.sem[e], 1)
        tok = (self.sem[e], self.cnt[e], e)
        self._commit(tok, r, w)
        self.ninst += 1
        return tok

    def dma(self, q, out, in_, r=(), w=(), **kw):
        i = self.dnext
        self.dnext = (self.dnext + 1) % len(self.dsem)
        sem = self.dsem[i]
        if self.dval[i] > 0:
            self._wait(q, (sem, self.dval[i], "dma"))
        self._deps(q, r, w)
        ins = self.eng[q].dma_start(out=out, in_=in_, **kw)
        self.dval[i] += 16
        ins.then_inc(sem, 16)
        tok = (sem, self.dval[i], "dma")
        self._commit(tok, r, w)
        self.ninst += 1
        return tok

    def barrier(self):
        toks = [(self.sem[e], self.cnt[e], e) for e in self.sem if self.cnt[e] > 0]
        toks += [(self.dsem[i], self.dval[i], "dma") for i in range(len(self.dsem)) if self.dval[i] > 0]
        for e in self.eng:
            for t in toks:
                self._wait(e, t, force=True)
        self.lastw.clear()
        self.readers.clear()


class KB:
    def __init__(self, debug=None):
        self.debug = debug or {}
        self.nc = bass.Bass("TRN2", target_bir_lowering=False)
        self.I = {}
        self.O = {}

    def din(self, name, shape, dt=F32):
        self.I[name] = self.nc.dram_tensor(name, list(shape), dt, kind="ExternalInput").ap()
        return self.I[name]

    def dout(self, name, shape, dt=F32):
        self.O[name] = self.nc.dram_tensor(name, list(shape), dt, kind="ExternalOutput").ap()
        return self.O[name]

    def dscr(self, name, shape, dt=F32):
        kind = "ExternalOutput" if name in self.debug.get("dump", ()) else "Internal"
        return self.nc.dram_tensor(name, list(shape), dt, kind=kind).ap()


PARAM_SHAPES = {
    "w_mod": (DEPTH, D, 6 * D), "b_mod": (DEPTH, 6 * D),
    "norm_mix_pre": (DEPTH, D), "norm_mix_post": (DEPTH, D), "norm_ffn_pre": (DEPTH, D), "norm_ffn_post": (DEPTH, D),
    "w_in": (DEPTH, D, N_IN),
    "ssm_conv_w": (DEPTH, 3, 2560), "ssm_conv_b": (DEPTH, 2560), "ssm_dt_bias": (DEPTH, 48),
    "ssm_a_log": (DEPTH, 48), "ssm_d": (DEPTH, 24), "ssm_norm": (DEPTH, 1536), "w_ssm_out": (DEPTH, 1536, D),
    "wkv_mu_prev": (DEPTH, NWKV), "wkv_mu_next": (DEPTH, NWKV), "wkv_w0": (DEPTH, 2, 1536),
    "wkv_w_up": (DEPTH, 2, 96, 1536), "wkv_a0": (DEPTH, 2, 1536), "wkv_a_up": (DEPTH, 2, 96, 1536),
    "wkv_g_up": (DEPTH, 256, 1536), "wkv_k_k": (DEPTH, 1536), "wkv_k_a": (DEPTH, 1536), "wkv_r_k": (DEPTH, 1536),
    "wkv_ln_w": (DEPTH, 1536), "wkv_ln_b": (DEPTH, 1536), "w_wkv_out": (DEPTH, 1536, D),
    "s5_lam_re": (DEPTH, 2, 64, 64), "s5_lam_im": (DEPTH, 2, 64, 64), "s5_log_dt": (DEPTH, 2, 64),
    "s5_b_re": (DEPTH, 2, 64, 64, 16), "s5_b_im": (DEPTH, 2, 64, 64, 16),
    "s5_c_re": (DEPTH, 2, 64, 16, 64), "s5_c_im": (DEPTH, 2, 64, 16, 64), "s5_d": (DEPTH, 1024),
    "w_s5_glu": (DEPTH, 1024, 2 * D), "w_out": (DEPTH, D, D), "w_ffn_in": (DEPTH, D, 2 * D_FF),
    "ffn_conv_w": (DEPTH, 3, D_FF), "ffn_conv_b": (DEPTH, D_FF), "w_ffn_out": (DEPTH, D_FF, D),
}


RELAID = {"w_mod": (D, 6 * D, 512), "w_in": (D, N_IN, 512), "w_ssm_out": (1536, D, 512), "w_wkv_out": (1536, D, 512),
          "w_out": (D, D, 512), "w_ffn_in": (D, 2 * D_FF, 512), "w_ffn_out": (D_FF, D, 128)}


def _relayout(W, cb):
    K_, N_ = W.shape
    kch = K_ // 128
    out = np.empty(K_ * N_, np.float32)
    off = 0
    for c0 in range(0, N_, cb):
        cw = min(cb, N_ - c0)
        t = W[:, c0:c0 + cw].reshape(kch, 128, cw).transpose(1, 0, 2)
        out[off:off + K_ * cw] = t.reshape(-1)
        off += K_ * cw
    return out


def _relayout_glu(W):
    K_ = W.shape[0]
    out = np.empty(W.size, np.float32)
    off = 0
    for c0 in range(0, D, 256):
        t = np.concatenate([W[:, c0:c0 + 256], W[:, D + c0:D + c0 + 256]], axis=1)
        t = t.reshape(K_ // 128, 128, 512).transpose(1, 0, 2)
        out[off:off + K_ * 512] = t.reshape(-1)
        off += K_ * 512
    return out


def build(debug=None):
    debug = debug or {}
    kb = KB(debug)
    nc = kb.nc
    I = kb.I
    O = kb.O
    es = contextlib.ExitStack()

    kb.din("xs", [TS, D])
    kb.din("xp", [TP, D])
    kb.din("cond2", [2, D])
    kb.din("st_ssm", [DEPTH, 2, 24, 64, 128])
    kb.din("st_wkv", [DEPTH, 2, 24, 64, 64])
    kb.din("st_s5re", [DEPTH, 2, 64, 64])
    kb.din("st_s5im", [DEPTH, 2, 64, 64])
    for nm, shp in PARAM_SHAPES.items():
        if nm in RELAID or nm == "w_s5_glu":
            kb.din(nm, [DEPTH, int(np.prod(shp[1:]))])
        else:
            kb.din(nm, shp)
    kb.din("ident", [128, 128])
    kb.din("tri", [128, 128])
    kb.din("trit", [128, 128])
    kb.din("jrev", [128, 128])
    kb.din("mask48", [48, 768])
    kb.din("jj", [128, 512])
    kb.din("zrow", [1, 512])

    kb.dout("ys", [TS, D])
    kb.dout("yp", [TP, D])
    kb.dout("ns_ssm", [NSP, DEPTH, 2, 24, 64, 128])
    kb.dout("ns_wkv", [NSP, DEPTH, 2, 24, 64, 64])
    kb.dout("ns_s5re", [NSP, DEPTH, 2, 64, 64])
    kb.dout("ns_s5im", [NSP, DEPTH, 2, 64, 64])

    MOD = kb.dscr("MOD", [2, 6 * D])
    PA = kb.dscr("PA", [TS, C_U])
    PB = kb.dscr("PB", [TS, N_IN - C_U])

    class _P:
        def __getitem__(self, key):
            rs, cs = key
            c0, c1 = cs.start, cs.stop
            if c1 <= C_U:
                return PA[rs, c0:c1]
            assert c0 >= C_U, (c0, c1)
            return PB[rs, c0 - C_U:c1 - C_U]
    P = _P()
    GU = kb.dscr("GU", [TS, 2 * D_FF])
    XA = kb.dscr("XA", [TS, D])
    XB = kb.dscr("XB", [TS, D])
    XPA = kb.dscr("XPA", [TP, D])
    XPB = kb.dscr("XPB", [TP, D])
    MG = kb.dscr("MG", [TS, D])
    XC = kb.dscr("XC", [TS, 2560])
    SH = kb.dscr("SH", [TS, NWKV])
    BCT = kb.dscr("BCT", [TS // 128, 128, 8, 128], BF16)
    DT = kb.dscr("DT", [TS, 48])
    YS = kb.dscr("YS", [TS, 1536])
    ZW = kb.dscr("ZW", [TS, 1536])
    BD = [kb.dscr("BD%d" % d, [TS, 1536], BF16) for d in range(2)]
    KD = [kb.dscr("KD%d" % d, [TS, 1536], BF16) for d in range(2)]
    BRKR = [kb.dscr("BRKR%d" % d, [TS, 48]) for d in range(2)]
    WT = [kb.dscr("WT%d" % d, [128, 12, TS]) for d in range(2)]
    WRT = [kb.dscr("WRT%d" % d, [128, 12, TS]) for d in range(2)]
    NKKT = kb.dscr("NKKT", [128, 12, TS])
    GG = kb.dscr("GG", [TS, 1536])
    VBD = [kb.dscr("VBD%d" % d, [TS, 12, 768], BF16) for d in range(2)]
    RK = kb.dscr("RK", [TS, 24])
    SAY = [kb.dscr("SAY%d" % d, [48, TS, 64]) for d in range(2)]
    UT2 = [kb.dscr("UT2_%d" % d, [32, 32, TS]) for d in range(2)]
    YS5 = [kb.dscr("YS5_%d" % d, [TS, 1024]) for d in range(2)]
    ACTS = kb.dscr("ACTS", [TS, D_FF])

    with es:
        S = Sched(nc, es)
        kb.S = S
        gsb = lambda name, shape, dt=F32: es.enter_context(nc.sbuf_tensor(name, list(shape), dt))
        psb = [es.enter_context(nc.psum_tensor("psb%d" % i, [128, 512], F32)) for i in range(8)]
        PK = ["psb%d" % i for i in range(8)]
        ident = gsb("ident_sb", [128, 128])
        tri = gsb("tri_sb", [128, 128])
        trit = gsb("trit_sb", [128, 128])
        jrev = gsb("jrev_sb", [128, 128])
        ones = gsb("ones_sb", [128, 128])
        S.dma("sp", ident[:], I["ident"][:, :], w=["ident"])
        S.dma("sp", tri[:], I["tri"][:, :], w=["tri"])
        S.dma("sp", trit[:], I["trit"][:, :], w=["trit"])
        S.dma("sp", jrev[:], I["jrev"][:, :], w=["jrev"])
        S.op("dve", lambda e: e.memset(ones[:], 1.0), w=["ones"])

        _uid = [0]

        def uname(n):
            _uid[0] += 1
            return "%s_%d" % (n, _uid[0])

        def bcload(dst, key, src1d, q="sp"):
            S.dma(q, dst, src1d.partition_broadcast(dst.shape[0]), r=["MOD"], w=[key])

        def transposes_to(src, skey, nblk, dst_fn, dkey, bw=128, pbanks=(6, 7), eng="act", scale=None, rhs=None,
                          rkey=None, inw=128):
            per = 512 // inw
            for q in range(0, nblk, per):
                nj = min(per, nblk - q)
                bi = pbanks[(q // per) % len(pbanks)]
                pt = psb[bi]
                for j in range(nj):
                    blk = src[:, (q + j) * bw:(q + j + 1) * bw]
                    if rhs is None:
                        S.op("pe", lambda e, j=j, blk=blk: e.transpose(pt[0:bw, j * inw:(j + 1) * inw], blk,
                                                                        ident[0:inw, 0:inw]),
                             r=[skey, "ident"], w=[PK[bi]])
                    else:
                        S.op("pe", lambda e, j=j, blk=blk: e.matmul(pt[0:bw, j * inw:(j + 1) * inw], lhsT=blk,
                                                                     rhs=rhs, start=True, stop=True),
                             r=[skey, rkey], w=[PK[bi]])
                src_ps = pt[0:bw, 0:nj * inw].rearrange("p (j t) -> p j t", j=nj)
                dst = dst_fn(q, nj)
                if eng == "act":
                    if scale is None:
                        S.op("act", lambda e: e.activation(out=dst, in_=src_ps, func=AF.Copy), r=[PK[bi]], w=[dkey])
                    else:
                        S.op("act", lambda e: e.activation(out=dst, in_=src_ps, func=AF.Copy, scale=scale),
                             r=[PK[bi]], w=[dkey])
                else:
                    S.op("dve", lambda e: e.tensor_copy(out=dst, in_=src_ps), r=[PK[bi]], w=[dkey])

        def phase_mod(l):
            with contextlib.ExitStack() as ps:
                lsb = lambda name, shape, dt=F32: ps.enter_context(nc.sbuf_tensor(uname(name), list(shape), dt))
                cT = lsb("cT", [128, 2, KC])
                sg = lsb("sg", [128, 2, KC])
                wm = [lsb("wm%d" % i, [128, KC, 512]) for i in range(2)]
                bm = lsb("bm", [1, 6 * D])
                mo = lsb("mo", [2, 6 * D])
                with nc.allow_non_contiguous_dma(reason="tiny cond transpose"):
                    S.dma("sp", cT[:], I["cond2"].rearrange("j (k p) -> p j k", p=128), w=["cT"])
                S.dma("sp", bm[:], I["b_mod"][l:l + 1, :], w=["bm"])
                S.op("act", lambda e: e.activation(out=sg[:], in_=cT[:], func=AF.Sigmoid), r=["cT"], w=["sg"])
                S.op("dve", lambda e: e.tensor_mul(out=sg[:], in0=sg[:], in1=cT[:]), r=["cT", "sg"], w=["sg"])
                for nb in range(24):
                    wb = wm[nb % 2]
                    wk = "wm%d" % (nb % 2)
                    S.dma("sp", wb[:], I["w_mod"][l, nb * 128 * KC * 512:(nb + 1) * 128 * KC * 512].rearrange(
                        "(p k n) -> p k n", p=128, k=KC),
                          w=[wk])
                    pt = psb[nb % 2]
                    pk = PK[nb % 2]
                    for k in range(KC):
                        S.op("pe", lambda e, k=k: e.matmul(pt[0:2, :], lhsT=sg[:, :, k], rhs=wb[:, k, :],
                                                           start=(k == 0), stop=False), r=["sg", wk], w=[pk])
                    S.op("pe", lambda e: e.matmul(pt[0:2, :], lhsT=ones[0:1, 0:2], rhs=bm[:, nb * 512:(nb + 1) * 512],
                                                  start=False, stop=True), r=["ones", "bm"], w=[pk])
                    S.op("act", lambda e: e.activation(out=mo[:, nb * 512:(nb + 1) * 512], in_=pt[0:2, :],
                                                       func=AF.Copy), r=[pk], w=["mo"])
                S.dma("sp", MOD[:, :], mo[:], r=["mo"], w=["MOD"])
                S.barrier()

        def rms_rstd(ss, key, n, eps):
            S.op("dve", lambda e: e.tensor_scalar(out=ss, in0=ss, scalar1=1.0 / n, scalar2=eps, op0=ALU.mult,
                                                  op1=ALU.add), r=[key], w=[key])
            S.op("act", lambda e: e.activation(out=ss, in_=ss, func=AF.Sqrt), r=[key], w=[key])
            S.op("dve", lambda e: e.reciprocal(out=ss, in_=ss), r=[key], w=[key])

        def norm_to_fm(ls, xsrc, t0, ntile, hT, Gt, SHt):
            for i in range(ntile):
                xt = ls["xt"][i % 2]
                xk = "xt%d" % (i % 2)
                S.dma("sp", xt[:], xsrc[t0 + i * 128:t0 + (i + 1) * 128, :], r=["XSRC"], w=[xk])
                S.op("act", lambda e: e.activation(out=ls["junk"][:], in_=xt[:], func=AF.Square,
                                                   accum_out=ls["ss"][:]), r=[xk], w=["junk", "ss"])
                rms_rstd(ls["ss"][:], "ss", D, EPS)
                S.op("dve", lambda e: e.scalar_tensor_tensor(out=ls["junk"][:], in0=xt[:], scalar=ls["ss"][:, 0:1],
                                                             in1=Gt[:], op0=ALU.mult, op1=ALU.mult),
                     r=[xk, "ss", "Gt"], w=["junk"])
                S.op("dve", lambda e: e.tensor_add(out=ls["junk"][:], in0=ls["junk"][:], in1=SHt[:]),
                     r=["junk", "SHt"], w=["junk"])
                transposes_to(ls["junk"][:], "junk", KC,
                              lambda q, nj, i=i: hT[:, q:q + nj, i * 128:(i + 1) * 128], "hT")

        def proj(wbs, hT, hkey, kchunks, ncols, ntile, epilogue, wload, cwmax=512, wmul=1):
            nb = (ncols + cwmax - 1) // cwmax

            def blk(b):
                c0 = b * cwmax
                cw = min(cwmax, ncols - c0)
                nw = cw * wmul
                wb = wbs[b % 2][:, 0:kchunks * nw].rearrange("p (k n) -> p k n", k=kchunks)
                return c0, cw, nw, wb, "wb%d" % (b % 2)

            c0, cw, nw, wb, wk = blk(0)
            wload(wb, wk, c0, cw)
            for b in range(nb):
                c0, cw, nw, wb, wk = blk(b)
                if b + 1 < nb:
                    c0n, cwn, nwn, wbn, wkn = blk(b + 1)
                    wload(wbn, wkn, c0n, cwn)
                for i in range(ntile):
                    pt = psb[i % 4]
                    pk = PK[i % 4]
                    for k in range(kchunks):
                        S.op("pe", lambda e, k=k: e.matmul(pt[:, 0:nw], lhsT=hT[:, k, i * 128:(i + 1) * 128],
                                                           rhs=wb[:, k, :], start=(k == 0),
                                                           stop=(k == kchunks - 1)), r=[hkey, wk], w=[pk])
                    epilogue(i, c0, cw, pt, pk)

        WSTG = {}

        def wload_plain(Wflat, cb=512):
            def f(wb, wk, c0, cw):
                wst = WSTG["t"]
                kch = wb.shape[1]
                n = 128 * kch * cw
                off = 128 * kch * c0
                stv = wst[:, 0:kch * cw].rearrange("p (k n) -> p k n", k=kch)
                S.dma("sp", stv, Wflat[off:off + n].rearrange("(p k n) -> p k n", p=128, k=kch), w=["wst"])
                S.op("dve", lambda e: e.tensor_copy(out=wb, in_=stv), r=["wst"], w=[wk])
            return f

        def load3(src, sc, cw, t0, GW, bufs, keys):
            prv, cur, nxt = bufs
            kp, kc_, kn = keys
            S.dma("sp", cur[:, 0:cw], src[t0:t0 + 128, sc:sc + cw], r=["CSRC"], w=[kc_])
            a = 0
            while a < 128:
                g0 = t0 + a
                b = min(128, a + GW - (g0 % GW))
                sb_ = (g0 % GW) == 0
                eb_ = ((t0 + b) % GW) == 0
                if sb_:
                    S.dma("sp", prv[a:a + 1, 0:cw], I["zrow"][0:1, 0:cw], w=[kp])
                    if b - a > 1:
                        S.dma("sp", prv[a + 1:b, 0:cw], src[t0 + a:t0 + b - 1, sc:sc + cw], r=["CSRC"], w=[kp])
                else:
                    S.dma("sp", prv[a:b, 0:cw], src[t0 + a - 1:t0 + b - 1, sc:sc + cw], r=["CSRC"], w=[kp])
                if eb_:
                    S.dma("sp", nxt[b - 1:b, 0:cw], I["zrow"][0:1, 0:cw], w=[kn])
                    if b - a > 1:
                        S.dma("sp", nxt[a:b - 1, 0:cw], src[t0 + a + 1:t0 + b, sc:sc + cw], r=["CSRC"], w=[kn])
                else:
                    S.dma("sp", nxt[a:b, 0:cw], src[t0 + a + 1:t0 + b + 1, sc:sc + cw], r=["CSRC"], w=[kn])
                a = b

        def conv_pass(ps, T, GW, src, sc0, ncols, wrows, post, prep=None):
            lsb = lambda name, shape, dt=F32: ps.enter_context(nc.sbuf_tensor(uname(name), list(shape), dt))
            wt = [lsb("cvw%d" % j, [128, 512]) for j in range(4)]
            bufs = [[lsb("cv%s%d" % (n, j), [128, 512]) for n in ("p", "c", "n")] for j in range(2)]
            yb = [lsb("cvy%d" % j, [128, 512]) for j in range(2)]
            tb = lsb("cvt", [128, 512])
            for c0 in range(0, ncols, 512):
                cw = min(512, ncols - c0)
                rows = wrows(c0, cw)
                for j in range(4):
                    if rows[j] is not None:
                        bcload(wt[j][:, 0:cw], "cvw%d" % j, rows[j])
                if prep is not None:
                    prep(wt, cw)
                for i in range(T // 128):
                    bb = bufs[i % 2]
                    keys = ["cv%s%d" % (n, i % 2) for n in ("p", "c", "n")]
                    load3(src, sc0 + c0, cw, i * 128, GW, bb, keys)
                    y = yb[i % 2]
                    yk = "cvy%d" % (i % 2)
                    S.op("dve", lambda e: e.tensor_mul(out=y[:, 0:cw], in0=bb[1][:, 0:cw], in1=wt[1][:, 0:cw]),
                         r=[keys[1], "cvw1"], w=[yk])
                    S.op("dve", lambda e: e.tensor_mul(out=tb[:, 0:cw], in0=bb[0][:, 0:cw], in1=wt[0][:, 0:cw]),
                         r=[keys[0], "cvw0"], w=["cvt"])
                    S.op("dve", lambda e: e.tensor_add(out=y[:, 0:cw], in0=y[:, 0:cw], in1=tb[:, 0:cw]),
                         r=[yk, "cvt"], w=[yk])
                    S.op("dve", lambda e: e.tensor_mul(out=tb[:, 0:cw], in0=bb[2][:, 0:cw], in1=wt[2][:, 0:cw]),
                         r=[keys[2], "cvw2"], w=["cvt"])
                    S.op("dve", lambda e: e.tensor_add(out=y[:, 0:cw], in0=y[:, 0:cw], in1=tb[:, 0:cw]),
                         r=[yk, "cvt"], w=[yk])
                    if rows[3] is not None:
                        S.op("dve", lambda e: e.tensor_add(out=y[:, 0:cw], in0=y[:, 0:cw], in1=wt[3][:, 0:cw]),
                             r=[yk, "cvw3"], w=[yk])
                    post(i, c0, cw, y, yk)

        def postnorm_residual(ls, Ybuf, ntile, t0, xsrc, xdst, GPt):
            for i in range(ntile):
                y = Ybuf[:, i, :]
                S.op("act", lambda e: e.activation(out=ls["junk"][:], in_=y, func=AF.Square, accum_out=ls["ss"][:]),
                     r=["Ybuf"], w=["junk", "ss"])
                rms_rstd(ls["ss"][:], "ss", D, EPS)
                xt = ls["xt"][i % 2]
                xk = "xt%d" % (i % 2)
                S.dma("sp", xt[:], xsrc[t0 + i * 128:t0 + (i + 1) * 128, :], r=["XSRC"], w=[xk])
                S.op("dve", lambda e: e.scalar_tensor_tensor(out=ls["junk"][:], in0=y, scalar=ls["ss"][:, 0:1],
                                                             in1=GPt[:], op0=ALU.mult, op1=ALU.mult),
                     r=["Ybuf", "ss", "GPt"], w=["junk"])
                S.op("dve", lambda e: e.tensor_add(out=xt[:], in0=xt[:], in1=ls["junk"][:]), r=["junk", xk], w=[xk])
                S.dma("sp", xdst[t0 + i * 128:t0 + (i + 1) * 128, :], xt[:], r=[xk], w=["XDST"])

        def mod_tiles(lsb, ls, l, jrow, i_sc, i_sh, i_g, pre, post):
            Gt = lsb("Gt", [128, D])
            SHt = lsb("SHt", [128, D])
            bcload(Gt[:], "Gt", MOD[jrow, i_sc * D:(i_sc + 1) * D])
            bcload(ls["junk"][:], "junk", I[pre][l, :])
            S.op("dve", lambda e: e.scalar_tensor_tensor(out=Gt[:], in0=Gt[:], scalar=1.0, in1=ls["junk"][:],
                                                         op0=ALU.add, op1=ALU.mult), r=["Gt", "junk"], w=["Gt"])
            bcload(SHt[:], "SHt", MOD[jrow, i_sh * D:(i_sh + 1) * D])
            return Gt, SHt, None

        def phase_inproj(l, xsrc, T, jrow):
            with contextlib.ExitStack() as ps:
                lsb = lambda name, shape, dt=F32: ps.enter_context(nc.sbuf_tensor(uname(name), list(shape), dt))
                ls = {"xt": [lsb("xt0", [128, D]), lsb("xt1", [128, D])], "junk": lsb("junk", [128, D]),
                      "ss": lsb("ss", [128, 1])}
                wbs = [lsb("wb0", [128, 8192], BF16), lsb("wb1", [128, 8192], BF16)]
                WSTG["t"] = lsb("wst", [128, 8192])
                BI = min(BLK_IN, T)
                st = [lsb("st0", [128, 512]), lsb("st1", [128, 512])]
                hT = lsb("hT", [128, KC, BI], BF16)
                Gt, SHt, GPt = mod_tiles(lsb, ls, l, jrow, 1, 0, 2, "norm_mix_pre", "norm_mix_post")
                cnt = [0]
                for bi in range(T // BI):
                    t0 = bi * BI
                    norm_to_fm(ls, xsrc, t0, BI // 128, hT, Gt, SHt)

                    def epi(i, c0, cw, pt, pk, t0=t0):
                        cnt[0] += 1
                        s_ = st[cnt[0] % 2]
                        sk = "st%d" % (cnt[0] % 2)
                        S.op("act", lambda e: e.activation(out=s_[:, 0:cw], in_=pt[:, 0:cw], func=AF.Copy),
                             r=[pk], w=[sk])
                        rs_ = slice(t0 + i * 128, t0 + (i + 1) * 128)
                        if c0 < C_U < c0 + cw:
                            m_ = C_U - c0
                            S.dma("sp", P[rs_, c0:C_U], s_[:, 0:m_], r=[sk], w=["P"])
                            S.dma("sp", P[rs_, C_U:c0 + cw], s_[:, m_:cw], r=[sk], w=["P"])
                        else:
                            S.dma("sp", P[rs_, c0:c0 + cw], s_[:, 0:cw], r=[sk], w=["P"])

                    proj(wbs, hT, "hT", KC, debug.get("ncols", N_IN), BI // 128, epi, wload_plain(I["w_in"][l]))
                S.barrier()

        def phase_convs(l, T, GW):
            with contextlib.ExitStack() as ps:
                def post_ssm(i, c0, cw, y, yk):
                    lsg = post_ssm.sg
                    S.op("act", lambda e: e.activation(out=lsg[:, 0:cw], in_=y[:, 0:cw], func=AF.Sigmoid),
                         r=[yk], w=["cvsg"])
                    S.op("dve", lambda e: e.tensor_mul(out=y[:, 0:cw], in0=y[:, 0:cw], in1=lsg[:, 0:cw]),
                         r=[yk, "cvsg"], w=[yk])
                    S.dma("sp", XC[i * 128:(i + 1) * 128, c0:c0 + cw], y[:, 0:cw], r=[yk], w=["XC"])
                post_ssm.sg = ps.enter_context(nc.sbuf_tensor(uname("cvsg"), [128, 512], F32))
                conv_pass(ps, T, GW, P, C_XBC, 2560,
                          lambda c0, cw: (I["ssm_conv_w"][l, 0, c0:c0 + cw], I["ssm_conv_w"][l, 1, c0:c0 + cw],
                                          I["ssm_conv_w"][l, 2, c0:c0 + cw], I["ssm_conv_b"][l, c0:c0 + cw]),
                          post_ssm)
                S.barrier()
            with contextlib.ExitStack() as ps:
                def post_wkv(i, c0, cw, y, yk):
                    S.dma("sp", SH[i * 128:(i + 1) * 128, c0:c0 + cw], y[:, 0:cw], r=[yk], w=["SH"])

                def prep(wt, cw):
                    S.op("dve", lambda e: e.tensor_add(out=wt[1][:, 0:cw], in0=wt[0][:, 0:cw], in1=wt[2][:, 0:cw]),
                         r=["cvw0", "cvw2"], w=["cvw1"])
                    S.op("dve", lambda e: e.tensor_scalar(out=wt[1][:, 0:cw], in0=wt[1][:, 0:cw], scalar1=-1.0,
                                                          scalar2=1.0, op0=ALU.mult, op1=ALU.add),
                         r=["cvw1"], w=["cvw1"])
                conv_pass(ps, T, GW, P, C_WKV, NWKV,
                          lambda c0, cw: (I["wkv_mu_prev"][l, c0:c0 + cw], None, I["wkv_mu_next"][l, c0:c0 + cw],
                                          None), post_wkv, prep=prep)
                S.barrier()

        def mm_cols(ps3, c0, c1, lhsT, rhs_fn, rkeys):
            c = c0
            while c < c1:
                bnk = c // 512
                ce = min(c1, (bnk + 1) * 512)
                S.op("pe", lambda e, c=c, ce=ce, bnk=bnk: e.matmul(psb[ps3[bnk]][:, c - bnk * 512:ce - bnk * 512],
                                                                   lhsT=lhsT, rhs=rhs_fn(c, ce), start=True,
                                                                   stop=True), r=rkeys, w=[PK[ps3[bnk]]])
                c = ce

        def phase_ssd(l, T, L, sample):
            NT = T // 128
            with contextlib.ExitStack() as ps:
                lsb = lambda name, shape, dt=F32: ps.enter_context(nc.sbuf_tensor(uname(name), list(shape), dt))
                bcv = [lsb("sbc%d" % j, [128, 1024]) for j in range(2)]
                bct = [lsb("sbct%d" % j, [128, 8, 128], BF16) for j in range(2)]
                dtt = [lsb("sdt%d" % j, [128, 48]) for j in range(2)]
                dbias = lsb("dbias", [128, 48])
                bcload(dbias[:], "dbias", I["ssm_dt_bias"][l, :])
                for i in range(NT):
                    b_ = bcv[i % 2]
                    bk = "sbc%d" % (i % 2)
                    S.dma("sp", b_[:], XC[i * 128:(i + 1) * 128, 1536:2560], r=["XC"], w=[bk])
                    o_ = bct[i % 2]
                    ok = "sbct%d" % (i % 2)
                    transposes_to(b_[:], bk, 8, lambda q, nj: o_[:, q:q + nj, :], ok)
                    S.dma("sp", BCT[i], o_[:], r=[ok], w=["BCT"])
                    d_ = dtt[i % 2]
                    dk = "sdt%d" % (i % 2)
                    S.dma("sp", d_[:], P[i * 128:(i + 1) * 128, C_DTF:C_DTF + 48], r=["P"], w=[dk])
                    S.op("dve", lambda e: e.tensor_add(out=d_[:], in0=d_[:], in1=dbias[:]), r=[dk, "dbias"], w=[dk])
                    S.op("act", lambda e: e.activation(out=d_[:], in_=d_[:], func=AF.Exp), r=[dk], w=[dk])
                    S.op("act", lambda e: e.activation(out=d_[:], in_=d_[:], func=AF.Ln, bias=1.0), r=[dk], w=[dk])
                    S.dma("sp", DT[i * 128:(i + 1) * 128, :], d_[:], r=[dk], w=["DT"])
                S.barrier()
            with contextlib.ExitStack() as ps:
                lsb = lambda name, shape, dt=F32: ps.enter_context(nc.sbuf_tensor(uname(name), list(shape), dt))
                xc = [lsb("xc%d" % j, [128, 2560]) for j in range(2)]
                bct = [lsb("bct%d" % j, [128, 8, 128], BF16) for j in range(2)]
                dtt = [lsb("dt%d" % j, [128, 24]) for j in range(2)]
                Abc = lsb("Abc", [128, 24])
                Dbc = lsb("Dbc", [128, 24])
                nrm = lsb("nrm", [128, 1536])
                dtA = lsb("dtA", [128, 24])
                acol = lsb("acol", [128, 24])
                tot = lsb("tot", [128, 24])
                cd = lsb("cd", [128, 24])
                ea = lsb("ea", [128, 24])
                dte = lsb("dte", [128, 24])
                xdt = lsb("xdt", [128, 1536], BF16)
                xw = lsb("xw", [128, 1536], BF16)
                bB = lsb("bB", [128, 512], BF16)
                Gm = lsb("Gm", [128, 512])
                dd = [lsb("dd%d" % j, [128, 512]) for j in range(2)]
                wt = [lsb("wt%d" % j, [128, 4, 128], BF16) for j in range(2)]
                HT = lsb("HT", [128, 1536])
                HTb = lsb("HTb", [128, 1536], BF16)
                yo = lsb("yo", [128, 1536])
                yy = lsb("yy", [128, 1536])
                zz = lsb("zz", [128, 1536])
                zs = lsb("zs", [128, 1536])
                ss4 = lsb("ss4", [128, 4])
                hio = lsb("hio", [128, 12, 128])
                bcload(Dbc[:], "Dbc", I["ssm_d"][l, :])
                bcload(nrm[:], "nrm", I["ssm_norm"][l, :])
                PY = (4, 5, 6)
                v3 = lambda t: t[:].rearrange("p (h q) -> p h q", h=24)
                for d in range(2):
                    bcload(Abc[:], "Abc", I["ssm_a_log"][l, d * 24:(d + 1) * 24])
                    S.op("act", lambda e: e.activation(out=Abc[:], in_=Abc[:], func=AF.Exp), r=["Abc"], w=["Abc"])
                    S.op("dve", lambda e: e.tensor_scalar(out=Abc[:], in0=Abc[:], scalar1=-1.0, scalar2=None,
                                                          op0=ALU.mult), r=["Abc"], w=["Abc"])
                    TR = tri if d == 0 else trit
                    TRk = "tri" if d == 0 else "trit"
                    for s_ in range(T // L):
                        if sample:
                            S.dma("sp", hio[:], I["st_ssm"][l, d].rearrange("(g h2) p n -> (h2 p) g n", h2=2),
                                  w=["hio"])
                            for g in range(12):
                                bnk = PY[(g * 128) // 512]
                                S.op("pe", lambda e, g=g, bnk=bnk: e.transpose(
                                    psb[bnk][:, (g * 128) % 512:(g * 128) % 512 + 128], hio[:, g, :], ident[:]),
                                    r=["hio", "ident"], w=[PK[bnk]])
                            for j in range(3):
                                S.op("act", lambda e, j=j: e.activation(out=HT[:, j * 512:(j + 1) * 512],
                                                                        in_=psb[PY[j]][:, :], func=AF.Copy),
                                     r=[PK[PY[j]]], w=["HT"])
                        else:
                            S.op("dve", lambda e: e.memset(HT[:], 0.0), w=["HT"])
                        S.op("act", lambda e: e.activation(out=HTb[:], in_=HT[:], func=AF.Copy), r=["HT"], w=["HTb"])
                        tiles = list(range(L // 128))
                        if d == 1:
                            tiles = tiles[::-1]
                        for ti in tiles:
                            i = s_ * (L // 128) + ti
                            t0 = i * 128
                            x_ = xc[i % 2]
                            xk = "xc%d" % (i % 2)
                            b_ = bct[i % 2]
                            bk = "bct%d" % (i % 2)
                            d_ = dtt[i % 2]
                            dk = "dt%d" % (i % 2)
                            S.dma("sp", x_[:], XC[t0:t0 + 128, :], r=["XC"], w=[xk])
                            S.dma("sp", b_[:], BCT[i], r=["BCT"], w=[bk])
                            S.dma("sp", d_[:], DT[t0:t0 + 128, d * 24:(d + 1) * 24], r=["DT"], w=[dk])
                            S.op("dve", lambda e: e.tensor_mul(out=dtA[:], in0=d_[:], in1=Abc[:]),
                                 r=[dk, "Abc"], w=["dtA"])
                            S.op("pe", lambda e: e.matmul(psb[0][:, 0:24], lhsT=TR[:], rhs=dtA[:], start=True,
                                                          stop=True), r=[TRk, "dtA"], w=[PK[0]])
                            S.op("pe", lambda e: e.matmul(psb[0][:, 32:56], lhsT=ones[:], rhs=dtA[:], start=True,
                                                          stop=True), r=["ones", "dtA"], w=[PK[0]])
                            S.op("act", lambda e: e.activation(out=acol[:], in_=psb[0][:, 0:24], func=AF.Copy),
                                 r=[PK[0]], w=["acol"])
                            S.op("act", lambda e: e.activation(out=ea[:], in_=psb[0][:, 0:24], func=AF.Exp),
                                 r=[PK[0]], w=["ea"])
                            S.op("act", lambda e: e.activation(out=cd[:], in_=psb[0][:, 32:56], func=AF.Exp),
                                 r=[PK[0]], w=["cd"])
                            S.op("dve", lambda e: e.tensor_sub(out=dte[:], in0=psb[0][:, 32:56], in1=acol[:]),
                                 r=[PK[0], "acol"], w=["dte"])
                            S.op("act", lambda e: e.activation(out=dte[:], in_=dte[:], func=AF.Exp), r=["dte"],
                                 w=["dte"])
                            S.op("dve", lambda e: e.tensor_mul(out=dte[:], in0=dte[:], in1=d_[:]), r=["dte", dk],
                                 w=["dte"])
                            S.op("dve", lambda e: e.tensor_tensor(
                                out=v3(xdt), in0=x_[:, 0:1536].rearrange("p (h q) -> p h q", h=24),
                                in1=d_[:, :].unsqueeze(2).to_broadcast([128, 24, 64]), op=ALU.mult),
                                r=[xk, dk], w=["xdt"])
                            S.op("dve", lambda e: e.tensor_tensor(
                                out=v3(xw), in0=x_[:, 0:1536].rearrange("p (h q) -> p h q", h=24),
                                in1=dte[:, :].unsqueeze(2).to_broadcast([128, 24, 64]), op=ALU.mult),
                                r=[xk, "dte"], w=["xw"])
                            S.op("act", lambda e: e.activation(out=bB[:], in_=x_[:, 1536:2048], func=AF.Copy),
                                 r=[xk], w=["bB"])
                            for g in range(4):
                                S.op("pe", lambda e, g=g: e.matmul(psb[1][:, g * 128:(g + 1) * 128], lhsT=b_[:, g, :],
                                                                   rhs=b_[:, 4 + g, :], start=True, stop=True),
                                     r=[bk], w=[PK[1]])
                            S.op("dve", lambda e: e.tensor_tensor(
                                out=Gm[:].rearrange("p (g t) -> p g t", g=4),
                                in0=psb[1][:, :].rearrange("p (g t) -> p g t", g=4),
                                in1=TR[:, :].unsqueeze(1).to_broadcast([128, 4, 128]), op=ALU.mult),
                                r=[PK[1], TRk], w=["Gm"])
                            for g in range(4):
                                mm_cols(PY, g * 384, (g + 1) * 384, b_[:, 4 + g, :], lambda c, ce: HTb[:, c:ce],
                                        [bk, "HTb"])
                            for j in range(3):
                                S.op("dve", lambda e, j=j: e.tensor_tensor(
                                    out=yo[:, j * 512:(j + 1) * 512].rearrange("p (h q) -> p h q", h=8),
                                    in0=psb[PY[j]][:, :].rearrange("p (h q) -> p h q", h=8),
                                    in1=ea[:, j * 8:(j + 1) * 8].unsqueeze(2).to_broadcast([128, 8, 64]),
                                    op=ALU.mult), r=[PK[PY[j]], "ea"], w=["yo"])
                            for hq in range(6):
                                pb = 2 + hq % 2
                                ddq = dd[hq % 2]
                                dkq = "dd%d" % (hq % 2)
                                wq = wt[hq % 2]
                                wkq = "wt%d" % (hq % 2)
                                for j in range(4):
                                    h = hq * 4 + j
                                    S.op("pe", lambda e, j=j, h=h: e.matmul(
                                        psb[pb][:, j * 128:(j + 1) * 128],
                                        lhsT=dtA[:, h:h + 1].to_broadcast([128, 128]), rhs=TR[:], start=True,
                                        stop=True), r=["dtA", TRk], w=[PK[pb]])
                                for j in range(4):
                                    h = hq * 4 + j
                                    S.op("dve", lambda e, j=j, h=h: e.tensor_scalar(
                                        out=ddq[:, j * 128:(j + 1) * 128], in0=psb[pb][:, j * 128:(j + 1) * 128],
                                        scalar1=acol[:, h:h + 1], scalar2=0.0, op0=ALU.subtract, op1=ALU.min),
                                        r=[PK[pb], "acol"], w=[dkq])
                                S.op("act", lambda e: e.activation(out=ddq[:], in_=ddq[:], func=AF.Exp), r=[dkq],
                                     w=[dkq])
                                for j in range(4):
                                    h = hq * 4 + j
                                    g = h // 6
                                    S.op("dve", lambda e, j=j, g=g: e.tensor_mul(
                                        out=wq[:, j, :], in0=Gm[:, g * 128:(g + 1) * 128],
                                        in1=ddq[:, j * 128:(j + 1) * 128]), r=["Gm", dkq], w=[wkq])
                                for j in range(4):
                                    h = hq * 4 + j
                                    bnk = PY[(h * 64) // 512]
                                    S.op("pe", lambda e, j=j, h=h, bnk=bnk: e.matmul(
                                        psb[bnk][:, (h * 64) % 512:(h * 64) % 512 + 64], lhsT=wq[:, j, :],
                                        rhs=xdt[:, h * 64:(h + 1) * 64], start=True, stop=True),
                                        r=[wkq, "xdt"], w=[PK[bnk]])
                            for j in range(3):
                                S.op("dve", lambda e, j=j: e.tensor_add(out=yy[:, j * 512:(j + 1) * 512],
                                                                        in0=yo[:, j * 512:(j + 1) * 512],
                                                                        in1=psb[PY[j]][:, :]),
                                     r=["yo", PK[PY[j]]], w=["yy"])
                            for g in range(4):
                                mm_cols(PY, g * 384, (g + 1) * 384, bB[:, g * 128:(g + 1) * 128],
                                        lambda c, ce: xw[:, c:ce], ["bB", "xw"])
                            S.op("dve", lambda e: e.tensor_tensor(out=v3(HT), in0=v3(HT),
                                                                  in1=cd[:, :].unsqueeze(2).to_broadcast([128, 24, 64]),
                                                                  op=ALU.mult), r=["HT", "cd"], w=["HT"])
                            for j in range(3):
                                S.op("dve", lambda e, j=j: e.tensor_add(out=HT[:, j * 512:(j + 1) * 512],
                                                                        in0=HT[:, j * 512:(j + 1) * 512],
                                                                        in1=psb[PY[j]][:, :]),
                                     r=["HT", PK[PY[j]]], w=["HT"])
                            S.op("act", lambda e: e.activation(out=HTb[:], in_=HT[:], func=AF.Copy), r=["HT"],
                                 w=["HTb"])
                            if d == 0:
                                S.dma("sp", YS[t0:t0 + 128, :], yy[:], r=["yy"], w=["YS"])
                            else:
                                S.dma("sp", yo[:], YS[t0:t0 + 128, :], r=["YS"], w=["yo"])
                                S.op("dve", lambda e: e.tensor_add(out=yy[:], in0=yy[:], in1=yo[:]), r=["yy", "yo"],
                                     w=["yy"])
                                S.op("dve", lambda e: e.tensor_tensor(
                                    out=v3(yo), in0=x_[:, 0:1536].rearrange("p (h q) -> p h q", h=24),
                                    in1=Dbc[:, :].unsqueeze(2).to_broadcast([128, 24, 64]), op=ALU.mult),
                                    r=[xk, "Dbc"], w=["yo"])
                                S.op("dve", lambda e: e.tensor_add(out=yy[:], in0=yy[:], in1=yo[:]), r=["yy", "yo"],
                                     w=["yy"])
                                S.dma("sp", zz[:], P[t0:t0 + 128, C_Z:C_Z + 1536], r=["P"], w=["zz"])
                                S.op("act", lambda e: e.activation(out=zs[:], in_=zz[:], func=AF.Sigmoid), r=["zz"],
                                     w=["zs"])
                                S.op("dve", lambda e: e.tensor_mul(out=zs[:], in0=zs[:], in1=zz[:]), r=["zz", "zs"],
                                     w=["zs"])
                                S.op("dve", lambda e: e.tensor_mul(out=yy[:], in0=yy[:], in1=zs[:]), r=["yy", "zs"],
                                     w=["yy"])
                                S.op("act", lambda e: e.activation(out=zs[:], in_=yy[:], func=AF.Square), r=["yy"],
                                     w=["zs"])
                                S.op("dve", lambda e: e.tensor_reduce(out=ss4[:],
                                                                      in_=zs[:].rearrange("p (g q) -> p g q", g=4),
                                                                      axis=AX.X, op=ALU.add), r=["zs"], w=["ss4"])
                                rms_rstd(ss4[:], "ss4", 384, EPS)
                                S.op("dve", lambda e: e.tensor_tensor(
                                    out=yy[:].rearrange("p (g q) -> p g q", g=4),
                                    in0=yy[:].rearrange("p (g q) -> p g q", g=4),
                                    in1=ss4[:, :].unsqueeze(2).to_broadcast([128, 4, 384]), op=ALU.mult),
                                    r=["yy", "ss4"], w=["yy"])
                                S.op("dve", lambda e: e.tensor_mul(out=yy[:], in0=yy[:], in1=nrm[:]),
                                     r=["yy", "nrm"], w=["yy"])
                                S.dma("sp", YS[t0:t0 + 128, :], yy[:], r=["yy"], w=["YS"])
                        if not sample:
                            for g in range(12):
                                bnk = PY[(g * 128) // 512]
                                S.op("pe", lambda e, g=g, bnk=bnk: e.transpose(
                                    psb[bnk][:, (g * 128) % 512:(g * 128) % 512 + 128],
                                    HT[:, g * 128:(g + 1) * 128], ident[:]), r=["HT", "ident"], w=[PK[bnk]])
                            for j in range(3):
                                S.op("act", lambda e, j=j: e.activation(
                                    out=hio[:, j * 4:(j + 1) * 4, :],
                                    in_=psb[PY[j]][:, :].rearrange("p (g n) -> p g n", g=4), func=AF.Copy),
                                    r=[PK[PY[j]]], w=["hio"])
                            S.dma("sp", O["ns_ssm"][s_, l, d].rearrange("(g h2) p n -> (h2 p) g n", h2=2), hio[:],
                                  r=["hio"], w=["ns_ssm"])
                S.barrier()

        def phase_wkv_prep(l, T):
            NT = T // 128
            with contextlib.ExitStack() as ps:
                lsb = lambda name, shape, dt=F32: ps.enter_context(nc.sbuf_tensor(uname(name), list(shape), dt))
                sh = lsb("sh", [128, NWKV])
                kkbc = lsb("kkbc", [128, 1536])
                kabc = lsb("kabc", [128, 1536])
                omka = lsb("omka", [128, 1536])
                rkbc = lsb("rkbc", [128, 1536])
                w0bc = lsb("w0bc", [128, 1536])
                a0bc = lsb("a0bc", [128, 1536])
                wup = [lsb("wup%d" % d, [96, 1536]) for d in range(2)]
                aup = [lsb("aup%d" % d, [96, 1536]) for d in range(2)]
                gup = lsb("gup", [128, 2, 1536])
                A = lsb("wA", [128, 1536])
                Bt = lsb("wB", [128, 1536])
                Ct = lsb("wC", [128, 1536])
                Dt_ = lsb("wD", [128, 1536])
                E = lsb("wE", [128, 1536])
                nkk = lsb("nkk", [128, 1536])
                vb16 = lsb("vb16", [128, 1536], BF16)
                kb16 = lsb("kb16", [128, 1536], BF16)
                bb16 = lsb("bb16", [128, 1536], BF16)
                Vx = lsb("Vx", [128, 12, 768], BF16)
                S.op("dve", lambda e: e.memset(Vx[:], 0.0), w=["Vx"])
                sm = lsb("wsm", [128, 48])
                rs = lsb("wrs", [128, 24])
                twT = lsb("twT", [96, 2, 128])
                aT = lsb("aT", [96, 2, 128])
                sgT = lsb("sgT", [128, 2, 128])
                fm = [lsb("wfm%d" % j, [128, 12, 128]) for j in range(2)]
                fmc = [0]
                bcload(kkbc[:], "kkbc", I["wkv_k_k"][l, :])
                bcload(kabc[:], "kabc", I["wkv_k_a"][l, :])
                bcload(rkbc[:], "rkbc", I["wkv_r_k"][l, :])
                S.op("dve", lambda e: e.tensor_scalar(out=omka[:], in0=kabc[:], scalar1=-1.0, scalar2=1.0,
                                                      op0=ALU.mult, op1=ALU.add), r=["kabc"], w=["omka"])
                for d in range(2):
                    S.dma("sp", wup[d][:], I["wkv_w_up"][l, d], w=["wup%d" % d])
                    S.dma("sp", aup[d][:], I["wkv_a_up"][l, d], w=["aup%d" % d])
                S.dma("sp", gup[:], I["wkv_g_up"][l].rearrange("(c p) n -> p c n", p=128), w=["gup"])
                PY = (3, 4, 5)
                h3 = lambda ap: ap.rearrange("p (h q) -> p h q", h=24)

                def fm_store(src, skey, dst3):
                    f = fm[fmc[0] % 2]
                    fk = "wfm%d" % (fmc[0] % 2)
                    fmc[0] += 1
                    transposes_to(src, skey, 12, lambda q, nj: f[:, q:q + nj, :], fk, eng="dve")
                    S.dma("sp", dst3, f[:], r=[fk], w=["FMOUT"])

                for i in range(NT):
                    t0 = i * 128
                    S.dma("sp", sh[:], SH[t0:t0 + 128, :], r=["SH"], w=["sh"])
                    r_ = sh[:, 0:1536]
                    k_ = sh[:, 1536:3072]
                    S.op("act", lambda e: e.activation(out=vb16[:], in_=sh[:, 3072:4608], func=AF.Copy), r=["sh"],
                         w=["vb16"])
                    vb4 = vb16[:].rearrange("p (g h v) -> p g h v", g=12, h=2)
                    for hh in range(2):
                        for g in range(12):
                            en = "act" if g % 2 == 0 else "dve"
                            if en == "act":
                                S.op("act", lambda e, g=g: e.activation(out=Vx[:, g, g * 64:(g + 1) * 64],
                                                                        in_=vb4[:, g, hh, :], func=AF.Copy),
                                     r=["vb16"], w=["Vx"])
                            else:
                                S.op("dve", lambda e, g=g: e.tensor_copy(out=Vx[:, g, g * 64:(g + 1) * 64],
                                                                         in_=vb4[:, g, hh, :]), r=["vb16"], w=["Vx"])
                        S.dma("sp", VBD[hh][t0:t0 + 128, :, :], Vx[:], r=["Vx"], w=["VBD"])
                    S.op("dve", lambda e: e.tensor_mul(out=A[:], in0=k_, in1=kkbc[:]), r=["sh", "kkbc"], w=["wA"])
                    S.op("act", lambda e: e.activation(out=Bt[:], in_=A[:], func=AF.Square), r=["wA"], w=["wB"])
                    S.op("dve", lambda e: e.tensor_reduce(out=rs[:], in_=h3(Bt[:]), axis=AX.X, op=ALU.add),
                         r=["wB"], w=["wrs"])
                    S.op("dve", lambda e: e.tensor_scalar(out=rs[:], in0=rs[:], scalar1=1e-24, scalar2=None,
                                                          op0=ALU.max), r=["wrs"], w=["wrs"])
                    S.op("act", lambda e: e.activation(out=rs[:], in_=rs[:], func=AF.Sqrt), r=["wrs"], w=["wrs"])
                    S.op("dve", lambda e: e.reciprocal(out=rs[:], in_=rs[:]), r=["wrs"], w=["wrs"])
                    S.op("dve", lambda e: e.tensor_scalar(out=rs[:], in0=rs[:], scalar1=-1.0, scalar2=None,
                                                          op0=ALU.mult), r=["wrs"], w=["wrs"])
                    S.op("dve", lambda e: e.tensor_tensor(out=h3(nkk[:]), in0=h3(A[:]),
                                                          in1=rs[:, :].unsqueeze(2).to_broadcast([128, 24, 64]),
                                                          op=ALU.mult), r=["wA", "wrs"], w=["nkk"])
                    fm_store(nkk[:], "nkk", NKKT[:, :, t0:t0 + 128])
                    S.op("dve", lambda e: e.tensor_mul(out=Bt[:], in0=r_, in1=k_), r=["sh"], w=["wB"])
                    S.op("dve", lambda e: e.tensor_mul(out=Bt[:], in0=Bt[:], in1=rkbc[:]), r=["wB", "rkbc"], w=["wB"])
                    S.op("dve", lambda e: e.tensor_reduce(out=sm[:, 0:24], in_=h3(Bt[:]), axis=AX.X, op=ALU.add),
                         r=["wB"], w=["wsm"])
                    S.dma("sp", RK[t0:t0 + 128, :], sm[:, 0:24], r=["wsm"], w=["RK"])
                    S.op("act", lambda e: e.activation(out=Ct[:, 0:192], in_=sh[:, 4608:4800], func=AF.Tanh),
                         r=["sh"], w=["wC"])
                    S.op("act", lambda e: e.activation(out=Ct[:, 192:448], in_=sh[:, 4992:5248], func=AF.Sigmoid),
                         r=["sh"], w=["wC"])
                    transposes_to(Ct[:, 0:192], "wC", 2, lambda q, nj: twT[:, q:q + nj, :], "twT", bw=96, eng="dve")
                    transposes_to(sh[:, 4800:4992], "sh", 2, lambda q, nj: aT[:, q:q + nj, :], "aT", bw=96, eng="dve")
                    transposes_to(Ct[:, 192:448], "wC", 2, lambda q, nj: sgT[:, q:q + nj, :], "sgT", eng="dve")
                    for cb in range(3):
                        for c in range(2):
                            S.op("pe", lambda e, cb=cb, c=c: e.matmul(psb[PY[cb]][:, :], lhsT=sgT[:, c, :],
                                                                      rhs=gup[:, c, cb * 512:(cb + 1) * 512],
                                                                      start=(c == 0), stop=(c == 1)),
                                 r=["sgT", "gup"], w=[PK[PY[cb]]])
                        S.op("act", lambda e, cb=cb: e.activation(out=E[:, cb * 512:(cb + 1) * 512],
                                                                  in_=psb[PY[cb]][:, :], func=AF.Copy),
                             r=[PK[PY[cb]]], w=["wE"])
                    S.dma("sp", GG[t0:t0 + 128, :], E[:], r=["wE"], w=["GG"])
                    for d in range(2):
                        bcload(w0bc[:], "w0bc", I["wkv_w0"][l, d, :])
                        bcload(a0bc[:], "a0bc", I["wkv_a0"][l, d, :])
                        for cb in range(3):
                            S.op("pe", lambda e, cb=cb: e.matmul(psb[PY[cb]][:, :], lhsT=twT[:, d, :],
                                                                 rhs=wup[d][:, cb * 512:(cb + 1) * 512], start=True,
                                                                 stop=True), r=["twT", "wup%d" % d], w=[PK[PY[cb]]])
                            S.op("dve", lambda e, cb=cb: e.tensor_add(out=Bt[:, cb * 512:(cb + 1) * 512],
                                                                      in0=psb[PY[cb]][:, :],
                                                                      in1=w0bc[:, cb * 512:(cb + 1) * 512]),
                                 r=[PK[PY[cb]], "w0bc"], w=["wB"])
                        S.op("act", lambda e: e.activation(out=Bt[:], in_=Bt[:], func=AF.Sigmoid), r=["wB"], w=["wB"])
                        S.op("act", lambda e: e.activation(out=Bt[:], in_=Bt[:], func=AF.Exp,
                                                           scale=-math.exp(-0.5)), r=["wB"], w=["wB"])
                        for cb in range(3):
                            S.op("pe", lambda e, cb=cb: e.matmul(psb[PY[cb]][:, :], lhsT=aT[:, d, :],
                                                                 rhs=aup[d][:, cb * 512:(cb + 1) * 512], start=True,
                                                                 stop=True), r=["aT", "aup%d" % d], w=[PK[PY[cb]]])
                            S.op("dve", lambda e, cb=cb: e.tensor_add(out=Dt_[:, cb * 512:(cb + 1) * 512],
                                                                      in0=psb[PY[cb]][:, :],
                                                                      in1=a0bc[:, cb * 512:(cb + 1) * 512]),
                                 r=[PK[PY[cb]], "a0bc"], w=["wD"])
                        S.op("act", lambda e: e.activation(out=Dt_[:], in_=Dt_[:], func=AF.Sigmoid), r=["wD"],
                             w=["wD"])
                        S.op("dve", lambda e: e.tensor_mul(out=A[:], in0=Dt_[:], in1=kabc[:]), r=["wD", "kabc"],
                             w=["wA"])
                        S.op("dve", lambda e: e.tensor_add(out=A[:], in0=A[:], in1=omka[:]), r=["wA", "omka"],
                             w=["wA"])
                        S.op("dve", lambda e: e.tensor_mul(out=A[:], in0=A[:], in1=k_), r=["wA", "sh"], w=["wA"])
                        S.op("act", lambda e: e.activation(out=kb16[:], in_=A[:], func=AF.Copy), r=["wA"], w=["kb16"])
                        S.dma("sp", KD[d][t0:t0 + 128, :], kb16[:], r=["kb16"], w=["KD"])
                        S.op("dve", lambda e: e.scalar_tensor_tensor(out=Ct[:], in0=nkk[:], scalar=-1.0, in1=Dt_[:],
                                                                     op0=ALU.mult, op1=ALU.mult),
                             r=["nkk", "wD"], w=["wC"])
                        S.op("act", lambda e: e.activation(out=bb16[:], in_=Ct[:], func=AF.Copy), r=["wC"], w=["bb16"])
                        S.dma("sp", BD[d][t0:t0 + 128, :], bb16[:], r=["bb16"], w=["BD"])
                        S.op("dve", lambda e: e.tensor_mul(out=E[:], in0=Ct[:], in1=r_), r=["wC", "sh"], w=["wE"])
                        S.op("dve", lambda e: e.tensor_reduce(out=sm[:, 0:24], in_=h3(E[:]), axis=AX.X, op=ALU.add),
                             r=["wE"], w=["wsm"])
                        S.op("dve", lambda e: e.tensor_mul(out=E[:], in0=A[:], in1=r_), r=["wA", "sh"], w=["wE"])
                        S.op("dve", lambda e: e.tensor_reduce(out=sm[:, 24:48], in_=h3(E[:]), axis=AX.X, op=ALU.add),
                             r=["wE"], w=["wsm"])
                        S.dma("sp", BRKR[d][t0:t0 + 128, :], sm[:], r=["wsm"], w=["BRKR"])
                        S.op("dve", lambda e: e.tensor_tensor(out=h3(A[:]), in0=h3(nkk[:]),
                                                              in1=sm[:, 0:24].unsqueeze(2).to_broadcast([128, 24, 64]),
                                                              op=ALU.mult), r=["nkk", "wsm"], w=["wA"])
                        S.op("dve", lambda e: e.tensor_mul(out=E[:], in0=Bt[:], in1=r_), r=["wB", "sh"], w=["wE"])
                        S.op("dve", lambda e: e.tensor_add(out=E[:], in0=E[:], in1=A[:]), r=["wE", "wA"], w=["wE"])
                        fm_store(E[:], "wE", WRT[d][:, :, t0:t0 + 128])
                        fm_store(Bt[:], "wB", WT[d][:, :, t0:t0 + 128])
                S.barrier()

        def phase_wkv_scan(l, T, L, sample):
            TBK = 4
            LT = L // 128
            with contextlib.ExitStack() as ps:
                lsb = lambda name, shape, dt=F32: ps.enter_context(nc.sbuf_tensor(uname(name), list(shape), dt))
                m48 = lsb("m48", [48, 768])
                m24b = lsb("m24b", [24, 768], BF16)
                sio = lsb("sio", [64, 12, 128])
                B = []
                for d in range(2):
                    b = {n: lsb("%s_%d" % (n, d), shp, dt) for (n, shp, dt) in (
                        ("Ap", [128, 128, 48], F32), ("nkT", [128, 12, 128], F32), ("wrT", [128, 12, 128], F32),
                        ("wT", [128, 12, 128], F32), ("Lb", [24, 2, TBK, 128], BF16), ("Lk", [24, 2, TBK, 128], BF16),
                        ("Rv", [24, 2, TBK, 768], BF16), ("Ra", [48, TBK, 768], F32), ("Rb", [24, TBK, 768], BF16),
                        ("Cst", [48, TBK, 64], F32), ("ST", [128, 768], F32))}
                    b["k"] = {n: "%s_%d" % (n, d) for n in ("Ap", "nkT", "wrT", "wT", "Lb", "Lk", "Rv", "Ra", "Rb",
                                                            "Cst", "ST")}
                    b["pb"] = (0, 1, 2, 3) if d == 0 else (4, 5, 6, 7)
                    B.append(b)
                S.dma("sp", m48[:], I["mask48"][:, :], w=["m48"])
                S.op("dve", lambda e: e.tensor_copy(out=m24b[:], in_=m48[0:24, :]), r=["m48"], w=["m24b"])
                for d in range(2):
                    b = B[d]
                    S.op("dve", lambda e: e.memset(b["Ap"][:], 0.0), w=[b["k"]["Ap"]])
                    S.op("dve", lambda e: e.memset(b["Lb"][:], 0.0), w=[(b["k"]["Lb"], 0), (b["k"]["Lb"], 1)])
                    S.op("dve", lambda e: e.memset(b["Lk"][:], 0.0), w=[(b["k"]["Lk"], 0), (b["k"]["Lk"], 1)])
                for s_ in range(T // L):
                    for d in range(2):
                        b = B[d]
                        ST = b["ST"]
                        if sample:
                            S.dma("sp", sio[:].rearrange("v g (h k) -> v g h k", h=2),
                                  I["st_wkv"][l, d].rearrange("(g h) v k -> v g h k", h=2), w=["sio"])
                            transposes_to(sio[:].rearrange("v g q -> v (g q)"), "sio", 12,
                                          lambda q, nj: ST[:, q * 64:(q + nj) * 64].rearrange("p (j v) -> p j v", j=nj),
                                          b["k"]["ST"], bw=128, inw=64, pbanks=b["pb"][2:4])
                            S.op("dve", lambda e: e.tensor_copy(out=ST[:, 0:1], in_=ST[:, 0:1]), r=[b["k"]["ST"]],
                                 w=[(b["k"]["ST"], g_) for g_ in range(12)])
                        else:
                            S.op("dve", lambda e: e.memset(ST[:], 0.0),
                                 w=[(b["k"]["ST"], g_) for g_ in range(12)] + [b["k"]["ST"]])
                    for kk_ in range(LT):
                        tis = (kk_, LT - 1 - kk_)
                        t0s = [s_ * L + ti * 128 for ti in tis]
                        for d in range(2):
                            b = B[d]
                            k = b["k"]
                            t0 = t0s[d]
                            S.dma("sp", b["nkT"][:], NKKT[:, :, t0:t0 + 128], r=["NKKT"], w=[k["nkT"]])
                            S.dma("sp", b["wrT"][:], WRT[d][:, :, t0:t0 + 128], r=["WRT"], w=[k["wrT"]])
                            S.dma("sp", b["wT"][:], WT[d][:, :, t0:t0 + 128], r=["WT"], w=[k["wT"]])
                            for (lo, hi, c0_, src, sk) in ((0, 64, 0, "nkT", k["nkT"]), (64, 128, 12, "nkT", k["nkT"]),
                                                           (0, 64, 24, "wrT", k["wrT"]), (64, 128, 36, "wrT", k["wrT"])):
                                S.op("act", lambda e, lo=lo, hi=hi, c0_=c0_, src=src: e.activation(
                                    out=b["Ap"][lo:hi, :, c0_:c0_ + 12], in_=b[src][lo:hi].rearrange("p g t -> p t g"),
                                    func=AF.Copy), r=[sk], w=[k["Ap"]])
                        for cc in range(128 // TBK):
                            chs = (cc, 128 // TBK - 1 - cc)
                            NCH = 128 // TBK
                            par = cc % 2

                            def stage(cidx, pr):
                                chx = (cidx, NCH - 1 - cidx)
                                for d in range(2):
                                    b = B[d]
                                    k = b["k"]
                                    c0 = t0s[d] + chx[d] * TBK
                                    bsrc = BD[d][c0:c0 + TBK, :].rearrange("t (g h k) -> g t h k", g=12, h=2)
                                    ksrc = KD[d][c0:c0 + TBK, :].rearrange("t (g h k) -> g t h k", g=12, h=2)
                                    S.dma("sp", b["Lb"][0:12, pr, :, 0:64], bsrc[:, :, 0, :], r=["BD"],
                                          w=[(k["Lb"], pr)])
                                    S.dma("sp", b["Lb"][12:24, pr, :, 64:128], bsrc[:, :, 1, :], r=["BD"],
                                          w=[(k["Lb"], pr)])
                                    S.dma("sp", b["Lk"][0:12, pr, :, 0:64], ksrc[:, :, 0, :], r=["KD"],
                                          w=[(k["Lk"], pr)])
                                    S.dma("sp", b["Lk"][12:24, pr, :, 64:128], ksrc[:, :, 1, :], r=["KD"],
                                          w=[(k["Lk"], pr)])
                                    for hh in range(2):
                                        S.dma("sp", b["Rv"][hh * 12:(hh + 1) * 12, pr, :, :],
                                              VBD[hh][c0:c0 + TBK, :, :].rearrange("t g q -> g t q"),
                                              r=["VBD"], w=[(k["Rv"], pr)])

                            if cc == 0:
                                stage(0, 0)
                            if cc + 1 < NCH:
                                stage(cc + 1, 1 - par)
                            for st_ in range(TBK):
                                tls = (st_, TBK - 1 - st_)
                                toks = [chs[d] * TBK + tls[d] for d in range(2)]
                                for d in range(2):
                                    b = B[d]
                                    k = b["k"]
                                    for hf in range(2):
                                        S.op("pe", lambda e, hf=hf: e.matmul(
                                            psb[b["pb"][hf]][0:48, 0:384], lhsT=b["Ap"][:, toks[d], :],
                                            rhs=b["ST"][:, hf * 384:(hf + 1) * 384], start=True, stop=True),
                                            r=[k["Ap"]] + [(k["ST"], g_) for g_ in range(hf * 6, hf * 6 + 6)],
                                            w=[PK[b["pb"][hf]]])
                                for d in range(2):
                                    b = B[d]
                                    k = b["k"]
                                    for hf in range(2):
                                        S.op("dve", lambda e, hf=hf: e.tensor_mul(
                                            out=b["Ra"][:, tls[d], hf * 384:(hf + 1) * 384],
                                            in0=psb[b["pb"][hf]][0:48, 0:384],
                                            in1=m48[:, hf * 384:(hf + 1) * 384]),
                                            r=[PK[b["pb"][hf]], "m48"], w=[(k["Ra"], hf)])
                                for d in range(2):
                                    b = B[d]
                                    k = b["k"]
                                    S.op("act", lambda e: e.activation(out=b["Rb"][:, tls[d], :],
                                                                       in_=b["Ra"][0:24, tls[d], :], func=AF.Copy),
                                         r=[(k["Ra"], 0), (k["Ra"], 1)], w=[k["Rb"]])
                                for d in range(2):
                                    b = B[d]
                                    k = b["k"]
                                    for hf in range(2):
                                        pb = b["pb"][2 + hf]
                                        S.op("pe", lambda e, hf=hf, pb=pb: e.matmul(
                                            psb[pb][:, 0:384], lhsT=b["Lk"][:, par, tls[d], :],
                                            rhs=b["Rv"][:, par, tls[d], hf * 384:(hf + 1) * 384], start=True,
                                            stop=False), r=[(k["Lk"], par), (k["Rv"], par)], w=[PK[pb]])
                                        S.op("pe", lambda e, hf=hf, pb=pb: e.matmul(
                                            psb[pb][:, 0:384], lhsT=b["Lb"][:, par, tls[d], :],
                                            rhs=b["Rb"][:, tls[d], hf * 384:(hf + 1) * 384], start=False, stop=True),
                                            r=[(k["Lb"], par), k["Rb"]], w=[PK[pb]])
                                for d in range(2):
                                    b = B[d]
                                    k = b["k"]
                                    for g in range(12):
                                        pb = b["pb"][2 + g // 6]
                                        S.op("dve", lambda e, g=g, pb=pb: e.scalar_tensor_tensor(
                                            out=b["ST"][:, g * 64:(g + 1) * 64], in0=b["ST"][:, g * 64:(g + 1) * 64],
                                            scalar=b["wT"][:, g, toks[d]:toks[d] + 1],
                                            in1=psb[pb][:, (g % 6) * 64:(g % 6 + 1) * 64], op0=ALU.mult, op1=ALU.add),
                                            r=[(k["ST"], g), k["wT"], PK[pb]], w=[(k["ST"], g)])
                            for d in range(2):
                                b = B[d]
                                k = b["k"]
                                c0 = t0s[d] + chs[d] * TBK
                                S.op("dve", lambda e: e.tensor_reduce(
                                    out=b["Cst"][:], in_=b["Ra"][:].rearrange("p t (g v) -> p t v g", g=12),
                                    axis=AX.X, op=ALU.add), r=[(k["Ra"], 0), (k["Ra"], 1)], w=[k["Cst"]])
                                S.dma("sp", SAY[d][:, c0:c0 + TBK, :], b["Cst"][:], r=[k["Cst"]], w=["SAY"])
                    if not sample:
                        for d in range(2):
                            b = B[d]
                            S.op("dve", lambda e: e.tensor_copy(out=b["ST"][:, 0:1], in_=b["ST"][:, 0:1]),
                                 r=[(b["k"]["ST"], g_) for g_ in range(12)], w=[b["k"]["ST"]])
                            transposes_to(b["ST"][:], b["k"]["ST"], 12, lambda q, nj: sio[:, q:q + nj, :], "sio",
                                          bw=64, inw=128, pbanks=b["pb"][2:4])
                            S.dma("sp", O["ns_wkv"][s_, l, d].rearrange("(g h) v k -> v g h k", h=2),
                                  sio[:].rearrange("v g (h k) -> v g h k", h=2), r=["sio"], w=["ns_wkv"])
                S.barrier()

        def phase_wkv_post(l, T):
            with contextlib.ExitStack() as ps:
                lsb = lambda name, shape, dt=F32: ps.enter_context(nc.sbuf_tensor(uname(name), list(shape), dt))
                y0 = [lsb("py0%d" % d, [128, 1536]) for d in range(2)]
                vv = lsb("pvv", [128, 1536])
                gg = lsb("pgg", [128, 1536])
                o = lsb("po", [128, 1536])
                t_ = lsb("pt", [128, 1536])
                lnw = lsb("lnw", [128, 1536])
                lnb = lsb("lnb", [128, 1536])
                bk = [lsb("pbk%d" % d, [128, 48]) for d in range(2)]
                rk = lsb("prk", [128, 24])
                mu = lsb("pmu", [128, 24])
                bcload(lnw[:], "lnw", I["wkv_ln_w"][l, :])
                bcload(lnb[:], "lnb", I["wkv_ln_b"][l, :])
                h3 = lambda ap: ap.rearrange("p (h q) -> p h q", h=24)
                bc3 = lambda ap: ap.unsqueeze(2).to_broadcast([128, 24, 64])
                for i in range(T // 128):
                    t0 = i * 128
                    for d in range(2):
                        for hh in range(2):
                            S.dma("sp", y0[d][:].rearrange("p (g h v) -> p g h v", g=12, h=2)[:, :, hh, :],
                                  SAY[d][(2 + hh) * 12:(3 + hh) * 12, t0:t0 + 128, :].rearrange("g t v -> t g v"),
                                  r=["SAY"], w=["py0%d" % d])
                        S.dma("sp", bk[d][:], BRKR[d][t0:t0 + 128, :], r=["BRKR"], w=["pbk%d" % d])
                    S.dma("sp", vv[:], SH[t0:t0 + 128, 3072:4608], r=["SH"], w=["pvv"])
                    S.dma("sp", gg[:], GG[t0:t0 + 128, :], r=["GG"], w=["pgg"])
                    S.dma("sp", rk[:], RK[t0:t0 + 128, :], r=["RK"], w=["prk"])
                    S.op("dve", lambda e: e.tensor_add(out=o[:], in0=y0[0][:], in1=y0[1][:]), r=["py00", "py01"],
                         w=["po"])
                    S.op("dve", lambda e: e.tensor_add(out=mu[:], in0=bk[0][:, 24:48], in1=bk[1][:, 24:48]),
                         r=["pbk0", "pbk1"], w=["pmu"])
                    S.op("dve", lambda e: e.tensor_tensor(out=h3(t_[:]), in0=h3(vv[:]), in1=bc3(mu[:, :]),
                                                          op=ALU.mult), r=["pvv", "pmu"], w=["pt"])
                    S.op("dve", lambda e: e.tensor_add(out=o[:], in0=o[:], in1=t_[:]), r=["po", "pt"], w=["po"])
                    S.op("dve", lambda e: e.tensor_reduce(out=mu[:], in_=h3(o[:]), axis=AX.X, op=ALU.add), r=["po"],
                         w=["pmu"])
                    S.op("dve", lambda e: e.tensor_scalar(out=mu[:], in0=mu[:], scalar1=1.0 / 64, scalar2=None,
                                                          op0=ALU.mult), r=["pmu"], w=["pmu"])
                    S.op("dve", lambda e: e.tensor_tensor(out=h3(o[:]), in0=h3(o[:]), in1=bc3(mu[:, :]),
                                                          op=ALU.subtract), r=["po", "pmu"], w=["po"])
                    S.op("act", lambda e: e.activation(out=t_[:], in_=o[:], func=AF.Square), r=["po"], w=["pt"])
                    S.op("dve", lambda e: e.tensor_reduce(out=mu[:], in_=h3(t_[:]), axis=AX.X, op=ALU.add), r=["pt"],
                         w=["pmu"])
                    rms_rstd(mu[:], "pmu", 64, 64e-5)
                    S.op("dve", lambda e: e.tensor_tensor(out=h3(o[:]), in0=h3(o[:]), in1=bc3(mu[:, :]),
                                                          op=ALU.mult), r=["po", "pmu"], w=["po"])
                    S.op("dve", lambda e: e.tensor_mul(out=o[:], in0=o[:], in1=lnw[:]), r=["po", "lnw"], w=["po"])
                    S.op("dve", lambda e: e.tensor_add(out=o[:], in0=o[:], in1=lnb[:]), r=["po", "lnb"], w=["po"])
                    S.op("dve", lambda e: e.tensor_tensor(out=h3(t_[:]), in0=h3(vv[:]), in1=bc3(rk[:, :]),
                                                          op=ALU.mult), r=["pvv", "prk"], w=["pt"])
                    S.op("dve", lambda e: e.tensor_add(out=o[:], in0=o[:], in1=t_[:]), r=["po", "pt"], w=["po"])
                    S.op("dve", lambda e: e.tensor_mul(out=o[:], in0=o[:], in1=gg[:]), r=["po", "pgg"], w=["po"])
                    S.dma("sp", ZW[t0:t0 + 128, :], o[:], r=["po"], w=["ZW"])
                S.barrier()

        def phase_s5(l, T, L, sample):
            NT = T // 128
            nseq = T // L
            LT = L // 128
            Ls = min(512, L)
            nseg = L // Ls
            nsub = Ls // 128
            with contextlib.ExitStack() as ps:
                lsb = lambda name, shape, dt=F32: ps.enter_context(nc.sbuf_tensor(uname(name), list(shape), dt))
                ut = [lsb("ut%d" % j, [128, 1024]) for j in range(2)]
                stg = [lsb("ustg%d" % j, [32, 32, 128]) for j in range(2)]
                cnt = 0
                for i in range(NT):
                    t0 = i * 128
                    s_ = t0 // L
                    ti = (t0 % L) // 128
                    t0r = s_ * L + (LT - 1 - ti) * 128
                    u = ut[i % 2]
                    uk = "ut%d" % (i % 2)
                    S.dma("sp", u[:], P[t0:t0 + 128, C_U:C_U + 1024], r=["P"], w=[uk])
                    for d in range(2):
                        st_ = stg[cnt % 2]
                        sk = "ustg%d" % (cnt % 2)
                        cnt += 1
                        transposes_to(u[:], uk, 32, lambda q, nj: st_[:, q:q + nj, :], sk, bw=32, inw=128,
                                      eng=("act" if d == 0 else "dve"), rhs=(None if d == 0 else jrev[:]),
                                      rkey="jrev", pbanks=((6, 7) if d == 0 else (4, 5)))
                        dt_ = t0 if d == 0 else t0r
                        S.dma("sp", UT2[d][:, :, dt_:dt_ + 128].rearrange("k r t -> r k t"), st_[:], r=[sk],
                              w=["UT2"])
                S.barrier()
            with contextlib.ExitStack() as ps:
                lsb = lambda name, shape, dt=F32: ps.enter_context(nc.sbuf_tensor(uname(name), list(shape), dt))
                pt_ = {n: lsb("s5_" + n, [128, 32]) for n in
                       ("lre", "lim", "ldt", "rho", "tht", "cs", "sn", "ar", "ai", "t1", "t2", "t3", "cr", "ci")}
                it_ = lsb("s5_it", [128, 32], I32)
                bre = lsb("bre", [128, 32, 16])
                bim = lsb("bim", [128, 32, 16])
                Bbr = lsb("Bbr", [128, 32, 16])
                Bbi = lsb("Bbi", [128, 32, 16])
                btmp = lsb("btmp", [128, 32, 16])
                BDr = lsb("BDr", [128, 32, 32])
                BDi = lsb("BDi", [128, 32, 32])
                BpTr = lsb("BpTr", [32, 32, 128])
                BpTi = lsb("BpTi", [32, 32, 128])
                Zr = lsb("Zr", [32, 32, 128])
                Zi = lsb("Zi", [32, 32, 128])
                Cre = lsb("Cre", [128, 32, 32], BF16)
                nCre = lsb("nCre", [128, 32, 32], BF16)
                nCim = lsb("nCim", [128, 32, 32], BF16)
                jjt = lsb("jjt", [128, 512])
                tj = lsb("tj", [128, 512])
                tf = lsb("tf", [128, 512])
                iti = lsb("iti", [128, 512], I32)
                cst = lsb("cst", [128, 512])
                snt = lsb("snt", [128, 512])
                u2 = [lsb("u2_%d" % j, [32, 512]) for j in range(2)]
                p1 = lsb("p1", [128, 512])
                p2 = lsb("p2", [128, 512])
                inre = lsb("inre", [128, 512])
                inim = lsb("inim", [128, 512])
                zr = lsb("zr", [128, 512])
                zi = lsb("zi", [128, 512])
                qq = [lsb("qq%d" % j, [128, 512], BF16) for j in range(4)]
                xr = lsb("xr", [128, 1])
                xi = lsb("xi", [128, 1])
                c4 = lsb("c4", [128, 4])
                hre = lsb("hre", [128, 32])
                him = lsb("him", [128, 32])
                finr = lsb("finr", [128, nseq, 32])
                fini = lsb("fini", [128, nseq, 32])
                ystg = [lsb("ystg%d" % j, [128, 4, 32]) for j in range(2)]
                XS = [dict(sfx="_a", tj=tj, tf=tf, iti=iti, cst=cst, snt=snt, p1=p1, p2=p2, inre=inre, inim=inim, zr=zr,
                           zi=zi, qq=qq, xr=xr, xi=xi, c4=c4, u2=u2, ystg=ystg, pb=(0, 1, 2), ucnt=0, ycnt=0)]
                XS.append(dict(
                    sfx="_b", tj=lsb("tjb", [128, 512]), tf=lsb("tfb", [128, 512]), iti=lsb("itib", [128, 512], I32),
                    cst=lsb("cstb", [128, 512]), snt=lsb("sntb", [128, 512]), p1=lsb("p1b", [128, 512]),
                    p2=lsb("p2b", [128, 512]), inre=lsb("inreb", [128, 512]), inim=lsb("inimb", [128, 512]),
                    zr=lsb("zrb", [128, 512]), zi=lsb("zib", [128, 512]),
                    qq=[lsb("qqb%d" % j, [128, 512], BF16) for j in range(4)], xr=lsb("xrb", [128, 1]),
                    xi=lsb("xib", [128, 1]), c4=lsb("c4b", [128, 4]),
                    u2=[lsb("u2b_%d" % j, [32, 512]) for j in range(2)],
                    ystg=[lsb("ystgb%d" % j, [128, 4, 32]) for j in range(2)], pb=(3, 4, 5), ucnt=0, ycnt=0))
                S.dma("sp", jjt[:], I["jj"][:, :], w=["jjt"])
                for zt, zk in ((BDr, "BDr"), (BDi, "BDi"), (Zr, "Zr"), (Zi, "Zi")):
                    S.op("dve", lambda e, zt=zt: e.memset(zt[:], 0.0), w=[zk])
                K_ = "s5p"

                def tt(o, a, b, op):
                    S.op("dve", lambda e: e.tensor_tensor(out=pt_[o][:], in0=pt_[a][:], in1=pt_[b][:], op=op),
                         r=[K_], w=[K_])

                def ts(o, a, s1, s2, op0, op1=None):
                    if op1 is None:
                        S.op("dve", lambda e: e.tensor_scalar(out=pt_[o][:], in0=pt_[a][:], scalar1=s1, scalar2=None,
                                                              op0=op0), r=[K_], w=[K_])
                    else:
                        S.op("dve", lambda e: e.tensor_scalar(out=pt_[o][:], in0=pt_[a][:], scalar1=s1, scalar2=s2,
                                                              op0=op0, op1=op1), r=[K_], w=[K_])

                def frac_sin(o, a):
                    S.op("dve", lambda e: e.tensor_copy(out=it_[:], in_=pt_[a][:]), r=[K_], w=[K_])
                    S.op("dve", lambda e: e.tensor_copy(out=pt_["t2"][:], in_=it_[:]), r=[K_], w=[K_])
                    tt("t2", a, "t2", ALU.subtract)
                    S.op("act", lambda e: e.activation(out=pt_[o][:], in_=pt_["t2"][:], func=AF.Sin, scale=SIN_SCALE),
                         r=[K_], w=[K_])

                cnt = 0
                ucnt = 0
                for d in range(2):
                    with nc.allow_non_contiguous_dma(reason="small s5 parameter transposes"):
                        for g2 in range(2):
                            sl = slice(g2 * 64, (g2 + 1) * 64)
                            S.dma("sp", pt_["lre"][sl, :],
                                  I["s5_lam_re"][l, d].rearrange("(k g) p -> g p k", g=2)[g2], w=[K_])
                            S.dma("sp", pt_["lim"][sl, :],
                                  I["s5_lam_im"][l, d].rearrange("(k g) p -> g p k", g=2)[g2], w=[K_])
                            S.dma("sp", pt_["ldt"][sl, :],
                                  I["s5_log_dt"][l, d].rearrange("(k g) -> g k", g=2)[g2].partition_broadcast(64),
                                  w=[K_])
                            if sample:
                                S.dma("sp", hre[sl, :], I["st_s5re"][l, d].rearrange("(k g) p -> g p k", g=2)[g2],
                                      w=["hre"])
                                S.dma("sp", him[sl, :], I["st_s5im"][l, d].rearrange("(k g) p -> g p k", g=2)[g2],
                                      w=["him"])
                    ts("lre", "lre", -1e-4, None, ALU.min)
                    S.op("act", lambda e: e.activation(out=pt_["ldt"][:], in_=pt_["ldt"][:], func=AF.Exp), r=[K_],
                         w=[K_])
                    tt("t1", "lre", "ldt", ALU.mult)
                    S.op("act", lambda e: e.activation(out=pt_["rho"][:], in_=pt_["t1"][:], func=AF.Exp), r=[K_],
                         w=[K_])
                    tt("tht", "lim", "ldt", ALU.mult)
                    ts("tht", "tht", 1.0 / TWO_PI, None, ALU.mult)
                    frac_sin("sn", "tht")
                    ts("t3", "tht", 0.25, None, ALU.add)
                    frac_sin("cs", "t3")
                    tt("ar", "rho", "cs", ALU.mult)
                    tt("ai", "rho", "sn", ALU.mult)
                    ts("ar", "ar", -1.0, None, ALU.add)
                    tt("t1", "ar", "lre", ALU.mult)
                    tt("t2", "ai", "lim", ALU.mult)
                    tt("cr", "t1", "t2", ALU.add)
                    tt("t1", "ai", "lre", ALU.mult)
                    tt("t2", "ar", "lim", ALU.mult)
                    tt("ci", "t1", "t2", ALU.subtract)
                    tt("t1", "lre", "lre", ALU.mult)
                    tt("t2", "lim", "lim", ALU.mult)
                    tt("t1", "t1", "t2", ALU.add)
                    S.op("dve", lambda e: e.reciprocal(out=pt_["t1"][:], in_=pt_["t1"][:]), r=[K_], w=[K_])
                    tt("cr", "cr", "t1", ALU.mult)
                    tt("ci", "ci", "t1", ALU.mult)
                    S.dma("sp", bre[:], I["s5_b_re"][l, d].rearrange("(k g) p c -> (g p) k c", g=2), w=["bre"])
                    S.dma("sp", bim[:], I["s5_b_im"][l, d].rearrange("(k g) p c -> (g p) k c", g=2), w=["bim"])
                    crb = pt_["cr"][:, :].unsqueeze(2).to_broadcast([128, 32, 16])
                    cib = pt_["ci"][:, :].unsqueeze(2).to_broadcast([128, 32, 16])
                    S.op("dve", lambda e: e.tensor_tensor(out=Bbr[:], in0=bre[:], in1=crb, op=ALU.mult),
                         r=["bre", K_], w=["Bbr"])
                    S.op("dve", lambda e: e.tensor_tensor(out=btmp[:], in0=bim[:], in1=cib, op=ALU.mult),
                         r=["bim", K_], w=["btmp"])
                    S.op("dve", lambda e: e.tensor_sub(out=Bbr[:], in0=Bbr[:], in1=btmp[:]), r=["Bbr", "btmp"],
                         w=["Bbr"])
                    S.op("dve", lambda e: e.tensor_tensor(out=Bbi[:], in0=bre[:], in1=cib, op=ALU.mult),
                         r=["bre", K_], w=["Bbi"])
                    S.op("dve", lambda e: e.tensor_tensor(out=btmp[:], in0=bim[:], in1=crb, op=ALU.mult),
                         r=["bim", K_], w=["btmp"])
                    S.op("dve", lambda e: e.tensor_add(out=Bbi[:], in0=Bbi[:], in1=btmp[:]), r=["Bbi", "btmp"],
                         w=["Bbi"])
                    for (bsrc, bkey, bd, bdk, bp, bpk) in ((Bbr, "Bbr", BDr, "BDr", BpTr, "BpTr"),
                                                           (Bbi, "Bbi", BDi, "BDi", BpTi, "BpTi")):
                        S.op("dve", lambda e: e.tensor_copy(out=bd[0:64, :, 0:16], in_=bsrc[0:64, :, :]), r=[bkey],
                             w=[bdk])
                        S.op("dve", lambda e: e.tensor_copy(out=bd[64:128, :, 16:32], in_=bsrc[64:128, :, :]),
                             r=[bkey], w=[bdk])
                        transposes_to(bd[:].rearrange("p k c -> p (k c)"), bdk, 32,
                                      lambda q, nj, bp=bp: bp[:, q:q + nj, :], bpk, bw=32, inw=128)
                    for (zt, zk, nm) in ((Zr, "Zr", "s5_c_re"), (Zi, "Zi", "s5_c_im")):
                        csrc = I[nm][l, d].rearrange("(k g) c p -> g c k p", g=2)
                        S.dma("sp", zt[0:16, :, 0:64], csrc[0], w=[zk])
                        S.dma("sp", zt[16:32, :, 64:128], csrc[1], w=[zk])
                    zf = lambda zt: zt[:].rearrange("r k q -> r (k q)")
                    transposes_to(zf(Zr), "Zr", 32, lambda q, nj: Cre[:, q:q + nj, :], "Cre", bw=128, inw=32)
                    transposes_to(zf(Zr), "Zr", 32, lambda q, nj: nCre[:, q:q + nj, :], "nCre", bw=128, inw=32,
                                  scale=-1.0)
                    transposes_to(zf(Zi), "Zi", 32, lambda q, nj: nCim[:, q:q + nj, :], "nCim", bw=128, inw=32,
                                  scale=-1.0)
                    Cm = (Cre, nCre, nCim, nCim)
                    Ck = ("Cre", "nCre", "nCim", "nCim")
                    def chain(kt, X):
                        sfx = X["sfx"]
                        kx = lambda n: n + sfx
                        tj, tf, iti, cst, snt = X["tj"], X["tf"], X["iti"], X["cst"], X["snt"]
                        p1, p2, inre, inim, zr, zi, qq = X["p1"], X["p2"], X["inre"], X["inim"], X["zr"], X["zi"], X["qq"]
                        xr, xi, c4 = X["xr"], X["xi"], X["c4"]
                        pb0, pb1, pb2 = X["pb"]
                        S.op("dve", lambda e: e.tensor_scalar(out=tj[:, 0:Ls], in0=jjt[:, 0:Ls],
                                                              scalar1=pt_["tht"][:, kt:kt + 1], scalar2=None,
                                                              op0=ALU.mult), r=["jjt", K_], w=[kx("tj")])
                        yield
                        for (dst, dk_, off) in ((snt, "snt", 0.0), (cst, "cst", 0.25)):
                            if off != 0.0:
                                S.op("dve", lambda e: e.tensor_scalar(out=tj[:, 0:Ls], in0=tj[:, 0:Ls], scalar1=off,
                                                                      scalar2=None, op0=ALU.add), r=[kx("tj")],
                                     w=[kx("tj")])
                                yield
                            S.op("dve", lambda e: e.tensor_copy(out=iti[:, 0:Ls], in_=tj[:, 0:Ls]), r=[kx("tj")],
                                 w=[kx("iti")])
                            yield
                            S.op("dve", lambda e: e.tensor_copy(out=tf[:, 0:Ls], in_=iti[:, 0:Ls]), r=[kx("iti")],
                                 w=[kx("tf")])
                            yield
                            S.op("dve", lambda e: e.tensor_sub(out=tf[:, 0:Ls], in0=tj[:, 0:Ls], in1=tf[:, 0:Ls]),
                                 r=[kx("tj"), kx("tf")], w=[kx("tf")])
                            yield
                            S.op("act", lambda e, dst=dst: e.activation(out=dst[:, 0:Ls], in_=tf[:, 0:Ls],
                                                                        func=AF.Sin, scale=SIN_SCALE),
                                 r=[kx("tf")], w=[kx(dk_)])
                            yield
                        rhob = pt_["rho"][:, kt:kt + 1].to_broadcast([128, Ls])
                        for s_ in range(nseq):
                            if sample:
                                S.op("dve", lambda e: e.tensor_copy(out=xr[:], in_=hre[:, kt:kt + 1]), r=["hre"],
                                     w=[kx("xr")])
                                S.op("dve", lambda e: e.tensor_copy(out=xi[:], in_=him[:, kt:kt + 1]), r=["him"],
                                     w=[kx("xi")])
                            else:
                                S.op("dve", lambda e: e.memset(xr[:], 0.0), w=[kx("xr")])
                                S.op("dve", lambda e: e.memset(xi[:], 0.0), w=[kx("xi")])
                            yield
                            for seg in range(nseg):
                                tau0 = s_ * L + seg * Ls
                                u_ = X["u2"][X["ucnt"] % 2]
                                uk = kx("u2_%d" % (X["ucnt"] % 2))
                                X["ucnt"] += 1
                                S.dma("sp", u_[:, 0:Ls], UT2[d][kt, :, tau0:tau0 + Ls], r=["UT2"], w=[uk])
                                S.op("pe", lambda e: e.matmul(psb[pb0][:, 0:Ls], lhsT=BpTr[:, kt, :], rhs=u_[:, 0:Ls],
                                                              start=True, stop=True), r=["BpTr", uk], w=[PK[pb0]])
                                S.op("pe", lambda e: e.matmul(psb[pb1][:, 0:Ls], lhsT=BpTi[:, kt, :], rhs=u_[:, 0:Ls],
                                                              start=True, stop=True), r=["BpTi", uk], w=[PK[pb1]])
                                yield
                                c_ = cst[:, 0:Ls]
                                s__ = snt[:, 0:Ls]
                                S.op("dve", lambda e: e.tensor_mul(out=p1[:, 0:Ls], in0=psb[pb0][:, 0:Ls], in1=c_),
                                     r=[PK[pb0], kx("cst")], w=[kx("p1")])
                                yield
                                S.op("dve", lambda e: e.tensor_mul(out=p2[:, 0:Ls], in0=psb[pb1][:, 0:Ls], in1=s__),
                                     r=[PK[pb1], kx("snt")], w=[kx("p2")])
                                yield
                                S.op("dve", lambda e: e.tensor_add(out=inre[:, 0:Ls], in0=p1[:, 0:Ls],
                                                                   in1=p2[:, 0:Ls]), r=[kx("p1"), kx("p2")],
                                     w=[kx("inre")])
                                yield
                                S.op("dve", lambda e: e.tensor_mul(out=p1[:, 0:Ls], in0=psb[pb1][:, 0:Ls], in1=c_),
                                     r=[PK[pb1], kx("cst")], w=[kx("p1")])
                                yield
                                S.op("dve", lambda e: e.tensor_mul(out=p2[:, 0:Ls], in0=psb[pb0][:, 0:Ls], in1=s__),
                                     r=[PK[pb0], kx("snt")], w=[kx("p2")])
                                yield
                                S.op("dve", lambda e: e.tensor_sub(out=inim[:, 0:Ls], in0=p1[:, 0:Ls],
                                                                   in1=p2[:, 0:Ls]), r=[kx("p1"), kx("p2")],
                                     w=[kx("inim")])
                                yield
                                S.op("dve", lambda e: e.tensor_tensor_scan(out=zr[:, 0:Ls], data0=rhob,
                                                                           data1=inre[:, 0:Ls], initial=xr[:, 0:1],
                                                                           op0=ALU.mult, op1=ALU.add),
                                     r=[K_, kx("inre"), kx("xr")], w=[kx("zr")])
                                yield
                                S.op("dve", lambda e: e.tensor_tensor_scan(out=zi[:, 0:Ls], data0=rhob,
                                                                           data1=inim[:, 0:Ls], initial=xi[:, 0:1],
                                                                           op0=ALU.mult, op1=ALU.add),
                                     r=[K_, kx("inim"), kx("xi")], w=[kx("zi")])
                                yield
                                for (qi, a_, ak, b_, bk_) in ((0, cst, "cst", zr, "zr"), (1, snt, "snt", zi, "zi"),
                                                              (2, snt, "snt", zr, "zr"), (3, cst, "cst", zi, "zi")):
                                    S.op("dve", lambda e, qi=qi, a_=a_, b_=b_: e.tensor_mul(
                                        out=qq[qi][:, 0:Ls], in0=a_[:, 0:Ls], in1=b_[:, 0:Ls]),
                                        r=[kx(ak), kx(bk_)], w=[kx("qq%d" % qi)])
                                    yield
                                e0 = Ls - 1
                                for (ci_, a_, ak, b_, bk_) in ((0, cst, "cst", zr, "zr"), (1, snt, "snt", zi, "zi"),
                                                               (2, snt, "snt", zr, "zr"), (3, cst, "cst", zi, "zi")):
                                    S.op("dve", lambda e, ci_=ci_, a_=a_, b_=b_: e.tensor_mul(
                                        out=c4[:, ci_:ci_ + 1], in0=a_[:, e0:e0 + 1], in1=b_[:, e0:e0 + 1]),
                                        r=[kx(ak), kx(bk_)], w=[(kx("c4"), ci_)])
                                    yield
                                S.op("dve", lambda e: e.tensor_sub(out=xr[:], in0=c4[:, 0:1], in1=c4[:, 1:2]),
                                     r=[(kx("c4"), 0), (kx("c4"), 1)], w=[kx("xr")])
                                yield
                                S.op("dve", lambda e: e.tensor_add(out=xi[:], in0=c4[:, 2:3], in1=c4[:, 3:4]),
                                     r=[(kx("c4"), 2), (kx("c4"), 3)], w=[kx("xi")])
                                yield
                                for jb in range(nsub):
                                    for qi in range(4):
                                        S.op("pe", lambda e, jb=jb, qi=qi: e.matmul(
                                            psb[pb2][:, jb * 32:(jb + 1) * 32], lhsT=qq[qi][:, jb * 128:(jb + 1) * 128],
                                            rhs=Cm[qi][:, kt, :], start=(qi == 0), stop=(qi == 3)),
                                            r=[kx("qq%d" % qi), Ck[qi]], w=[PK[pb2]])
                                ys_ = X["ystg"][X["ycnt"] % 2]
                                yk = kx("ystg%d" % (X["ycnt"] % 2))
                                X["ycnt"] += 1
                                S.op("act", lambda e: e.activation(
                                    out=ys_[:, 0:nsub, :], in_=psb[pb2][:, 0:nsub * 32].rearrange("p (j c) -> p j c",
                                                                                               j=nsub),
                                    func=AF.Copy), r=[PK[pb2]], w=[yk])
                                S.dma("sp", YS5[d][tau0:tau0 + Ls, kt * 32:(kt + 1) * 32].rearrange(
                                    "(j t) c -> t j c", t=128), ys_[:, 0:nsub, :], r=[yk], w=["YS5"])
                                yield
                            if not sample:
                                S.op("dve", lambda e: e.tensor_copy(out=finr[:, s_, kt:kt + 1], in_=xr[:]),
                                     r=[kx("xr")], w=["finr"])
                                S.op("dve", lambda e: e.tensor_copy(out=fini[:, s_, kt:kt + 1], in_=xi[:]),
                                     r=[kx("xi")], w=["fini"])
                                yield

                    for kt in range(0, 32, 2):
                        gens = [chain(kt, XS[0]), chain(kt + 1, XS[1])]
                        while gens:
                            for g_ in list(gens):
                                try:
                                    next(g_)
                                except StopIteration:
                                    gens.remove(g_)
                    if not sample:
                        with nc.allow_non_contiguous_dma(reason="small s5 state outputs"):
                            for s_ in range(nseq):
                                for g2 in range(2):
                                    sl = slice(g2 * 64, (g2 + 1) * 64)
                                    S.dma("sp", O["ns_s5re"][s_, l, d].rearrange("(k g) p -> g p k", g=2)[g2],
                                          finr[sl, s_, :], r=["finr"], w=["ns_s5"])
                                    S.dma("sp", O["ns_s5im"][s_, l, d].rearrange("(k g) p -> g p k", g=2)[g2],
                                          fini[sl, s_, :], r=["fini"], w=["ns_s5"])
                S.barrier()

        def phase_tail(l, T, L, jrow, xsrc, xdst):
            LT = L // 128
            NB = BLK // 128
            with contextlib.ExitStack() as ps:
                lsb = lambda name, shape, dt=F32: ps.enter_context(nc.sbuf_tensor(uname(name), list(shape), dt))
                ls = {"xt": [lsb("xt0", [128, D]), lsb("xt1", [128, D])], "junk": lsb("junk", [128, D]),
                      "ss": lsb("ss", [128, 1])}
                wbs = [lsb("wb0", [128, 8192], BF16), lsb("wb1", [128, 8192], BF16)]
                WSTG["t"] = lsb("wst", [128, 8192])
                hTa = lsb("hTa", [128, KC, BLK], BF16)
                Ybuf = lsb("Ybuf", [128, NB, D])
                GPt = lsb("GPt", [128, D])
                yt = [lsb("yt%d" % j, [128, 1536]) for j in range(2)]
                s5d = lsb("s5d", [128, 1024])
                gt = [lsb("gt%d" % j, [128, 512]) for j in range(2)]
                vt = [lsb("vt%d" % j, [128, 512]) for j in range(2)]
                mgt = [lsb("mgt%d" % j, [128, 512]) for j in range(2)]
                xg = lsb("xg", [128, 512])
                tg = lsb("tg", [128, 512])
                bcload(GPt[:], "GPt", MOD[jrow, 2 * D:3 * D])
                bcload(ls["junk"][:], "junk", I["norm_mix_post"][l, :])
                S.op("dve", lambda e: e.tensor_mul(out=GPt[:], in0=GPt[:], in1=ls["junk"][:]), r=["GPt", "junk"],
                     w=["GPt"])
                bcload(s5d[:], "s5d", I["s5_d"][l, :])
                ec = [0]

                def epi_merge(bi, t0):
                    def f(i, c0, cw, pt, pk):
                        n = ec[0] % 2
                        ec[0] += 1
                        rs_ = slice(t0 + i * 128, t0 + (i + 1) * 128)
                        g_, gk = gt[n], "gt%d" % n
                        v_, vk = vt[n], "vt%d" % n
                        m_, mk = mgt[n], "mgt%d" % n
                        gc = C_GATE + bi * D + c0
                        S.dma("sp", g_[:, 0:cw], P[rs_, gc:gc + cw], r=["P"], w=[gk])
                        S.op("act", lambda e: e.activation(out=g_[:, 0:cw], in_=g_[:, 0:cw], func=AF.Sigmoid), r=[gk],
                             w=[gk])
                        if bi == 2:
                            S.op("act", lambda e: e.activation(out=v_[:, 0:cw], in_=pt[:, cw:2 * cw],
                                                               func=AF.Sigmoid), r=[pk], w=[vk])
                            S.op("dve", lambda e: e.tensor_mul(out=v_[:, 0:cw], in0=pt[:, 0:cw], in1=v_[:, 0:cw]),
                                 r=[pk, vk], w=[vk])
                            S.op("dve", lambda e: e.tensor_mul(out=v_[:, 0:cw], in0=v_[:, 0:cw], in1=g_[:, 0:cw]),
                                 r=[vk, gk], w=[vk])
                        else:
                            S.op("dve", lambda e: e.tensor_mul(out=v_[:, 0:cw], in0=pt[:, 0:cw], in1=g_[:, 0:cw]),
                                 r=[pk, gk], w=[vk])
                        mgk = ("MG", i, c0 // 512)
                        if bi > 0:
                            S.dma("sp", m_[:, 0:cw], MG[rs_, c0:c0 + cw], r=[mgk], w=[mk])
                            S.op("dve", lambda e: e.tensor_add(out=v_[:, 0:cw], in0=v_[:, 0:cw], in1=m_[:, 0:cw]),
                                 r=[vk, mk], w=[vk])
                        S.dma("sp", MG[rs_, c0:c0 + cw], v_[:, 0:cw], r=[vk], w=[mgk])
                    return f

                def wload_glu(Wflat):
                    def f(wb, wk, c0, cw):
                        wst = WSTG["t"]
                        kch = wb.shape[1]
                        n = 128 * kch * 2 * cw
                        off = 128 * kch * 2 * c0
                        stv = wst[:, 0:kch * 2 * cw].rearrange("p (k n) -> p k n", k=kch)
                        S.dma("sp", stv, Wflat[off:off + n].rearrange("(p k n) -> p k n", p=128, k=kch), w=["wst"])
                        S.op("dve", lambda e: e.tensor_copy(out=wb, in_=stv), r=["wst"], w=[wk])
                    return f

                def epi_y(i, c0, cw, pt, pk):
                    S.op("act", lambda e: e.activation(out=Ybuf[:, i, c0:c0 + cw], in_=pt[:, 0:cw], func=AF.Copy),
                         r=[pk], w=["Ybuf"])

                for bi_ in range(T // BLK):
                    t0 = bi_ * BLK
                    for (br, src, Wn) in ((0, YS, "w_ssm_out"), (1, ZW, "w_wkv_out")):
                        for i in range(NB):
                            y_ = yt[i % 2]
                            yk = "yt%d" % (i % 2)
                            S.dma("sp", y_[:], src[t0 + i * 128:t0 + (i + 1) * 128, :], r=["BSRC"], w=[yk])
                            transposes_to(y_[:], yk, 12, lambda q, nj, i=i: hTa[:, q:q + nj, i * 128:(i + 1) * 128],
                                          "hTa")
                        proj(wbs, hTa, "hTa", 12, D, NB, epi_merge(br, t0), wload_plain(I[Wn][l]))
                    for i in range(NB):
                        ta = t0 + i * 128
                        s_ = ta // L
                        ti = (ta % L) // 128
                        tr = s_ * L + (LT - 1 - ti) * 128
                        yf = ls["xt"][0]
                        yb = ls["xt"][1]
                        uu = yt[i % 2]
                        uk = "yt%d" % (i % 2)
                        S.dma("sp", yf[:, 0:1024], YS5[0][ta:ta + 128, :], r=["YS5"], w=["xt0"])
                        S.dma("sp", yb[:, 0:1024], YS5[1][tr:tr + 128, :], r=["YS5"], w=["xt1"])
                        S.dma("sp", uu[:, 0:1024], P[ta:ta + 128, C_U:C_U + 1024], r=["P"], w=[uk])
                        S.op("dve", lambda e: e.tensor_mul(out=uu[:, 0:1024], in0=uu[:, 0:1024], in1=s5d[:]),
                             r=[uk, "s5d"], w=[uk])
                        for hb in range(2):
                            pb = 4 + hb
                            for j in range(4):
                                cb = hb * 4 + j
                                cs_ = slice(cb * 128, (cb + 1) * 128)
                                o_ = psb[pb][:, j * 128:(j + 1) * 128]
                                S.op("pe", lambda e: e.matmul(o_, lhsT=yf[:, cs_], rhs=ident[:], start=True,
                                                              stop=False), r=["xt0", "ident"], w=[PK[pb]])
                                S.op("pe", lambda e: e.matmul(o_, lhsT=yb[:, cs_], rhs=jrev[:], start=False,
                                                              stop=False), r=["xt1", "jrev"], w=[PK[pb]])
                                S.op("pe", lambda e: e.matmul(o_, lhsT=uu[:, cs_], rhs=ident[:], start=False,
                                                              stop=True), r=[uk, "ident"], w=[PK[pb]])
                            S.op("act", lambda e: e.activation(out=xg[:], in_=psb[pb][:, :], func=AF.Copy),
                                 r=[PK[pb]], w=["xg"])
                            S.op("dve", lambda e: e.tensor_mul(out=tg[:], in0=xg[:], in1=xg[:]), r=["xg"], w=["tg"])
                            S.op("dve", lambda e: e.tensor_scalar(out=tg[:], in0=tg[:], scalar1=0.044715, scalar2=1.0,
                                                                  op0=ALU.mult, op1=ALU.add), r=["tg"], w=["tg"])
                            S.op("dve", lambda e: e.tensor_mul(out=tg[:], in0=tg[:], in1=xg[:]), r=["tg", "xg"],
                                 w=["tg"])
                            S.op("act", lambda e: e.activation(out=tg[:], in_=tg[:], func=AF.Sigmoid,
                                                               scale=2.0 * math.sqrt(2.0 / math.pi)), r=["tg"],
                                 w=["tg"])
                            S.op("dve", lambda e: e.tensor_tensor(
                                out=hTa[:, hb * 4:(hb + 1) * 4, i * 128:(i + 1) * 128],
                                in0=xg[:].rearrange("p (j t) -> p j t", j=4),
                                in1=tg[:].rearrange("p (j t) -> p j t", j=4), op=ALU.mult), r=["xg", "tg"], w=["hTa"])
                    proj(wbs, hTa, "hTa", 8, D, NB, epi_merge(2, t0), wload_glu(I["w_s5_glu"][l]), cwmax=256, wmul=2)
                    for i in range(NB):
                        xt = ls["xt"][i % 2]
                        xk = "xt%d" % (i % 2)
                        S.dma("sp", xt[:], MG[t0 + i * 128:t0 + (i + 1) * 128, :],
                              r=[("MG", i, c) for c in range(4)],
                              w=[xk])
                        transposes_to(xt[:], xk, KC, lambda q, nj, i=i: hTa[:, q:q + nj, i * 128:(i + 1) * 128],
                                      "hTa")
                    proj(wbs, hTa, "hTa", KC, D, NB, epi_y, wload_plain(I["w_out"][l]))
                    postnorm_residual(ls, Ybuf, NB, t0, xsrc, xdst, GPt)
                S.barrier()

        def phase_ffn(l, T, GW, jrow, xsrc, xdst):
            NB = BLK // 128
            with contextlib.ExitStack() as ps:
                lsb = lambda name, shape, dt=F32: ps.enter_context(nc.sbuf_tensor(uname(name), list(shape), dt))
                ls = {"xt": [lsb("xt0", [128, D]), lsb("xt1", [128, D])], "junk": lsb("junk", [128, D]),
                      "ss": lsb("ss", [128, 1])}
                wbs = [lsb("wb0", [128, 8192], BF16), lsb("wb1", [128, 8192], BF16)]
                WSTG["t"] = lsb("wst", [128, 8192])
                st = [lsb("st0", [128, 512]), lsb("st1", [128, 512])]
                BI = min(BLK_IN, T)
                hT = lsb("hT", [128, KC, BI], BF16)
                Gt, SHt, GPt = mod_tiles(lsb, ls, l, jrow, 4, 3, 5, "norm_ffn_pre", "norm_ffn_post")
                cnt = [0]
                for bi in range(T // BI):
                    t0 = bi * BI
                    norm_to_fm(ls, xsrc, t0, BI // 128, hT, Gt, SHt)

                    def epi(i, c0, cw, pt, pk, t0=t0):
                        cnt[0] += 1
                        s_ = st[cnt[0] % 2]
                        sk = "st%d" % (cnt[0] % 2)
                        S.op("act", lambda e: e.activation(out=s_[:, 0:cw], in_=pt[:, 0:cw], func=AF.Copy),
                             r=[pk], w=[sk])
                        S.dma("sp", GU[t0 + i * 128:t0 + (i + 1) * 128, c0:c0 + cw], s_[:, 0:cw], r=[sk], w=["GU"])

                    proj(wbs, hT, "hT", KC, 2 * D_FF, BI // 128, epi, wload_plain(I["w_ffn_in"][l]))
                S.barrier()
            with contextlib.ExitStack() as ps:
                up = [ps.enter_context(nc.sbuf_tensor(uname("fup%d" % j), [128, 512], F32)) for j in range(2)]
                sg = ps.enter_context(nc.sbuf_tensor(uname("fsg"), [128, 512], F32))
                uc = [0]

                def post(i, c0, cw, y, yk):
                    u_ = up[uc[0] % 2]
                    uk = "fup%d" % (uc[0] % 2)
                    uc[0] += 1
                    S.dma("sp", u_[:, 0:cw], GU[i * 128:(i + 1) * 128, D_FF + c0:D_FF + c0 + cw], r=["GU"], w=[uk])
                    S.op("act", lambda e: e.activation(out=sg[:, 0:cw], in_=y[:, 0:cw], func=AF.Sigmoid), r=[yk],
                         w=["fsg"])
                    S.op("dve", lambda e: e.tensor_mul(out=y[:, 0:cw], in0=y[:, 0:cw], in1=sg[:, 0:cw]),
                         r=[yk, "fsg"], w=[yk])
                    S.op("dve", lambda e: e.tensor_mul(out=y[:, 0:cw], in0=y[:, 0:cw], in1=u_[:, 0:cw]),
                         r=[yk, uk], w=[yk])
                    S.dma("sp", ACTS[i * 128:(i + 1) * 128, c0:c0 + cw], y[:, 0:cw], r=[yk], w=["ACTS"])

                conv_pass(ps, T, GW, GU, 0, D_FF,
                          lambda c0, cw: (I["ffn_conv_w"][l, 0, c0:c0 + cw], I["ffn_conv_w"][l, 1, c0:c0 + cw],
                                          I["ffn_conv_w"][l, 2, c0:c0 + cw], I["ffn_conv_b"][l, c0:c0 + cw]), post)
                S.barrier()
            with contextlib.ExitStack() as ps:
                lsb = lambda name, shape, dt=F32: ps.enter_context(nc.sbuf_tensor(uname(name), list(shape), dt))
                ls = {"xt": [lsb("xt0", [128, D]), lsb("xt1", [128, D])], "junk": lsb("junk", [128, D]),
                      "ss": lsb("ss", [128, 1])}
                wbs = [lsb("wb0", [128, 5632], BF16), lsb("wb1", [128, 5632], BF16)]
                WSTG["t"] = lsb("wst", [128, 5632])
                hTf = lsb("hTf", [128, 44, BLK], BF16)
                Ybuf = lsb("Ybuf", [128, NB, D])
                GPt = lsb("GPt", [128, D])
                at = [lsb("at%d" % j, [128, 2816]) for j in range(2)]
                bcload(GPt[:], "GPt", MOD[jrow, 5 * D:6 * D])
                bcload(ls["junk"][:], "junk", I["norm_ffn_post"][l, :])
                S.op("dve", lambda e: e.tensor_mul(out=GPt[:], in0=GPt[:], in1=ls["junk"][:]), r=["GPt", "junk"],
                     w=["GPt"])

                def epi_y(i, c0, cw, pt, pk):
                    S.op("act", lambda e: e.activation(out=Ybuf[:, i, c0:c0 + cw], in_=pt[:, 0:cw], func=AF.Copy),
                         r=[pk], w=["Ybuf"])

                ac = 0
                for bi in range(T // BLK):
                    t0 = bi * BLK
                    for i in range(NB):
                        for hf in range(2):
                            a_ = at[ac % 2]
                            ak = "at%d" % (ac % 2)
                            ac += 1
                            S.dma("sp", a_[:], ACTS[t0 + i * 128:t0 + (i + 1) * 128, hf * 2816:(hf + 1) * 2816],
                                  r=["ACTS"], w=[ak])
                            transposes_to(a_[:], ak, 22,
                                          lambda q, nj, i=i, hf=hf: hTf[:, hf * 22 + q:hf * 22 + q + nj,
                                                                        i * 128:(i + 1) * 128], "hTf")
                    proj(wbs, hTf, "hTf", 44, D, NB, epi_y, wload_plain(I["w_ffn_out"][l]), cwmax=128)
                    postnorm_residual(ls, Ybuf, NB, t0, xsrc, xdst, GPt)
                S.barrier()

        kb.fns = dict(phase_mod=phase_mod, phase_inproj=phase_inproj, phase_convs=phase_convs, phase_ssd=phase_ssd)

        groups = debug.get("groups", ["s", "p"])
        skip = debug.get("skip", ())
        nl = debug.get("layers", DEPTH)
        for l in range(nl):
            phase_mod(l)
            for gname in groups:
                last = (l == DEPTH - 1)
                if gname == "s":
                    T, L, GW, jrow, sample = TS, TS, 64, 0, True
                    xin = I["xs"] if l == 0 else XB
                    xmid = XA
                    xout = O["ys"] if last else XB
                else:
                    T, L, GW, jrow, sample = TP, 256, 256, 1, False
                    xin = I["xp"] if l == 0 else XPB
                    xmid = XPA
                    xout = O["yp"] if last else XPB
                phase_inproj(l, xin, T, jrow)
                phase_convs(l, T, GW)
                if "ssd" not in skip:
                    phase_ssd(l, T, L, sample)
                if "wkv" not in skip:
                    wp = debug.get("wkv_parts", ("prep", "scan", "post"))
                    if "prep" in wp:
                        phase_wkv_prep(l, T)
                    if "scan" in wp:
                        phase_wkv_scan(l, T, L, sample)
                    if "post" in wp:
                        phase_wkv_post(l, T)
                if "s5" not in skip:
                    phase_s5(l, T, L, sample)
                if "tail" not in skip:
                    phase_tail(l, T, L, jrow, xin, xmid)
                if "ffn" not in skip:
                    phase_ffn(l, T, GW, jrow, xmid, xout)
        S.barrier()
    return nc, S


def _consts():
    k = np.arange(128)
    tri = (k[:, None] <= k[None, :]).astype(np.float32)
    m48 = np.zeros((48, 768), np.float32)
    for j in range(4):
        for g in range(12):
            m48[j * 12 + g, g * 64:(g + 1) * 64] = 1.0
    return {
        "ident": np.eye(128, dtype=np.float32), "tri": tri, "trit": np.ascontiguousarray(tri.T),
        "jrev": np.ascontiguousarray(np.eye(128, dtype=np.float32)[::-1]), "mask48": m48,
        "jj": np.ascontiguousarray(np.broadcast_to(np.arange(1, 513, dtype=np.float32), (128, 512))),
        "zrow": np.zeros((1, 512), np.float32),
    }


def _prep_inputs(inputs):
    f = lambda a: np.ascontiguousarray(np.asarray(a, dtype=np.float32))
    shared = {}
    for nm, shp in PARAM_SHAPES.items():
        a = f(inputs[nm]).reshape(shp)
        if nm in RELAID:
            a = np.stack([_relayout(a[l], RELAID[nm][2]) for l in range(DEPTH)], 0)
        elif nm == "w_s5_glu":
            a = np.stack([_relayout_glu(a[l]) for l in range(DEPTH)], 0)
        shared[nm] = a
    shared.update(_consts())
    maps = []
    for c in range(8):
        b = c % 4
        m = dict(shared)
        m["xs"] = f(inputs["x_sample"][b])
        m["xp"] = f(np.asarray(inputs["x_prompt"])[NSP * c:NSP * c + NSP].reshape(TP, D))
        m["cond2"] = f(np.stack([np.asarray(inputs["c"])[b], np.asarray(inputs["c_ctx"])], 0))
        m["st_ssm"] = f(inputs["state_ssm"][b])
        m["st_wkv"] = f(inputs["state_wkv"][b])
        m["st_s5re"] = f(inputs["state_s5_re"][b])
        m["st_s5im"] = f(inputs["state_s5_im"][b])
        maps.append(m)
    return maps


def kernel(**inputs):
    nc, S = build()
    maps = _prep_inputs(inputs)
    res = run_bass_kernel_spmd(nc, maps, core_ids=list(range(8)))
    r = res.results
    y_prompt = np.zeros((16, 256, D), np.float32)
    y_sample = np.zeros((4, TS, D), np.float32)
    ns_ssm = np.zeros((16, DEPTH, 2, 24, 64, 128), np.float32)
    ns_wkv = np.zeros((16, DEPTH, 2, 24, 64, 64), np.float32)
    ns_re = np.zeros((16, DEPTH, 2, 64, 64), np.float32)
    ns_im = np.zeros((16, DEPTH, 2, 64, 64), np.float32)
    for b in range(4):
        y_sample[b] = np.asarray(r[b]["ys"])
    for c in range(8):
        o = r[c]
        sl = slice(NSP * c, NSP * c + NSP)
        y_prompt[sl] = np.asarray(o["yp"]).reshape(NSP, 256, D)
        ns_ssm[sl] = np.asarray(o["ns_ssm"])
        ns_wkv[sl] = np.asarray(o["ns_wkv"])
        ns_re[sl] = np.asarray(o["ns_s5re"])
        ns_im[sl] = np.asarray(o["ns_s5im"])
    return (y_prompt, y_sample, ns_ssm, ns_wkv, ns_re, ns_im)
```

```python
import contextlib
import math
import numpy as np
import concourse.bass as bass
import concourse.mybir as mybir
import concourse.ap as apm
from concourse.bass_utils import run_bass_kernel_spmd

F32 = mybir.dt.float32
BF16 = mybir.dt.bfloat16
I32 = mybir.dt.int32
AF = mybir.ActivationFunctionType
ALU = mybir.AluOpType
AX = mybir.AxisListType

D = 2048
KC = 16
DEPTH = 2
N_IN = 16560
D_FF = 5632
TS = 4096
NSP = 2
TP = NSP * 256
BLK = 512
BLK_IN = 2048
CV = 1024
EPS = 1e-6
TWO_PI = 2.0 * math.pi
SIN_SCALE = 6.28318

C_Z = 0
C_XBC = 1536
C_DTF = 4096
C_WKV = 4144
C_U = 9392
C_GATE = 10416
NWKV = 5248


class Sched:
    def __init__(self, nc, es, n_dma_sems=24):
        self.nc = nc
        self.eng = {"pe": nc.tensor, "act": nc.scalar, "dve": nc.vector, "pool": nc.gpsimd, "sp": nc.sync}
        self.sem = {e: es.enter_context(nc.semaphore("sem_" + e)) for e in ("pe", "act", "dve", "pool")}
        self.cnt = {e: 0 for e in self.sem}
        self.dsem = [es.enter_context(nc.semaphore("dsem%d" % i)) for i in range(n_dma_sems)]
        self.dval = [0] * n_dma_sems
        self.dnext = 0
        self.waited = {e: {} for e in self.eng}
        self.lastw = {}
        self.readers = {}
        self.ninst = 0

    def _wait(self, e, tok, force=False):
        sem, val, src = tok
        w = self.waited[e]
        if w.get(id(sem), 0) >= val:
            return
        if src == e == "pe" and not force:
            return
        self.eng[e].wait_ge(sem, val)
        w[id(sem)] = val

    def _deps(self, e, r, w):
        for k in list(r) + list(w):
            t = self.lastw.get(k)
            if t is not None:
                self._wait(e, t)
        for k in w:
            for t in self.readers.get(k, ()):
                self._wait(e, t)

    def _commit(self, tok, r, w):
        for k in w:
            self.lastw[k] = tok
            self.readers[k] = []
        for k in r:
            lst = self.readers.setdefault(k, [])
            lst.append(tok)
            if len(lst) > 64:
                best = {}
                for t in lst:
                    if id(t[0]) not in best or best[id(t[0])][1] < t[1]:
                        best[id(t[0])] = t
                self.readers[k] = list(best.values())

    def op(self, e, fn, r=(), w=()):
        self._deps(e, r, w)
        ins = fn(self.eng[e])
        self.cnt[e] += 1
        ins.then_inc(self.sem[e], 1)
        tok = (self.sem[e], self.cnt[e], e)
        self._commit(tok, r, w)
        self.ninst += 1
        return tok

    def dma(self, q, out, in_, r=(), w=(), **kw):
        i = self.dnext
        self.dnext = (self.dnext + 1) % len(self.dsem)
        sem = self.dsem[i]
        if self.dval[i] > 0:
            self._wait(q, (sem, self.dval[i], "dma"))
        self._deps(q, r, w)
        ins = self.eng[q].dma_start(out=out, in_=in_, **kw)
        self.dval[i] += 16
        ins.then_inc(sem, 16)
        tok = (sem, self.dval[i], "dma")
        self._commit(tok, r, w)
        self.ninst += 1
        return tok

    def barrier(self):
        toks = [(self.sem[e], self.cnt[e], e) for e in self.sem if self.cnt[e] > 0]
        toks += [(self.dsem[i], self.dval[i], "dma") for i in range(len(self.dsem)) if self.dval[i] > 0]
        for e in self.eng:
            for t in toks:
                self._wait(e, t, force=True)
        self.lastw.clear()
        self.readers.clear()


class KB:
    def __init__(self, debug=None):
        self.debug = debug or {}
        self.nc = bass.Bass("TRN2", target_bir_lowering=False)
        self.I = {}
        self.O = {}

    def din(self, name, shape, dt=F32):
        self.I[name] = self.nc.dram_tensor(name, list(shape), dt, kind="ExternalInput").ap()
        return self.I[name]

    def dout(self, name, shape, dt=F32):
        self.O[name] = self.nc.dram_tensor(name, list(shape), dt, kind="ExternalOutput").ap()
        return self.O[name]

    def dscr(self, name, shape, dt=F32):
        kind = "ExternalOutput" if name in self.debug.get("dump", ()) else "Internal"
        return self.nc.dram_tensor(name, list(shape), dt, kind=kind).ap()


PARAM_SHAPES = {
    "w_mod": (DEPTH, D, 6 * D), "b_mod": (DEPTH, 6 * D),
    "norm_mix_pre": (DEPTH, D), "norm_mix_post": (DEPTH, D), "norm_ffn_pre": (DEPTH, D), "norm_ffn_post": (DEPTH, D),
    "w_in": (DEPTH, D, N_IN),
    "ssm_conv_w": (DEPTH, 3, 2560), "ssm_conv_b": (DEPTH, 2560), "ssm_dt_bias": (DEPTH, 48),
    "ssm_a_log": (DEPTH, 48), "ssm_d": (DEPTH, 24), "ssm_norm": (DEPTH, 1536), "w_ssm_out": (DEPTH, 1536, D),
    "wkv_mu_prev": (DEPTH, NWKV), "wkv_mu_next": (DEPTH, NWKV), "wkv_w0": (DEPTH, 2, 1536),
    "wkv_w_up": (DEPTH, 2, 96, 1536), "wkv_a0": (DEPTH, 2, 1536), "wkv_a_up": (DEPTH, 2, 96, 1536),
    "wkv_g_up": (DEPTH, 256, 1536), "wkv_k_k": (DEPTH, 1536), "wkv_k_a": (DEPTH, 1536), "wkv_r_k": (DEPTH, 1536),
    "wkv_ln_w": (DEPTH, 1536), "wkv_ln_b": (DEPTH, 1536), "w_wkv_out": (DEPTH, 1536, D),
    "s5_lam_re": (DEPTH, 2, 64, 64), "s5_lam_im": (DEPTH, 2, 64, 64), "s5_log_dt": (DEPTH, 2, 64),
    "s5_b_re": (DEPTH, 2, 64, 64, 16), "s5_b_im": (DEPTH, 2, 64, 64, 16),
    "s5_c_re": (DEPTH, 2, 64, 16, 64), "s5_c_im": (DEPTH, 2, 64, 16, 64), "s5_d": (DEPTH, 1024),
    "w_s5_glu": (DEPTH, 1024, 2 * D), "w_out": (DEPTH, D, D), "w_ffn_in": (DEPTH, D, 2 * D_FF),
    "ffn_conv_w": (DEPTH, 3, D_FF), "ffn_conv_b": (DEPTH, D_FF), "w_ffn_out": (DEPTH, D_FF, D),
}


RELAID = {"w_mod": (D, 6 * D, 512), "w_in": (D, N_IN, 512), "w_ssm_out": (1536, D, 512), "w_wkv_out": (1536, D, 512),
          "w_out": (D, D, 512), "w_ffn_in": (D, 2 * D_FF, 512), "w_ffn_out": (D_FF, D, 128)}


def _relayout(W, cb):
    K_, N_ = W.shape
    kch = K_ // 128
    out = np.empty(K_ * N_, np.float32)
    off = 0
    for c0 in range(0, N_, cb):
        cw = min(cb, N_ - c0)
        t = W[:, c0:c0 + cw].reshape(kch, 128, cw).transpose(1, 0, 2)
        out[off:off + K_ * cw] = t.reshape(-1)
        off += K_ * cw
    return out


def _relayout_glu(W):
    K_ = W.shape[0]
    out = np.empty(W.size, np.float32)
    off = 0
    for c0 in range(0, D, 256):
        t = np.concatenate([W[:, c0:c0 + 256], W[:, D + c0:D + c0 + 256]], axis=1)
        t = t.reshape(K_ // 128, 128, 512).transpose(1, 0, 2)
        out[off:off + K_ * 512] = t.reshape(-1)
        off += K_ * 512
    return out


def build(debug=None):
    debug = debug or {}
    kb = KB(debug)
    nc = kb.nc
    I = kb.I
    O = kb.O
    es = contextlib.ExitStack()

    kb.din("xs", [TS, D])
    kb.din("xp", [TP, D])
    kb.din("cond2", [2, D])
    kb.din("st_ssm", [DEPTH, 2, 24, 64, 128])
    kb.din("st_wkv", [DEPTH, 2, 24, 64, 64])
    kb.din("st_s5re", [DEPTH, 2, 64, 64])
    kb.din("st_s5im", [DEPTH, 2, 64, 64])
    for nm, shp in PARAM_SHAPES.items():
        if nm in RELAID or nm == "w_s5_glu":
            kb.din(nm, [DEPTH, int(np.prod(shp[1:]))])
        else:
            kb.din(nm, shp)
    kb.din("ident", [128, 128])
    kb.din("tri", [128, 128])
    kb.din("trit", [128, 128])
    kb.din("jrev", [128, 128])
    kb.din("mask48", [48, 768])
    kb.din("jj", [128, 512])
    kb.din("zrow", [1, CV])

    kb.dout("ys", [TS, D])
    kb.dout("yp", [TP, D])
    kb.dout("ns_ssm", [NSP, DEPTH, 2, 24, 64, 128])
    kb.dout("ns_wkv", [NSP, DEPTH, 2, 24, 64, 64])
    kb.dout("ns_s5re", [NSP, DEPTH, 2, 64, 64])
    kb.dout("ns_s5im", [NSP, DEPTH, 2, 64, 64])

    MOD = kb.dscr("MOD", [2, 6 * D])
    PA = kb.dscr("PA", [TS, C_U])
    PB = kb.dscr("PB", [TS, N_IN - C_U])

    class _P:
        def __getitem__(self, key):
            rs, cs = key
            c0, c1 = cs.start, cs.stop
            if c1 <= C_U:
                return PA[rs, c0:c1]
            assert c0 >= C_U, (c0, c1)
            return PB[rs, c0 - C_U:c1 - C_U]
    P = _P()
    GU = kb.dscr("GU", [TS, 2 * D_FF])
    XA = kb.dscr("XA", [TS, D])
    XB = kb.dscr("XB", [TS, D])
    XPA = kb.dscr("XPA", [TP, D])
    XPB = kb.dscr("XPB", [TP, D])
    MG = kb.dscr("MG", [TS, D])
    XC = kb.dscr("XC", [TS, 2560])
    SH = kb.dscr("SH", [TS, NWKV])
    BCT = kb.dscr("BCT", [TS // 128, 128, 8, 128], BF16)
    DT = kb.dscr("DT", [TS, 48])
    YS = kb.dscr("YS", [TS, 1536])
    ZW = kb.dscr("ZW", [TS, 1536])
    BD = [kb.dscr("BD%d" % d, [TS, 1536], BF16) for d in range(2)]
    KD = [kb.dscr("KD%d" % d, [TS, 1536], BF16) for d in range(2)]
    BRKR = [kb.dscr("BRKR%d" % d, [TS, 48]) for d in range(2)]
    WT = [kb.dscr("WT%d" % d, [128, 12, TS]) for d in range(2)]
    WRT = [kb.dscr("WRT%d" % d, [128, 12, TS]) for d in range(2)]
    NKKT = kb.dscr("NKKT", [128, 12, TS])
    GG = kb.dscr("GG", [TS, 1536])
    VBD = [kb.dscr("VBD%d" % d, [TS, 12, 768], BF16) for d in range(2)]
    RK = kb.dscr("RK", [TS, 24])
    SAY = [kb.dscr("SAY%d" % d, [48, TS, 64]) for d in range(2)]
    UT2 = [kb.dscr("UT2_%d" % d, [32, 32, TS]) for d in range(2)]
    YS5 = [kb.dscr("YS5_%d" % d, [TS, 1024]) for d in range(2)]
    ACTS = kb.dscr("ACTS", [TS, D_FF])

    with es:
        S = Sched(nc, es)
        kb.S = S
        gsb = lambda name, shape, dt=F32: es.enter_context(nc.sbuf_tensor(name, list(shape), dt))
        psb = [es.enter_context(nc.psum_tensor("psb%d" % i, [128, 512], F32)) for i in range(8)]
        PK = ["psb%d" % i for i in range(8)]
        ident = gsb("ident_sb", [128, 128])
        tri = gsb("tri_sb", [128, 128])
        trit = gsb("trit_sb", [128, 128])
        jrev = gsb("jrev_sb", [128, 128])
        ones = gsb("ones_sb", [128, 128])
        S.dma("sp", ident[:], I["ident"][:, :], w=["ident"])
        S.dma("sp", tri[:], I["tri"][:, :], w=["tri"])
        S.dma("sp", trit[:], I["trit"][:, :], w=["trit"])
        S.dma("sp", jrev[:], I["jrev"][:, :], w=["jrev"])
        S.op("dve", lambda e: e.memset(ones[:], 1.0), w=["ones"])

        _uid = [0]

        def uname(n):
            _uid[0] += 1
            return "%s_%d" % (n, _uid[0])

        def bcload(dst, key, src1d, q="sp"):
            S.dma(q, dst, src1d.partition_broadcast(dst.shape[0]), r=["MOD"], w=[key])

        def transposes_to(src, skey, nblk, dst_fn, dkey, bw=128, pbanks=(6, 7), eng="act", scale=None, rhs=None,
                          rkey=None, inw=128):
            per = 512 // inw
            for q in range(0, nblk, per):
                nj = min(per, nblk - q)
                bi = pbanks[(q // per) % len(pbanks)]
                pt = psb[bi]
                for j in range(nj):
                    blk = src[:, (q + j) * bw:(q + j + 1) * bw]
                    if rhs is None:
                        S.op("pe", lambda e, j=j, blk=blk: e.transpose(pt[0:bw, j * inw:(j + 1) * inw], blk,
                                                                        ident[0:inw, 0:inw]),
                             r=[skey, "ident"], w=[PK[bi]])
                    else:
                        S.op("pe", lambda e, j=j, blk=blk: e.matmul(pt[0:bw, j * inw:(j + 1) * inw], lhsT=blk,
                                                                     rhs=rhs, start=True, stop=True),
                             r=[skey, rkey], w=[PK[bi]])
                src_ps = pt[0:bw, 0:nj * inw].rearrange("p (j t) -> p j t", j=nj)
                dst = dst_fn(q, nj)
                if eng == "act":
                    if scale is None:
                        S.op("act", lambda e: e.activation(out=dst, in_=src_ps, func=AF.Copy), r=[PK[bi]], w=[dkey])
                    else:
                        S.op("act", lambda e: e.activation(out=dst, in_=src_ps, func=AF.Copy, scale=scale),
                             r=[PK[bi]], w=[dkey])
                else:
                    S.op("dve", lambda e: e.tensor_copy(out=dst, in_=src_ps), r=[PK[bi]], w=[dkey])

        def phase_mod(l):
            with contextlib.ExitStack() as ps:
                lsb = lambda name, shape, dt=F32: ps.enter_context(nc.sbuf_tensor(uname(name), list(shape), dt))
                cT = lsb("cT", [128, 2, KC])
                sg = lsb("sg", [128, 2, KC])
                wm = [lsb("wm%d" % i, [128, KC, 512]) for i in range(2)]
                bm = lsb("bm", [1, 6 * D])
                mo = lsb("mo", [2, 6 * D])
                with nc.allow_non_contiguous_dma(reason="tiny cond transpose"):
                    S.dma("sp", cT[:], I["cond2"].rearrange("j (k p) -> p j k", p=128), w=["cT"])
                S.dma("sp", bm[:], I["b_mod"][l:l + 1, :], w=["bm"])
                S.op("act", lambda e: e.activation(out=sg[:], in_=cT[:], func=AF.Sigmoid), r=["cT"], w=["sg"])
                S.op("dve", lambda e: e.tensor_mul(out=sg[:], in0=sg[:], in1=cT[:]), r=["cT", "sg"], w=["sg"])
                for nb in range(24):
                    wb = wm[nb % 2]
                    wk = "wm%d" % (nb % 2)
                    S.dma("sp", wb[:], I["w_mod"][l, nb * 128 * KC * 512:(nb + 1) * 128 * KC * 512].rearrange(
                        "(p k n) -> p k n", p=128, k=KC),
                          w=[wk])
                    pt = psb[nb % 2]
                    pk = PK[nb % 2]
                    for k in range(KC):
                        S.op("pe", lambda e, k=k: e.matmul(pt[0:2, :], lhsT=sg[:, :, k], rhs=wb[:, k, :],
                                                           start=(k == 0), stop=False), r=["sg", wk], w=[pk])
                    S.op("pe", lambda e: e.matmul(pt[0:2, :], lhsT=ones[0:1, 0:2], rhs=bm[:, nb * 512:(nb + 1) * 512],
                                                  start=False, stop=True), r=["ones", "bm"], w=[pk])
                    S.op("act", lambda e: e.activation(out=mo[:, nb * 512:(nb + 1) * 512], in_=pt[0:2, :],
                                                       func=AF.Copy), r=[pk], w=["mo"])
                S.dma("sp", MOD[:, :], mo[:], r=["mo"], w=["MOD"])
                S.barrier()

        def rms_rstd(ss, key, n, eps):
            S.op("dve", lambda e: e.tensor_scalar(out=ss, in0=ss, scalar1=1.0 / n, scalar2=eps, op0=ALU.mult,
                                                  op1=ALU.add), r=[key], w=[key])
            S.op("act", lambda e: e.activation(out=ss, in_=ss, func=AF.Sqrt), r=[key], w=[key])
            S.op("dve", lambda e: e.reciprocal(out=ss, in_=ss), r=[key], w=[key])

        def norm_to_fm(ls, xsrc, t0, ntile, hT, Gt, SHt):
            for i in range(ntile):
                xt = ls["xt"][i % 2]
                xk = "xt%d" % (i % 2)
                S.dma("sp", xt[:], xsrc[t0 + i * 128:t0 + (i + 1) * 128, :], r=["XSRC"], w=[xk])
                S.op("act", lambda e: e.activation(out=ls["junk"][:], in_=xt[:], func=AF.Square,
                                                   accum_out=ls["ss"][:]), r=[xk], w=["junk", "ss"])
                rms_rstd(ls["ss"][:], "ss", D, EPS)
                S.op("dve", lambda e: e.scalar_tensor_tensor(out=ls["junk"][:], in0=xt[:], scalar=ls["ss"][:, 0:1],
                                                             in1=Gt[:], op0=ALU.mult, op1=ALU.mult),
                     r=[xk, "ss", "Gt"], w=["junk"])
                S.op("dve", lambda e: e.tensor_add(out=ls["junk"][:], in0=ls["junk"][:], in1=SHt[:]),
                     r=["junk", "SHt"], w=["junk"])
                transposes_to(ls["junk"][:], "junk", KC,
                              lambda q, nj, i=i: hT[:, q:q + nj, i * 128:(i + 1) * 128], "hT")

        def proj(wbs, hT, hkey, kchunks, ncols, ntile, epilogue, wload, cwmax=512, wmul=1):
            nb = (ncols + cwmax - 1) // cwmax

            def blk(b):
                c0 = b * cwmax
                cw = min(cwmax, ncols - c0)
                nw = cw * wmul
                wb = wbs[b % 2][:, 0:kchunks * nw].rearrange("p (k n) -> p k n", k=kchunks)
                return c0, cw, nw, wb, "wb%d" % (b % 2)

            c0, cw, nw, wb, wk = blk(0)
            wload(wb, wk, c0, cw)
            for b in range(nb):
                c0, cw, nw, wb, wk = blk(b)
                if b + 1 < nb:
                    c0n, cwn, nwn, wbn, wkn = blk(b + 1)
                    wload(wbn, wkn, c0n, cwn)
                for i in range(ntile):
                    pt = psb[i % 4]
                    pk = PK[i % 4]
                    for k in range(kchunks):
                        S.op("pe", lambda e, k=k: e.matmul(pt[:, 0:nw], lhsT=hT[:, k, i * 128:(i + 1) * 128],
                                                           rhs=wb[:, k, :], start=(k == 0),
                                                           stop=(k == kchunks - 1)), r=[hkey, wk], w=[pk])
                    epilogue(i, c0, cw, pt, pk)

        WSTG = {}

        def wload_plain(Wflat, cb=512):
            def f(wb, wk, c0, cw):
                wst = WSTG["t"]
                kch = wb.shape[1]
                n = 128 * kch * cw
                off = 128 * kch * c0
                stv = wst[:, 0:kch * cw].rearrange("p (k n) -> p k n", k=kch)
                S.dma("sp", stv, Wflat[off:off + n].rearrange("(p k n) -> p k n", p=128, k=kch), w=["wst"])
                S.op("dve", lambda e: e.tensor_copy(out=wb, in_=stv), r=["wst"], w=[wk])
            return f

        def load3(src, sc, cw, t0, GW, bufs, keys):
            prv, cur, nxt = bufs
            kp, kc_, kn = keys
            S.dma("sp", cur[:, 0:cw], src[t0:t0 + 128, sc:sc + cw], r=["CSRC"], w=[kc_])
            a = 0
            while a < 128:
                g0 = t0 + a
                b = min(128, a + GW - (g0 % GW))
                sb_ = (g0 % GW) == 0
                eb_ = ((t0 + b) % GW) == 0
                if sb_:
                    S.dma("sp", prv[a:a + 1, 0:cw], I["zrow"][0:1, 0:cw], w=[kp])
                    if b - a > 1:
                        S.dma("sp", prv[a + 1:b, 0:cw], src[t0 + a:t0 + b - 1, sc:sc + cw], r=["CSRC"], w=[kp])
                else:
                    S.dma("sp", prv[a:b, 0:cw], src[t0 + a - 1:t0 + b - 1, sc:sc + cw], r=["CSRC"], w=[kp])
                if eb_:
                    S.dma("sp", nxt[b - 1:b, 0:cw], I["zrow"][0:1, 0:cw], w=[kn])
                    if b - a > 1:
                        S.dma("sp", nxt[a:b - 1, 0:cw], src[t0 + a + 1:t0 + b, sc:sc + cw], r=["CSRC"], w=[kn])
                else:
                    S.dma("sp", nxt[a:b, 0:cw], src[t0 + a + 1:t0 + b + 1, sc:sc + cw], r=["CSRC"], w=[kn])
                a = b

        def conv_pass(ps, T, GW, src, sc0, ncols, wrows, post, prep=None):
            lsb = lambda name, shape, dt=F32: ps.enter_context(nc.sbuf_tensor(uname(name), list(shape), dt))
            wt = [lsb("cvw%d" % j, [128, CV]) for j in range(4)]
            bufs = [[lsb("cv%s%d" % (n, j), [128, CV]) for n in ("p", "c", "n")] for j in range(2)]
            yb = [lsb("cvy%d" % j, [128, CV]) for j in range(2)]
            tb = lsb("cvt", [128, CV])
            for c0 in range(0, ncols, CV):
                cw = min(CV, ncols - c0)
                rows = wrows(c0, cw)
                for j in range(4):
                    if rows[j] is not None:
                        bcload(wt[j][:, 0:cw], "cvw%d" % j, rows[j])
                if prep is not None:
                    prep(wt, cw)
                for i in range(T // 128):
                    bb = bufs[i % 2]
                    keys = ["cv%s%d" % (n, i % 2) for n in ("p", "c", "n")]
                    load3(src, sc0 + c0, cw, i * 128, GW, bb, keys)
                    y = yb[i % 2]
                    yk = "cvy%d" % (i % 2)
                    S.op("dve", lambda e: e.tensor_mul(out=y[:, 0:cw], in0=bb[1][:, 0:cw], in1=wt[1][:, 0:cw]),
                         r=[keys[1], "cvw1"], w=[yk])
                    S.op("dve", lambda e: e.tensor_mul(out=tb[:, 0:cw], in0=bb[0][:, 0:cw], in1=wt[0][:, 0:cw]),
                         r=[keys[0], "cvw0"], w=["cvt"])
                    S.op("dve", lambda e: e.tensor_add(out=y[:, 0:cw], in0=y[:, 0:cw], in1=tb[:, 0:cw]),
                         r=[yk, "cvt"], w=[yk])
                    S.op("dve", lambda e: e.tensor_mul(out=tb[:, 0:cw], in0=bb[2][:, 0:cw], in1=wt[2][:, 0:cw]),
                         r=[keys[2], "cvw2"], w=["cvt"])
                    S.op("dve", lambda e: e.tensor_add(out=y[:, 0:cw], in0=y[:, 0:cw], in1=tb[:, 0:cw]),
                         r=[yk, "cvt"], w=[yk])
                    if rows[3] is not None:
                        S.op("dve", lambda e: e.tensor_add(out=y[:, 0:cw], in0=y[:, 0:cw], in1=wt[3][:, 0:cw]),
                             r=[yk, "cvw3"], w=[yk])
                    post(i, c0, cw, y, yk)

        def postnorm_residual(ls, Ybuf, ntile, t0, xsrc, xdst, GPt):
            for i in range(ntile):
                y = Ybuf[:, i, :]
                S.op("act", lambda e: e.activation(out=ls["junk"][:], in_=y, func=AF.Square, accum_out=ls["ss"][:]),
                     r=["Ybuf"], w=["junk", "ss"])
                rms_rstd(ls["ss"][:], "ss", D, EPS)
                xt = ls["xt"][i % 2]
                xk = "xt%d" % (i % 2)
                S.dma("sp", xt[:], xsrc[t0 + i * 128:t0 + (i + 1) * 128, :], r=["XSRC"], w=[xk])
                S.op("dve", lambda e: e.scalar_tensor_tensor(out=ls["junk"][:], in0=y, scalar=ls["ss"][:, 0:1],
                                                             in1=GPt[:], op0=ALU.mult, op1=ALU.mult),
                     r=["Ybuf", "ss", "GPt"], w=["junk"])
                S.op("dve", lambda e: e.tensor_add(out=xt[:], in0=xt[:], in1=ls["junk"][:]), r=["junk", xk], w=[xk])
                S.dma("sp", xdst[t0 + i * 128:t0 + (i + 1) * 128, :], xt[:], r=[xk], w=["XDST"])

        def mod_tiles(lsb, ls, l, jrow, i_sc, i_sh, i_g, pre, post):
            Gt = lsb("Gt", [128, D])
            SHt = lsb("SHt", [128, D])
            bcload(Gt[:], "Gt", MOD[jrow, i_sc * D:(i_sc + 1) * D])
            bcload(ls["junk"][:], "junk", I[pre][l, :])
            S.op("dve", lambda e: e.scalar_tensor_tensor(out=Gt[:], in0=Gt[:], scalar=1.0, in1=ls["junk"][:],
                                                         op0=ALU.add, op1=ALU.mult), r=["Gt", "junk"], w=["Gt"])
            bcload(SHt[:], "SHt", MOD[jrow, i_sh * D:(i_sh + 1) * D])
            return Gt, SHt, None

        def phase_inproj(l, xsrc, T, jrow):
            with contextlib.ExitStack() as ps:
                lsb = lambda name, shape, dt=F32: ps.enter_context(nc.sbuf_tensor(uname(name), list(shape), dt))
                ls = {"xt": [lsb("xt0", [128, D]), lsb("xt1", [128, D])], "junk": lsb("junk", [128, D]),
                      "ss": lsb("ss", [128, 1])}
                wbs = [lsb("wb0", [128, 8192], BF16), lsb("wb1", [128, 8192], BF16)]
                WSTG["t"] = lsb("wst", [128, 8192])
                BI = min(BLK_IN, T)
                st = [lsb("st0", [128, 512]), lsb("st1", [128, 512])]
                hT = lsb("hT", [128, KC, BI], BF16)
                Gt, SHt, GPt = mod_tiles(lsb, ls, l, jrow, 1, 0, 2, "norm_mix_pre", "norm_mix_post")
                cnt = [0]
                for bi in range(T // BI):
                    t0 = bi * BI
                    norm_to_fm(ls, xsrc, t0, BI // 128, hT, Gt, SHt)

                    def epi(i, c0, cw, pt, pk, t0=t0):
                        cnt[0] += 1
                        s_ = st[cnt[0] % 2]
                        sk = "st%d" % (cnt[0] % 2)
                        S.op("act", lambda e: e.activation(out=s_[:, 0:cw], in_=pt[:, 0:cw], func=AF.Copy),
                             r=[pk], w=[sk])
                        rs_ = slice(t0 + i * 128, t0 + (i + 1) * 128)
                        if c0 < C_U < c0 + cw:
                            m_ = C_U - c0
                            S.dma("sp", P[rs_, c0:C_U], s_[:, 0:m_], r=[sk], w=["P"])
                            S.dma("sp", P[rs_, C_U:c0 + cw], s_[:, m_:cw], r=[sk], w=["P"])
                        else:
                            S.dma("sp", P[rs_, c0:c0 + cw], s_[:, 0:cw], r=[sk], w=["P"])

                    proj(wbs, hT, "hT", KC, debug.get("ncols", N_IN), BI // 128, epi, wload_plain(I["w_in"][l]))
                S.barrier()

        def phase_convs(l, T, GW):
            with contextlib.ExitStack() as ps:
                def post_ssm(i, c0, cw, y, yk):
                    lsg = post_ssm.sg
                    S.op("act", lambda e: e.activation(out=lsg[:, 0:cw], in_=y[:, 0:cw], func=AF.Sigmoid),
                         r=[yk], w=["cvsg"])
                    S.op("dve", lambda e: e.tensor_mul(out=y[:, 0:cw], in0=y[:, 0:cw], in1=lsg[:, 0:cw]),
                         r=[yk, "cvsg"], w=[yk])
                    S.dma("sp", XC[i * 128:(i + 1) * 128, c0:c0 + cw], y[:, 0:cw], r=[yk], w=["XC"])
                post_ssm.sg = ps.enter_context(nc.sbuf_tensor(uname("cvsg"), [128, CV], F32))
                conv_pass(ps, T, GW, P, C_XBC, 2560,
                          lambda c0, cw: (I["ssm_conv_w"][l, 0, c0:c0 + cw], I["ssm_conv_w"][l, 1, c0:c0 + cw],
                                          I["ssm_conv_w"][l, 2, c0:c0 + cw], I["ssm_conv_b"][l, c0:c0 + cw]),
                          post_ssm)
                S.barrier()
            with contextlib.ExitStack() as ps:
                def post_wkv(i, c0, cw, y, yk):
                    S.dma("sp", SH[i * 128:(i + 1) * 128, c0:c0 + cw], y[:, 0:cw], r=[yk], w=["SH"])

                def prep(wt, cw):
                    S.op("dve", lambda e: e.tensor_add(out=wt[1][:, 0:cw], in0=wt[0][:, 0:cw], in1=wt[2][:, 0:cw]),
                         r=["cvw0", "cvw2"], w=["cvw1"])
                    S.op("dve", lambda e: e.tensor_scalar(out=wt[1][:, 0:cw], in0=wt[1][:, 0:cw], scalar1=-1.0,
                                                          scalar2=1.0, op0=ALU.mult, op1=ALU.add),
                         r=["cvw1"], w=["cvw1"])
                conv_pass(ps, T, GW, P, C_WKV, NWKV,
                          lambda c0, cw: (I["wkv_mu_prev"][l, c0:c0 + cw], None, I["wkv_mu_next"][l, c0:c0 + cw],
                                          None), post_wkv, prep=prep)
                S.barrier()

        def mm_cols(ps3, c0, c1, lhsT, rhs_fn, rkeys):
            c = c0
            while c < c1:
                bnk = c // 512
                ce = min(c1, (bnk + 1) * 512)
                S.op("pe", lambda e, c=c, ce=ce, bnk=bnk: e.matmul(psb[ps3[bnk]][:, c - bnk * 512:ce - bnk * 512],
                                                                   lhsT=lhsT, rhs=rhs_fn(c, ce), start=True,
                                                                   stop=True), r=rkeys, w=[PK[ps3[bnk]]])
                c = ce

        def phase_ssd(l, T, L, sample):
            NT = T // 128
            with contextlib.ExitStack() as ps:
                lsb = lambda name, shape, dt=F32: ps.enter_context(nc.sbuf_tensor(uname(name), list(shape), dt))
                bcv = [lsb("sbc%d" % j, [128, 1024]) for j in range(2)]
                bct = [lsb("sbct%d" % j, [128, 8, 128], BF16) for j in range(2)]
                dtt = [lsb("sdt%d" % j, [128, 48]) for j in range(2)]
                dbias = lsb("dbias", [128, 48])
                bcload(dbias[:], "dbias", I["ssm_dt_bias"][l, :])
                for i in range(NT):
                    b_ = bcv[i % 2]
                    bk = "sbc%d" % (i % 2)
                    S.dma("sp", b_[:], XC[i * 128:(i + 1) * 128, 1536:2560], r=["XC"], w=[bk])
                    o_ = bct[i % 2]
                    ok = "sbct%d" % (i % 2)
                    transposes_to(b_[:], bk, 8, lambda q, nj: o_[:, q:q + nj, :], ok)
                    S.dma("sp", BCT[i], o_[:], r=[ok], w=["BCT"])
                    d_ = dtt[i % 2]
                    dk = "sdt%d" % (i % 2)
                    S.dma("sp", d_[:], P[i * 128:(i + 1) * 128, C_DTF:C_DTF + 48], r=["P"], w=[dk])
                    S.op("dve", lambda e: e.tensor_add(out=d_[:], in0=d_[:], in1=dbias[:]), r=[dk, "dbias"], w=[dk])
                    S.op("act", lambda e: e.activation(out=d_[:], in_=d_[:], func=AF.Exp), r=[dk], w=[dk])
                    S.op("act", lambda e: e.activation(out=d_[:], in_=d_[:], func=AF.Ln, bias=1.0), r=[dk], w=[dk])
                    S.dma("sp", DT[i * 128:(i + 1) * 128, :], d_[:], r=[dk], w=["DT"])
                S.barrier()
            with contextlib.ExitStack() as ps:
                lsb = lambda name, shape, dt=F32: ps.enter_context(nc.sbuf_tensor(uname(name), list(shape), dt))
                xc = [lsb("xc%d" % j, [128, 2560]) for j in range(2)]
                bct = [lsb("bct%d" % j, [128, 8, 128], BF16) for j in range(2)]
                dtt = [lsb("dt%d" % j, [128, 24]) for j in range(2)]
                Abc = lsb("Abc", [128, 24])
                Dbc = lsb("Dbc", [128, 24])
                nrm = lsb("nrm", [128, 1536])
                dtA = lsb("dtA", [128, 24])
                acol = lsb("acol", [128, 24])
                tot = lsb("tot", [128, 24])
                cd = lsb("cd", [128, 24])
                ea = lsb("ea", [128, 24])
                dte = lsb("dte", [128, 24])
                xdt = lsb("xdt", [128, 1536], BF16)
                xw = lsb("xw", [128, 1536], BF16)
                bB = lsb("bB", [128, 512], BF16)
                Gm = lsb("Gm", [128, 512])
                dd = [lsb("dd%d" % j, [128, 512]) for j in range(2)]
                wt = [lsb("wt%d" % j, [128, 4, 128], BF16) for j in range(2)]
                HT = lsb("HT", [128, 1536])
                HTb = lsb("HTb", [128, 1536], BF16)
                yo = lsb("yo", [128, 1536])
                yy = lsb("yy", [128, 1536])
                zz = lsb("zz", [128, 1536])
                zs = lsb("zs", [128, 1536])
                ss4 = lsb("ss4", [128, 4])
                hio = lsb("hio", [128, 12, 128])
                bcload(Dbc[:], "Dbc", I["ssm_d"][l, :])
                bcload(nrm[:], "nrm", I["ssm_norm"][l, :])
                PY = (4, 5, 6)
                v3 = lambda t: t[:].rearrange("p (h q) -> p h q", h=24)
                for d in range(2):
                    bcload(Abc[:], "Abc", I["ssm_a_log"][l, d * 24:(d + 1) * 24])
                    S.op("act", lambda e: e.activation(out=Abc[:], in_=Abc[:], func=AF.Exp), r=["Abc"], w=["Abc"])
                    S.op("dve", lambda e: e.tensor_scalar(out=Abc[:], in0=Abc[:], scalar1=-1.0, scalar2=None,
                                                          op0=ALU.mult), r=["Abc"], w=["Abc"])
                    TR = tri if d == 0 else trit
                    TRk = "tri" if d == 0 else "trit"
                    for s_ in range(T // L):
                        if sample:
                            S.dma("sp", hio[:], I["st_ssm"][l, d].rearrange("(g h2) p n -> (h2 p) g n", h2=2),
                                  w=["hio"])
                            for g in range(12):
                                bnk = PY[(g * 128) // 512]
                                S.op("pe", lambda e, g=g, bnk=bnk: e.transpose(
                                    psb[bnk][:, (g * 128) % 512:(g * 128) % 512 + 128], hio[:, g, :], ident[:]),
                                    r=["hio", "ident"], w=[PK[bnk]])
                            for j in range(3):
                                S.op("act", lambda e, j=j: e.activation(out=HT[:, j * 512:(j + 1) * 512],
                                                                        in_=psb[PY[j]][:, :], func=AF.Copy),
                                     r=[PK[PY[j]]], w=["HT"])
                        else:
                            S.op("dve", lambda e: e.memset(HT[:], 0.0), w=["HT"])
                        S.op("act", lambda e: e.activation(out=HTb[:], in_=HT[:], func=AF.Copy), r=["HT"], w=["HTb"])
                        tiles = list(range(L // 128))
                        if d == 1:
                            tiles = tiles[::-1]
                        for ti in tiles:
                            i = s_ * (L // 128) + ti
                            t0 = i * 128
                            x_ = xc[i % 2]
                            xk = "xc%d" % (i % 2)
                            b_ = bct[i % 2]
                            bk = "bct%d" % (i % 2)
                            d_ = dtt[i % 2]
                            dk = "dt%d" % (i % 2)
                            S.dma("sp", x_[:], XC[t0:t0 + 128, :], r=["XC"], w=[xk])
                            S.dma("sp", b_[:], BCT[i], r=["BCT"], w=[bk])
                            S.dma("sp", d_[:], DT[t0:t0 + 128, d * 24:(d + 1) * 24], r=["DT"], w=[dk])
                            S.op("dve", lambda e: e.tensor_mul(out=dtA[:], in0=d_[:], in1=Abc[:]),
                                 r=[dk, "Abc"], w=["dtA"])
                            S.op("pe", lambda e: e.matmul(psb[0][:, 0:24], lhsT=TR[:], rhs=dtA[:], start=True,
                                                          stop=True), r=[TRk, "dtA"], w=[PK[0]])
                            S.op("pe", lambda e: e.matmul(psb[0][:, 32:56], lhsT=ones[:], rhs=dtA[:], start=True,
                                                          stop=True), r=["ones", "dtA"], w=[PK[0]])
                            S.op("act", lambda e: e.activation(out=acol[:], in_=psb[0][:, 0:24], func=AF.Copy),
                                 r=[PK[0]], w=["acol"])
                            S.op("act", lambda e: e.activation(out=ea[:], in_=psb[0][:, 0:24], func=AF.Exp),
                                 r=[PK[0]], w=["ea"])
                            S.op("act", lambda e: e.activation(out=cd[:], in_=psb[0][:, 32:56], func=AF.Exp),
                                 r=[PK[0]], w=["cd"])
                            S.op("dve", lambda e: e.tensor_sub(out=dte[:], in0=psb[0][:, 32:56], in1=acol[:]),
                                 r=[PK[0], "acol"], w=["dte"])
                            S.op("act", lambda e: e.activation(out=dte[:], in_=dte[:], func=AF.Exp), r=["dte"],
                                 w=["dte"])
                            S.op("dve", lambda e: e.tensor_mul(out=dte[:], in0=dte[:], in1=d_[:]), r=["dte", dk],
                                 w=["dte"])
                            S.op("dve", lambda e: e.tensor_tensor(
                                out=v3(xdt), in0=x_[:, 0:1536].rearrange("p (h q) -> p h q", h=24),
                                in1=d_[:, :].unsqueeze(2).to_broadcast([128, 24, 64]), op=ALU.mult),
                                r=[xk, dk], w=["xdt"])
                            S.op("dve", lambda e: e.tensor_tensor(
                                out=v3(xw), in0=x_[:, 0:1536].rearrange("p (h q) -> p h q", h=24),
                                in1=dte[:, :].unsqueeze(2).to_broadcast([128, 24, 64]), op=ALU.mult),
                                r=[xk, "dte"], w=["xw"])
                            S.op("act", lambda e: e.activation(out=bB[:], in_=x_[:, 1536:2048], func=AF.Copy),
                                 r=[xk], w=["bB"])
                            for g in range(4):
                                S.op("pe", lambda e, g=g: e.matmul(psb[1][:, g * 128:(g + 1) * 128], lhsT=b_[:, g, :],
                                                                   rhs=b_[:, 4 + g, :], start=True, stop=True),
                                     r=[bk], w=[PK[1]])
                            S.op("dve", lambda e: e.tensor_tensor(
                                out=Gm[:].rearrange("p (g t) -> p g t", g=4),
                                in0=psb[1][:, :].rearrange("p (g t) -> p g t", g=4),
                                in1=TR[:, :].unsqueeze(1).to_broadcast([128, 4, 128]), op=ALU.mult),
                                r=[PK[1], TRk], w=["Gm"])
                            for g in range(4):
                                mm_cols(PY, g * 384, (g + 1) * 384, b_[:, 4 + g, :], lambda c, ce: HTb[:, c:ce],
                                        [bk, "HTb"])
                            for j in range(3):
                                S.op("dve", lambda e, j=j: e.tensor_tensor(
                                    out=yo[:, j * 512:(j + 1) * 512].rearrange("p (h q) -> p h q", h=8),
                                    in0=psb[PY[j]][:, :].rearrange("p (h q) -> p h q", h=8),
                                    in1=ea[:, j * 8:(j + 1) * 8].unsqueeze(2).to_broadcast([128, 8, 64]),
                                    op=ALU.mult), r=[PK[PY[j]], "ea"], w=["yo"])
                            for hq in range(6):
                                pb = 2 + hq % 2
                                ddq = dd[hq % 2]
                                dkq = "dd%d" % (hq % 2)
                                wq = wt[hq % 2]
                                wkq = "wt%d" % (hq % 2)
                                for j in range(4):
                                    h = hq * 4 + j
                                    S.op("pe", lambda e, j=j, h=h: e.matmul(
                                        psb[pb][:, j * 128:(j + 1) * 128],
                                        lhsT=dtA[:, h:h + 1].to_broadcast([128, 128]), rhs=TR[:], start=True,
                                        stop=True), r=["dtA", TRk], w=[PK[pb]])
                                for j in range(4):
                                    h = hq * 4 + j
                                    S.op("dve", lambda e, j=j, h=h: e.tensor_scalar(
                                        out=ddq[:, j * 128:(j + 1) * 128], in0=psb[pb][:, j * 128:(j + 1) * 128],
                                        scalar1=acol[:, h:h + 1], scalar2=0.0, op0=ALU.subtract, op1=ALU.min),
                                        r=[PK[pb], "acol"], w=[dkq])
                                S.op("act", lambda e: e.activation(out=ddq[:], in_=ddq[:], func=AF.Exp), r=[dkq],
                                     w=[dkq])
                                for j in range(4):
                                    h = hq * 4 + j
                                    g = h // 6
                                    S.op("dve", lambda e, j=j, g=g: e.tensor_mul(
                                        out=wq[:, j, :], in0=Gm[:, g * 128:(g + 1) * 128],
                                        in1=ddq[:, j * 128:(j + 1) * 128]), r=["Gm", dkq], w=[wkq])
                                for j in range(4):
                                    h = hq * 4 + j
                                    bnk = PY[(h * 64) // 512]
                                    S.op("pe", lambda e, j=j, h=h, bnk=bnk: e.matmul(
                                        psb[bnk][:, (h * 64) % 512:(h * 64) % 512 + 64], lhsT=wq[:, j, :],
                                        rhs=xdt[:, h * 64:(h + 1) * 64], start=True, stop=True),
                                        r=[wkq, "xdt"], w=[PK[bnk]])
                            for j in range(3):
                                S.op("dve", lambda e, j=j: e.tensor_add(out=yy[:, j * 512:(j + 1) * 512],
                                                                        in0=yo[:, j * 512:(j + 1) * 512],
                                                                        in1=psb[PY[j]][:, :]),
                                     r=["yo", PK[PY[j]]], w=["yy"])
                            for g in range(4):
                                mm_cols(PY, g * 384, (g + 1) * 384, bB[:, g * 128:(g + 1) * 128],
                                        lambda c, ce: xw[:, c:ce], ["bB", "xw"])
                            S.op("dve", lambda e: e.tensor_tensor(out=v3(HT), in0=v3(HT),
                                                                  in1=cd[:, :].unsqueeze(2).to_broadcast([128, 24, 64]),
                                                                  op=ALU.mult), r=["HT", "cd"], w=["HT"])
                            for j in range(3):
                                S.op("dve", lambda e, j=j: e.tensor_add(out=HT[:, j * 512:(j + 1) * 512],
                                                                        in0=HT[:, j * 512:(j + 1) * 512],
                                                                        in1=psb[PY[j]][:, :]),
                                     r=["HT", PK[PY[j]]], w=["HT"])
                            S.op("act", lambda e: e.activation(out=HTb[:], in_=HT[:], func=AF.Copy), r=["HT"],
                                 w=["HTb"])
                            if d == 0:
                                S.dma("sp", YS[t0:t0 + 128, :], yy[:], r=["yy"], w=["YS"])
                            else:
                                S.dma("sp", yo[:], YS[t0:t0 + 128, :], r=["YS"], w=["yo"])
                                S.op("dve", lambda e: e.tensor_add(out=yy[:], in0=yy[:], in1=yo[:]), r=["yy", "yo"],
                                     w=["yy"])
                                S.op("dve", lambda e: e.tensor_tensor(
                                    out=v3(yo), in0=x_[:, 0:1536].rearrange("p (h q) -> p h q", h=24),
                                    in1=Dbc[:, :].unsqueeze(2).to_broadcast([128, 24, 64]), op=ALU.mult),
                                    r=[xk, "Dbc"], w=["yo"])
                                S.op("dve", lambda e: e.tensor_add(out=yy[:], in0=yy[:], in1=yo[:]), r=["yy", "yo"],
                                     w=["yy"])
                                S.dma("sp", zz[:], P[t0:t0 + 128, C_Z:C_Z + 1536], r=["P"], w=["zz"])
                                S.op("act", lambda e: e.activation(out=zs[:], in_=zz[:], func=AF.Sigmoid), r=["zz"],
                                     w=["zs"])
                                S.op("dve", lambda e: e.tensor_mul(out=zs[:], in0=zs[:], in1=zz[:]), r=["zz", "zs"],
                                     w=["zs"])
                                S.op("dve", lambda e: e.tensor_mul(out=yy[:], in0=yy[:], in1=zs[:]), r=["yy", "zs"],
                                     w=["yy"])
                                S.op("act", lambda e: e.activation(out=zs[:], in_=yy[:], func=AF.Square), r=["yy"],
                                     w=["zs"])
                                S.op("dve", lambda e: e.tensor_reduce(out=ss4[:],
                                                                      in_=zs[:].rearrange("p (g q) -> p g q", g=4),
                                                                      axis=AX.X, op=ALU.add), r=["zs"], w=["ss4"])
                                rms_rstd(ss4[:], "ss4", 384, EPS)
                                S.op("dve", lambda e: e.tensor_tensor(
                                    out=yy[:].rearrange("p (g q) -> p g q", g=4),
                                    in0=yy[:].rearrange("p (g q) -> p g q", g=4),
                                    in1=ss4[:, :].unsqueeze(2).to_broadcast([128, 4, 384]), op=ALU.mult),
                                    r=["yy", "ss4"], w=["yy"])
                                S.op("dve", lambda e: e.tensor_mul(out=yy[:], in0=yy[:], in1=nrm[:]),
                                     r=["yy", "nrm"], w=["yy"])
                                S.dma("sp", YS[t0:t0 + 128, :], yy[:], r=["yy"], w=["YS"])
                        if not sample:
                            for g in range(12):
                                bnk = PY[(g * 128) // 512]
                                S.op("pe", lambda e, g=g, bnk=bnk: e.transpose(
                                    psb[bnk][:, (g * 128) % 512:(g * 128) % 512 + 128],
                                    HT[:, g * 128:(g + 1) * 128], ident[:]), r=["HT", "ident"], w=[PK[bnk]])
                            for j in range(3):
                                S.op("act", lambda e, j=j: e.activation(
                                    out=hio[:, j * 4:(j + 1) * 4, :],
                                    in_=psb[PY[j]][:, :].rearrange("p (g n) -> p g n", g=4), func=AF.Copy),
                                    r=[PK[PY[j]]], w=["hio"])
                            S.dma("sp", O["ns_ssm"][s_, l, d].rearrange("(g h2) p n -> (h2 p) g n", h2=2), hio[:],
                                  r=["hio"], w=["ns_ssm"])
                S.barrier()

        def phase_wkv_prep(l, T):
            NT = T // 128
            with contextlib.ExitStack() as ps:
                lsb = lambda name, shape, dt=F32: ps.enter_context(nc.sbuf_tensor(uname(name), list(shape), dt))
                sh = lsb("sh", [128, NWKV])
                kkbc = lsb("kkbc", [128, 1536])
                kabc = lsb("kabc", [128, 1536])
                omka = lsb("omka", [128, 1536])
                rkbc = lsb("rkbc", [128, 1536])
                w0bc = lsb("w0bc", [128, 1536])
                a0bc = lsb("a0bc", [128, 1536])
                wup = [lsb("wup%d" % d, [96, 1536]) for d in range(2)]
                aup = [lsb("aup%d" % d, [96, 1536]) for d in range(2)]
                gup = lsb("gup", [128, 2, 1536])
                A = lsb("wA", [128, 1536])
                Bt = lsb("wB", [128, 1536])
                Ct = lsb("wC", [128, 1536])
                Dt_ = lsb("wD", [128, 1536])
                E = lsb("wE", [128, 1536])
                nkk = lsb("nkk", [128, 1536])
                vb16 = lsb("vb16", [128, 1536], BF16)
                kb16 = lsb("kb16", [128, 1536], BF16)
                bb16 = lsb("bb16", [128, 1536], BF16)
                Vx = lsb("Vx", [128, 12, 768], BF16)
                S.op("dve", lambda e: e.memset(Vx[:], 0.0), w=["Vx"])
                sm = lsb("wsm", [128, 48])
                rs = lsb("wrs", [128, 24])
                twT = lsb("twT", [96, 2, 128])
                aT = lsb("aT", [96, 2, 128])
                sgT = lsb("sgT", [128, 2, 128])
                fm = [lsb("wfm%d" % j, [128, 12, 128]) for j in range(2)]
                fmc = [0]
                bcload(kkbc[:], "kkbc", I["wkv_k_k"][l, :])
                bcload(kabc[:], "kabc", I["wkv_k_a"][l, :])
                bcload(rkbc[:], "rkbc", I["wkv_r_k"][l, :])
                S.op("dve", lambda e: e.tensor_scalar(out=omka[:], in0=kabc[:], scalar1=-1.0, scalar2=1.0,
                                                      op0=ALU.mult, op1=ALU.add), r=["kabc"], w=["omka"])
                for d in range(2):
                    S.dma("sp", wup[d][:], I["wkv_w_up"][l, d], w=["wup%d" % d])
                    S.dma("sp", aup[d][:], I["wkv_a_up"][l, d], w=["aup%d" % d])
                S.dma("sp", gup[:], I["wkv_g_up"][l].rearrange("(c p) n -> p c n", p=128), w=["gup"])
                PY = (3, 4, 5)
                h3 = lambda ap: ap.rearrange("p (h q) -> p h q", h=24)

                def fm_store(src, skey, dst3):
                    f = fm[fmc[0] % 2]
                    fk = "wfm%d" % (fmc[0] % 2)
                    fmc[0] += 1
                    transposes_to(src, skey, 12, lambda q, nj: f[:, q:q + nj, :], fk, eng="dve")
                    S.dma("sp", dst3, f[:], r=[fk], w=["FMOUT"])

                for i in range(NT):
                    t0 = i * 128
                    S.dma("sp", sh[:], SH[t0:t0 + 128, :], r=["SH"], w=["sh"])
                    r_ = sh[:, 0:1536]
                    k_ = sh[:, 1536:3072]
                    S.op("act", lambda e: e.activation(out=vb16[:], in_=sh[:, 3072:4608], func=AF.Copy), r=["sh"],
                         w=["vb16"])
                    vb4 = vb16[:].rearrange("p (g h v) -> p g h v", g=12, h=2)
                    for hh in range(2):
                        for g in range(12):
                            en = "act" if g % 2 == 0 else "dve"
                            if en == "act":
                                S.op("act", lambda e, g=g: e.activation(out=Vx[:, g, g * 64:(g + 1) * 64],
                                                                        in_=vb4[:, g, hh, :], func=AF.Copy),
                                     r=["vb16"], w=["Vx"])
                            else:
                                S.op("dve", lambda e, g=g: e.tensor_copy(out=Vx[:, g, g * 64:(g + 1) * 64],
                                                                         in_=vb4[:, g, hh, :]), r=["vb16"], w=["Vx"])
                        S.dma("sp", VBD[hh][t0:t0 + 128, :, :], Vx[:], r=["Vx"], w=["VBD"])
                    S.op("dve", lambda e: e.tensor_mul(out=A[:], in0=k_, in1=kkbc[:]), r=["sh", "kkbc"], w=["wA"])
                    S.op("act", lambda e: e.activation(out=Bt[:], in_=A[:], func=AF.Square), r=["wA"], w=["wB"])
                    S.op("dve", lambda e: e.tensor_reduce(out=rs[:], in_=h3(Bt[:]), axis=AX.X, op=ALU.add),
                         r=["wB"], w=["wrs"])
                    S.op("dve", lambda e: e.tensor_scalar(out=rs[:], in0=rs[:], scalar1=1e-24, scalar2=None,
                                                          op0=ALU.max), r=["wrs"], w=["wrs"])
                    S.op("act", lambda e: e.activation(out=rs[:], in_=rs[:], func=AF.Sqrt), r=["wrs"], w=["wrs"])
                    S.op("dve", lambda e: e.reciprocal(out=rs[:], in_=rs[:]), r=["wrs"], w=["wrs"])
                    S.op("dve", lambda e: e.tensor_scalar(out=rs[:], in0=rs[:], scalar1=-1.0, scalar2=None,
                                                          op0=ALU.mult), r=["wrs"], w=["wrs"])
                    S.op("dve", lambda e: e.tensor_tensor(out=h3(nkk[:]), in0=h3(A[:]),
                                                          in1=rs[:, :].unsqueeze(2).to_broadcast([128, 24, 64]),
                                                          op=ALU.mult), r=["wA", "wrs"], w=["nkk"])
                    fm_store(nkk[:], "nkk", NKKT[:, :, t0:t0 + 128])
                    S.op("dve", lambda e: e.tensor_mul(out=Bt[:], in0=r_, in1=k_), r=["sh"], w=["wB"])
                    S.op("dve", lambda e: e.tensor_mul(out=Bt[:], in0=Bt[:], in1=rkbc[:]), r=["wB", "rkbc"], w=["wB"])
                    S.op("dve", lambda e: e.tensor_reduce(out=sm[:, 0:24], in_=h3(Bt[:]), axis=AX.X, op=ALU.add),
                         r=["wB"], w=["wsm"])
                    S.dma("sp", RK[t0:t0 + 128, :], sm[:, 0:24], r=["wsm"], w=["RK"])
                    S.op("act", lambda e: e.activation(out=Ct[:, 0:192], in_=sh[:, 4608:4800], func=AF.Tanh),
                         r=["sh"], w=["wC"])
                    S.op("act", lambda e: e.activation(out=Ct[:, 192:448], in_=sh[:, 4992:5248], func=AF.Sigmoid),
                         r=["sh"], w=["wC"])
                    transposes_to(Ct[:, 0:192], "wC", 2, lambda q, nj: twT[:, q:q + nj, :], "twT", bw=96, eng="dve")
                    transposes_to(sh[:, 4800:4992], "sh", 2, lambda q, nj: aT[:, q:q + nj, :], "aT", bw=96, eng="dve")
                    transposes_to(Ct[:, 192:448], "wC", 2, lambda q, nj: sgT[:, q:q + nj, :], "sgT", eng="dve")
                    for cb in range(3):
                        for c in range(2):
                            S.op("pe", lambda e, cb=cb, c=c: e.matmul(psb[PY[cb]][:, :], lhsT=sgT[:, c, :],
                                                                      rhs=gup[:, c, cb * 512:(cb + 1) * 512],
                                                                      start=(c == 0), stop=(c == 1)),
                                 r=["sgT", "gup"], w=[PK[PY[cb]]])
                        S.op("act", lambda e, cb=cb: e.activation(out=E[:, cb * 512:(cb + 1) * 512],
                                                                  in_=psb[PY[cb]][:, :], func=AF.Copy),
                             r=[PK[PY[cb]]], w=["wE"])
                    S.dma("sp", GG[t0:t0 + 128, :], E[:], r=["wE"], w=["GG"])
                    for d in range(2):
                        bcload(w0bc[:], "w0bc", I["wkv_w0"][l, d, :])
                        bcload(a0bc[:], "a0bc", I["wkv_a0"][l, d, :])
                        for cb in range(3):
                            S.op("pe", lambda e, cb=cb: e.matmul(psb[PY[cb]][:, :], lhsT=twT[:, d, :],
                                                                 rhs=wup[d][:, cb * 512:(cb + 1) * 512], start=True,
                                                                 stop=True), r=["twT", "wup%d" % d], w=[PK[PY[cb]]])
                            S.op("dve", lambda e, cb=cb: e.tensor_add(out=Bt[:, cb * 512:(cb + 1) * 512],
                                                                      in0=psb[PY[cb]][:, :],
                                                                      in1=w0bc[:, cb * 512:(cb + 1) * 512]),
                                 r=[PK[PY[cb]], "w0bc"], w=["wB"])
                        S.op("act", lambda e: e.activation(out=Bt[:], in_=Bt[:], func=AF.Sigmoid), r=["wB"], w=["wB"])
                        S.op("act", lambda e: e.activation(out=Bt[:], in_=Bt[:], func=AF.Exp,
                                                           scale=-math.exp(-0.5)), r=["wB"], w=["wB"])
                        for cb in range(3):
                            S.op("pe", lambda e, cb=cb: e.matmul(psb[PY[cb]][:, :], lhsT=aT[:, d, :],
                                                                 rhs=aup[d][:, cb * 512:(cb + 1) * 512], start=True,
                                                                 stop=True), r=["aT", "aup%d" % d], w=[PK[PY[cb]]])
                            S.op("dve", lambda e, cb=cb: e.tensor_add(out=Dt_[:, cb * 512:(cb + 1) * 512],
                                                                      in0=psb[PY[cb]][:, :],
                                                                      in1=a0bc[:, cb * 512:(cb + 1) * 512]),
                                 r=[PK[PY[cb]], "a0bc"], w=["wD"])
                        S.op("act", lambda e: e.activation(out=Dt_[:], in_=Dt_[:], func=AF.Sigmoid), r=["wD"],
                             w=["wD"])
                        S.op("dve", lambda e: e.tensor_mul(out=A[:], in0=Dt_[:], in1=kabc[:]), r=["wD", "kabc"],
                             w=["wA"])
                        S.op("dve", lambda e: e.tensor_add(out=A[:], in0=A[:], in1=omka[:]), r=["wA", "omka"],
                             w=["wA"])
                        S.op("dve", lambda e: e.tensor_mul(out=A[:], in0=A[:], in1=k_), r=["wA", "sh"], w=["wA"])
                        S.op("act", lambda e: e.activation(out=kb16[:], in_=A[:], func=AF.Copy), r=["wA"], w=["kb16"])
                        S.dma("sp", KD[d][t0:t0 + 128, :], kb16[:], r=["kb16"], w=["KD"])
                        S.op("dve", lambda e: e.scalar_tensor_tensor(out=Ct[:], in0=nkk[:], scalar=-1.0, in1=Dt_[:],
                                                                     op0=ALU.mult, op1=ALU.mult),
                             r=["nkk", "wD"], w=["wC"])
                        S.op("act", lambda e: e.activation(out=bb16[:], in_=Ct[:], func=AF.Copy), r=["wC"], w=["bb16"])
                        S.dma("sp", BD[d][t0:t0 + 128, :], bb16[:], r=["bb16"], w=["BD"])
                        S.op("dve", lambda e: e.tensor_mul(out=E[:], in0=Ct[:], in1=r_), r=["wC", "sh"], w=["wE"])
                        S.op("dve", lambda e: e.tensor_reduce(out=sm[:, 0:24], in_=h3(E[:]), axis=AX.X, op=ALU.add),
                             r=["wE"], w=["wsm"])
                        S.op("dve", lambda e: e.tensor_mul(out=E[:], in0=A[:], in1=r_), r=["wA", "sh"], w=["wE"])
                        S.op("dve", lambda e: e.tensor_reduce(out=sm[:, 24:48], in_=h3(E[:]), axis=AX.X, op=ALU.add),
                             r=["wE"], w=["wsm"])
                        S.dma("sp", BRKR[d][t0:t0 + 128, :], sm[:], r=["wsm"], w=["BRKR"])
                        S.op("dve", lambda e: e.tensor_tensor(out=h3(A[:]), in0=h3(nkk[:]),
                                                              in1=sm[:, 0:24].unsqueeze(2).to_broadcast([128, 24, 64]),
                                                              op=ALU.mult), r=["nkk", "wsm"], w=["wA"])
                        S.op("dve", lambda e: e.tensor_mul(out=E[:], in0=Bt[:], in1=r_), r=["wB", "sh"], w=["wE"])
                        S.op("dve", lambda e: e.tensor_add(out=E[:], in0=E[:], in1=A[:]), r=["wE", "wA"], w=["wE"])
                        fm_store(E[:], "wE", WRT[d][:, :, t0:t0 + 128])
                        fm_store(Bt[:], "wB", WT[d][:, :, t0:t0 + 128])
                S.barrier()

        def phase_wkv_scan(l, T, L, sample):
            TBK = 4
            LT = L // 128
            with contextlib.ExitStack() as ps:
                lsb = lambda name, shape, dt=F32: ps.enter_context(nc.sbuf_tensor(uname(name), list(shape), dt))
                m48 = lsb("m48", [48, 768])
                m24b = lsb("m24b", [24, 768], BF16)
                sio = lsb("sio", [64, 12, 128])
                B = []
                for d in range(2):
                    b = {n: lsb("%s_%d" % (n, d), shp, dt) for (n, shp, dt) in (
                        ("Ap", [128, 128, 48], F32), ("nkT", [128, 12, 128], F32), ("wrT", [128, 12, 128], F32),
                        ("wT", [128, 12, 128], F32), ("Lb", [24, 2, TBK, 128], BF16), ("Lk", [24, 2, TBK, 128], BF16),
                        ("Rv", [24, 2, TBK, 768], BF16), ("Ra", [48, TBK, 768], F32), ("Rb", [24, TBK, 768], BF16),
                        ("Cst", [48, TBK, 64], F32), ("ST", [128, 768], F32), ("T1", [128, 384], F32))}
                    b["k"] = {n: "%s_%d" % (n, d) for n in ("Ap", "nkT", "wrT", "wT", "Lb", "Lk", "Rv", "Ra", "Rb",
                                                            "Cst", "ST", "T1")}
                    b["pb"] = (0, 1, 2, 3) if d == 0 else (4, 5, 6, 7)
                    B.append(b)
                S.dma("sp", m48[:], I["mask48"][:, :], w=["m48"])
                S.op("dve", lambda e: e.tensor_copy(out=m24b[:], in_=m48[0:24, :]), r=["m48"], w=["m24b"])
                for d in range(2):
                    b = B[d]
                    S.op("dve", lambda e: e.memset(b["Ap"][:], 0.0), w=[b["k"]["Ap"]])
                    S.op("dve", lambda e: e.memset(b["Lb"][:], 0.0), w=[(b["k"]["Lb"], 0), (b["k"]["Lb"], 1)])
                    S.op("dve", lambda e: e.memset(b["Lk"][:], 0.0), w=[(b["k"]["Lk"], 0), (b["k"]["Lk"], 1)])
                for s_ in range(T // L):
                    for d in range(2):
                        b = B[d]
                        ST = b["ST"]
                        if sample:
                            S.dma("sp", sio[:].rearrange("v g (h k) -> v g h k", h=2),
                                  I["st_wkv"][l, d].rearrange("(g h) v k -> v g h k", h=2), w=["sio"])
                            transposes_to(sio[:].rearrange("v g q -> v (g q)"), "sio", 12,
                                          lambda q, nj: ST[:, q * 64:(q + nj) * 64].rearrange("p (j v) -> p j v", j=nj),
                                          b["k"]["ST"], bw=128, inw=64, pbanks=b["pb"][2:4])
                            S.op("dve", lambda e: e.tensor_copy(out=ST[:, 0:1], in_=ST[:, 0:1]), r=[b["k"]["ST"]],
                                 w=[(b["k"]["ST"], g_) for g_ in range(12)])
                        else:
                            S.op("dve", lambda e: e.memset(ST[:], 0.0),
                                 w=[(b["k"]["ST"], g_) for g_ in range(12)] + [b["k"]["ST"]])
                    for kk_ in range(LT):
                        tis = (kk_, LT - 1 - kk_)
                        t0s = [s_ * L + ti * 128 for ti in tis]
                        for d in range(2):
                            b = B[d]
                            k = b["k"]
                            t0 = t0s[d]
                            S.dma("sp", b["nkT"][:], NKKT[:, :, t0:t0 + 128], r=["NKKT"], w=[k["nkT"]])
                            S.dma("sp", b["wrT"][:], WRT[d][:, :, t0:t0 + 128], r=["WRT"], w=[k["wrT"]])
                            S.dma("sp", b["wT"][:], WT[d][:, :, t0:t0 + 128], r=["WT"], w=[k["wT"]])
                            for (lo, hi, c0_, src, sk) in ((0, 64, 0, "nkT", k["nkT"]), (64, 128, 12, "nkT", k["nkT"]),
                                                           (0, 64, 24, "wrT", k["wrT"]), (64, 128, 36, "wrT", k["wrT"])):
                                S.op("act", lambda e, lo=lo, hi=hi, c0_=c0_, src=src: e.activation(
                                    out=b["Ap"][lo:hi, :, c0_:c0_ + 12], in_=b[src][lo:hi].rearrange("p g t -> p t g"),
                                    func=AF.Copy), r=[sk], w=[k["Ap"]])
                        for cc in range(128 // TBK):
                            chs = (cc, 128 // TBK - 1 - cc)
                            NCH = 128 // TBK
                            par = cc % 2

                            def stage(cidx, pr):
                                chx = (cidx, NCH - 1 - cidx)
                                for d in range(2):
                                    b = B[d]
                                    k = b["k"]
                                    c0 = t0s[d] + chx[d] * TBK
                                    bsrc = BD[d][c0:c0 + TBK, :].rearrange("t (g h k) -> g t h k", g=12, h=2)
                                    ksrc = KD[d][c0:c0 + TBK, :].rearrange("t (g h k) -> g t h k", g=12, h=2)
                                    S.dma("sp", b["Lb"][0:12, pr, :, 0:64], bsrc[:, :, 0, :], r=["BD"],
                                          w=[(k["Lb"], pr)])
                                    S.dma("sp", b["Lb"][12:24, pr, :, 64:128], bsrc[:, :, 1, :], r=["BD"],
                                          w=[(k["Lb"], pr)])
                                    S.dma("sp", b["Lk"][0:12, pr, :, 0:64], ksrc[:, :, 0, :], r=["KD"],
                                          w=[(k["Lk"], pr)])
                                    S.dma("sp", b["Lk"][12:24, pr, :, 64:128], ksrc[:, :, 1, :], r=["KD"],
                                          w=[(k["Lk"], pr)])
                                    for hh in range(2):
                                        S.dma("sp", b["Rv"][hh * 12:(hh + 1) * 12, pr, :, :],
                                              VBD[hh][c0:c0 + TBK, :, :].rearrange("t g q -> g t q"),
                                              r=["VBD"], w=[(k["Rv"], pr)])

                            if cc == 0:
                                stage(0, 0)
                            if cc + 1 < NCH:
                                stage(cc + 1, 1 - par)
                            for st_ in range(TBK):
                                tls = (st_, TBK - 1 - st_)
                                toks = [chs[d] * TBK + tls[d] for d in range(2)]
                                for d in range(2):
                                    b = B[d]
                                    k = b["k"]
                                    for hf in range(2):
                                        S.op("pe", lambda e, hf=hf: e.matmul(
                                            psb[b["pb"][hf]][0:48, 0:384], lhsT=b["Ap"][:, toks[d], :],
                                            rhs=b["ST"][:, hf * 384:(hf + 1) * 384], start=True, stop=True),
                                            r=[k["Ap"]] + [(k["ST"], g_) for g_ in range(hf * 6, hf * 6 + 6)],
                                            w=[PK[b["pb"][hf]]])
                                for d in range(2):
                                    b = B[d]
                                    k = b["k"]
                                    for hf in range(2):
                                        S.op("dve", lambda e, hf=hf: e.tensor_mul(
                                            out=b["Ra"][:, tls[d], hf * 384:(hf + 1) * 384],
                                            in0=psb[b["pb"][hf]][0:48, 0:384],
                                            in1=m48[:, hf * 384:(hf + 1) * 384]),
                                            r=[PK[b["pb"][hf]], "m48"], w=[(k["Ra"], hf)])
                                for d in range(2):
                                    b = B[d]
                                    k = b["k"]
                                    for g in range(6, 12):
                                        S.op("act", lambda e, g=g: e.activation(
                                            out=b["T1"][:, (g - 6) * 64:(g - 5) * 64],
                                            in_=b["ST"][:, g * 64:(g + 1) * 64], func=AF.Copy,
                                            scale=b["wT"][:, g, toks[d]:toks[d] + 1]),
                                            r=[(k["ST"], g), k["wT"]], w=[(k["T1"], g)])
                                for d in range(2):
                                    b = B[d]
                                    k = b["k"]
                                    S.op("act", lambda e: e.activation(out=b["Rb"][:, tls[d], :],
                                                                       in_=b["Ra"][0:24, tls[d], :], func=AF.Copy),
                                         r=[(k["Ra"], 0), (k["Ra"], 1)], w=[k["Rb"]])
                                for d in range(2):
                                    b = B[d]
                                    k = b["k"]
                                    for hf in range(2):
                                        pb = b["pb"][2 + hf]
                                        S.op("pe", lambda e, hf=hf, pb=pb: e.matmul(
                                            psb[pb][:, 0:384], lhsT=b["Lk"][:, par, tls[d], :],
                                            rhs=b["Rv"][:, par, tls[d], hf * 384:(hf + 1) * 384], start=True,
                                            stop=False), r=[(k["Lk"], par), (k["Rv"], par)], w=[PK[pb]])
                                        S.op("pe", lambda e, hf=hf, pb=pb: e.matmul(
                                            psb[pb][:, 0:384], lhsT=b["Lb"][:, par, tls[d], :],
                                            rhs=b["Rb"][:, tls[d], hf * 384:(hf + 1) * 384], start=False, stop=True),
                                            r=[(k["Lb"], par), k["Rb"]], w=[PK[pb]])
                                for d in range(2):
                                    b = B[d]
                                    k = b["k"]
                                    for g in range(6):
                                        pb = b["pb"][2]
                                        S.op("dve", lambda e, g=g, pb=pb: e.scalar_tensor_tensor(
                                            out=b["ST"][:, g * 64:(g + 1) * 64], in0=b["ST"][:, g * 64:(g + 1) * 64],
                                            scalar=b["wT"][:, g, toks[d]:toks[d] + 1],
                                            in1=psb[pb][:, g * 64:(g + 1) * 64], op0=ALU.mult, op1=ALU.add),
                                            r=[(k["ST"], g), k["wT"], PK[pb]], w=[(k["ST"], g)])
                                    pb = b["pb"][3]
                                    S.op("dve", lambda e, pb=pb: e.tensor_add(
                                        out=b["ST"][:, 384:768], in0=b["T1"][:, :], in1=psb[pb][:, 0:384]),
                                        r=[(k["T1"], g_) for g_ in range(6, 12)] + [PK[pb]],
                                        w=[(k["ST"], g_) for g_ in range(6, 12)])
                            for d in range(2):
                                b = B[d]
                                k = b["k"]
                                c0 = t0s[d] + chs[d] * TBK
                                S.op("dve", lambda e: e.tensor_reduce(
                                    out=b["Cst"][:], in_=b["Ra"][:].rearrange("p t (g v) -> p t v g", g=12),
                                    axis=AX.X, op=ALU.add), r=[(k["Ra"], 0), (k["Ra"], 1)], w=[k["Cst"]])
                                S.dma("sp", SAY[d][:, c0:c0 + TBK, :], b["Cst"][:], r=[k["Cst"]], w=["SAY"])
                    if not sample:
                        for d in range(2):
                            b = B[d]
                            S.op("dve", lambda e: e.tensor_copy(out=b["ST"][:, 0:1], in_=b["ST"][:, 0:1]),
                                 r=[(b["k"]["ST"], g_) for g_ in range(12)], w=[b["k"]["ST"]])
                            transposes_to(b["ST"][:], b["k"]["ST"], 12, lambda q, nj: sio[:, q:q + nj, :], "sio",
                                          bw=64, inw=128, pbanks=b["pb"][2:4])
                            S.dma("sp", O["ns_wkv"][s_, l, d].rearrange("(g h) v k -> v g h k", h=2),
                                  sio[:].rearrange("v g (h k) -> v g h k", h=2), r=["sio"], w=["ns_wkv"])
                S.barrier()

        def phase_wkv_post(l, T):
            with contextlib.ExitStack() as ps:
                lsb = lambda name, shape, dt=F32: ps.enter_context(nc.sbuf_tensor(uname(name), list(shape), dt))
                y0 = [lsb("py0%d" % d, [128, 1536]) for d in range(2)]
                vv = lsb("pvv", [128, 1536])
                gg = lsb("pgg", [128, 1536])
                o = lsb("po", [128, 1536])
                t_ = lsb("pt", [128, 1536])
                lnw = lsb("lnw", [128, 1536])
                lnb = lsb("lnb", [128, 1536])
                bk = [lsb("pbk%d" % d, [128, 48]) for d in range(2)]
                rk = lsb("prk", [128, 24])
                mu = lsb("pmu", [128, 24])
                bcload(lnw[:], "lnw", I["wkv_ln_w"][l, :])
                bcload(lnb[:], "lnb", I["wkv_ln_b"][l, :])
                h3 = lambda ap: ap.rearrange("p (h q) -> p h q", h=24)
                bc3 = lambda ap: ap.unsqueeze(2).to_broadcast([128, 24, 64])
                for i in range(T // 128):
                    t0 = i * 128
                    for d in range(2):
                        for hh in range(2):
                            S.dma("sp", y0[d][:].rearrange("p (g h v) -> p g h v", g=12, h=2)[:, :, hh, :],
                                  SAY[d][(2 + hh) * 12:(3 + hh) * 12, t0:t0 + 128, :].rearrange("g t v -> t g v"),
                                  r=["SAY"], w=["py0%d" % d])
                        S.dma("sp", bk[d][:], BRKR[d][t0:t0 + 128, :], r=["BRKR"], w=["pbk%d" % d])
                    S.dma("sp", vv[:], SH[t0:t0 + 128, 3072:4608], r=["SH"], w=["pvv"])
                    S.dma("sp", gg[:], GG[t0:t0 + 128, :], r=["GG"], w=["pgg"])
                    S.dma("sp", rk[:], RK[t0:t0 + 128, :], r=["RK"], w=["prk"])
                    S.op("dve", lambda e: e.tensor_add(out=o[:], in0=y0[0][:], in1=y0[1][:]), r=["py00", "py01"],
                         w=["po"])
                    S.op("dve", lambda e: e.tensor_add(out=mu[:], in0=bk[0][:, 24:48], in1=bk[1][:, 24:48]),
                         r=["pbk0", "pbk1"], w=["pmu"])
                    S.op("dve", lambda e: e.tensor_tensor(out=h3(t_[:]), in0=h3(vv[:]), in1=bc3(mu[:, :]),
                                                          op=ALU.mult), r=["pvv", "pmu"], w=["pt"])
                    S.op("dve", lambda e: e.tensor_add(out=o[:], in0=o[:], in1=t_[:]), r=["po", "pt"], w=["po"])
                    S.op("dve", lambda e: e.tensor_reduce(out=mu[:], in_=h3(o[:]), axis=AX.X, op=ALU.add), r=["po"],
                         w=["pmu"])
                    S.op("dve", lambda e: e.tensor_scalar(out=mu[:], in0=mu[:], scalar1=1.0 / 64, scalar2=None,
                                                          op0=ALU.mult), r=["pmu"], w=["pmu"])
                    S.op("dve", lambda e: e.tensor_tensor(out=h3(o[:]), in0=h3(o[:]), in1=bc3(mu[:, :]),
                                                          op=ALU.subtract), r=["po", "pmu"], w=["po"])
                    S.op("act", lambda e: e.activation(out=t_[:], in_=o[:], func=AF.Square), r=["po"], w=["pt"])
                    S.op("dve", lambda e: e.tensor_reduce(out=mu[:], in_=h3(t_[:]), axis=AX.X, op=ALU.add), r=["pt"],
                         w=["pmu"])
                    rms_rstd(mu[:], "pmu", 64, 64e-5)
                    S.op("dve", lambda e: e.tensor_tensor(out=h3(o[:]), in0=h3(o[:]), in1=bc3(mu[:, :]),
                                                          op=ALU.mult), r=["po", "pmu"], w=["po"])
                    S.op("dve", lambda e: e.tensor_mul(out=o[:], in0=o[:], in1=lnw[:]), r=["po", "lnw"], w=["po"])
                    S.op("dve", lambda e: e.tensor_add(out=o[:], in0=o[:], in1=lnb[:]), r=["po", "lnb"], w=["po"])
                    S.op("dve", lambda e: e.tensor_tensor(out=h3(t_[:]), in0=h3(vv[:]), in1=bc3(rk[:, :]),
                                                          op=ALU.mult), r=["pvv", "prk"], w=["pt"])
                    S.op("dve", lambda e: e.tensor_add(out=o[:], in0=o[:], in1=t_[:]), r=["po", "pt"], w=["po"])
                    S.op("dve", lambda e: e.tensor_mul(out=o[:], in0=o[:], in1=gg[:]), r=["po", "pgg"], w=["po"])
                    S.dma("sp", ZW[t0:t0 + 128, :], o[:], r=["po"], w=["ZW"])
                S.barrier()

        def phase_s5(l, T, L, sample):
            NT = T // 128
            nseq = T // L
            LT = L // 128
            Ls = min(512, L)
            nseg = L // Ls
            nsub = Ls // 128
            with contextlib.ExitStack() as ps:
                lsb = lambda name, shape, dt=F32: ps.enter_context(nc.sbuf_tensor(uname(name), list(shape), dt))
                ut = [lsb("ut%d" % j, [128, 1024]) for j in range(2)]
                stg = [lsb("ustg%d" % j, [32, 32, 128]) for j in range(2)]
                cnt = 0
                for i in range(NT):
                    t0 = i * 128
                    s_ = t0 // L
                    ti = (t0 % L) // 128
                    t0r = s_ * L + (LT - 1 - ti) * 128
                    u = ut[i % 2]
                    uk = "ut%d" % (i % 2)
                    S.dma("sp", u[:], P[t0:t0 + 128, C_U:C_U + 1024], r=["P"], w=[uk])
                    for d in range(2):
                        st_ = stg[cnt % 2]
                        sk = "ustg%d" % (cnt % 2)
                        cnt += 1
                        transposes_to(u[:], uk, 32, lambda q, nj: st_[:, q:q + nj, :], sk, bw=32, inw=128,
                                      eng=("act" if d == 0 else "dve"), rhs=(None if d == 0 else jrev[:]),
                                      rkey="jrev", pbanks=((6, 7) if d == 0 else (4, 5)))
                        dt_ = t0 if d == 0 else t0r
                        S.dma("sp", UT2[d][:, :, dt_:dt_ + 128].rearrange("k r t -> r k t"), st_[:], r=[sk],
                              w=["UT2"])
                S.barrier()
            with contextlib.ExitStack() as ps:
                lsb = lambda name, shape, dt=F32: ps.enter_context(nc.sbuf_tensor(uname(name), list(shape), dt))
                pt_ = {n: lsb("s5_" + n, [128, 32]) for n in
                       ("lre", "lim", "ldt", "rho", "tht", "cs", "sn", "ar", "ai", "t1", "t2", "t3", "cr", "ci")}
                it_ = lsb("s5_it", [128, 32], I32)
                bre = lsb("bre", [128, 32, 16])
                bim = lsb("bim", [128, 32, 16])
                Bbr = lsb("Bbr", [128, 32, 16])
                Bbi = lsb("Bbi", [128, 32, 16])
                btmp = lsb("btmp", [128, 32, 16])
                BDr = lsb("BDr", [128, 32, 32])
                BDi = lsb("BDi", [128, 32, 32])
                BpTr = lsb("BpTr", [32, 32, 128])
                BpTi = lsb("BpTi", [32, 32, 128])
                Zr = lsb("Zr", [32, 32, 128])
                Zi = lsb("Zi", [32, 32, 128])
                Cre = lsb("Cre", [128, 32, 32], BF16)
                nCre = lsb("nCre", [128, 32, 32], BF16)
                nCim = lsb("nCim", [128, 32, 32], BF16)
                jjt = lsb("jjt", [128, 512])
                tj = lsb("tj", [128, 512])
                tf = lsb("tf", [128, 512])
                iti = lsb("iti", [128, 512], I32)
                cst = lsb("cst", [128, 512])
                snt = lsb("snt", [128, 512])
                u2 = [lsb("u2_%d" % j, [32, 512]) for j in range(2)]
                p1 = lsb("p1", [128, 512])
                p2 = lsb("p2", [128, 512])
                inre = lsb("inre", [128, 512])
                inim = lsb("inim", [128, 512])
                zr = lsb("zr", [128, 512])
                zi = lsb("zi", [128, 512])
                qq = [lsb("qq%d" % j, [128, 512], BF16) for j in range(4)]
                xr = lsb("xr", [128, 1])
                xi = lsb("xi", [128, 1])
                c4 = lsb("c4", [128, 4])
                hre = lsb("hre", [128, 32])
                him = lsb("him", [128, 32])
                finr = lsb("finr", [128, nseq, 32])
                fini = lsb("fini", [128, nseq, 32])
                ystg = [lsb("ystg%d" % j, [128, 4, 32]) for j in range(2)]
                XS = [dict(sfx="_a", tj=tj, tf=tf, iti=iti, cst=cst, snt=snt, p1=p1, p2=p2, inre=inre, inim=inim, zr=zr,
                           zi=zi, qq=qq, xr=xr, xi=xi, c4=c4, u2=u2, ystg=ystg, pb=(0, 1, 2), ucnt=0, ycnt=0)]
                XS.append(dict(
                    sfx="_b", tj=lsb("tjb", [128, 512]), tf=lsb("tfb", [128, 512]), iti=lsb("itib", [128, 512], I32),
                    cst=lsb("cstb", [128, 512]), snt=lsb("sntb", [128, 512]), p1=lsb("p1b", [128, 512]),
                    p2=lsb("p2b", [128, 512]), inre=lsb("inreb", [128, 512]), inim=lsb("inimb", [128, 512]),
                    zr=lsb("zrb", [128, 512]), zi=lsb("zib", [128, 512]),
                    qq=[lsb("qqb%d" % j, [128, 512], BF16) for j in range(4)], xr=lsb("xrb", [128, 1]),
                    xi=lsb("xib", [128, 1]), c4=lsb("c4b", [128, 4]),
                    u2=[lsb("u2b_%d" % j, [32, 512]) for j in range(2)],
                    ystg=[lsb("ystgb%d" % j, [128, 4, 32]) for j in range(2)], pb=(3, 4, 5), ucnt=0, ycnt=0))
                S.dma("sp", jjt[:], I["jj"][:, :], w=["jjt"])
                for zt, zk in ((BDr, "BDr"), (BDi, "BDi"), (Zr, "Zr"), (Zi, "Zi")):
                    S.op("dve", lambda e, zt=zt: e.memset(zt[:], 0.0), w=[zk])
                K_ = "s5p"

                def tt(o, a, b, op):
                    S.op("dve", lambda e: e.tensor_tensor(out=pt_[o][:], in0=pt_[a][:], in1=pt_[b][:], op=op),
                         r=[K_], w=[K_])

                def ts(o, a, s1, s2, op0, op1=None):
                    if op1 is None:
                        S.op("dve", lambda e: e.tensor_scalar(out=pt_[o][:], in0=pt_[a][:], scalar1=s1, scalar2=None,
                                                              op0=op0), r=[K_], w=[K_])
                    else:
                        S.op("dve", lambda e: e.tensor_scalar(out=pt_[o][:], in0=pt_[a][:], scalar1=s1, scalar2=s2,
                                                              op0=op0, op1=op1), r=[K_], w=[K_])

                def frac_sin(o, a):
                    S.op("dve", lambda e: e.tensor_copy(out=it_[:], in_=pt_[a][:]), r=[K_], w=[K_])
                    S.op("dve", lambda e: e.tensor_copy(out=pt_["t2"][:], in_=it_[:]), r=[K_], w=[K_])
                    tt("t2", a, "t2", ALU.subtract)
                    S.op("act", lambda e: e.activation(out=pt_[o][:], in_=pt_["t2"][:], func=AF.Sin, scale=SIN_SCALE),
                         r=[K_], w=[K_])

                cnt = 0
                ucnt = 0
                for d in range(2):
                    with nc.allow_non_contiguous_dma(reason="small s5 parameter transposes"):
                        for g2 in range(2):
                            sl = slice(g2 * 64, (g2 + 1) * 64)
                            S.dma("sp", pt_["lre"][sl, :],
                                  I["s5_lam_re"][l, d].rearrange("(k g) p -> g p k", g=2)[g2], w=[K_])
                            S.dma("sp", pt_["lim"][sl, :],
                                  I["s5_lam_im"][l, d].rearrange("(k g) p -> g p k", g=2)[g2], w=[K_])
                            S.dma("sp", pt_["ldt"][sl, :],
                                  I["s5_log_dt"][l, d].rearrange("(k g) -> g k", g=2)[g2].partition_broadcast(64),
                                  w=[K_])
                            if sample:
                                S.dma("sp", hre[sl, :], I["st_s5re"][l, d].rearrange("(k g) p -> g p k", g=2)[g2],
                                      w=["hre"])
                                S.dma("sp", him[sl, :], I["st_s5im"][l, d].rearrange("(k g) p -> g p k", g=2)[g2],
                                      w=["him"])
                    ts("lre", "lre", -1e-4, None, ALU.min)
                    S.op("act", lambda e: e.activation(out=pt_["ldt"][:], in_=pt_["ldt"][:], func=AF.Exp), r=[K_],
                         w=[K_])
                    tt("t1", "lre", "ldt", ALU.mult)
                    S.op("act", lambda e: e.activation(out=pt_["rho"][:], in_=pt_["t1"][:], func=AF.Exp), r=[K_],
                         w=[K_])
                    tt("tht", "lim", "ldt", ALU.mult)
                    ts("tht", "tht", 1.0 / TWO_PI, None, ALU.mult)
                    frac_sin("sn", "tht")
                    ts("t3", "tht", 0.25, None, ALU.add)
                    frac_sin("cs", "t3")
                    tt("ar", "rho", "cs", ALU.mult)
                    tt("ai", "rho", "sn", ALU.mult)
                    ts("ar", "ar", -1.0, None, ALU.add)
                    tt("t1", "ar", "lre", ALU.mult)
                    tt("t2", "ai", "lim", ALU.mult)
                    tt("cr", "t1", "t2", ALU.add)
                    tt("t1", "ai", "lre", ALU.mult)
                    tt("t2", "ar", "lim", ALU.mult)
                    tt("ci", "t1", "t2", ALU.subtract)
                    tt("t1", "lre", "lre", ALU.mult)
                    tt("t2", "lim", "lim", ALU.mult)
                    tt("t1", "t1", "t2", ALU.add)
                    S.op("dve", lambda e: e.reciprocal(out=pt_["t1"][:], in_=pt_["t1"][:]), r=[K_], w=[K_])
                    tt("cr", "cr", "t1", ALU.mult)
                    tt("ci", "ci", "t1", ALU.mult)
                    S.dma("sp", bre[:], I["s5_b_re"][l, d].rearrange("(k g) p c -> (g p) k c", g=2), w=["bre"])
                    S.dma("sp", bim[:], I["s5_b_im"][l, d].rearrange("(k g) p c -> (g p) k c", g=2), w=["bim"])
                    crb = pt_["cr"][:, :].unsqueeze(2).to_broadcast([128, 32, 16])
                    cib = pt_["ci"][:, :].unsqueeze(2).to_broadcast([128, 32, 16])
                    S.op("dve", lambda e: e.tensor_tensor(out=Bbr[:], in0=bre[:], in1=crb, op=ALU.mult),
                         r=["bre", K_], w=["Bbr"])
                    S.op("dve", lambda e: e.tensor_tensor(out=btmp[:], in0=bim[:], in1=cib, op=ALU.mult),
                         r=["bim", K_], w=["btmp"])
                    S.op("dve", lambda e: e.tensor_sub(out=Bbr[:], in0=Bbr[:], in1=btmp[:]), r=["Bbr", "btmp"],
                         w=["Bbr"])
                    S.op("dve", lambda e: e.tensor_tensor(out=Bbi[:], in0=bre[:], in1=cib, op=ALU.mult),
                         r=["bre", K_], w=["Bbi"])
                    S.op("dve", lambda e: e.tensor_tensor(out=btmp[:], in0=bim[:], in1=crb, op=ALU.mult),
                         r=["bim", K_], w=["btmp"])
                    S.op("dve", lambda e: e.tensor_add(out=Bbi[:], in0=Bbi[:], in1=btmp[:]), r=["Bbi", "btmp"],
                         w=["Bbi"])
                    for (bsrc, bkey, bd, bdk, bp, bpk) in ((Bbr, "Bbr", BDr, "BDr", BpTr, "BpTr"),
                                                           (Bbi, "Bbi", BDi, "BDi", BpTi, "BpTi")):
                        S.op("dve", lambda e: e.tensor_copy(out=bd[0:64, :, 0:16], in_=bsrc[0:64, :, :]), r=[bkey],
                             w=[bdk])
                        S.op("dve", lambda e: e.tensor_copy(out=bd[64:128, :, 16:32], in_=bsrc[64:128, :, :]),
                             r=[bkey], w=[bdk])
                        transposes_to(bd[:].rearrange("p k c -> p (k c)"), bdk, 32,
                                      lambda q, nj, bp=bp: bp[:, q:q + nj, :], bpk, bw=32, inw=128)
                    for (zt, zk, nm) in ((Zr, "Zr", "s5_c_re"), (Zi, "Zi", "s5_c_im")):
                        csrc = I[nm][l, d].rearrange("(k g) c p -> g c k p", g=2)
                        S.dma("sp", zt[0:16, :, 0:64], csrc[0], w=[zk])
                        S.dma("sp", zt[16:32, :, 64:128], csrc[1], w=[zk])
                    zf = lambda zt: zt[:].rearrange("r k q -> r (k q)")
                    transposes_to(zf(Zr), "Zr", 32, lambda q, nj: Cre[:, q:q + nj, :], "Cre", bw=128, inw=32)
                    transposes_to(zf(Zr), "Zr", 32, lambda q, nj: nCre[:, q:q + nj, :], "nCre", bw=128, inw=32,
                                  scale=-1.0)
                    transposes_to(zf(Zi), "Zi", 32, lambda q, nj: nCim[:, q:q + nj, :], "nCim", bw=128, inw=32,
                                  scale=-1.0)
                    Cm = (Cre, nCre, nCim, nCim)
                    Ck = ("Cre", "nCre", "nCim", "nCim")
                    def chain(kt, X):
                        sfx = X["sfx"]
                        kx = lambda n: n + sfx
                        tj, tf, iti, cst, snt = X["tj"], X["tf"], X["iti"], X["cst"], X["snt"]
                        p1, p2, inre, inim, zr, zi, qq = X["p1"], X["p2"], X["inre"], X["inim"], X["zr"], X["zi"], X["qq"]
                        xr, xi, c4 = X["xr"], X["xi"], X["c4"]
                        pb0, pb1, pb2 = X["pb"]
                        S.op("dve", lambda e: e.tensor_scalar(out=tj[:, 0:Ls], in0=jjt[:, 0:Ls],
                                                              scalar1=pt_["tht"][:, kt:kt + 1], scalar2=None,
                                                              op0=ALU.mult), r=["jjt", K_], w=[kx("tj")])
                        yield
                        for (dst, dk_, off) in ((snt, "snt", 0.0), (cst, "cst", 0.25)):
                            if off != 0.0:
                                S.op("dve", lambda e: e.tensor_scalar(out=tj[:, 0:Ls], in0=tj[:, 0:Ls], scalar1=off,
                                                                      scalar2=None, op0=ALU.add), r=[kx("tj")],
                                     w=[kx("tj")])
                                yield
                            S.op("dve", lambda e: e.tensor_copy(out=iti[:, 0:Ls], in_=tj[:, 0:Ls]), r=[kx("tj")],
                                 w=[kx("iti")])
                            yield
                            S.op("dve", lambda e: e.tensor_copy(out=tf[:, 0:Ls], in_=iti[:, 0:Ls]), r=[kx("iti")],
                                 w=[kx("tf")])
                            yield
                            S.op("dve", lambda e: e.tensor_sub(out=tf[:, 0:Ls], in0=tj[:, 0:Ls], in1=tf[:, 0:Ls]),
                                 r=[kx("tj"), kx("tf")], w=[kx("tf")])
                            yield
                            S.op("act", lambda e, dst=dst: e.activation(out=dst[:, 0:Ls], in_=tf[:, 0:Ls],
                                                                        func=AF.Sin, scale=SIN_SCALE),
                                 r=[kx("tf")], w=[kx(dk_)])
                            yield
                        rhob = pt_["rho"][:, kt:kt + 1].to_broadcast([128, Ls])
                        for s_ in range(nseq):
                            if sample:
                                S.op("dve", lambda e: e.tensor_copy(out=xr[:], in_=hre[:, kt:kt + 1]), r=["hre"],
                                     w=[kx("xr")])
                                S.op("dve", lambda e: e.tensor_copy(out=xi[:], in_=him[:, kt:kt + 1]), r=["him"],
                                     w=[kx("xi")])
                            else:
                                S.op("dve", lambda e: e.memset(xr[:], 0.0), w=[kx("xr")])
                                S.op("dve", lambda e: e.memset(xi[:], 0.0), w=[kx("xi")])
                            yield
                            for seg in range(nseg):
                                tau0 = s_ * L + seg * Ls
                                u_ = X["u2"][X["ucnt"] % 2]
                                uk = kx("u2_%d" % (X["ucnt"] % 2))
                                X["ucnt"] += 1
                                S.dma("sp", u_[:, 0:Ls], UT2[d][kt, :, tau0:tau0 + Ls], r=["UT2"], w=[uk])
                                S.op("pe", lambda e: e.matmul(psb[pb0][:, 0:Ls], lhsT=BpTr[:, kt, :], rhs=u_[:, 0:Ls],
                                                              start=True, stop=True), r=["BpTr", uk], w=[PK[pb0]])
                                S.op("pe", lambda e: e.matmul(psb[pb1][:, 0:Ls], lhsT=BpTi[:, kt, :], rhs=u_[:, 0:Ls],
                                                              start=True, stop=True), r=["BpTi", uk], w=[PK[pb1]])
                                yield
                                c_ = cst[:, 0:Ls]
                                s__ = snt[:, 0:Ls]
                                S.op("dve", lambda e: e.tensor_mul(out=p1[:, 0:Ls], in0=psb[pb0][:, 0:Ls], in1=c_),
                                     r=[PK[pb0], kx("cst")], w=[kx("p1")])
                                yield
                                S.op("dve", lambda e: e.tensor_mul(out=p2[:, 0:Ls], in0=psb[pb1][:, 0:Ls], in1=s__),
                                     r=[PK[pb1], kx("snt")], w=[kx("p2")])
                                yield
                                S.op("dve", lambda e: e.tensor_add(out=inre[:, 0:Ls], in0=p1[:, 0:Ls],
                                                                   in1=p2[:, 0:Ls]), r=[kx("p1"), kx("p2")],
                                     w=[kx("inre")])
                                yield
                                S.op("dve", lambda e: e.tensor_mul(out=p1[:, 0:Ls], in0=psb[pb1][:, 0:Ls], in1=c_),
                                     r=[PK[pb1], kx("cst")], w=[kx("p1")])
                                yield
                                S.op("dve", lambda e: e.tensor_mul(out=p2[:, 0:Ls], in0=psb[pb0][:, 0:Ls], in1=s__),
                                     r=[PK[pb0], kx("snt")], w=[kx("p2")])
                                yield
                                S.op("dve", lambda e: e.tensor_sub(out=inim[:, 0:Ls], in0=p1[:, 0:Ls],
                                                                   in1=p2[:, 0:Ls]), r=[kx("p1"), kx("p2")],
                                     w=[kx("inim")])
                                yield
                                S.op("dve", lambda e: e.tensor_tensor_scan(out=zr[:, 0:Ls], data0=rhob,
                                                                           data1=inre[:, 0:Ls], initial=xr[:, 0:1],
                                                                           op0=ALU.mult, op1=ALU.add),
                                     r=[K_, kx("inre"), kx("xr")], w=[kx("zr")])
                                yield
                                S.op("dve", lambda e: e.tensor_tensor_scan(out=zi[:, 0:Ls], data0=rhob,
                                                                           data1=inim[:, 0:Ls], initial=xi[:, 0:1],
                                                                           op0=ALU.mult, op1=ALU.add),
                                     r=[K_, kx("inim"), kx("xi")], w=[kx("zi")])
                                yield
                                for (qi, a_, ak, b_, bk_) in ((0, cst, "cst", zr, "zr"), (1, snt, "snt", zi, "zi"),
                                                              (2, snt, "snt", zr, "zr"), (3, cst, "cst", zi, "zi")):
                                    S.op("dve", lambda e, qi=qi, a_=a_, b_=b_: e.tensor_mul(
                                        out=qq[qi][:, 0:Ls], in0=a_[:, 0:Ls], in1=b_[:, 0:Ls]),
                                        r=[kx(ak), kx(bk_)], w=[kx("qq%d" % qi)])
                                    yield
                                e0 = Ls - 1
                                for (ci_, a_, ak, b_, bk_) in ((0, cst, "cst", zr, "zr"), (1, snt, "snt", zi, "zi"),
                                                               (2, snt, "snt", zr, "zr"), (3, cst, "cst", zi, "zi")):
                                    S.op("dve", lambda e, ci_=ci_, a_=a_, b_=b_: e.tensor_mul(
                                        out=c4[:, ci_:ci_ + 1], in0=a_[:, e0:e0 + 1], in1=b_[:, e0:e0 + 1]),
                                        r=[kx(ak), kx(bk_)], w=[(kx("c4"), ci_)])
                                    yield
                                S.op("dve", lambda e: e.tensor_sub(out=xr[:], in0=c4[:, 0:1], in1=c4[:, 1:2]),
                                     r=[(kx("c4"), 0), (kx("c4"), 1)], w=[kx("xr")])
                                yield
                                S.op("dve", lambda e: e.tensor_add(out=xi[:], in0=c4[:, 2:3], in1=c4[:, 3:4]),
                                     r=[(kx("c4"), 2), (kx("c4"), 3)], w=[kx("xi")])
                                yield
                                for jb in range(nsub):
                                    for qi in range(4):
                                        S.op("pe", lambda e, jb=jb, qi=qi: e.matmul(
                                            psb[pb2][:, jb * 32:(jb + 1) * 32], lhsT=qq[qi][:, jb * 128:(jb + 1) * 128],
                                            rhs=Cm[qi][:, kt, :], start=(qi == 0), stop=(qi == 3)),
                                            r=[kx("qq%d" % qi), Ck[qi]], w=[PK[pb2]])
                                ys_ = X["ystg"][X["ycnt"] % 2]
                                yk = kx("ystg%d" % (X["ycnt"] % 2))
                                X["ycnt"] += 1
                                S.op("act", lambda e: e.activation(
                                    out=ys_[:, 0:nsub, :], in_=psb[pb2][:, 0:nsub * 32].rearrange("p (j c) -> p j c",
                                                                                               j=nsub),
                                    func=AF.Copy), r=[PK[pb2]], w=[yk])
                                S.dma("sp", YS5[d][tau0:tau0 + Ls, kt * 32:(kt + 1) * 32].rearrange(
                                    "(j t) c -> t j c", t=128), ys_[:, 0:nsub, :], r=[yk], w=["YS5"])
                                yield
                            if not sample:
                                S.op("dve", lambda e: e.tensor_copy(out=finr[:, s_, kt:kt + 1], in_=xr[:]),
                                     r=[kx("xr")], w=["finr"])
                                S.op("dve", lambda e: e.tensor_copy(out=fini[:, s_, kt:kt + 1], in_=xi[:]),
                                     r=[kx("xi")], w=["fini"])
                                yield

                    for kt in range(0, 32, 2):
                        gens = [chain(kt, XS[0]), chain(kt + 1, XS[1])]
                        while gens:
                            for g_ in list(gens):
                                try:
                                    next(g_)
                                except StopIteration:
                                    gens.remove(g_)
                    if not sample:
                        with nc.allow_non_contiguous_dma(reason="small s5 state outputs"):
                            for s_ in range(nseq):
                                for g2 in range(2):
                                    sl = slice(g2 * 64, (g2 + 1) * 64)
                                    S.dma("sp", O["ns_s5re"][s_, l, d].rearrange("(k g) p -> g p k", g=2)[g2],
                                          finr[sl, s_, :], r=["finr"], w=["ns_s5"])
                                    S.dma("sp", O["ns_s5im"][s_, l, d].rearrange("(k g) p -> g p k", g=2)[g2],
                                          fini[sl, s_, :], r=["fini"], w=["ns_s5"])
                S.barrier()

        def phase_tail(l, T, L, jrow, xsrc, xdst):
            LT = L // 128
            NB = BLK // 128
            with contextlib.ExitStack() as ps:
                lsb = lambda name, shape, dt=F32: ps.enter_context(nc.sbuf_tensor(uname(name), list(shape), dt))
                ls = {"xt": [lsb("xt0", [128, D]), lsb("xt1", [128, D])], "junk": lsb("junk", [128, D]),
                      "ss": lsb("ss", [128, 1])}
                wbs = [lsb("wb0", [128, 8192], BF16), lsb("wb1", [128, 8192], BF16)]
                WSTG["t"] = lsb("wst", [128, 8192])
                hTa = lsb("hTa", [128, KC, BLK], BF16)
                Ybuf = lsb("Ybuf", [128, NB, D])
                GPt = lsb("GPt", [128, D])
                yt = [lsb("yt%d" % j, [128, 1536]) for j in range(2)]
                s5d = lsb("s5d", [128, 1024])
                gt = [lsb("gt%d" % j, [128, 512]) for j in range(2)]
                vt = [lsb("vt%d" % j, [128, 512]) for j in range(2)]
                mgt = [lsb("mgt%d" % j, [128, 512]) for j in range(2)]
                xg = lsb("xg", [128, 512])
                tg = lsb("tg", [128, 512])
                bcload(GPt[:], "GPt", MOD[jrow, 2 * D:3 * D])
                bcload(ls["junk"][:], "junk", I["norm_mix_post"][l, :])
                S.op("dve", lambda e: e.tensor_mul(out=GPt[:], in0=GPt[:], in1=ls["junk"][:]), r=["GPt", "junk"],
                     w=["GPt"])
                bcload(s5d[:], "s5d", I["s5_d"][l, :])
                ec = [0]

                def epi_merge(bi, t0):
                    def f(i, c0, cw, pt, pk):
                        n = ec[0] % 2
                        ec[0] += 1
                        rs_ = slice(t0 + i * 128, t0 + (i + 1) * 128)
                        g_, gk = gt[n], "gt%d" % n
                        v_, vk = vt[n], "vt%d" % n
                        m_, mk = mgt[n], "mgt%d" % n
                        gc = C_GATE + bi * D + c0
                        S.dma("sp", g_[:, 0:cw], P[rs_, gc:gc + cw], r=["P"], w=[gk])
                        S.op("act", lambda e: e.activation(out=g_[:, 0:cw], in_=g_[:, 0:cw], func=AF.Sigmoid), r=[gk],
                             w=[gk])
                        if bi == 2:
                            S.op("act", lambda e: e.activation(out=v_[:, 0:cw], in_=pt[:, cw:2 * cw],
                                                               func=AF.Sigmoid), r=[pk], w=[vk])
                            S.op("dve", lambda e: e.tensor_mul(out=v_[:, 0:cw], in0=pt[:, 0:cw], in1=v_[:, 0:cw]),
                                 r=[pk, vk], w=[vk])
                            S.op("dve", lambda e: e.tensor_mul(out=v_[:, 0:cw], in0=v_[:, 0:cw], in1=g_[:, 0:cw]),
                                 r=[vk, gk], w=[vk])
                        else:
                            S.op("dve", lambda e: e.tensor_mul(out=v_[:, 0:cw], in0=pt[:, 0:cw], in1=g_[:, 0:cw]),
                                 r=[pk, gk], w=[vk])
                        mgk = ("MG", i, c0 // 512)
                        if bi > 0:
                            S.dma("sp", m_[:, 0:cw], MG[rs_, c0:c0 + cw], r=[mgk], w=[mk])
                            S.op("dve", lambda e: e.tensor_add(out=v_[:, 0:cw], in0=v_[:, 0:cw], in1=m_[:, 0:cw]),
                                 r=[vk, mk], w=[vk])
                        S.dma("sp", MG[rs_, c0:c0 + cw], v_[:, 0:cw], r=[vk], w=[mgk])
                    return f

                def wload_glu(Wflat):
                    def f(wb, wk, c0, cw):
                        wst = WSTG["t"]
                        kch = wb.shape[1]
                        n = 128 * kch * 2 * cw
                        off = 128 * kch * 2 * c0
                        stv = wst[:, 0:kch * 2 * cw].rearrange("p (k n) -> p k n", k=kch)
                        S.dma("sp", stv, Wflat[off:off + n].rearrange("(p k n) -> p k n", p=128, k=kch), w=["wst"])
                        S.op("dve", lambda e: e.tensor_copy(out=wb, in_=stv), r=["wst"], w=[wk])
                    return f

                def epi_y(i, c0, cw, pt, pk):
                    S.op("act", lambda e: e.activation(out=Ybuf[:, i, c0:c0 + cw], in_=pt[:, 0:cw], func=AF.Copy),
                         r=[pk], w=["Ybuf"])

                for bi_ in range(T // BLK):
                    t0 = bi_ * BLK
                    for (br, src, Wn) in ((0, YS, "w_ssm_out"), (1, ZW, "w_wkv_out")):
                        for i in range(NB):
                            y_ = yt[i % 2]
                            yk = "yt%d" % (i % 2)
                            S.dma("sp", y_[:], src[t0 + i * 128:t0 + (i + 1) * 128, :], r=["BSRC"], w=[yk])
                            transposes_to(y_[:], yk, 12, lambda q, nj, i=i: hTa[:, q:q + nj, i * 128:(i + 1) * 128],
                                          "hTa")
                        proj(wbs, hTa, "hTa", 12, D, NB, epi_merge(br, t0), wload_plain(I[Wn][l]))
                    for i in range(NB):
                        ta = t0 + i * 128
                        s_ = ta // L
                        ti = (ta % L) // 128
                        tr = s_ * L + (LT - 1 - ti) * 128
                        yf = ls["xt"][0]
                        yb = ls["xt"][1]
                        uu = yt[i % 2]
                        uk = "yt%d" % (i % 2)
                        S.dma("sp", yf[:, 0:1024], YS5[0][ta:ta + 128, :], r=["YS5"], w=["xt0"])
                        S.dma("sp", yb[:, 0:1024], YS5[1][tr:tr + 128, :], r=["YS5"], w=["xt1"])
                        S.dma("sp", uu[:, 0:1024], P[ta:ta + 128, C_U:C_U + 1024], r=["P"], w=[uk])
                        S.op("dve", lambda e: e.tensor_mul(out=uu[:, 0:1024], in0=uu[:, 0:1024], in1=s5d[:]),
                             r=[uk, "s5d"], w=[uk])
                        for hb in range(2):
                            pb = 4 + hb
                            for j in range(4):
                                cb = hb * 4 + j
                                cs_ = slice(cb * 128, (cb + 1) * 128)
                                o_ = psb[pb][:, j * 128:(j + 1) * 128]
                                S.op("pe", lambda e: e.matmul(o_, lhsT=yf[:, cs_], rhs=ident[:], start=True,
                                                              stop=False), r=["xt0", "ident"], w=[PK[pb]])
                                S.op("pe", lambda e: e.matmul(o_, lhsT=yb[:, cs_], rhs=jrev[:], start=False,
                                                              stop=False), r=["xt1", "jrev"], w=[PK[pb]])
                                S.op("pe", lambda e: e.matmul(o_, lhsT=uu[:, cs_], rhs=ident[:], start=False,
                                                              stop=True), r=[uk, "ident"], w=[PK[pb]])
                            S.op("act", lambda e: e.activation(out=xg[:], in_=psb[pb][:, :], func=AF.Copy),
                                 r=[PK[pb]], w=["xg"])
                            S.op("dve", lambda e: e.tensor_mul(out=tg[:], in0=xg[:], in1=xg[:]), r=["xg"], w=["tg"])
                            S.op("dve", lambda e: e.tensor_scalar(out=tg[:], in0=tg[:], scalar1=0.044715, scalar2=1.0,
                                                                  op0=ALU.mult, op1=ALU.add), r=["tg"], w=["tg"])
                            S.op("dve", lambda e: e.tensor_mul(out=tg[:], in0=tg[:], in1=xg[:]), r=["tg", "xg"],
                                 w=["tg"])
                            S.op("act", lambda e: e.activation(out=tg[:], in_=tg[:], func=AF.Sigmoid,
                                                               scale=2.0 * math.sqrt(2.0 / math.pi)), r=["tg"],
                                 w=["tg"])
                            S.op("dve", lambda e: e.tensor_tensor(
                                out=hTa[:, hb * 4:(hb + 1) * 4, i * 128:(i + 1) * 128],
                                in0=xg[:].rearrange("p (j t) -> p j t", j=4),
                                in1=tg[:].rearrange("p (j t) -> p j t", j=4), op=ALU.mult), r=["xg", "tg"], w=["hTa"])
                    proj(wbs, hTa, "hTa", 8, D, NB, epi_merge(2, t0), wload_glu(I["w_s5_glu"][l]), cwmax=256, wmul=2)
                    for i in range(NB):
                        xt = ls["xt"][i % 2]
                        xk = "xt%d" % (i % 2)
                        S.dma("sp", xt[:], MG[t0 + i * 128:t0 + (i + 1) * 128, :],
                              r=[("MG", i, c) for c in range(4)],
                              w=[xk])
                        transposes_to(xt[:], xk, KC, lambda q, nj, i=i: hTa[:, q:q + nj, i * 128:(i + 1) * 128],
                                      "hTa")
                    proj(wbs, hTa, "hTa", KC, D, NB, epi_y, wload_plain(I["w_out"][l]))
                    postnorm_residual(ls, Ybuf, NB, t0, xsrc, xdst, GPt)
                S.barrier()

        def phase_ffn(l, T, GW, jrow, xsrc, xdst):
            NB = BLK // 128
            with contextlib.ExitStack() as ps:
                lsb = lambda name, shape, dt=F32: ps.enter_context(nc.sbuf_tensor(uname(name), list(shape), dt))
                ls = {"xt": [lsb("xt0", [128, D]), lsb("xt1", [128, D])], "junk": lsb("junk", [128, D]),
                      "ss": lsb("ss", [128, 1])}
                wbs = [lsb("wb0", [128, 8192], BF16), lsb("wb1", [128, 8192], BF16)]
                WSTG["t"] = lsb("wst", [128, 8192])
                st = [lsb("st0", [128, 512]), lsb("st1", [128, 512])]
                BI = min(BLK_IN, T)
                hT = lsb("hT", [128, KC, BI], BF16)
                Gt, SHt, GPt = mod_tiles(lsb, ls, l, jrow, 4, 3, 5, "norm_ffn_pre", "norm_ffn_post")
                cnt = [0]
                for bi in range(T // BI):
                    t0 = bi * BI
                    norm_to_fm(ls, xsrc, t0, BI // 128, hT, Gt, SHt)

                    def epi(i, c0, cw, pt, pk, t0=t0):
                        cnt[0] += 1
                        s_ = st[cnt[0] % 2]
                        sk = "st%d" % (cnt[0] % 2)
                        S.op("act", lambda e: e.activation(out=s_[:, 0:cw], in_=pt[:, 0:cw], func=AF.Copy),
                             r=[pk], w=[sk])
                        S.dma("sp", GU[t0 + i * 128:t0 + (i + 1) * 128, c0:c0 + cw], s_[:, 0:cw], r=[sk], w=["GU"])

                    proj(wbs, hT, "hT", KC, 2 * D_FF, BI // 128, epi, wload_plain(I["w_ffn_in"][l]))
                S.barrier()
            with contextlib.ExitStack() as ps:
                up = [ps.enter_context(nc.sbuf_tensor(uname("fup%d" % j), [128, CV], F32)) for j in range(2)]
                sg = ps.enter_context(nc.sbuf_tensor(uname("fsg"), [128, CV], F32))
                uc = [0]

                def post(i, c0, cw, y, yk):
                    u_ = up[uc[0] % 2]
                    uk = "fup%d" % (uc[0] % 2)
                    uc[0] += 1
                    S.dma("sp", u_[:, 0:cw], GU[i * 128:(i + 1) * 128, D_FF + c0:D_FF + c0 + cw], r=["GU"], w=[uk])
                    S.op("act", lambda e: e.activation(out=sg[:, 0:cw], in_=y[:, 0:cw], func=AF.Sigmoid), r=[yk],
                         w=["fsg"])
                    S.op("dve", lambda e: e.tensor_mul(out=y[:, 0:cw], in0=y[:, 0:cw], in1=sg[:, 0:cw]),
                         r=[yk, "fsg"], w=[yk])
                    S.op("dve", lambda e: e.tensor_mul(out=y[:, 0:cw], in0=y[:, 0:cw], in1=u_[:, 0:cw]),
                         r=[yk, uk], w=[yk])
                    S.dma("sp", ACTS[i * 128:(i + 1) * 128, c0:c0 + cw], y[:, 0:cw], r=[yk], w=["ACTS"])

                conv_pass(ps, T, GW, GU, 0, D_FF,
                          lambda c0, cw: (I["ffn_conv_w"][l, 0, c0:c0 + cw], I["ffn_conv_w"][l, 1, c0:c0 + cw],
                                          I["ffn_conv_w"][l, 2, c0:c0 + cw], I["ffn_conv_b"][l, c0:c0 + cw]), post)
                S.barrier()
            with contextlib.ExitStack() as ps:
                lsb = lambda name, shape, dt=F32: ps.enter_context(nc.sbuf_tensor(uname(name), list(shape), dt))
                ls = {"xt": [lsb("xt0", [128, D]), lsb("xt1", [128, D])], "junk": lsb("junk", [128, D]),
                      "ss": lsb("ss", [128, 1])}
                wbs = [lsb("wb0", [128, 5632], BF16), lsb("wb1", [128, 5632], BF16)]
                WSTG["t"] = lsb("wst", [128, 5632])
                hTf = lsb("hTf", [128, 44, BLK], BF16)
                Ybuf = lsb("Ybuf", [128, NB, D])
                GPt = lsb("GPt", [128, D])
                at = [lsb("at%d" % j, [128, 2816]) for j in range(2)]
                bcload(GPt[:], "GPt", MOD[jrow, 5 * D:6 * D])
                bcload(ls["junk"][:], "junk", I["norm_ffn_post"][l, :])
                S.op("dve", lambda e: e.tensor_mul(out=GPt[:], in0=GPt[:], in1=ls["junk"][:]), r=["GPt", "junk"],
                     w=["GPt"])

                def epi_y(i, c0, cw, pt, pk):
                    S.op("act", lambda e: e.activation(out=Ybuf[:, i, c0:c0 + cw], in_=pt[:, 0:cw], func=AF.Copy),
                         r=[pk], w=["Ybuf"])

                ac = 0
                for bi in range(T // BLK):
                    t0 = bi * BLK
                    for i in range(NB):
                        for hf in range(2):
                            a_ = at[ac % 2]
                            ak = "at%d" % (ac % 2)
                            ac += 1
                            S.dma("sp", a_[:], ACTS[t0 + i * 128:t0 + (i + 1) * 128, hf * 2816:(hf + 1) * 2816],
                                  r=["ACTS"], w=[ak])
                            transposes_to(a_[:], ak, 22,
                                          lambda q, nj, i=i, hf=hf: hTf[:, hf * 22 + q:hf * 22 + q + nj,
                                                                        i * 128:(i + 1) * 128], "hTf")
                    proj(wbs, hTf, "hTf", 44, D, NB, epi_y, wload_plain(I["w_ffn_out"][l]), cwmax=128)
                    postnorm_residual(ls, Ybuf, NB, t0, xsrc, xdst, GPt)
                S.barrier()

        kb.fns = dict(phase_mod=phase_mod, phase_inproj=phase_inproj, phase_convs=phase_convs, phase_ssd=phase_ssd)

        groups = debug.get("groups", ["s", "p"])
        skip = debug.get("skip", ())
        nl = debug.get("layers", DEPTH)
        for l in range(nl):
            phase_mod(l)
            for gname in groups:
                last = (l == DEPTH - 1)
                if gname == "s":
                    T, L, GW, jrow, sample = TS, TS, 64, 0, True
                    xin = I["xs"] if l == 0 else XB
                    xmid = XA
                    xout = O["ys"] if last else XB
                else:
                    T, L, GW, jrow, sample = TP, 256, 256, 1, False
                    xin = I["xp"] if l == 0 else XPB
                    xmid = XPA
                    xout = O["yp"] if last else XPB
                phase_inproj(l, xin, T, jrow)
                phase_convs(l, T, GW)
                if "ssd" not in skip:
                    phase_ssd(l, T, L, sample)
                if "wkv" not in skip:
                    wp = debug.get("wkv_parts", ("prep", "scan", "post"))
                    if "prep" in wp:
                        phase_wkv_prep(l, T)
                    if "scan" in wp:
                        phase_wkv_scan(l, T, L, sample)
                    if "post" in wp:
                        phase_wkv_post(l, T)
                if "s5" not in skip:
                    phase_s5(l, T, L, sample)
                if "tail" not in skip:
                    phase_tail(l, T, L, jrow, xin, xmid)
                if "ffn" not in skip:
                    phase_ffn(l, T, GW, jrow, xmid, xout)
        S.barrier()
    return nc, S


def _consts():
    k = np.arange(128)
    tri = (k[:, None] <= k[None, :]).astype(np.float32)
    m48 = np.zeros((48, 768), np.float32)
    for j in range(4):
        for g in range(12):
            m48[j * 12 + g, g * 64:(g + 1) * 64] = 1.0
    return {
        "ident": np.eye(128, dtype=np.float32), "tri": tri, "trit": np.ascontiguousarray(tri.T),
        "jrev": np.ascontiguousarray(np.eye(128, dtype=np.float32)[::-1]), "mask48": m48,
        "jj": np.ascontiguousarray(np.broadcast_to(np.arange(1, 513, dtype=np.float32), (128, 512))),
        "zrow": np.zeros((1, CV), np.float32),
    }


def _prep_inputs(inputs):
    f = lambda a: np.ascontiguousarray(np.asarray(a, dtype=np.float32))
    shared = {}
    for nm, shp in PARAM_SHAPES.items():
        a = f(inputs[nm]).reshape(shp)
        if nm in RELAID:
            a = np.stack([_relayout(a[l], RELAID[nm][2]) for l in range(DEPTH)], 0)
        elif nm == "w_s5_glu":
            a = np.stack([_relayout_glu(a[l]) for l in range(DEPTH)], 0)
        shared[nm] = a
    shared.update(_consts())
    maps = []
    for c in range(8):
        b = c % 4
        m = dict(shared)
        m["xs"] = f(inputs["x_sample"][b])
        m["xp"] = f(np.asarray(inputs["x_prompt"])[NSP * c:NSP * c + NSP].reshape(TP, D))
        m["cond2"] = f(np.stack([np.asarray(inputs["c"])[b], np.asarray(inputs["c_ctx"])], 0))
        m["st_ssm"] = f(inputs["state_ssm"][b])
        m["st_wkv"] = f(inputs["state_wkv"][b])
        m["st_s5re"] = f(inputs["state_s5_re"][b])
        m["st_s5im"] = f(inputs["state_s5_im"][b])
        maps.append(m)
    return maps


def kernel(**inputs):
    nc, S = build()
    maps = _prep_inputs(inputs)
    res = run_bass_kernel_spmd(nc, maps, core_ids=list(range(8)))
    r = res.results
    y_prompt = np.zeros((16, 256, D), np.float32)
    y_sample = np.zeros((4, TS, D), np.float32)
    ns_ssm = np.zeros((16, DEPTH, 2, 24, 64, 128), np.float32)
    ns_wkv = np.zeros((16, DEPTH, 2, 24, 64, 64), np.float32)
    ns_re = np.zeros((16, DEPTH, 2, 64, 64), np.float32)
    ns_im = np.zeros((16, DEPTH, 2, 64, 64), np.float32)
    for b in range(4):
        y_sample[b] = np.asarray(r[b]["ys"])
    for c in range(8):
        o = r[c]
        sl = slice(NSP * c, NSP * c + NSP)
        y_prompt[sl] = np.asarray(o["yp"]).reshape(NSP, 256, D)
        ns_ssm[sl] = np.asarray(o["ns_ssm"])
        ns_wkv[sl] = np.asarray(o["ns_wkv"])
        ns_re[sl] = np.asarray(o["ns_s5re"])
        ns_im[sl] = np.asarray(o["ns_s5im"])
    return (y_prompt, y_sample, ns_ssm, ns_wkv, ns_re, ns_im)
```
